# Optimizing a Trainium2 kernel written in Bass

```python
import math
import jax
import jax.numpy as jnp
from jax import lax
import numpy as np

D_MODEL = 1024
BATCH = 8
SEQ = 4096
DEPTH = 2
DEC_BATCH = 2
DEC_SEQ = 8192
PAST_LEN = 128

F32 = jnp.float32
D_PLE = 256
BRANCH_WIDTH = 512
N_BRANCH = 3
DN_HEADS = 4
DN_HEAD_DIM = 128
DN_WIDTH = DN_HEADS * DN_HEAD_DIM
DN_CHUNK = 64
HG_HEADS = 4
HG_HEAD_DIM = 128
HG_WIDTH = HG_HEADS * HG_HEAD_DIM
HG_CHUNK = 64
LB_FLOOR = 1e-30
LRU_WIDTH = 512
LRU_BLOCKS = 8
LRU_BLOCK_DIM = LRU_WIDTH // LRU_BLOCKS
LRU_C = 8.0
CONV_WIDTH = 4
CONV_PAD = (1, 2)
D_FF = 4 * D_MODEL
ALPHA = (2.0 * DEPTH) ** 0.25
OUT_SCALE = (8.0 * DEPTH) ** -0.25
LN_EPS = 1e-5
RMS_EPS = 1e-6
L2_EPS = 1e-6
IN_SPLITS = (DN_WIDTH, DN_WIDTH, DN_WIDTH, DN_WIDTH, DN_HEADS, DN_HEADS, DN_HEADS, DN_HEADS,
             HG_WIDTH, HG_WIDTH, HG_WIDTH, HG_WIDTH, HG_WIDTH,
             LRU_WIDTH, LRU_WIDTH,
             D_MODEL, D_MODEL, D_MODEL)
N_IN = 4 * DN_WIDTH + 4 * DN_HEADS + 5 * HG_WIDTH + 2 * LRU_WIDTH + N_BRANCH * D_MODEL

kernel_name = 'hybrid_bidir_deltanet_hgrn2_rglru_encoder'


def _split_cols(h):
    offs = []
    acc = 0
    for s in IN_SPLITS[:-1]:
        acc += s
        offs.append(acc)
    return jnp.split(h, offs, axis=-1)


def _flip(t):
    return jnp.flip(t, axis=1)


def _layer_norm(x, g, b):
    xf = x.astype(F32)
    mu = jnp.mean(xf, -1, keepdims=True)
    xc = xf - mu
    var = jnp.mean(xc * xc, -1, keepdims=True)
    return (xc * lax.rsqrt(var + LN_EPS) * g.astype(F32) + b.astype(F32)).astype(x.dtype)


def _gated_rms_norm(o, z, w):
    o = o * lax.rsqrt(jnp.mean(o * o, -1, keepdims=True) + RMS_EPS)
    return o * w.astype(F32) * jax.nn.silu(z)


def _l2norm(t):
    return t * lax.rsqrt(jnp.sum(t * t, -1, keepdims=True) + L2_EPS)


def _depthwise_conv(x, w):
    return lax.conv_general_dilated(x, w[:, None, :].astype(x.dtype), window_strides=(1,), padding=[CONV_PAD],
                                    dimension_numbers=('NWC', 'WIO', 'NWC'), feature_group_count=x.shape[-1])


def _to_chunks(t, C):
    B, T, H = t.shape[:3]
    t = t.reshape((B, T // C, C, H) + t.shape[3:])
    return jnp.moveaxis(t, 3, 1)


def _from_chunks(o):
    N, B, H, C, D = o.shape
    return jnp.transpose(o, (1, 0, 3, 2, 4)).reshape(B, N * C, H, D)


def _masked_exp(diff, mask):
    return jnp.where(mask, jnp.exp(jnp.where(mask, diff, 0.0)), 0.0)


def _gated_delta_chunked(q, k, v, g, beta):
    B, T, H, Dk = q.shape
    Dv = v.shape[-1]
    C = DN_CHUNK
    q, k, v = _to_chunks(q, C), _to_chunks(k, C), _to_chunks(v, C)
    g, beta = _to_chunks(g, C), _to_chunks(beta, C)
    g = jnp.cumsum(g, axis=-1)
    causal = jnp.tril(jnp.ones((C, C), bool))
    strict = jnp.tril(jnp.ones((C, C), bool), -1)
    decay = _masked_exp(g[..., :, None] - g[..., None, :], causal)
    k_beta = k * beta[..., None]
    v_beta = v * beta[..., None]
    L = jnp.where(strict, jnp.einsum('bhnid,bhnjd->bhnij', k_beta, k) * decay, 0.0)
    eye = jnp.eye(C, dtype=q.dtype)
    t_inv = lax.linalg.triangular_solve(eye + L, jnp.broadcast_to(eye, L.shape), left_side=True, lower=True)
    u = t_inv @ v_beta
    w = t_inv @ (k_beta * jnp.exp(g)[..., None])
    attn = jnp.einsum('bhnid,bhnjd->bhnij', q, k) * decay
    q_dec = q * jnp.exp(g)[..., None]
    k_dec = k * jnp.exp(g[..., -1:] - g)[..., None]
    g_last = jnp.exp(g[..., -1])

    def step(S, inp):
        u_n, w_n, attn_n, qd_n, kd_n, gl_n = inp
        v_new = u_n - w_n @ S
        o = qd_n @ S + attn_n @ v_new
        S = S * gl_n[..., None, None] + jnp.swapaxes(kd_n, -1, -2) @ v_new
        return S, o

    xs = (jnp.moveaxis(u, 2, 0), jnp.moveaxis(w, 2, 0), jnp.moveaxis(attn, 2, 0),
          jnp.moveaxis(q_dec, 2, 0), jnp.moveaxis(k_dec, 2, 0), jnp.moveaxis(g_last, 2, 0))
    S0 = jnp.zeros((B, H, Dk, Dv), q.dtype)
    _, o = lax.scan(step, S0, xs)
    return _from_chunks(o)


def _hgrn2_chunked(q, k, v, log_f):
    B, T, H, Dk = q.shape
    Dv = v.shape[-1]
    C = HG_CHUNK
    q, k, v = _to_chunks(q, C), _to_chunks(k, C), _to_chunks(v, C)
    b = jnp.cumsum(_to_chunks(log_f, C), axis=3)
    causal = jnp.tril(jnp.ones((C, C), bool))[:, :, None]

    def step(S, inp):
        q_n, k_n, v_n, b_n = inp
        dec = _masked_exp(b_n[..., :, None, :] - b_n[..., None, :, :], causal)
        A = jnp.einsum('bhid,bhjd,bhijd->bhij', q_n, k_n, dec)
        o = (q_n * jnp.exp(b_n)) @ S + A @ v_n
        S = (jnp.exp(b_n[..., -1, :])[..., None] * S
             + jnp.swapaxes(k_n * jnp.exp(b_n[..., -1:, :] - b_n), -1, -2) @ v_n)
        return S, o

    xs = (jnp.moveaxis(q, 2, 0), jnp.moveaxis(k, 2, 0), jnp.moveaxis(v, 2, 0), jnp.moveaxis(b, 2, 0))
    S0 = jnp.zeros((B, H, Dk, Dv), q.dtype)
    _, o = lax.scan(step, S0, xs)
    return _from_chunks(o)


def _lin_combine(left, right):
    a1, b1 = left
    a2, b2 = right
    return a1 * a2, a2 * b1 + b2


def _deltanet_branch(q, k, v, z, a_f, a_b, b_f, b_b, conv_w, A_log, dt_bias, norm_w):
    B, T, _ = q.shape
    qkv = jax.nn.silu(_depthwise_conv(jnp.concatenate([q, k, v], axis=-1), conv_w).astype(F32))
    q, k, v = jnp.split(qkv, 3, axis=-1)

    def heads(t):
        return t.astype(F32).reshape(B, T, DN_HEADS, DN_HEAD_DIM)

    q = _l2norm(heads(q)) * (DN_HEAD_DIM ** -0.5)
    k = _l2norm(heads(k))
    v = heads(v)

    def log_decay(a, d):
        return -jnp.exp(A_log[d].astype(F32)) * jax.nn.softplus(a.astype(F32) + dt_bias[d].astype(F32))

    g_f, g_b = log_decay(a_f, 0), log_decay(a_b, 1)
    beta_f, beta_b = jax.nn.sigmoid(b_f.astype(F32)), jax.nn.sigmoid(b_b.astype(F32))
    o_f = _gated_delta_chunked(q, k, v, g_f, beta_f)
    o_b = _flip(_gated_delta_chunked(_flip(q), _flip(k), _flip(v), _flip(g_b), _flip(beta_b)))
    o = _gated_rms_norm(o_f + o_b, heads(z), norm_w)
    return o.reshape(B, T, DN_WIDTH)


def _hgrn2_branch(q, f_f, f_b, i, g, lb, norm_w):
    B, T, _ = q.shape

    def heads(t):
        return t.astype(F32).reshape(B, T, HG_HEADS, HG_HEAD_DIM)

    log_lb = jnp.log(jnp.maximum(lb, LB_FLOOR))

    def gates(fz):
        fz = fz.astype(F32)
        log_f = jnp.logaddexp(jax.nn.log_sigmoid(fz), log_lb + jax.nn.log_sigmoid(-fz))
        k = (1.0 - lb) * jax.nn.sigmoid(-fz)
        return heads(log_f), heads(k)

    q, v = heads(q), heads(i)
    lf_f, k_f = gates(f_f)
    lf_b, k_b = gates(f_b)
    o = _hgrn2_chunked(q, k_f, v, lf_f) + _flip(_hgrn2_chunked(_flip(q), _flip(k_b), _flip(v), _flip(lf_b)))
    o = _gated_rms_norm(o, heads(g), norm_w)
    return o.reshape(B, T, HG_WIDTH)


def _rglru_branch(xc, gate, conv_w, conv_b, wa, ba, wx, bx, lam):
    B, T, _ = xc.shape
    xc = (_depthwise_conv(xc, conv_w) + conv_b).astype(F32)
    xb = xc.reshape(B, T, LRU_BLOCKS, LRU_BLOCK_DIM)

    def block_diag(w, b):
        return jnp.einsum('btki,kij->btkj', xb, w.astype(F32)).reshape(B, T, LRU_WIDTH) + b.astype(F32)

    def direction(d, reverse):
        r = jax.nn.sigmoid(block_diag(wa[d], ba[d]))
        i = jax.nn.sigmoid(block_diag(wx[d], bx[d]))
        log_a = LRU_C * r * jax.nn.log_sigmoid(lam[d].astype(F32))
        a = jnp.exp(log_a)
        u = jnp.sqrt(jnp.maximum(-jnp.expm1(2.0 * log_a), 0.0)) * (i * xc)
        _, h = lax.associative_scan(_lin_combine, (a, u), reverse=reverse, axis=1)
        return h

    h = direction(0, False) + direction(1, True)
    return h * jax.nn.gelu(gate.astype(F32))


def _layer(x, p_l, lb, l, W):
    h = jnp.einsum('btd,de->bte', x, W['w_in'][l])
    (dq, dk, dv, dz, da_f, da_b, db_f, db_b,
     hq, hf_f, hf_b, hi, hgt,
     cx, cg, gA, gB, gC) = _split_cols(h)
    o_dn = _deltanet_branch(dq, dk, dv, dz, da_f, da_b, db_f, db_b,
                            W['dn_conv_w'][l], W['dn_A_log'][l], W['dn_dt_bias'][l], W['dn_norm_w'][l])
    o_hg = _hgrn2_branch(hq, hf_f, hf_b, hi, hgt, lb, W['hg_norm_w'][l])
    o_lru = _rglru_branch(cx, cg, W['lru_conv_w'][l], W['lru_conv_b'][l], W['lru_wa'][l], W['lru_ba'][l],
                          W['lru_wx'][l], W['lru_bx'][l], W['lru_lambda'][l])
    ob = jnp.stack([o_dn, o_hg, o_lru], axis=2).astype(x.dtype)
    branches = jnp.einsum('btnc,ncd->btnd', ob, W['w_branch'][l])
    gates = jax.nn.sigmoid(jnp.stack([gA, gB, gC], axis=2))
    mix = jnp.einsum('btd,de->bte', jnp.sum(gates * branches, axis=2), W['w_out'][l])
    x1 = _layer_norm(ALPHA * x + mix, W['ln1_g'][l], W['ln1_b'][l])
    ff = jnp.einsum('btf,fd->btd', jnp.square(jax.nn.relu(jnp.einsum('btd,df->btf', x1, W['w_mlp1'][l]))),
                    W['w_mlp2'][l])
    ple = (jax.nn.sigmoid(jnp.einsum('btd,de->bte', x1, W['w_ple_gate'][l]))
           * jnp.einsum('btp,pd->btd', p_l, W['w_ple_proj'][l]))
    return _layer_norm(ALPHA * x1 + ff + ple, W['ln2_g'][l], W['ln2_b'][l])


def _trunk(x, p, W):
    sm = jax.nn.softmax(W['hg_lb_logits'].astype(F32), axis=0)
    lbs = jnp.maximum(jnp.cumsum(sm, axis=0) - sm[0], 0.0)
    x = _layer_norm(x, W['emb_ln_g'], W['emb_ln_b'])
    for l in range(DEPTH):
        x = _layer(x, p[l], lbs[l], l, W)
    return x


def setup_inputs(seed: int = 0) -> dict:
    key = jax.random.key(seed)
    ks = iter(jax.random.split(key, 32))
    L = DEPTH

    def nrm(shape, scale):
        return jax.random.normal(next(ks), shape, F32) * scale

    x_prompt = nrm((BATCH, SEQ, D_MODEL), 1.0)
    x_sample = nrm((DEC_BATCH, DEC_SEQ, D_MODEL), 1.0)
    p_prompt = nrm((DEPTH, BATCH, SEQ, D_PLE), 1.0)
    p_sample = nrm((DEPTH, DEC_BATCH, DEC_SEQ, D_PLE), 1.0)
    emb_ln_g = 1.0 + nrm((D_MODEL,), 0.02)
    emb_ln_b = nrm((D_MODEL,), 0.02)
    w_in = nrm((L, D_MODEL, N_IN), D_MODEL ** -0.5)
    dn_conv_w = nrm((L, CONV_WIDTH, 3 * DN_WIDTH), CONV_WIDTH ** -0.5)
    dn_A_log = jnp.log(jax.random.uniform(next(ks), (L, 2, DN_HEADS), F32, 1.0, 16.0))
    dt = jnp.exp(jax.random.uniform(next(ks), (L, 2, DN_HEADS), F32, math.log(1e-3), math.log(1e-1)))
    dn_dt_bias = dt + jnp.log(-jnp.expm1(-dt))
    dn_norm_w = 1.0 + nrm((L, DN_HEAD_DIM), 0.02)
    hg_lb_logits = nrm((L, HG_WIDTH), 0.5)
    hg_norm_w = 1.0 + nrm((L, HG_HEAD_DIM), 0.02)
    lru_conv_w = nrm((L, CONV_WIDTH, LRU_WIDTH), CONV_WIDTH ** -0.5)
    lru_conv_b = nrm((L, LRU_WIDTH), 0.02)
    lru_wa = nrm((L, 2, LRU_BLOCKS, LRU_BLOCK_DIM, LRU_BLOCK_DIM), LRU_BLOCK_DIM ** -0.5)
    lru_ba = nrm((L, 2, LRU_WIDTH), 0.02)
    lru_wx = nrm((L, 2, LRU_BLOCKS, LRU_BLOCK_DIM, LRU_BLOCK_DIM), LRU_BLOCK_DIM ** -0.5)
    lru_bx = nrm((L, 2, LRU_WIDTH), 0.02)
    a_c = jax.random.uniform(next(ks), (L, 2, LRU_WIDTH), F32, 0.9, 0.999)
    s = a_c ** (1.0 / LRU_C)
    lru_lambda = jnp.log(s) - jnp.log1p(-s)
    w_branch = nrm((L, N_BRANCH, BRANCH_WIDTH, D_MODEL), BRANCH_WIDTH ** -0.5)
    w_out = nrm((L, D_MODEL, D_MODEL), OUT_SCALE * D_MODEL ** -0.5)
    ln1_g = 1.0 + nrm((L, D_MODEL), 0.02)
    ln1_b = nrm((L, D_MODEL), 0.02)
    ln2_g = 1.0 + nrm((L, D_MODEL), 0.02)
    ln2_b = nrm((L, D_MODEL), 0.02)
    w_mlp1 = nrm((L, D_MODEL, D_FF), D_MODEL ** -0.5)
    w_mlp2 = nrm((L, D_FF, D_MODEL), OUT_SCALE * D_FF ** -0.5)
    w_ple_gate = nrm((L, D_MODEL, D_MODEL), D_MODEL ** -0.5)
    w_ple_proj = nrm((L, D_PLE, D_MODEL), OUT_SCALE * D_PLE ** -0.5)
    return {'x_prompt': x_prompt, 'x_sample': x_sample, 'p_prompt': p_prompt, 'p_sample': p_sample,
            'emb_ln_g': emb_ln_g, 'emb_ln_b': emb_ln_b, 'w_in': w_in,
            'dn_conv_w': dn_conv_w, 'dn_A_log': dn_A_log, 'dn_dt_bias': dn_dt_bias, 'dn_norm_w': dn_norm_w,
            'hg_lb_logits': hg_lb_logits, 'hg_norm_w': hg_norm_w,
            'lru_conv_w': lru_conv_w, 'lru_conv_b': lru_conv_b, 'lru_wa': lru_wa, 'lru_ba': lru_ba,
            'lru_wx': lru_wx, 'lru_bx': lru_bx, 'lru_lambda': lru_lambda,
            'w_branch': w_branch, 'w_out': w_out,
            'ln1_g': ln1_g, 'ln1_b': ln1_b, 'ln2_g': ln2_g, 'ln2_b': ln2_b,
            'w_mlp1': w_mlp1, 'w_mlp2': w_mlp2, 'w_ple_gate': w_ple_gate, 'w_ple_proj': w_ple_proj}


def reference(x_prompt, x_sample, p_prompt, p_sample, emb_ln_g, emb_ln_b, w_in,
              dn_conv_w, dn_A_log, dn_dt_bias, dn_norm_w, hg_lb_logits, hg_norm_w,
              lru_conv_w, lru_conv_b, lru_wa, lru_ba, lru_wx, lru_bx, lru_lambda,
              w_branch, w_out, ln1_g, ln1_b, ln2_g, ln2_b, w_mlp1, w_mlp2, w_ple_gate, w_ple_proj):
    W = dict(emb_ln_g=emb_ln_g, emb_ln_b=emb_ln_b, w_in=w_in,
             dn_conv_w=dn_conv_w, dn_A_log=dn_A_log, dn_dt_bias=dn_dt_bias, dn_norm_w=dn_norm_w,
             hg_lb_logits=hg_lb_logits, hg_norm_w=hg_norm_w,
             lru_conv_w=lru_conv_w, lru_conv_b=lru_conv_b, lru_wa=lru_wa, lru_ba=lru_ba,
             lru_wx=lru_wx, lru_bx=lru_bx, lru_lambda=lru_lambda,
             w_branch=w_branch, w_out=w_out, ln1_g=ln1_g, ln1_b=ln1_b, ln2_g=ln2_g, ln2_b=ln2_b,
             w_mlp1=w_mlp1, w_mlp2=w_mlp2, w_ple_gate=w_ple_gate, w_ple_proj=w_ple_proj)
    y_prompt = _trunk(x_prompt, p_prompt, W)
    y_sample = _trunk(x_sample, p_sample, W)
    return (y_prompt, y_sample)
```

```python
import numpy as np
from contextlib import ExitStack
import concourse.bass as bass
import concourse.mybir as mybir
from concourse.bass_utils import run_bass_kernel_spmd

F32 = mybir.dt.float32
BF16 = mybir.dt.bfloat16
AF = mybir.ActivationFunctionType
ALU = mybir.AluOpType

D = 1024
NIN = 8720
DFF = 4096
DPLE = 256
DEPTH = 2
ALPHA = (2.0 * DEPTH) ** 0.25
NB = 512
NCORES = 8
DQ, DK, DV, DZ, HQ, HFF, HFB, HI, HGT, CX, CG, GA, GB, GC, AB = 0, 4, 8, 12, 16, 20, 24, 28, 32, 36, 40, 44, 52, 60, 68
NHT = 69
WIN_TILES = [(i * 128, 128) for i in range(16)] + [(2064 + 128 * i, 128) for i in range(52)] + [(2048, 16)]

def _consts():
    i = np.arange(128)
    c = {}
    c['ident'] = np.eye(128)
    c['ones'] = np.ones((128, 128))
    c['negones'] = -np.ones((128, 128))
    c['tri_le'] = (i[:, None] <= i[None, :]) * 1.0
    c['tri_ge'] = (i[:, None] >= i[None, :]) * 1.0
    c['tri_gt'] = (i[:, None] > i[None, :]) * 1.0
    c['tri_lt'] = (i[:, None] < i[None, :]) * 1.0
    c['nm_f'] = np.where(i[:, None] < i[None, :], -30000.0, 0.0)
    c['nm_b'] = np.where(i[:, None] > i[None, :], -30000.0, 0.0)
    c['sm_f4'] = np.tile((i[:, None] > i[None, :]) * 1.0, (1, 4))
    c['sm_b4'] = np.tile((i[:, None] < i[None, :]) * 1.0, (1, 4))
    c['ident4'] = np.tile(np.eye(128), (1, 4))
    j = np.arange(64)
    c['hgm_f4'] = np.tile((j[None, :] >= (i[:, None] % 64)) * 1.0, (1, 4))
    c['hgm_b4'] = np.tile((j[None, :] <= (i[:, None] % 64)) * 1.0, (1, 4))
    t = np.arange(NB)
    c['rst_f'] = np.tile(((t % 64) != 0) * 1.0, (128, 1))
    c['rst_b'] = np.tile(((t % 64) != 63) * 1.0, (128, 1))
    offs = {}
    o = 0
    arrs = []
    for k, v in c.items():
        offs[k] = (o, v.shape[1])
        o += v.shape[1]
        arrs.append(v.astype(np.float32))
    return np.ascontiguousarray(np.concatenate(arrs, axis=1)), offs


CONST_ARR, CONST_OFF = _consts()
CP = {}
_o = 0
for _n, _w in [('dn_conv', 48), ('lru_conv', 16), ('lru_cb', 4), ('lru_ba', 8), ('lru_bx', 8), ('lru_lam', 8),
               ('dn_nw', 1), ('hg_nw', 1), ('ln1_g', 8), ('ln1_b', 8), ('ln2_g', 8), ('ln2_b', 8),
               ('lb0', 4), ('lb1', 4), ('emb_g', 8), ('emb_b', 8), ('alog', 8), ('dtb', 8)]:
    CP[_n] = (_o, _w)
    _o += _w
NCP = _o


class View:
    def __init__(self, bufs, ap):
        self.bufs = bufs
        self.ap = ap

    def __getitem__(self, idx):
        return View(self.bufs, self.ap[idx])


class Buf:
    def __init__(self, t=None, name=""):
        self.t = t
        self.w = {}
        self.r = {}
        self.name = name

    def __getitem__(self, idx):
        return View([self], self.t[idx])

    def v(self, ap):
        return View([self], ap)


class Eng:
    def __init__(self, h, sid, name):
        self.h = h
        self.sid = sid
        self.cnt = 0
        self.waited = {}
        self.name = name
        self.old = []


class KB:
    NDS = 48
    EPOCH = 20000

    def __init__(self, nc, stack):
        self.nc = nc
        self.sems = {}
        self.nsid = 0

        def mk(n):
            s = stack.enter_context(nc.semaphore(n))
            sid = self.nsid
            self.nsid += 1
            self.sems[sid] = s
            return sid
        self.mk = mk
        self.PE = Eng(nc.tensor, mk("s_pe"), "pe")
        self.DVE = Eng(nc.vector, mk("s_dve"), "dve")
        self.ACT = Eng(nc.scalar, mk("s_act"), "act")
        self.POOL = Eng(nc.gpsimd, mk("s_pool"), "pool")
        self.SP = Eng(nc.sync, None, "sp")
        self.engs = [self.PE, self.DVE, self.ACT, self.POOL]
        self.dsid = [mk("s_dma%d" % i) for i in range(self.NDS)]
        self.dval = [0] * self.NDS
        self.dnext = 0
        self.dbufs = {}
        self.ninst = 0

    def db(self, name, blk):
        k = (name, blk)
        if k not in self.dbufs:
            self.dbufs[k] = Buf(None, "%s_%s" % (name, blk))
        return self.dbufs[k]

    def _wait(self, eng, sid, val):
        if val <= 0 or eng.waited.get(sid, 0) >= val:
            return
        eng.h.wait_ge(self.sems[sid], val)
        eng.waited[sid] = val

    def _sync(self, eng, reads, writes):
        own = eng.sid
        for b in reads:
            for sid, v in b.w.items():
                self._wait(eng, sid, v)
        for b in writes:
            for sid, v in b.w.items():
                if sid != own:
                    self._wait(eng, sid, v)
            for sid, v in b.r.items():
                if sid != own:
                    self._wait(eng, sid, v)

    def _mark(self, reads, writes, sid, val):
        for b in reads:
            b.r[sid] = max(b.r.get(sid, 0), val)
        for b in writes:
            b.w = {sid: val}
            b.r = {}

    def E(self, eng, meth, **kw):
        reads, writes, args = [], [], {}
        for k_, v in kw.items():
            if isinstance(v, View):
                if k_ in ('out', 'accum_out', 'ap'):
                    writes.extend(v.bufs)
                else:
                    reads.extend(v.bufs)
                args[k_] = v.ap
            else:
                args[k_] = v
        if eng.cnt >= self.EPOCH:
            eng.old.append((eng.sid, eng.cnt))
            eng.sid = self.mk("s_%s_e%d" % (eng.name, len(eng.old)))
            eng.cnt = 0
        self._sync(eng, reads, writes)
        inst = getattr(eng.h, meth)(**args)
        inst.then_inc(self.sems[eng.sid], 1)
        eng.cnt += 1
        self.ninst += 1
        self._mark(reads, writes, eng.sid, eng.cnt)
        return inst

    def mm(self, out, lhsT, rhs, start=True, stop=True, extra_w=()):
        return self.E(self.PE, 'matmul', out=out, lhsT=lhsT, rhs=rhs, start=start, stop=stop)

    def tr(self, out, in_, ident):
        return self.E(self.PE, 'transpose', out=out, in_=in_, identity=ident)

    def dma(self, out, in_, q=None):
        eng = self.SP
        s = self.dnext
        self.dnext = (self.dnext + 1) % self.NDS
        sid = self.dsid[s]
        self._wait(eng, sid, self.dval[s])
        self._sync(eng, in_.bufs, out.bufs)
        self.dval[s] += 16
        eng.h.dma_start(out=out.ap, in_=in_.ap).then_inc(self.sems[sid], 16)
        self.ninst += 1
        self._mark(in_.bufs, out.bufs, sid, self.dval[s])

    def barrier(self):
        for e in self.engs + [self.SP]:
            for o in self.engs:
                if o is not e:
                    if o.cnt > 0:
                        self._wait(e, o.sid, o.cnt)
                    elif o.old:
                        self._wait(e, o.old[-1][0], o.old[-1][1])
            for s in range(self.NDS):
                self._wait(e, self.dsid[s], self.dval[s])


def bc_last(ap, n):
    l = [list(x) for x in ap.ap]
    return bass.AP(ap.tensor, ap.offset, l + [[0, n]])


def bc_mid(ap, n):
    l = [list(x) for x in ap.ap]
    return bass.AP(ap.tensor, ap.offset, [l[0], [0, n]] + l[1:])


def build(T, SEG, phases=None, debug_out=(), nlayers=DEPTH):
    NBLK = T // NB
    nc = bass.Bass("TRN2", target_bir_lowering=False)
    dt = nc.dram_tensor
    x_in = dt("x", [T, D], F32, kind="ExternalInput").ap()
    p_in = dt("p", [DEPTH, T, DPLE], F32, kind="ExternalInput").ap()
    cst_in = dt("cst", list(CONST_ARR.shape), F32, kind="ExternalInput").ap()
    cp_in = dt("cp", [DEPTH, 128, NCP], F32, kind="ExternalInput").ap()
    bd_in = dt("bd", [DEPTH, 128, 16, 128], F32, kind="ExternalInput").ap()
    flag_in = dt("flag", [128, 1], F32, kind="ExternalInput").ap()
    w_in = dt("w_in", [DEPTH, D, NIN], F32, kind="ExternalInput").ap()
    w_branch = dt("w_branch", [DEPTH, 3, 512, D], F32, kind="ExternalInput").ap()
    w_out = dt("w_out", [DEPTH, D, D], F32, kind="ExternalInput").ap()
    w_mlp1 = dt("w_mlp1", [DEPTH, D, DFF], F32, kind="ExternalInput").ap()
    w_mlp2 = dt("w_mlp2", [DEPTH, DFF, D], F32, kind="ExternalInput").ap()
    w_pg = dt("w_ple_gate", [DEPTH, D, D], F32, kind="ExternalInput").ap()
    w_pp = dt("w_ple_proj", [DEPTH, DPLE, D], F32, kind="ExternalInput").ap()
    y_out = dt("y", [T, D], F32, kind="ExternalOutput").ap()
    dbg = {}
    okind = lambda n: "ExternalOutput" if n in debug_out else "Internal"
    xT = dt("xT", [8, 128, T], F32, kind=okind("xT")).ap()
    class _HT:
        SPLIT = 36

        def __init__(self):
            self.a = dt("hT", [self.SPLIT, 128, T], F32, kind=okind("hT")).ap()
            self.b = dt("hTb", [NHT - self.SPLIT, 128, T], F32, kind=okind("hT")).ap()

        def __getitem__(self, idx):
            f = idx[0]
            rest = tuple(idx[1:])
            if isinstance(f, slice):
                if f.start < self.SPLIT:
                    assert f.stop <= self.SPLIT
                    return self.a[(f,) + rest]
                return self.b[(slice(f.start - self.SPLIT, f.stop - self.SPLIT),) + rest]
            if f < self.SPLIT:
                return self.a[(f,) + rest]
            return self.b[(f - self.SPLIT,) + rest]
    hT = _HT()
    ofT = dt("ofT", [12, 128, T], F32, kind=okind("ofT")).ap()
    obT = dt("obT", [12, 128, T], BF16, kind=okind("obT")).ap()
    wspec = {
        'win': (8, WIN_TILES, lambda l: w_in[l]),
        'wb0': (4, [(i * 128, 128) for i in range(8)], lambda l: w_branch[l, 0]),
        'wb1': (4, [(i * 128, 128) for i in range(8)], lambda l: w_branch[l, 1]),
        'wb2': (4, [(i * 128, 128) for i in range(8)], lambda l: w_branch[l, 2]),
        'wout': (8, [(i * 128, 128) for i in range(8)], lambda l: w_out[l]),
        'wm1': (8, [(i * 128, 128) for i in range(32)], lambda l: w_mlp1[l]),
        'wm2': (32, [(i * 128, 128) for i in range(8)], lambda l: w_mlp2[l]),
        'wpg': (8, [(i * 128, 128) for i in range(8)], lambda l: w_pg[l]),
        'wpp': (2, [(i * 128, 128) for i in range(8)], lambda l: w_pp[l]),
    }
    ws = {n: dt("ws_" + n, [DEPTH, len(s[1]), 128, s[0], 128], BF16, kind="Internal").ap() for n, s in wspec.items()}

    with ExitStack() as st0:
        kb = KB(nc, st0)
        PE, DVE, ACT, POOL = kb.PE, kb.DVE, kb.ACT, kb.POOL
        E = kb.E

        uniq = [0]

        def sb(st, name, shape, dtype=F32):
            uniq[0] += 1
            return Buf(st.enter_context(nc.sbuf_tensor("sb%d_%s" % (uniq[0], name), shape, dtype)), name)

        ps = [Buf(st0.enter_context(nc.psum_tensor("ps%d" % i, [128, 512], F32)), "ps%d" % i) for i in range(8)]
        cst = sb(st0, "cst", list(CONST_ARR.shape))
        cpt = [sb(st0, "cp%d" % l, [128, NCP]) for l in range(DEPTH)]
        flag = sb(st0, "flag", [128, 1])
        cbf = sb(st0, "cbf", [128, 128 * 2 + 512 * 3 + 256 * 2], BF16)
        kb.dma(cst[:], View([], cst_in[:, :]))
        for l in range(DEPTH):
            kb.dma(cpt[l][:], View([], cp_in[l]))
        kb.dma(flag[:], View([], flag_in[:, :]))

        def C(name):
            o, w = CONST_OFF[name]
            return cst[:, o:o + w]

        cb_off = {}
        o = 0
        for n in ['ident', 'ones', 'sm_f4', 'sm_b4', 'ident4', 'hgm_f4', 'hgm_b4']:
            w = CONST_OFF[n][1]
            cb_off[n] = (o, w)
            E(DVE, 'tensor_copy', out=cbf[:, o:o + w], in_=C(n))
            o += w

        def CB(name):
            o, w = cb_off[name]
            return cbf[:, o:o + w]

        def cpv(l, name, i=0, n=1):
            o, w = CP[name]
            return cpt[l][:, o + i:o + i + n]

        def phase_wprep():
            with ExitStack() as st:
                sf = [sb(st, "wpf%d" % i, [128, 4096]) for i in range(2)]
                sbf = [sb(st, "wpb%d" % i, [128, 4096], BF16) for i in range(2)]
                it = 0
                for l in range(DEPTH):
                    for name, (n_k, tiles, srcf) in wspec.items():
                        src = srcf(l).rearrange("(k p) n -> p k n", p=128)
                        gmax = 4 if n_k <= 8 else 1
                        ti = 0
                        while ti < len(tiles):
                            g = 1
                            while (g < gmax and ti + g < len(tiles) and tiles[ti + g][1] == 128 and tiles[ti][1] == 128
                                   and tiles[ti + g][0] == tiles[ti][0] + 128 * g):
                                g += 1
                            c0 = tiles[ti][0]
                            gw = sum(tiles[ti + j][1] for j in range(g))
                            s = it % 2
                            it += 1
                            fv = sf[s].t[:, 0:n_k * gw].rearrange("p (k c) -> p k c", k=n_k)
                            bv = sbf[s].t[:, 0:n_k * gw].rearrange("p (k c) -> p k c", k=n_k)
                            kb.dma(sf[s].v(fv), View([], src[:, :, c0:c0 + gw]))
                            E(POOL if it % 2 else ACT, 'tensor_copy' if it % 2 else 'copy', out=sbf[s].v(bv), in_=sf[s].v(fv))
                            for j in range(g):
                                wdt = tiles[ti + j][1]
                                kb.dma(View([kb.db('ws_' + name, l)], ws[name][l, ti + j, :, :, 0:wdt]),
                                       sbf[s].v(bv[:, :, j * 128:j * 128 + wdt]))
                            ti += g
            kb.barrier()

        wctr = [0]
        pctr = [0]

        def dense(wb, l, name, tile_ids, rhs_fn, N, evac, psbanks=(0, 1, 2)):
            n_k = wspec[name][0]
            tiles = wspec[name][1]
            for ti in tile_ids:
                wdt = tiles[ti][1]
                s = wctr[0] % len(wb)
                wctr[0] += 1
                kb.dma(wb[s][:, 0:n_k, :], View([kb.db('ws_' + name, l)], ws[name][l, ti]))
                pb = ps[psbanks[pctr[0] % len(psbanks)]]
                pctr[0] += 1
                for k in range(n_k):
                    kb.mm(pb[:, 0:N], wb[s][:, k, :], rhs_fn(k), start=(k == 0), stop=(k == n_k - 1))
                evac(ti, pb[0:wdt, 0:N], wdt)

        def layer_norm(st_bufs, src, g_fn, b_fn, dst_f=None, dst_b=None, N=NB):
            sq, xb_, mean, m2, rstd, tmp = st_bufs
            pm, pq = ps[3], ps[4]
            for c in range(8):
                E(ACT, 'activation', out=sq[c % 2][:, 0:N], in_=src[:, c, 0:N], func=AF.Square)
                E(POOL, 'tensor_copy', out=xb_[c % 2][:, 0:N], in_=src[:, c, 0:N])
                kb.mm(pm[:, 0:N], CB('ones'), xb_[c % 2][:, 0:N], start=(c == 0), stop=(c == 7))
                kb.mm(pq[:, 0:N], CB('ones'), sq[c % 2][:, 0:N], start=(c == 0), stop=(c == 7))
            E(ACT, 'activation', out=mean[:, 0:N], in_=pm[:, 0:N], func=AF.Copy, scale=1.0 / D)
            E(DVE, 'tensor_tensor', out=m2[:, 0:N], in0=mean[:, 0:N], in1=mean[:, 0:N], op=ALU.mult)
            E(DVE, 'scalar_tensor_tensor', out=m2[:, 0:N], in0=pq[:, 0:N], scalar=1.0 / D, in1=m2[:, 0:N],
              op0=ALU.mult, op1=ALU.subtract)
            E(DVE, 'tensor_scalar', out=m2[:, 0:N], in0=m2[:, 0:N], scalar1=0.0, scalar2=1e-5, op0=ALU.max, op1=ALU.add)
            E(ACT, 'activation', out=m2[:, 0:N], in_=m2[:, 0:N], func=AF.Sqrt)
            E(DVE, 'reciprocal', out=rstd[:, 0:N], in_=m2[:, 0:N])
            for c in range(8):
                t_ = tmp[c % 2]
                E(DVE, 'tensor_tensor', out=t_[:, 0:N], in0=src[:, c, 0:N], in1=mean[:, 0:N], op=ALU.subtract)
                E(DVE, 'tensor_tensor', out=t_[:, 0:N], in0=t_[:, 0:N], in1=rstd[:, 0:N], op=ALU.mult)
                if dst_f is not None:
                    E(ACT, 'activation', out=dst_f[:, c, 0:N], in_=t_[:, 0:N], func=AF.Identity, scale=g_fn(c), bias=b_fn(c))
                if dst_b is not None:
                    E(ACT, 'activation', out=dst_b[:, c, 0:N], in_=t_[:, 0:N], func=AF.Identity, scale=g_fn(c), bias=b_fn(c))

        def ln_bufs(st):
            return ([sb(st, "ln_sq%d" % i, [128, NB], BF16) for i in range(2)],
                    [sb(st, "ln_xb%d" % i, [128, NB], BF16) for i in range(2)],
                    sb(st, "ln_mean", [128, NB]), sb(st, "ln_m2", [128, NB]), sb(st, "ln_rstd", [128, NB]),
                    [sb(st, "ln_tmp%d" % i, [128, NB]) for i in range(2)])

        def phase_embed():
            with ExitStack() as st:
                xin = [sb(st, "e_xin%d" % i, [128, 4, D]) for i in range(2)]
                xf = sb(st, "e_xf", [128, 8, NB])
                xo = sb(st, "e_xo", [128, 8, NB])
                lb = ln_bufs(st)
                for b in range(NBLK):
                    t0 = b * NB
                    xi = xin[b % 2]
                    kb.dma(xi[:], View([], x_in[t0:t0 + NB, :].rearrange("(j p) d -> p j d", p=128)))
                    for c in range(8):
                        pb = ps[c % 3]
                        for j in range(4):
                            kb.tr(pb[:, j * 128:(j + 1) * 128], xi[:, j, c * 128:(c + 1) * 128], C('ident'))
                        E(ACT if c % 2 else DVE, 'copy' if c % 2 else 'tensor_copy', out=xf[:, c, :], in_=pb[:, :])
                    layer_norm(lb, xf, lambda c: cpv(0, 'emb_g', c), lambda c: cpv(0, 'emb_b', c), dst_f=xo)
                    kb.dma(View([kb.db('xT', b)], xT[:, :, t0:t0 + NB].rearrange("c p t -> p c t")), xo[:])
            kb.barrier()

        def phase_proj(l):
            with ExitStack() as st:
                xin = [sb(st, "p_xin%d" % i, [128, 8, NB]) for i in range(2)]
                xb_ = [sb(st, "p_xb%d" % i, [128, 8, NB], BF16) for i in range(2)]
                stg = [sb(st, "p_stg%d" % i, [128, NB]) for i in range(6)]
                wb = [sb(st, "p_wb%d" % i, [128, 8, 128], BF16) for i in range(4)]
                sctr = [0]
                for b in range(NBLK):
                    t0 = b * NB
                    xi, xbb = xin[b % 2], xb_[b % 2]
                    kb.dma(xi[:], View([kb.db('xT', b)], xT[:, :, t0:t0 + NB].rearrange("c p t -> p c t")))
                    E(POOL, 'tensor_copy', out=xbb[:, 0:4, :], in_=xi[:, 0:4, :])
                    E(DVE, 'tensor_copy', out=xbb[:, 4:8, :], in_=xi[:, 4:8, :])

                    def evac(ti, pv, wdt):
                        sg_ = stg[sctr[0] % len(stg)]
                        sctr[0] += 1
                        if GA <= ti < AB:
                            E(ACT, 'activation', out=sg_[0:wdt, :], in_=pv, func=AF.Sigmoid)
                        elif sctr[0] % 2:
                            E(DVE, 'tensor_copy', out=sg_[0:wdt, :], in_=pv)
                        else:
                            E(ACT, 'copy', out=sg_[0:wdt, :], in_=pv)
                        kb.dma(View([kb.db('hT%d' % ti, b)], hT[ti, 0:wdt, t0:t0 + NB]), sg_[0:wdt, :])
                    dense(wb, l, 'win', range(NHT), lambda k: xbb[:, k, :], NB, evac)
            kb.barrier()

        def phase_post(l, last):
            with ExitStack() as st:
                xres = sb(st, "q_xres", [128, 8, NB])
                hid = sb(st, "q_hid", [128, 32, NB], BF16)
                sig = sb(st, "q_sig", [128, 8, NB])
                acc = sb(st, "q_acc", [128, 8 * NB])
                accb = sb(st, "q_accb", [128, 8, NB], BF16)
                x1 = sb(st, "q_x1", [128, 8, NB])
                x1b = sb(st, "q_x1b", [128, 8, NB], BF16)
                rl = [sb(st, "q_rl%d" % i, [128, NB]) for i in range(2)]
                pin = sb(st, "q_pin", [128, 4, DPLE])
                pT = sb(st, "q_pT", [128, 2, NB], BF16)
                wb = [sb(st, "q_wb%d" % i, [128, 32, 128], BF16) for i in range(3)]
                lb = ln_bufs(st)
                acc3 = acc.v(acc.t[:, :].rearrange("p (c t) -> p c t", c=8))
                accv = lambda c: acc.v(acc.t[:, c * NB:(c + 1) * NB])
                for b in range(NBLK):
                    t0 = b * NB
                    kb.dma(xres[:], View([kb.db('xT', b)], xT[:, :, t0:t0 + NB].rearrange("c p t -> p c t")))
                    kb.dma(hid[:, 0:12, :], View([kb.db('obT', b)], obT[:, :, t0:t0 + NB].rearrange("c p t -> p c t")))
                    kb.dma(pin[:], View([], p_in[l, t0:t0 + NB, :].rearrange("(j p) d -> p j d", p=128)))
                    for c in range(2):
                        pb = ps[5 + c]
                        for j in range(4):
                            kb.tr(pb[:, j * 128:(j + 1) * 128], pin[:, j, c * 128:(c + 1) * 128], C('ident'))
                        E(ACT, 'copy', out=pT[:, c, :], in_=pb[:, :])
                    for n in range(3):
                        gt0 = [GA, GB, GC][n]
                        kb.dma(sig[:], View([kb.db('hT%d' % (gt0 + c), b) for c in range(8)],
                                            hT[gt0:gt0 + 8, :, t0:t0 + NB].rearrange("c p t -> p c t")))

                        def evac(ti, pv, wdt, n=n):
                            if n == 0:
                                E(DVE, 'tensor_tensor', out=accv(ti), in0=pv, in1=sig[:, ti, :], op=ALU.mult)
                            else:
                                r_ = rl[ti % 2]
                                E(DVE, 'tensor_tensor', out=r_[:, :], in0=pv, in1=sig[:, ti, :], op=ALU.mult)
                                if n == 1:
                                    E(POOL, 'tensor_tensor', out=accv(ti), in0=accv(ti), in1=r_[:, :], op=ALU.add)
                                else:
                                    E(POOL, 'tensor_tensor', out=accb[:, ti, :], in0=accv(ti), in1=r_[:, :], op=ALU.add)
                        dense(wb, l, 'wb%d' % n, range(8), lambda k, n=n: hid[:, n * 4 + k, :], NB, evac)

                    def evac(ti, pv, wdt):
                        E(DVE, 'scalar_tensor_tensor', out=accv(ti), in0=xres[:, ti, :], scalar=ALPHA, in1=pv,
                          op0=ALU.mult, op1=ALU.add)
                    dense(wb, l, 'wout', range(8), lambda k: accb[:, k, :], NB, evac)
                    layer_norm(lb, acc3, lambda c: cpv(l, 'ln1_g', c), lambda c: cpv(l, 'ln1_b', c), dst_f=x1, dst_b=x1b)

                    def evac(ti, pv, wdt):
                        r_ = rl[ti % 2]
                        E(ACT, 'activation', out=r_[:, :], in_=pv, func=AF.Relu)
                        E(POOL if ti % 2 else DVE, 'tensor_tensor', out=hid[:, ti, :], in0=r_[:, :], in1=r_[:, :], op=ALU.mult)
                    dense(wb, l, 'wm1', range(32), lambda k: x1b[:, k, :], NB, evac)

                    def evac(ti, pv, wdt):
                        E(ACT, 'activation', out=sig[:, ti, :], in_=pv, func=AF.Sigmoid)
                    dense(wb, l, 'wpg', range(8), lambda k: x1b[:, k, :], NB, evac)

                    def evac(ti, pv, wdt):
                        E(DVE, 'tensor_tensor', out=sig[:, ti, :], in0=pv, in1=sig[:, ti, :], op=ALU.mult)
                    dense(wb, l, 'wpp', range(8), lambda k: pT[:, k, :], NB, evac)

                    def evac(ti, pv, wdt):
                        E(DVE, 'scalar_tensor_tensor', out=accv(ti), in0=x1[:, ti, :], scalar=ALPHA, in1=pv,
                          op0=ALU.mult, op1=ALU.add)
                        E(POOL, 'tensor_tensor', out=accv(ti), in0=accv(ti), in1=sig[:, ti, :], op=ALU.add)
                    dense(wb, l, 'wm2', range(8), lambda k: hid[:, k, :], NB, evac)
                    layer_norm(lb, acc3, lambda c: cpv(l, 'ln2_g', c), lambda c: cpv(l, 'ln2_b', c), dst_f=xres)
                    if not last:
                        kb.dma(View([kb.db('xT', b)], xT[:, :, t0:t0 + NB].rearrange("c p t -> p c t")), xres[:])
                    else:
                        ov = x1.v(x1.t[:, :, :].rearrange("p c t -> p (c t)").rearrange("p (j d) -> p j d", j=4))
                        for j in range(4):
                            for c2 in range(2):
                                pb = ps[5 + c2]
                                for cc in range(4):
                                    c = c2 * 4 + cc
                                    kb.tr(pb[:, cc * 128:(cc + 1) * 128], xres[:, c, j * 128:(j + 1) * 128], C('ident'))
                                E(ACT if c2 else DVE, 'copy' if c2 else 'tensor_copy',
                                  out=x1.v(ov.ap[:, j, c2 * 512:(c2 + 1) * 512]), in_=pb[:, :])
                        kb.dma(View([], y_out[t0:t0 + NB, :].rearrange("(j p) d -> p j d", p=128)), ov)
            kb.barrier()

        SEGB = SEG // NB

        def blk_order(d):
            return list(range(NBLK)) if d == 0 else list(range(NBLK - 1, -1, -1))

        def load_halo(raw, tile0, nt, b):
            t0 = b * NB
            lo, hi = max(t0 - 1, 0), min(t0 + NB + 2, T)
            if lo > t0 - 1:
                E(POOL, 'memset', ap=raw[:, :, 0:1], constant=0.0)
            if hi < t0 + NB + 2:
                E(POOL, 'memset', ap=raw[:, :, NB + 1:NB + 3], constant=0.0)
            bl = [kb.db('hT%d' % (tile0 + c), bb) for c in range(nt) for bb in (b - 1, b, b + 1) if 0 <= bb < NBLK]
            kb.dma(raw[:, :, lo - (t0 - 1):hi - (t0 - 1)], View(bl, hT[tile0:tile0 + nt, :, lo:hi].rearrange("c p t -> p c t")))
            if b % SEGB == 0 and b > 0:
                E(DVE, 'tensor_scalar', out=raw[:, :, 0:1], in0=raw[:, :, 0:1], scalar1=flag[:, 0:1], scalar2=None, op0=ALU.mult)
            if b % SEGB == SEGB - 1 and b < NBLK - 1:
                E(DVE, 'tensor_scalar', out=raw[:, :, NB + 1:NB + 3], in0=raw[:, :, NB + 1:NB + 3], scalar1=flag[:, 0:1],
                  scalar2=None, op0=ALU.mult)

        def conv4(eng, out, raw, t, wcol, bias=None):
            if bias is None:
                E(eng, 'tensor_scalar', out=out, in0=raw[:, t, 0:NB], scalar1=wcol(0), scalar2=None, op0=ALU.mult)
            else:
                E(eng, 'tensor_scalar', out=out, in0=raw[:, t, 0:NB], scalar1=wcol(0), scalar2=bias, op0=ALU.mult, op1=ALU.add)
            for k in range(1, 4):
                E(DVE, 'scalar_tensor_tensor', out=out, in0=raw[:, t, k:k + NB], scalar=wcol(k), in1=out, op0=ALU.mult, op1=ALU.add)

        def crosses(b, d):
            return (d == 0 and b > 0 and b % SEGB == 0) or (d == 1 and b < NBLK - 1 and b % SEGB == SEGB - 1)

        def phase_lru(l):
            with ExitStack() as st:
                bdf = sb(st, "l_bdf", [128, 16, 128])
                bdb = sb(st, "l_bdb", [128, 16, 128], BF16)
                ccol = sb(st, "l_ccol", [128, 8])
                raw = [sb(st, "l_raw%d" % i, [128, 4, NB + 3]) for i in range(2)]
                xc = sb(st, "l_xc", [128, 4, NB])
                xcb = sb(st, "l_xcb", [128, 4, NB], BF16)
                r_ = [sb(st, "l_r%d" % i, [128, NB]) for i in range(2)]
                i_ = [sb(st, "l_i%d" % i, [128, NB]) for i in range(2)]
                a_ = [sb(st, "l_a%d" % i, [128, NB]) for i in range(2)]
                u_ = [sb(st, "l_u%d" % i, [128, NB]) for i in range(2)]
                hh = [sb(st, "l_h%d" % i, [128, 4, NB]) for i in range(2)]
                hf = sb(st, "l_hf", [128, 4, NB])
                cg = sb(st, "l_cg", [128, 4, NB])
                ob = [sb(st, "l_ob%d" % i, [128, 4, NB], BF16) for i in range(2)]
                carry = sb(st, "l_carry", [128, 4])
                kb.dma(bdf[:], View([], bd_in[l]))
                E(DVE, 'tensor_copy', out=bdb[:], in_=bdf[:])
                o_, w_ = CP['lru_lam']
                E(ACT, 'activation', out=ccol[:], in_=cpt[l][:, o_:o_ + 8], func=AF.Sigmoid)
                E(ACT, 'activation', out=ccol[:], in_=ccol[:], func=AF.Ln)
                E(DVE, 'tensor_scalar', out=ccol[:], in0=ccol[:], scalar1=8.0, scalar2=None, op0=ALU.mult)
                for d in range(2):
                    for bi, b in enumerate(blk_order(d)):
                        t0 = b * NB
                        rw = raw[bi % 2]
                        h = hh[bi % 2]
                        load_halo(rw, CX, 4, b)
                        if d == 1:
                            kb.dma(hf[:], View([kb.db('ofT%d' % (8 + c), b) for c in range(4)],
                                               ofT[8:12, :, t0:t0 + NB].rearrange("c p t -> p c t")))
                            kb.dma(cg[:], View([kb.db('hT%d' % (CG + c), b) for c in range(4)],
                                               hT[CG:CG + 4, :, t0:t0 + NB].rearrange("c p t -> p c t")))
                        if bi == 0:
                            E(DVE, 'memset', ap=carry[:], constant=0.0)
                        elif crosses(b, d):
                            E(DVE, 'tensor_scalar', out=carry[:], in0=carry[:], scalar1=flag[:, 0:1], scalar2=None, op0=ALU.mult)
                        for t in range(4):
                            conv4(POOL, xc[:, t, :], rw, t, lambda k: cpv(l, 'lru_conv', t * 4 + k), bias=cpv(l, 'lru_cb', t))
                            E(ACT, 'copy', out=xcb[:, t, :], in_=xc[:, t, :])
                            pa, px = ps[(2 * t) % 4], ps[(2 * t + 1) % 4]
                            kb.mm(pa[:, :], bdb[:, d * 4 + t, :], xcb[:, t, :])
                            kb.mm(px[:, :], bdb[:, 8 + d * 4 + t, :], xcb[:, t, :])
                            rr, ii, aa, uu = r_[t % 2], i_[t % 2], a_[t % 2], u_[t % 2]
                            E(ACT, 'activation', out=rr[:, :], in_=pa[:, :], func=AF.Sigmoid, bias=cpv(l, 'lru_ba', d * 4 + t))
                            E(ACT, 'activation', out=ii[:, :], in_=px[:, :], func=AF.Sigmoid, bias=cpv(l, 'lru_bx', d * 4 + t))
                            E(ACT, 'activation', out=aa[:, :], in_=rr[:, :], func=AF.Exp, scale=ccol[:, d * 4 + t:d * 4 + t + 1])
                            E(POOL, 'tensor_tensor', out=rr[:, :], in0=aa[:, :], in1=aa[:, :], op=ALU.mult)
                            E(DVE, 'tensor_scalar', out=rr[:, :], in0=rr[:, :], scalar1=-1.0, scalar2=1.0, op0=ALU.mult, op1=ALU.add)
                            E(ACT, 'activation', out=rr[:, :], in_=rr[:, :], func=AF.Sqrt)
                            E(POOL, 'tensor_tensor', out=ii[:, :], in0=ii[:, :], in1=xc[:, t, :], op=ALU.mult)
                            E(DVE, 'tensor_tensor', out=uu[:, :], in0=rr[:, :], in1=ii[:, :], op=ALU.mult)
                            if d == 0:
                                E(DVE, 'tensor_tensor_scan', out=h[:, t, :], data0=aa[:, :], data1=uu[:, :],
                                  initial=carry[:, t:t + 1], op0=ALU.mult, op1=ALU.add)
                            else:
                                E(DVE, 'tensor_tensor_scan', out=h[:, t, ::-1], data0=aa[:, ::-1], data1=uu[:, ::-1],
                                  initial=carry[:, t:t + 1], op0=ALU.mult, op1=ALU.add)
                        E(DVE, 'tensor_copy', out=carry[:], in_=h[:, :, NB - 1] if d == 0 else h[:, :, 0])
                        if d == 0:
                            kb.dma(View([kb.db('ofT%d' % (8 + c), b) for c in range(4)],
                                        ofT[8:12, :, t0:t0 + NB].rearrange("c p t -> p c t")), h[:])
                        else:
                            o2 = ob[bi % 2]
                            E(ACT, 'activation', out=cg[:], in_=cg[:], func=AF.Gelu_apprx_tanh)
                            E(POOL, 'tensor_tensor', out=hf[:], in0=hf[:], in1=h[:], op=ALU.add)
                            E(DVE, 'tensor_tensor', out=o2[:], in0=hf[:], in1=cg[:], op=ALU.mult)
                            kb.dma(View([kb.db('obT', b)], obT[8:12, :, t0:t0 + NB].rearrange("c p t -> p c t")), o2[:])
            kb.barrier()

        def phase_hg(l):
            with ExitStack() as st:
                lbc = sb(st, "g_lbc", [128, 4])
                omlb = sb(st, "g_omlb", [128, 4])
                nomlb = sb(st, "g_nomlb", [128, 4])
                qT = [sb(st, "g_qT%d" % i, [128, 4, NB]) for i in range(2)]
                fT = [sb(st, "g_fT%d" % i, [128, 4, NB]) for i in range(2)]
                vT = [sb(st, "g_vT%d" % i, [128, 4, NB]) for i in range(2)]
                s_ = [sb(st, "g_s%d" % i, [128, NB]) for i in range(2)]
                f_ = [sb(st, "g_f%d" % i, [128, NB]) for i in range(2)]
                kk = [sb(st, "g_kk%d" % i, [128, NB]) for i in range(2)]
                bq = [sb(st, "g_bq%d" % i, [128, NB]) for i in range(2)]
                dq_ = [sb(st, "g_dq%d" % i, [128, NB]) for i in range(2)]
                e1 = [sb(st, "g_e1%d" % i, [128, NB]) for i in range(2)]
                e2 = [sb(st, "g_e2%d" % i, [128, NB]) for i in range(2)]
                qt = sb(st, "g_qt", [128, 4, NB], BF16)
                kdT = sb(st, "g_kdT", [128, 4, NB], BF16)
                ebl = sb(st, "g_ebl", [128, 4, 8])
                vtok = sb(st, "g_vtok", [128, 4, 512], BF16)
                ktok = sb(st, "g_ktok", [128, 4, 512], BF16)
                vTb = sb(st, "g_vTb", [128, 4, NB], BF16)
                AT = [sb(st, "g_AT%d" % i, [128, 4, 64], BF16) for i in range(2)]
                S = sb(st, "g_S", [128, 4, 128])
                Sp = sb(st, "g_Sp", [128, 4, 128])
                Spb = sb(st, "g_Spb", [128, 4, 128], BF16)
                ost = [sb(st, "g_ost%d" % i, [128, 4, NB]) for i in range(2)]
                of_ = sb(st, "g_of", [128, 4, NB])
                gz = sb(st, "g_gz", [128, 4, NB])
                oh = [sb(st, "g_oh%d" % i, [128, NB]) for i in range(2)]
                sqb = [sb(st, "g_sqb%d" % i, [128, NB], BF16) for i in range(2)]
                sd = [sb(st, "g_sd%d" % i, [128, NB]) for i in range(2)]
                obs = [sb(st, "g_obs%d" % i, [128, 4, NB], BF16) for i in range(2)]
                if l == 0:
                    E(DVE, 'memset', ap=lbc[:], constant=0.0)
                else:
                    o0, o1 = CP['lb0'][0], CP['lb1'][0]
                    E(DVE, 'tensor_tensor', out=lbc[:], in0=cpt[l][:, o1:o1 + 4], in1=cpt[l][:, o0:o0 + 4], op=ALU.subtract)
                    E(ACT, 'activation', out=lbc[:], in_=lbc[:], func=AF.Sigmoid)
                E(DVE, 'tensor_scalar', out=omlb[:], in0=lbc[:], scalar1=-1.0, scalar2=1.0, op0=ALU.mult, op1=ALU.add)
                E(DVE, 'tensor_scalar', out=nomlb[:], in0=omlb[:], scalar1=-1.0, scalar2=None, op0=ALU.mult)
                pso = [ps[0], ps[1], ps[2], ps[3]]
                psA, psS, psT, psN = ps[4], ps[5], ps[6], ps[7]
                psTb = psT.v(psT.t[:, :].bitcast(BF16))
                for d in range(2):
                    HF = HFF if d == 0 else HFB
                    mask4 = CB('hgm_f4') if d == 0 else CB('hgm_b4')
                    for bi, b in enumerate(blk_order(d)):
                        t0 = b * NB
                        q_, fz, v_ = qT[bi % 2], fT[bi % 2], vT[bi % 2]
                        for (dst, tl) in ((q_, HQ), (fz, HF), (v_, HI)):
                            kb.dma(dst[:], View([kb.db('hT%d' % (tl + c), b) for c in range(4)],
                                                hT[tl:tl + 4, :, t0:t0 + NB].rearrange("c p t -> p c t")))
                        if d == 1:
                            kb.dma(of_[:], View([kb.db('ofT%d' % (4 + c), b) for c in range(4)],
                                                ofT[4:8, :, t0:t0 + NB].rearrange("c p t -> p c t")))
                            kb.dma(gz[:], View([kb.db('hT%d' % (HGT + c), b) for c in range(4)],
                                               hT[HGT:HGT + 4, :, t0:t0 + NB].rearrange("c p t -> p c t")))
                        if bi == 0:
                            E(DVE, 'memset', ap=S[:], constant=0.0)
                        elif crosses(b, d):
                            E(DVE, 'tensor_scalar', out=S[:], in0=S[:], scalar1=flag[:, 0:1], scalar2=None, op0=ALU.mult)
                        E(POOL, 'tensor_copy', out=vTb[:], in_=v_[:])
                        for j in range(4):
                            for h in range(4):
                                kb.tr(psTb.bufs[0].v(psTb.ap[:, h * 128:(h + 1) * 128]), vTb[:, h, j * 128:(j + 1) * 128], CB('ident'))
                            E(ACT, 'copy', out=vtok[:, j, :], in_=psT.v(psTb.ap[:, 0:512]))
                        for h in range(4):
                            ss, ff, k2, bb, dd, x1_, x2_ = s_[h % 2], f_[h % 2], kk[h % 2], bq[h % 2], dq_[h % 2], e1[h % 2], e2[h % 2]
                            E(ACT, 'activation', out=ss[:, :], in_=fz[:, h, :], func=AF.Sigmoid)
                            E(DVE, 'tensor_scalar', out=ff[:, :], in0=ss[:, :], scalar1=omlb[:, h:h + 1], scalar2=lbc[:, h:h + 1],
                              op0=ALU.mult, op1=ALU.add)
                            E(ACT, 'activation', out=ff[:, :], in_=ff[:, :], func=AF.Ln)
                            E(POOL, 'tensor_scalar', out=k2[:, :], in0=ss[:, :], scalar1=nomlb[:, h:h + 1], scalar2=omlb[:, h:h + 1],
                              op0=ALU.mult, op1=ALU.add)
                            if d == 0:
                                E(DVE, 'tensor_tensor_scan', out=bb[:, :], data0=C('rst_f'), data1=ff[:, :], initial=0.0,
                                  op0=ALU.mult, op1=ALU.add)
                            else:
                                rb = C('rst_b')
                                E(DVE, 'tensor_tensor_scan', out=bb[:, ::-1], data0=View(rb.bufs, rb.ap[:, ::-1]), data1=ff[:, ::-1],
                                  initial=0.0, op0=ALU.mult, op1=ALU.add)
                            b3 = bb.t[:, :].rearrange("p (c j) -> p c j", j=64)
                            blv = b3[:, :, 63] if d == 0 else b3[:, :, 0]
                            E(ACT, 'activation', out=ebl[:, h, :], in_=bb.v(blv), func=AF.Exp)
                            E(DVE, 'tensor_tensor', out=dd.v(dd.t[:, :].rearrange("p (c j) -> p c j", j=64)), in0=bb.v(b3),
                              in1=bb.v(bc_last(blv, 64)), op=ALU.subtract)
                            E(ACT, 'activation', out=x1_[:, :], in_=dd[:, :], func=AF.Exp)
                            E(ACT, 'activation', out=x2_[:, :], in_=dd[:, :], func=AF.Exp, scale=-1.0)
                            E(POOL, 'tensor_tensor', out=qt[:, h, :], in0=q_[:, h, :], in1=x1_[:, :], op=ALU.mult)
                            E(DVE, 'tensor_tensor', out=kdT[:, h, :], in0=k2[:, :], in1=x2_[:, :], op=ALU.mult)
                        for j in range(4):
                            for h in range(4):
                                kb.tr(psTb.bufs[0].v(psTb.ap[:, h * 128:(h + 1) * 128]), kdT[:, h, j * 128:(j + 1) * 128], CB('ident'))
                            E(ACT, 'copy', out=ktok[:, j, :], in_=psT.v(psTb.ap[:, 0:512]))
                        jl = range(4) if d == 0 else range(3, -1, -1)
                        for j in jl:
                            at = AT[j % 2]
                            for hfh in range(2):
                                c = 2 * j + hfh
                                for h in range(4):
                                    kb.mm(psA[hfh * 64:(hfh + 1) * 64, h * 64:(h + 1) * 64], kdT[:, h, c * 64:(c + 1) * 64],
                                          qt[:, h, c * 64:(c + 1) * 64])
                            E(DVE, 'tensor_tensor', out=at.v(at.t[:, :, :].rearrange("p h i -> p (h i)")), in0=psA[:, 0:256], in1=mask4, op=ALU.mult)
                            for hfh in (range(2) if d == 0 else range(1, -1, -1)):
                                c = 2 * j + hfh
                                p0 = hfh * 64
                                eb = ebl.v(bc_last(ebl.t[:, :, c], 128))
                                E(DVE, 'tensor_tensor', out=Spb[:], in0=S[:], in1=eb, op=ALU.mult)
                                E(POOL, 'tensor_tensor', out=Sp[:], in0=S[:], in1=eb, op=ALU.mult)
                                for h in range(4):
                                    kb.mm(pso[h][:, c * 64:(c + 1) * 64], Spb[:, h, :], qt[:, h, c * 64:(c + 1) * 64], start=True, stop=False)
                                    kb.mm(pso[h][:, c * 64:(c + 1) * 64], vtok[p0:p0 + 64, j, h * 128:(h + 1) * 128], at[p0:p0 + 64, h, :],
                                          start=False, stop=True)
                                for h in range(4):
                                    kb.mm(psS[:, h * 128:(h + 1) * 128], ktok[p0:p0 + 64, j, h * 128:(h + 1) * 128],
                                          vtok[p0:p0 + 64, j, h * 128:(h + 1) * 128])
                                E(DVE, 'tensor_tensor', out=S.v(S.t[:, :, :].rearrange("p h d -> p (h d)")),
                                  in0=Sp.v(Sp.t[:, :, :].rearrange("p h d -> p (h d)")), in1=psS[:, :], op=ALU.add)
                        if d == 0:
                            o1 = ost[bi % 2]
                            for h in range(4):
                                E(ACT, 'copy', out=o1[:, h, :], in_=pso[h][:, :])
                            kb.dma(View([kb.db('ofT%d' % (4 + c), b) for c in range(4)],
                                        ofT[4:8, :, t0:t0 + NB].rearrange("c p t -> p c t")), o1[:])
                        else:
                            o2 = obs[bi % 2]
                            E(ACT, 'activation', out=gz[:], in_=gz[:], func=AF.Silu)
                            for h in range(4):
                                oo, sq_, sd_ = oh[h % 2], sqb[h % 2], sd[h % 2]
                                E(DVE, 'tensor_tensor', out=oo[:, :], in0=pso[h][:, :], in1=of_[:, h, :], op=ALU.add)
                                E(ACT, 'activation', out=sq_[:, :], in_=oo[:, :], func=AF.Square)
                                kb.mm(psN[:, :], CB('ones'), sq_[:, :])
                                E(DVE, 'tensor_scalar', out=sd_[:, :], in0=psN[:, :], scalar1=1.0 / 128, scalar2=1e-6, op0=ALU.mult, op1=ALU.add)
                                E(ACT, 'activation', out=sd_[:, :], in_=sd_[:, :], func=AF.Sqrt)
                                E(DVE, 'reciprocal', out=sd_[:, :], in_=sd_[:, :])
                                E(POOL, 'tensor_tensor', out=oo[:, :], in0=oo[:, :], in1=sd_[:, :], op=ALU.mult)
                                E(DVE, 'scalar_tensor_tensor', out=o2[:, h, :], in0=oo[:, :], scalar=cpv(l, 'hg_nw'), in1=gz[:, h, :],
                                  op0=ALU.mult, op1=ALU.mult)
                            kb.dma(View([kb.db('obT', b)], obT[4:8, :, t0:t0 + NB].rearrange("c p t -> p c t")), o2[:])
            kb.barrier()

        def phase_dn(l):
            with ExitStack() as st:
                raw = sb(st, "d_raw", [128, 12, NB + 3])
                cv = sb(st, "d_cv", [128, 12, NB])
                abT = sb(st, "d_abT", [16, NB])
                qn = sb(st, "d_qn", [128, 4, NB], BF16)
                kn = sb(st, "d_kn", [128, 4, NB], BF16)
                vTb = sb(st, "d_vTb", [128, 4, NB], BF16)
                sqb = [sb(st, "d_sqb%d" % i, [128, NB], BF16) for i in range(2)]
                sd = [sb(st, "d_sd%d" % i, [128, NB]) for i in range(2)]
                negA = sb(st, "d_negA", [128, 8])
                gt = sb(st, "d_gt", [128, 4, 16])
                g_ = sb(st, "d_g", [128, 4, 4])
                beta = sb(st, "d_beta", [128, 4, 4])
                nbeta = sb(st, "d_nbeta", [128, 4, 4])
                Mh = sb(st, "d_Mh", [128, 4, 128])
                Ecol = sb(st, "d_Ecol", [128, 12])
                sc1 = sb(st, "d_sc1", [128, 4])
                Dc = sb(st, "d_Dc", [128, 4, 128], BF16)
                Dm = sb(st, "d_Dm", [128, 4, 128], BF16)
                Nm = sb(st, "d_Nm", [128, 4, 128])
                attn = sb(st, "d_attn", [128, 4, 128], BF16)
                NT = sb(st, "d_NT", [128, 4, 128])
                AT = sb(st, "d_AT", [128, 4, 128], BF16)
                Pb = [sb(st, "d_P%d" % i, [128, 4, 128]) for i in range(2)]
                PTb = [sb(st, "d_PT%d" % i, [128, 4, 128]) for i in range(2)]
                Rb = [sb(st, "d_R%d" % i, [128, 4, 128]) for i in range(2)]
                kbg = sb(st, "d_kbg", [128, 4, 128])
                kd = sb(st, "d_kd", [128, 4, 128], BF16)
                vb = sb(st, "d_vb", [128, 4, 128])
                nwT = sb(st, "d_nwT", [128, 4, 128])
                EG = sb(st, "d_EG", [128, 4, 128])
                qd = sb(st, "d_qd", [128, 4, 128], BF16)
                vnew = sb(st, "d_vnew", [128, 4, 128], BF16)
                S = sb(st, "d_S", [128, 4, 128])
                Stmp = sb(st, "d_Stmp", [128, 4, 128])
                Sb = sb(st, "d_Sb", [128, 4, 128], BF16)
                ost = [sb(st, "d_ost%d" % i, [128, 4, NB]) for i in range(2)]
                of_ = sb(st, "d_of", [128, 4, NB])
                gz = sb(st, "d_gz", [128, 4, NB])
                oh = [sb(st, "d_oh%d" % i, [128, NB]) for i in range(2)]
                obs = [sb(st, "d_obs%d" % i, [128, 4, NB], BF16) for i in range(2)]
                psD, psG, psA, psT, psM, psKV, psE, psS = ps
                psW, psP, psQ, psR, psV, psO, psN = psD, psG, psA, psT, psKV, psE, psS
                f3 = lambda bf: bf.v(bf.t[:, :].rearrange("p (h i) -> p h i", h=4))
                bfv = lambda bf, a, b_: bf.v(bf.t[:, :].bitcast(BF16)[:, a:b_])
                flat = lambda bf: bf.v(bf.t[:, :, :].rearrange("p h i -> p (h i)"))
                o_ = CP['alog'][0]
                E(ACT, 'activation', out=negA[:], in_=cpt[l][:, o_:o_ + 8], func=AF.Exp)
                E(DVE, 'tensor_scalar', out=negA[:], in0=negA[:], scalar1=-1.0, scalar2=None, op0=ALU.mult)
                for d in range(2):
                    TRI = C('tri_le') if d == 0 else C('tri_ge')
                    TRIC = C('tri_gt') if d == 0 else C('tri_lt')
                    NM = C('nm_f') if d == 0 else C('nm_b')
                    SM = CB('sm_f4') if d == 0 else CB('sm_b4')
                    o_ = CP['dtb'][0]
                    dtb = cpt[l][:, o_ + d * 4:o_ + d * 4 + 4]
                    nAd = negA[:, d * 4:d * 4 + 4]
                    for bi, b in enumerate(blk_order(d)):
                        t0 = b * NB
                        load_halo(raw, DQ, 12, b)
                        kb.dma(abT[:], View([kb.db('hT%d' % AB, b)], hT[AB, 0:16, t0:t0 + NB]))
                        if d == 1:
                            kb.dma(of_[:], View([kb.db('ofT%d' % c, b) for c in range(4)],
                                                ofT[0:4, :, t0:t0 + NB].rearrange("c p t -> p c t")))
                            kb.dma(gz[:], View([kb.db('hT%d' % (DZ + c), b) for c in range(4)],
                                               hT[DZ:DZ + 4, :, t0:t0 + NB].rearrange("c p t -> p c t")))
                        if bi == 0:
                            E(DVE, 'memset', ap=S[:], constant=0.0)
                            E(POOL, 'memset', ap=Sb[:], constant=0.0)
                        elif crosses(b, d):
                            E(DVE, 'tensor_scalar', out=S[:], in0=S[:], scalar1=flag[:, 0:1], scalar2=None, op0=ALU.mult)
                            E(DVE, 'tensor_copy', out=Sb[:], in_=S[:])
                        for t in range(12):
                            conv4(POOL if t % 2 else DVE, cv[:, t, :], raw, t, lambda k: cpv(l, 'dn_conv', t * 4 + k))
                        for t3 in range(3):
                            E(ACT, 'activation', out=cv[:, t3 * 4:t3 * 4 + 4, :], in_=cv[:, t3 * 4:t3 * 4 + 4, :], func=AF.Silu)
                        for t in range(8):
                            sq_, sd_ = sqb[t % 2], sd[t % 2]
                            E(ACT, 'activation', out=sq_[:, :], in_=cv[:, t, :], func=AF.Square)
                            kb.mm(psN[:, :], CB('ones'), sq_[:, :])
                            E(DVE, 'tensor_scalar', out=sd_[:, :], in0=psN[:, :], scalar1=1e-6, scalar2=None, op0=ALU.add)
                            E(ACT, 'activation', out=sd_[:, :], in_=sd_[:, :], func=AF.Sqrt)
                            E(DVE, 'reciprocal', out=sd_[:, :], in_=sd_[:, :])
                            dst = qn[:, t, :] if t < 4 else kn[:, t - 4, :]
                            E(DVE, 'scalar_tensor_tensor', out=dst, in0=cv[:, t, :], scalar=(128 ** -0.5 if t < 4 else 1.0),
                              in1=sd_[:, :], op0=ALU.mult, op1=ALU.mult)
                        E(POOL, 'tensor_copy', out=vTb[:], in_=cv[:, 8:12, :])
                        for j in range(4):
                            idt = C('ident')
                            kb.tr(psM[:, j * 16:(j + 1) * 16], abT[0:16, j * 128:(j + 1) * 128], View(idt.bufs, idt.ap[0:16, 0:16]))
                        E(ACT, 'copy', out=gt.v(gt.t[:, :, :].rearrange("p j c -> p (j c)")), in_=psM[:, 0:64])
                        E(DVE, 'tensor_tensor', out=g_[:], in0=gt[:, :, d * 4:d * 4 + 4], in1=View(dtb.bufs, bc_mid(dtb.ap, 4)), op=ALU.add)
                        E(ACT, 'activation', out=g_[:], in_=g_[:], func=AF.Exp)
                        E(DVE, 'tensor_scalar', out=g_[:], in0=g_[:], scalar1=1.0, scalar2=None, op0=ALU.add)
                        E(ACT, 'activation', out=g_[:], in_=g_[:], func=AF.Ln)
                        E(DVE, 'tensor_tensor', out=g_[:], in0=g_[:], in1=View(nAd.bufs, bc_mid(nAd.ap, 4)), op=ALU.mult)
                        E(ACT, 'activation', out=beta[:], in_=gt[:, :, 8 + d * 4:12 + d * 4], func=AF.Sigmoid)
                        E(DVE, 'tensor_scalar', out=nbeta[:], in0=beta[:], scalar1=-1.0, scalar2=None, op0=ALU.mult)
                        o1 = ost[bi % 2]
                        for j in (range(4) if d == 0 else range(3, -1, -1)):
                            cs = slice(j * 128, (j + 1) * 128)
                            gj = g_[:, j, :]
                            E(DVE, 'tensor_tensor', out=Mh[:], in0=View(TRI.bufs, bc_mid(TRI.ap, 4)), in1=View(gj.bufs, bc_last(gj.ap, 128)),
                              op=ALU.mult)
                            for h in range(4):
                                hs = slice(h * 128, (h + 1) * 128)
                                kb.mm(psD[:, hs], Mh[:, h, :], C('ones'), start=True, stop=False)
                                kb.mm(psD[:, hs], C('negones'), Mh[:, h, :], start=False, stop=False)
                                kb.mm(psD[:, hs], C('ident'), NM, start=False, stop=True)
                            kb.mm(psM[:, 64:68], TRI, gj)
                            kb.mm(psM[:, 68:72], TRIC, gj)
                            kb.mm(psM[:, 72:76], C('ones'), gj)
                            E(ACT, 'activation', out=Ecol[:], in_=psM[:, 64:76], func=AF.Exp)
                            E(ACT, 'activation', out=flat(Dc), in_=psD[:, :], func=AF.Exp)
                            E(POOL, 'tensor_tensor', out=flat(Dm), in0=flat(Dc), in1=SM, op=ALU.mult)
                            for h in range(4):
                                hs = slice(h * 128, (h + 1) * 128)
                                kb.mm(psG[:, hs], kn[:, h, cs], kn[:, h, cs])
                                kb.mm(psA[:, hs], qn[:, h, cs], kn[:, h, cs])
                            for h in range(4):
                                hs = slice(h * 128, (h + 1) * 128)
                                E(DVE, 'scalar_tensor_tensor', out=Nm[:, h, :], in0=psG[:, hs], scalar=nbeta[:, j, h:h + 1], in1=Dm[:, h, :],
                                  op0=ALU.mult, op1=ALU.mult)
                            E(DVE, 'tensor_tensor', out=flat(attn), in0=psA[:, :], in1=flat(Dc), op=ALU.mult)
                            for h in range(4):
                                kb.tr(psT[:, h * 128:(h + 1) * 128], Nm[:, h, :], C('ident'))
                                kb.tr(bfv(psM, 512 + h * 128, 512 + (h + 1) * 128), attn[:, h, :], CB('ident'))
                            E(ACT, 'copy', out=flat(NT), in_=psT[:, :])
                            E(DVE, 'tensor_copy', out=flat(AT), in_=bfv(psM, 512, 1024))
                            P, PT = Nm, NT
                            R = Rb[0]
                            E(POOL, 'tensor_tensor', out=flat(R), in0=flat(NT), in1=C('ident4'),
                              op=ALU.add)
                            for k in range(6):
                                Pn, PTn, Rn = Pb[k % 2], PTb[k % 2], Rb[(k + 1) % 2]
                                for h in range(4):
                                    hs = slice(h * 128, (h + 1) * 128)
                                    kb.mm(psP[:, hs], PT[:, h, :], P[:, h, :])
                                E(ACT, 'copy', out=flat(Pn), in_=psP[:, :])
                                if k < 5:
                                    for h in range(4):
                                        hs = slice(h * 128, (h + 1) * 128)
                                        kb.mm(psQ[:, hs], P[:, h, :], PT[:, h, :])
                                    E(DVE, 'tensor_copy', out=flat(PTn), in_=psQ[:, :])
                                for h in range(4):
                                    hs = slice(h * 128, (h + 1) * 128)
                                    kb.mm(psR[:, hs], Pn[:, h, :], R[:, h, :])
                                E(DVE, 'tensor_tensor', out=flat(Rn), in0=psR[:, :], in1=flat(R), op=ALU.add)
                                P, PT, R = Pn, PTn, Rn
                            TT = R
                            for h in range(4):
                                kb.tr(bfv(psKV, h * 128, (h + 1) * 128), kn[:, h, cs], CB('ident'))
                                kb.tr(bfv(psKV, 512 + h * 128, 512 + (h + 1) * 128), vTb[:, h, cs], CB('ident'))
                            E(DVE, 'tensor_tensor', out=sc1[:], in0=beta[:, j, :], in1=Ecol[:, 0:4], op=ALU.mult)
                            kv3 = lambda a: psKV.v(psKV.t[:, :].bitcast(BF16)[:, a:a + 512].rearrange("p (h i) -> p h i", h=4))
                            E(DVE, 'tensor_tensor', out=kbg[:], in0=kv3(0), in1=sc1.v(bc_last(sc1.t[:, 0:4], 128)), op=ALU.mult)
                            E(DVE, 'tensor_tensor', out=kd[:], in0=kv3(0), in1=Ecol.v(bc_last(Ecol.t[:, 4:8], 128)), op=ALU.mult)
                            E(DVE, 'tensor_tensor', out=vb[:], in0=kv3(512), in1=beta.v(bc_last(beta.t[:, j, :], 128)), op=ALU.mult)
                            for h in range(4):
                                hs = slice(h * 128, (h + 1) * 128)
                                kb.mm(psW[:, hs], kbg[:, h, :], TT[:, h, :])
                            E(ACT, 'activation', out=flat(nwT), in_=psW[:, :], func=AF.Copy, scale=-1.0)
                            for h in range(4):
                                hs = slice(h * 128, (h + 1) * 128)
                                kb.mm(psE[:, hs], C('ones'), Mh[:, h, :])
                            E(ACT, 'activation', out=flat(EG), in_=psE[:, :], func=AF.Exp)
                            E(POOL, 'tensor_tensor', out=qd[:], in0=qn[:, :, cs], in1=EG[:], op=ALU.mult)
                            for h in range(4):
                                hs = slice(h * 128, (h + 1) * 128)
                                kb.mm(psV[:, hs], TT[:, h, :], vb[:, h, :], start=True, stop=False)
                                kb.mm(psV[:, hs], nwT[:, h, :], S[:, h, :], start=False, stop=True)
                            E(ACT, 'copy', out=flat(vnew), in_=psV[:, :])
                            for h in range(4):
                                hs = slice(h * 128, (h + 1) * 128)
                                kb.mm(psO[:, hs], Sb[:, h, :], qd[:, h, :], start=True, stop=False)
                                kb.mm(psO[:, hs], vnew[:, h, :], AT[:, h, :], start=False, stop=True)
                            for h in range(4):
                                hs = slice(h * 128, (h + 1) * 128)
                                kb.mm(psS[:, hs], kd[:, h, :], vnew[:, h, :])
                            E(ACT, 'copy', out=o1[:, :, cs], in_=f3(psO))
                            E(DVE, 'tensor_tensor', out=Stmp[:], in0=S[:], in1=Ecol.v(bc_last(Ecol.t[:, 8:12], 128)), op=ALU.mult)
                            E(DVE, 'tensor_tensor', out=flat(S), in0=flat(Stmp), in1=psS[:, :], op=ALU.add)
                            E(ACT, 'copy', out=Sb[:], in_=S[:])
                        if d == 0:
                            kb.dma(View([kb.db('ofT%d' % c, b) for c in range(4)],
                                        ofT[0:4, :, t0:t0 + NB].rearrange("c p t -> p c t")), o1[:])
                        else:
                            o2 = obs[bi % 2]
                            E(ACT, 'activation', out=gz[:], in_=gz[:], func=AF.Silu)
                            for h in range(4):
                                oo, sq_, sd_ = oh[h % 2], sqb[h % 2], sd[h % 2]
                                E(DVE, 'tensor_tensor', out=oo[:, :], in0=o1[:, h, :], in1=of_[:, h, :], op=ALU.add)
                                E(ACT, 'activation', out=sq_[:, :], in_=oo[:, :], func=AF.Square)
                                kb.mm(psN[:, :], CB('ones'), sq_[:, :])
                                E(DVE, 'tensor_scalar', out=sd_[:, :], in0=psN[:, :], scalar1=1.0 / 128, scalar2=1e-6, op0=ALU.mult, op1=ALU.add)
                                E(ACT, 'activation', out=sd_[:, :], in_=sd_[:, :], func=AF.Sqrt)
                                E(DVE, 'reciprocal', out=sd_[:, :], in_=sd_[:, :])
                                E(POOL, 'tensor_tensor', out=oo[:, :], in0=oo[:, :], in1=sd_[:, :], op=ALU.mult)
                                E(DVE, 'scalar_tensor_tensor', out=o2[:, h, :], in0=oo[:, :], scalar=cpv(l, 'dn_nw'), in1=gz[:, h, :],
                                  op0=ALU.mult, op1=ALU.mult)
                            kb.dma(View([kb.db('obT', b)], obT[0:4, :, t0:t0 + NB].rearrange("c p t -> p c t")), o2[:])
            kb.barrier()

        MIX = {'lru': phase_lru, 'hg': phase_hg, 'dn': phase_dn}
        run = phases if phases is not None else ['wprep', 'embed', 'proj', 'lru', 'hg', 'dn', 'post']
        if 'wprep' in run:
            phase_wprep()
        if 'embed' in run:
            phase_embed()
        for l in range(nlayers):
            if 'proj' in run:
                phase_proj(l)
            for m in ['lru', 'hg', 'dn']:
                if m in run and m in MIX:
                    MIX[m](l)
            if 'post' in run:
                phase_post(l, l == nlayers - 1)
        kb.barrier()
        print("instructions:", kb.ninst)
    return nc


def _colparams(W):
    L = DEPTH
    cp = np.zeros((L, 128, NCP), np.float32)

    def put(l, name, arr):
        o, w = CP[name]
        cp[l, :, o:o + w] = arr

    def chan(v, nt):
        return np.asarray(v).reshape(nt, 128).T
    for l in range(L):
        put(l, 'dn_conv', np.asarray(W['dn_conv_w'][l]).reshape(4, 12, 128).transpose(2, 1, 0).reshape(128, 48))
        put(l, 'lru_conv', np.asarray(W['lru_conv_w'][l]).reshape(4, 4, 128).transpose(2, 1, 0).reshape(128, 16))
        put(l, 'lru_cb', chan(W['lru_conv_b'][l], 4))
        put(l, 'lru_ba', np.asarray(W['lru_ba'][l]).reshape(2, 4, 128).transpose(2, 0, 1).reshape(128, 8))
        put(l, 'lru_bx', np.asarray(W['lru_bx'][l]).reshape(2, 4, 128).transpose(2, 0, 1).reshape(128, 8))
        put(l, 'lru_lam', np.asarray(W['lru_lambda'][l]).reshape(2, 4, 128).transpose(2, 0, 1).reshape(128, 8))
        put(l, 'dn_nw', np.asarray(W['dn_norm_w'][l]).reshape(128, 1))
        put(l, 'hg_nw', np.asarray(W['hg_norm_w'][l]).reshape(128, 1))
        for n in ['ln1_g', 'ln1_b', 'ln2_g', 'ln2_b']:
            put(l, n, chan(W[n][l], 8))
        put(l, 'lb0', chan(W['hg_lb_logits'][0], 4))
        put(l, 'lb1', chan(W['hg_lb_logits'][1], 4))
        put(l, 'emb_g', chan(W['emb_ln_g'], 8))
        put(l, 'emb_b', chan(W['emb_ln_b'], 8))
        put(l, 'alog', np.tile(np.asarray(W['dn_A_log'][l]).reshape(1, 8), (128, 1)))
        put(l, 'dtb', np.tile(np.asarray(W['dn_dt_bias'][l]).reshape(1, 8), (128, 1)))
    return cp


def _blockdiag(W):
    bd = np.zeros((DEPTH, 128, 16, 128), np.float32)
    for l in range(DEPTH):
        for ai, nm in enumerate(['lru_wa', 'lru_wx']):
            w = np.asarray(W[nm][l])
            for d in range(2):
                for t in range(4):
                    idx = ai * 8 + d * 4 + t
                    for s in range(2):
                        bd[l, s * 64:(s + 1) * 64, idx, s * 64:(s + 1) * 64] = w[d, 2 * t + s]
    return bd


def make_in_maps(W, xs, ps_, flags):
    cp = _colparams(W)
    bd = _blockdiag(W)
    common = dict(cst=CONST_ARR, cp=cp, bd=bd,
                  w_in=np.ascontiguousarray(W['w_in'], dtype=np.float32), w_branch=np.asarray(W['w_branch'], np.float32),
                  w_out=np.asarray(W['w_out'], np.float32), w_mlp1=np.asarray(W['w_mlp1'], np.float32),
                  w_mlp2=np.asarray(W['w_mlp2'], np.float32), w_ple_gate=np.asarray(W['w_ple_gate'], np.float32),
                  w_ple_proj=np.asarray(W['w_ple_proj'], np.float32))
    maps = []
    for x, p, f in zip(xs, ps_, flags):
        m = dict(common)
        m['x'] = np.ascontiguousarray(x, dtype=np.float32)
        m['p'] = np.ascontiguousarray(p, dtype=np.float32)
        m['flag'] = np.full((128, 1), f, np.float32)
        maps.append(m)
    return maps


_NC_CACHE = {}


def kernel(x_prompt, x_sample, p_prompt, p_sample, **W):
    x_prompt = np.asarray(x_prompt)
    x_sample = np.asarray(x_sample)
    p_prompt = np.asarray(p_prompt)
    p_sample = np.asarray(p_sample)
    T, SEG = 8192, 4096
    assign = [(0, 1), (2, 3), (4, 4), (5, 5), (6, 6), (7, 7)]
    xs, ps_, flags = [], [], []
    for c in range(2):
        xs.append(x_sample[c])
        ps_.append(p_sample[:, c])
        flags.append(1.0)
    for a, b in assign:
        xs.append(np.concatenate([x_prompt[a], x_prompt[b]], axis=0))
        ps_.append(np.concatenate([p_prompt[:, a], p_prompt[:, b]], axis=1))
        flags.append(0.0)
    if 'nc' not in _NC_CACHE:
        _NC_CACHE['nc'] = build(T, SEG)
    nc = _NC_CACHE['nc']
    maps = make_in_maps(W, xs, ps_, flags)
    res = run_bass_kernel_spmd(nc, maps, core_ids=list(range(NCORES)))
    y_prompt = np.zeros((8, 4096, D), np.float32)
    y_sample = np.zeros((2, 8192, D), np.float32)
    for c in range(2):
        y_sample[c] = res.results[c]['y']
    for i, (a, b) in enumerate(assign):
        y = res.results[2 + i]['y']
        y_prompt[a] = y[:4096]
        if b != a:
            y_prompt[b] = y[4096:]
    return (y_prompt, y_sample)
```

```python
import numpy as np
from contextlib import ExitStack
import concourse.bass as bass
import concourse.mybir as mybir
from concourse.bass_utils import run_bass_kernel_spmd

F32 = mybir.dt.float32
BF16 = mybir.dt.bfloat16
F32R = mybir.dt.float32r
AF = mybir.ActivationFunctionType
ALU = mybir.AluOpType

D = 1024
NIN = 8720
DFF = 4096
DPLE = 256
DEPTH = 2
ALPHA = (2.0 * DEPTH) ** 0.25
NB = 512
NCORES = 8
DQ, DK, DV, DZ, HQ, HFF, HFB, HI, HGT, CX, CG, GA, GB, GC, AB = 0, 4, 8, 12, 16, 20, 24, 28, 32, 36, 40, 44, 52, 60, 68
NHT = 69
WIN_TILES = [(i * 128, 128) for i in range(16)] + [(2064 + 128 * i, 128) for i in range(52)] + [(2048, 16)]

def _consts():
    i = np.arange(128)
    c = {}
    c['ident'] = np.eye(128)
    c['ones'] = np.ones((128, 128))
    c['negones'] = -np.ones((128, 128))
    c['tri_le'] = (i[:, None] <= i[None, :]) * 1.0
    c['tri_ge'] = (i[:, None] >= i[None, :]) * 1.0
    c['tri_gt'] = (i[:, None] > i[None, :]) * 1.0
    c['tri_lt'] = (i[:, None] < i[None, :]) * 1.0
    c['nm_f'] = np.where(i[:, None] < i[None, :], -30000.0, 0.0)
    c['nm_b'] = np.where(i[:, None] > i[None, :], -30000.0, 0.0)
    c['sm_f4'] = np.tile((i[:, None] > i[None, :]) * 1.0, (1, 4))
    c['sm_b4'] = np.tile((i[:, None] < i[None, :]) * 1.0, (1, 4))
    c['ident4'] = np.tile(np.eye(128), (1, 4))
    j = np.arange(64)
    c['hgm_f4'] = np.tile((j[None, :] >= (i[:, None] % 64)) * 1.0, (1, 4))
    c['hgm_b4'] = np.tile((j[None, :] <= (i[:, None] % 64)) * 1.0, (1, 4))
    t = np.arange(NB)
    c['rst_f'] = np.tile(((t % 64) != 0) * 1.0, (128, 1))
    c['rst_b'] = np.tile(((t % 64) != 63) * 1.0, (128, 1))
    offs = {}
    o = 0
    arrs = []
    for k, v in c.items():
        offs[k] = (o, v.shape[1])
        o += v.shape[1]
        arrs.append(v.astype(np.float32))
    return np.ascontiguousarray(np.concatenate(arrs, axis=1)), offs


CONST_ARR, CONST_OFF = _consts()
CP = {}
_o = 0
for _n, _w in [('dn_conv', 48), ('lru_conv', 16), ('lru_cb', 4), ('lru_ba', 8), ('lru_bx', 8), ('lru_lam', 8),
               ('dn_nw', 1), ('hg_nw', 1), ('ln1_g', 8), ('ln1_b', 8), ('ln2_g', 8), ('ln2_b', 8),
               ('lb0', 4), ('lb1', 4), ('emb_g', 8), ('emb_b', 8), ('alog', 8), ('dtb', 8)]:
    CP[_n] = (_o, _w)
    _o += _w
NCP = _o


class View:
    def __init__(self, bufs, ap):
        self.bufs = bufs
        self.ap = ap

    def __getitem__(self, idx):
        return View(self.bufs, self.ap[idx])


class Buf:
    def __init__(self, t=None, name=""):
        self.t = t
        self.w = {}
        self.r = {}
        self.name = name

    def __getitem__(self, idx):
        return View([self], self.t[idx])

    def v(self, ap):
        return View([self], ap)


class Eng:
    def __init__(self, h, sid, name):
        self.h = h
        self.sid = sid
        self.cnt = 0
        self.waited = {}
        self.name = name
        self.old = []


class KB:
    NDS = 48
    EPOCH = 20000

    def __init__(self, nc, stack):
        self.nc = nc
        self.sems = {}
        self.nsid = 0

        def mk(n):
            s = stack.enter_context(nc.semaphore(n))
            sid = self.nsid
            self.nsid += 1
            self.sems[sid] = s
            return sid
        self.mk = mk
        self.PE = Eng(nc.tensor, mk("s_pe"), "pe")
        self.DVE = Eng(nc.vector, mk("s_dve"), "dve")
        self.ACT = Eng(nc.scalar, mk("s_act"), "act")
        self.POOL = Eng(nc.gpsimd, mk("s_pool"), "pool")
        self.SP = Eng(nc.sync, None, "sp")
        self.engs = [self.PE, self.DVE, self.ACT, self.POOL]
        self.dsid = [mk("s_dma%d" % i) for i in range(self.NDS)]
        self.dval = [0] * self.NDS
        self.dnext = 0
        self.dbufs = {}
        self.ninst = 0

    def db(self, name, blk):
        k = (name, blk)
        if k not in self.dbufs:
            self.dbufs[k] = Buf(None, "%s_%s" % (name, blk))
        return self.dbufs[k]

    def _wait(self, eng, sid, val):
        if val <= 0 or eng.waited.get(sid, 0) >= val:
            return
        eng.h.wait_ge(self.sems[sid], val)
        eng.waited[sid] = val

    def _sync(self, eng, reads, writes):
        own = eng.sid
        for b in reads:
            for sid, v in b.w.items():
                self._wait(eng, sid, v)
        for b in writes:
            for sid, v in b.w.items():
                if sid != own:
                    self._wait(eng, sid, v)
            for sid, v in b.r.items():
                if sid != own:
                    self._wait(eng, sid, v)

    def _mark(self, reads, writes, sid, val):
        for b in reads:
            b.r[sid] = max(b.r.get(sid, 0), val)
        for b in writes:
            b.w = {sid: val}
            b.r = {}

    def E(self, eng, meth, **kw):
        reads, writes, args = [], [], {}
        for k_, v in kw.items():
            if isinstance(v, View):
                if k_ in ('out', 'accum_out', 'ap'):
                    writes.extend(v.bufs)
                else:
                    reads.extend(v.bufs)
                args[k_] = v.ap
            else:
                args[k_] = v
        if eng.cnt >= self.EPOCH:
            eng.old.append((eng.sid, eng.cnt))
            eng.sid = self.mk("s_%s_e%d" % (eng.name, len(eng.old)))
            eng.cnt = 0
        self._sync(eng, reads, writes)
        inst = getattr(eng.h, meth)(**args)
        inst.then_inc(self.sems[eng.sid], 1)
        eng.cnt += 1
        self.ninst += 1
        self._mark(reads, writes, eng.sid, eng.cnt)
        return inst

    def mm(self, out, lhsT, rhs, start=True, stop=True, extra_w=()):
        return self.E(self.PE, 'matmul', out=out, lhsT=lhsT, rhs=rhs, start=start, stop=stop)

    def tr(self, out, in_, ident):
        return self.E(self.PE, 'transpose', out=out, in_=in_, identity=ident)

    def dma(self, out, in_, q=None):
        eng = self.SP
        s = self.dnext
        self.dnext = (self.dnext + 1) % self.NDS
        sid = self.dsid[s]
        self._wait(eng, sid, self.dval[s])
        self._sync(eng, in_.bufs, out.bufs)
        self.dval[s] += 16
        eng.h.dma_start(out=out.ap, in_=in_.ap).then_inc(self.sems[sid], 16)
        self.ninst += 1
        self._mark(in_.bufs, out.bufs, sid, self.dval[s])

    def barrier(self):
        for e in self.engs + [self.SP]:
            for o in self.engs:
                if o is not e:
                    if o.cnt > 0:
                        self._wait(e, o.sid, o.cnt)
                    elif o.old:
                        self._wait(e, o.old[-1][0], o.old[-1][1])
            for s in range(self.NDS):
                self._wait(e, self.dsid[s], self.dval[s])


def bc_last(ap, n):
    l = [list(x) for x in ap.ap]
    return bass.AP(ap.tensor, ap.offset, l + [[0, n]])


def bc_mid(ap, n):
    l = [list(x) for x in ap.ap]
    return bass.AP(ap.tensor, ap.offset, [l[0], [0, n]] + l[1:])


def build(T, SEG, phases=None, debug_out=(), nlayers=DEPTH):
    NBLK = T // NB
    nc = bass.Bass("TRN2", target_bir_lowering=False)
    dt = nc.dram_tensor
    x_in = dt("x", [T, D], F32, kind="ExternalInput").ap()
    p_in = dt("p", [DEPTH, T, DPLE], F32, kind="ExternalInput").ap()
    cst_in = dt("cst", list(CONST_ARR.shape), F32, kind="ExternalInput").ap()
    cp_in = dt("cp", [DEPTH, 128, NCP], F32, kind="ExternalInput").ap()
    bd_in = dt("bd", [DEPTH, 128, 16, 128], F32, kind="ExternalInput").ap()
    flag_in = dt("flag", [128, 1], F32, kind="ExternalInput").ap()
    w_in = dt("w_in", [DEPTH, D, NIN], F32, kind="ExternalInput").ap()
    w_branch = dt("w_branch", [DEPTH, 3, 512, D], F32, kind="ExternalInput").ap()
    w_out = dt("w_out", [DEPTH, D, D], F32, kind="ExternalInput").ap()
    w_mlp1 = dt("w_mlp1", [DEPTH, D, DFF], F32, kind="ExternalInput").ap()
    w_mlp2 = dt("w_mlp2", [DEPTH, DFF, D], F32, kind="ExternalInput").ap()
    w_pg = dt("w_ple_gate", [DEPTH, D, D], F32, kind="ExternalInput").ap()
    w_pp = dt("w_ple_proj", [DEPTH, DPLE, D], F32, kind="ExternalInput").ap()
    y_out = dt("y", [T, D], F32, kind="ExternalOutput").ap()
    dbg = {}
    okind = lambda n: "ExternalOutput" if n in debug_out else "Internal"
    xT = dt("xT", [8, 128, T], F32, kind=okind("xT")).ap()
    class _HT:
        SPLIT = 36

        def __init__(self):
            self.a = dt("hT", [self.SPLIT, 128, T], F32, kind=okind("hT")).ap()
            self.b = dt("hTb", [NHT - self.SPLIT, 128, T], F32, kind=okind("hT")).ap()

        def __getitem__(self, idx):
            f = idx[0]
            rest = tuple(idx[1:])
            if isinstance(f, slice):
                if f.start < self.SPLIT:
                    assert f.stop <= self.SPLIT
                    return self.a[(f,) + rest]
                return self.b[(slice(f.start - self.SPLIT, f.stop - self.SPLIT),) + rest]
            if f < self.SPLIT:
                return self.a[(f,) + rest]
            return self.b[(f - self.SPLIT,) + rest]
    hT = _HT()
    ofT = dt("ofT", [12, 128, T], F32, kind=okind("ofT")).ap()
    obT = dt("obT", [12, 128, T], BF16, kind=okind("obT")).ap()
    wspec = {
        'win': (8, WIN_TILES, lambda l: w_in[l]),
        'wb0': (4, [(i * 128, 128) for i in range(8)], lambda l: w_branch[l, 0]),
        'wb1': (4, [(i * 128, 128) for i in range(8)], lambda l: w_branch[l, 1]),
        'wb2': (4, [(i * 128, 128) for i in range(8)], lambda l: w_branch[l, 2]),
        'wout': (8, [(i * 128, 128) for i in range(8)], lambda l: w_out[l]),
        'wm1': (8, [(i * 128, 128) for i in range(32)], lambda l: w_mlp1[l]),
        'wm2': (32, [(i * 128, 128) for i in range(8)], lambda l: w_mlp2[l]),
        'wpg': (8, [(i * 128, 128) for i in range(8)], lambda l: w_pg[l]),
        'wpp': (2, [(i * 128, 128) for i in range(8)], lambda l: w_pp[l]),
    }
    ws = {n: dt("ws_" + n, [DEPTH, len(s[1]), 128, s[0], 128], BF16, kind="Internal").ap() for n, s in wspec.items()}

    with ExitStack() as st0:
        kb = KB(nc, st0)
        PE, DVE, ACT, POOL = kb.PE, kb.DVE, kb.ACT, kb.POOL
        E = kb.E

        uniq = [0]

        def sb(st, name, shape, dtype=F32):
            uniq[0] += 1
            return Buf(st.enter_context(nc.sbuf_tensor("sb%d_%s" % (uniq[0], name), shape, dtype)), name)

        ps = [Buf(st0.enter_context(nc.psum_tensor("ps%d" % i, [128, 512], F32)), "ps%d" % i) for i in range(8)]
        cst = sb(st0, "cst", list(CONST_ARR.shape))
        cpt = [sb(st0, "cp%d" % l, [128, NCP]) for l in range(DEPTH)]
        flag = sb(st0, "flag", [128, 1])
        cbf = sb(st0, "cbf", [128, 128 * 2 + 512 * 3 + 256 * 2], BF16)
        kb.dma(cst[:], View([], cst_in[:, :]))
        for l in range(DEPTH):
            kb.dma(cpt[l][:], View([], cp_in[l]))
        kb.dma(flag[:], View([], flag_in[:, :]))

        def C(name):
            o, w = CONST_OFF[name]
            return cst[:, o:o + w]

        cb_off = {}
        o = 0
        for n in ['ident', 'ones', 'sm_f4', 'sm_b4', 'ident4', 'hgm_f4', 'hgm_b4']:
            w = CONST_OFF[n][1]
            cb_off[n] = (o, w)
            E(DVE, 'tensor_copy', out=cbf[:, o:o + w], in_=C(n))
            o += w

        def CB(name):
            o, w = cb_off[name]
            return cbf[:, o:o + w]

        crn = ['ident', 'ones', 'negones', 'tri_le', 'tri_ge', 'tri_gt', 'tri_lt', 'nm_f', 'nm_b']
        crt = sb(st0, "crt", [128, 128 * len(crn)], F32R)
        for i_, n in enumerate(crn):
            E(DVE, 'tensor_copy', out=crt[:, i_ * 128:(i_ + 1) * 128], in_=C(n))

        def CR(name):
            i_ = crn.index(name)
            return crt[:, i_ * 128:(i_ + 1) * 128]

        def asf(v):
            return View(v.bufs, v.ap.bitcast(F32))

        def cpv(l, name, i=0, n=1):
            o, w = CP[name]
            return cpt[l][:, o + i:o + i + n]

        def phase_wprep():
            with ExitStack() as st:
                sf = [sb(st, "wpf%d" % i, [128, 4096]) for i in range(2)]
                sbf = [sb(st, "wpb%d" % i, [128, 4096], BF16) for i in range(2)]
                it = 0
                for l in range(DEPTH):
                    for name, (n_k, tiles, srcf) in wspec.items():
                        src = srcf(l).rearrange("(k p) n -> p k n", p=128)
                        gmax = 4 if n_k <= 8 else 1
                        ti = 0
                        while ti < len(tiles):
                            g = 1
                            while (g < gmax and ti + g < len(tiles) and tiles[ti + g][1] == 128 and tiles[ti][1] == 128
                                   and tiles[ti + g][0] == tiles[ti][0] + 128 * g):
                                g += 1
                            c0 = tiles[ti][0]
                            gw = sum(tiles[ti + j][1] for j in range(g))
                            s = it % 2
                            it += 1
                            fv = sf[s].t[:, 0:n_k * gw].rearrange("p (k c) -> p k c", k=n_k)
                            bv = sbf[s].t[:, 0:n_k * gw].rearrange("p (k c) -> p k c", k=n_k)
                            kb.dma(sf[s].v(fv), View([], src[:, :, c0:c0 + gw]))
                            E(POOL if it % 2 else ACT, 'tensor_copy' if it % 2 else 'copy', out=sbf[s].v(bv), in_=sf[s].v(fv))
                            for j in range(g):
                                wdt = tiles[ti + j][1]
                                kb.dma(View([kb.db('ws_' + name, l)], ws[name][l, ti + j, :, :, 0:wdt]),
                                       sbf[s].v(bv[:, :, j * 128:j * 128 + wdt]))
                            ti += g
            kb.barrier()

        wctr = [0]
        pctr = [0]

        def dense(wb, l, name, tile_ids, rhs_fn, N, evac, psbanks=(0, 1, 2)):
            n_k = wspec[name][0]
            tiles = wspec[name][1]
            tile_ids = list(tile_ids)
            depth = len(wb) - 1
            base = wctr[0]
            wctr[0] += len(tile_ids)

            def wload(i):
                kb.dma(wb[(base + i) % len(wb)][:, 0:n_k, :], View([kb.db('ws_' + name, l)], ws[name][l, tile_ids[i]]))
            for i in range(min(depth, len(tile_ids))):
                wload(i)
            for i, ti in enumerate(tile_ids):
                wdt = tiles[ti][1]
                s = (base + i) % len(wb)
                if i + depth < len(tile_ids):
                    wload(i + depth)
                pb = ps[psbanks[pctr[0] % len(psbanks)]]
                pctr[0] += 1
                for k in range(n_k):
                    kb.mm(pb[:, 0:N], wb[s][:, k, :], rhs_fn(k), start=(k == 0), stop=(k == n_k - 1))
                evac(ti, pb[0:wdt, 0:N], wdt)

        def layer_norm(st_bufs, src, g_fn, b_fn, dst_f=None, dst_b=None, N=NB):
            sq, xb_, mean, m2, rstd, tmp = st_bufs
            pm, pq = ps[3], ps[4]
            for c in range(8):
                E(ACT, 'activation', out=sq[c % 2][:, 0:N], in_=src[:, c, 0:N], func=AF.Square)
                E(POOL, 'tensor_copy', out=xb_[c % 2][:, 0:N], in_=src[:, c, 0:N])
                kb.mm(pm[:, 0:N], CB('ones'), xb_[c % 2][:, 0:N], start=(c == 0), stop=(c == 7))
                kb.mm(pq[:, 0:N], CB('ones'), sq[c % 2][:, 0:N], start=(c == 0), stop=(c == 7))
            E(ACT, 'activation', out=mean[:, 0:N], in_=pm[:, 0:N], func=AF.Copy, scale=1.0 / D)
            E(DVE, 'tensor_tensor', out=m2[:, 0:N], in0=mean[:, 0:N], in1=mean[:, 0:N], op=ALU.mult)
            E(DVE, 'scalar_tensor_tensor', out=m2[:, 0:N], in0=pq[:, 0:N], scalar=1.0 / D, in1=m2[:, 0:N],
              op0=ALU.mult, op1=ALU.subtract)
            E(DVE, 'tensor_scalar', out=m2[:, 0:N], in0=m2[:, 0:N], scalar1=0.0, scalar2=1e-5, op0=ALU.max, op1=ALU.add)
            E(ACT, 'activation', out=m2[:, 0:N], in_=m2[:, 0:N], func=AF.Sqrt)
            E(DVE, 'reciprocal', out=rstd[:, 0:N], in_=m2[:, 0:N])
            for c in range(8):
                t_ = tmp[c % 2]
                E(DVE, 'tensor_tensor', out=t_[:, 0:N], in0=src[:, c, 0:N], in1=mean[:, 0:N], op=ALU.subtract)
                E(DVE, 'tensor_tensor', out=t_[:, 0:N], in0=t_[:, 0:N], in1=rstd[:, 0:N], op=ALU.mult)
                if dst_f is not None:
                    E(ACT, 'activation', out=dst_f[:, c, 0:N], in_=t_[:, 0:N], func=AF.Identity, scale=g_fn(c), bias=b_fn(c))
                if dst_b is not None:
                    E(ACT, 'activation', out=dst_b[:, c, 0:N], in_=t_[:, 0:N], func=AF.Identity, scale=g_fn(c), bias=b_fn(c))

        def ln_bufs(st):
            return ([sb(st, "ln_sq%d" % i, [128, NB], BF16) for i in range(2)],
                    [sb(st, "ln_xb%d" % i, [128, NB], BF16) for i in range(2)],
                    sb(st, "ln_mean", [128, NB]), sb(st, "ln_m2", [128, NB]), sb(st, "ln_rstd", [128, NB]),
                    [sb(st, "ln_tmp%d" % i, [128, NB]) for i in range(2)])

        def phase_embed():
            with ExitStack() as st:
                xin = [sb(st, "e_xin%d" % i, [128, 4, D]) for i in range(2)]
                xf = sb(st, "e_xf", [128, 8, NB])
                xo = sb(st, "e_xo", [128, 8, NB])
                lb = ln_bufs(st)
                for b in range(NBLK):
                    t0 = b * NB
                    xi = xin[b % 2]
                    kb.dma(xi[:], View([], x_in[t0:t0 + NB, :].rearrange("(j p) d -> p j d", p=128)))
                    for c in range(8):
                        pb = ps[c % 3]
                        for j in range(4):
                            kb.tr(pb[:, j * 128:(j + 1) * 128], xi[:, j, c * 128:(c + 1) * 128], C('ident'))
                        E(ACT if c % 2 else DVE, 'copy' if c % 2 else 'tensor_copy', out=xf[:, c, :], in_=pb[:, :])
                    layer_norm(lb, xf, lambda c: cpv(0, 'emb_g', c), lambda c: cpv(0, 'emb_b', c), dst_f=xo)
                    kb.dma(View([kb.db('xT', b)], xT[:, :, t0:t0 + NB].rearrange("c p t -> p c t")), xo[:])
            kb.barrier()

        def phase_proj(l):
            with ExitStack() as st:
                xin = [sb(st, "p_xin%d" % i, [128, 8, NB]) for i in range(2)]
                xb_ = [sb(st, "p_xb%d" % i, [128, 8, NB], BF16) for i in range(2)]
                stg = [sb(st, "p_stg%d" % i, [128, NB]) for i in range(6)]
                wb = [sb(st, "p_wb%d" % i, [128, 8, 128], BF16) for i in range(4)]
                sctr = [0]
                kb.dma(xin[0][:], View([kb.db('xT', 0)], xT[:, :, 0:NB].rearrange("c p t -> p c t")))
                for b in range(NBLK):
                    t0 = b * NB
                    xi, xbb = xin[b % 2], xb_[b % 2]
                    if b + 1 < NBLK:
                        kb.dma(xin[(b + 1) % 2][:], View([kb.db('xT', b + 1)], xT[:, :, t0 + NB:t0 + 2 * NB].rearrange("c p t -> p c t")))
                    E(POOL, 'tensor_copy', out=xbb[:, 0:4, :], in_=xi[:, 0:4, :])
                    E(DVE, 'tensor_copy', out=xbb[:, 4:8, :], in_=xi[:, 4:8, :])

                    def evac(ti, pv, wdt):
                        sg_ = stg[sctr[0] % len(stg)]
                        sctr[0] += 1
                        if GA <= ti < AB:
                            E(ACT, 'activation', out=sg_[0:wdt, :], in_=pv, func=AF.Sigmoid)
                        elif sctr[0] % 2:
                            E(DVE, 'tensor_copy', out=sg_[0:wdt, :], in_=pv)
                        else:
                            E(ACT, 'copy', out=sg_[0:wdt, :], in_=pv)
                        kb.dma(View([kb.db('hT%d' % ti, b)], hT[ti, 0:wdt, t0:t0 + NB]), sg_[0:wdt, :])
                    dense(wb, l, 'win', range(NHT), lambda k: xbb[:, k, :], NB, evac)
            kb.barrier()

        def phase_post(l, last):
            with ExitStack() as st:
                xres = sb(st, "q_xres", [128, 8, NB])
                hid = sb(st, "q_hid", [128, 32, NB], BF16)
                sig = sb(st, "q_sig", [128, 8, NB])
                acc = sb(st, "q_acc", [128, 8 * NB])
                accb = sb(st, "q_accb", [128, 8, NB], BF16)
                x1 = sb(st, "q_x1", [128, 8, NB])
                x1b = sb(st, "q_x1b", [128, 8, NB], BF16)
                rl = [sb(st, "q_rl%d" % i, [128, NB]) for i in range(2)]
                pin = sb(st, "q_pin", [128, 4, DPLE])
                pT = sb(st, "q_pT", [128, 2, NB], BF16)
                wb = [sb(st, "q_wb%d" % i, [128, 32, 128], BF16) for i in range(3)]
                lb = ln_bufs(st)
                acc3 = acc.v(acc.t[:, :].rearrange("p (c t) -> p c t", c=8))
                accv = lambda c: acc.v(acc.t[:, c * NB:(c + 1) * NB])
                for b in range(NBLK):
                    t0 = b * NB
                    kb.dma(xres[:], View([kb.db('xT', b)], xT[:, :, t0:t0 + NB].rearrange("c p t -> p c t")))
                    kb.dma(hid[:, 0:12, :], View([kb.db('obT', b)], obT[:, :, t0:t0 + NB].rearrange("c p t -> p c t")))
                    kb.dma(pin[:], View([], p_in[l, t0:t0 + NB, :].rearrange("(j p) d -> p j d", p=128)))
                    for c in range(2):
                        pb = ps[5 + c]
                        for j in range(4):
                            kb.tr(pb[:, j * 128:(j + 1) * 128], pin[:, j, c * 128:(c + 1) * 128], C('ident'))
                        E(ACT, 'copy', out=pT[:, c, :], in_=pb[:, :])
                    for n in range(3):
                        gt0 = [GA, GB, GC][n]
                        kb.dma(sig[:], View([kb.db('hT%d' % (gt0 + c), b) for c in range(8)],
                                            hT[gt0:gt0 + 8, :, t0:t0 + NB].rearrange("c p t -> p c t")))

                        def evac(ti, pv, wdt, n=n):
                            if n == 0:
                                E(DVE, 'tensor_tensor', out=accv(ti), in0=pv, in1=sig[:, ti, :], op=ALU.mult)
                            else:
                                r_ = rl[ti % 2]
                                E(DVE, 'tensor_tensor', out=r_[:, :], in0=pv, in1=sig[:, ti, :], op=ALU.mult)
                                if n == 1:
                                    E(POOL, 'tensor_tensor', out=accv(ti), in0=accv(ti), in1=r_[:, :], op=ALU.add)
                                else:
                                    E(POOL, 'tensor_tensor', out=accb[:, ti, :], in0=accv(ti), in1=r_[:, :], op=ALU.add)
                        dense(wb, l, 'wb%d' % n, range(8), lambda k, n=n: hid[:, n * 4 + k, :], NB, evac)

                    def evac(ti, pv, wdt):
                        E(DVE, 'scalar_tensor_tensor', out=accv(ti), in0=xres[:, ti, :], scalar=ALPHA, in1=pv,
                          op0=ALU.mult, op1=ALU.add)
                    dense(wb, l, 'wout', range(8), lambda k: accb[:, k, :], NB, evac)
                    layer_norm(lb, acc3, lambda c: cpv(l, 'ln1_g', c), lambda c: cpv(l, 'ln1_b', c), dst_f=x1, dst_b=x1b)

                    def evac(ti, pv, wdt):
                        r_ = rl[ti % 2]
                        E(ACT, 'activation', out=r_[:, :], in_=pv, func=AF.Relu)
                        E(POOL if ti % 2 else DVE, 'tensor_tensor', out=hid[:, ti, :], in0=r_[:, :], in1=r_[:, :], op=ALU.mult)
                    dense(wb, l, 'wm1', range(32), lambda k: x1b[:, k, :], NB, evac)

                    def evac(ti, pv, wdt):
                        E(ACT, 'activation', out=sig[:, ti, :], in_=pv, func=AF.Sigmoid)
                    dense(wb, l, 'wpg', range(8), lambda k: x1b[:, k, :], NB, evac)

                    def evac(ti, pv, wdt):
                        E(DVE, 'tensor_tensor', out=sig[:, ti, :], in0=pv, in1=sig[:, ti, :], op=ALU.mult)
                    dense(wb, l, 'wpp', range(8), lambda k: pT[:, k, :], NB, evac)

                    def evac(ti, pv, wdt):
                        E(DVE, 'scalar_tensor_tensor', out=accv(ti), in0=x1[:, ti, :], scalar=ALPHA, in1=pv,
                          op0=ALU.mult, op1=ALU.add)
                        E(POOL, 'tensor_tensor', out=accv(ti), in0=accv(ti), in1=sig[:, ti, :], op=ALU.add)
                    dense(wb, l, 'wm2', range(8), lambda k: hid[:, k, :], NB, evac)
                    layer_norm(lb, acc3, lambda c: cpv(l, 'ln2_g', c), lambda c: cpv(l, 'ln2_b', c), dst_f=xres)
                    if not last:
                        kb.dma(View([kb.db('xT', b)], xT[:, :, t0:t0 + NB].rearrange("c p t -> p c t")), xres[:])
                    else:
                        ov = x1.v(x1.t[:, :, :].rearrange("p c t -> p (c t)").rearrange("p (j d) -> p j d", j=4))
                        for j in range(4):
                            for c2 in range(2):
                                pb = ps[5 + c2]
                                for cc in range(4):
                                    c = c2 * 4 + cc
                                    kb.tr(pb[:, cc * 128:(cc + 1) * 128], xres[:, c, j * 128:(j + 1) * 128], C('ident'))
                                E(ACT if c2 else DVE, 'copy' if c2 else 'tensor_copy',
                                  out=x1.v(ov.ap[:, j, c2 * 512:(c2 + 1) * 512]), in_=pb[:, :])
                        kb.dma(View([], y_out[t0:t0 + NB, :].rearrange("(j p) d -> p j d", p=128)), ov)
            kb.barrier()

        SEGB = SEG // NB

        def blk_order(d):
            return list(range(NBLK)) if d == 0 else list(range(NBLK - 1, -1, -1))

        def load_halo(raw, tile0, nt, b):
            t0 = b * NB
            lo, hi = max(t0 - 1, 0), min(t0 + NB + 2, T)
            if lo > t0 - 1:
                E(POOL, 'memset', ap=raw[:, :, 0:1], constant=0.0)
            if hi < t0 + NB + 2:
                E(POOL, 'memset', ap=raw[:, :, NB + 1:NB + 3], constant=0.0)
            bl = [kb.db('hT%d' % (tile0 + c), bb) for c in range(nt) for bb in (b - 1, b, b + 1) if 0 <= bb < NBLK]
            kb.dma(raw[:, :, lo - (t0 - 1):hi - (t0 - 1)], View(bl, hT[tile0:tile0 + nt, :, lo:hi].rearrange("c p t -> p c t")))
            if b % SEGB == 0 and b > 0:
                E(DVE, 'tensor_scalar', out=raw[:, :, 0:1], in0=raw[:, :, 0:1], scalar1=flag[:, 0:1], scalar2=None, op0=ALU.mult)
            if b % SEGB == SEGB - 1 and b < NBLK - 1:
                E(DVE, 'tensor_scalar', out=raw[:, :, NB + 1:NB + 3], in0=raw[:, :, NB + 1:NB + 3], scalar1=flag[:, 0:1],
                  scalar2=None, op0=ALU.mult)

        def conv4(eng, out, raw, t, wcol, bias=None):
            if bias is None:
                E(eng, 'tensor_scalar', out=out, in0=raw[:, t, 0:NB], scalar1=wcol(0), scalar2=None, op0=ALU.mult)
            else:
                E(eng, 'tensor_scalar', out=out, in0=raw[:, t, 0:NB], scalar1=wcol(0), scalar2=bias, op0=ALU.mult, op1=ALU.add)
            for k in range(1, 4):
                E(DVE, 'scalar_tensor_tensor', out=out, in0=raw[:, t, k:k + NB], scalar=wcol(k), in1=out, op0=ALU.mult, op1=ALU.add)

        def crosses(b, d):
            return (d == 0 and b > 0 and b % SEGB == 0) or (d == 1 and b < NBLK - 1 and b % SEGB == SEGB - 1)

        def phase_lru(l):
            with ExitStack() as st:
                bdf = sb(st, "l_bdf", [128, 16, 128])
                bdb = sb(st, "l_bdb", [128, 16, 128], BF16)
                ccol = sb(st, "l_ccol", [128, 8])
                raw = [sb(st, "l_raw%d" % i, [128, 4, NB + 3]) for i in range(2)]
                xc = sb(st, "l_xc", [128, 4, NB])
                xcb = sb(st, "l_xcb", [128, 4, NB], BF16)
                r_ = [sb(st, "l_r%d" % i, [128, NB]) for i in range(2)]
                i_ = [sb(st, "l_i%d" % i, [128, NB]) for i in range(2)]
                a_ = [sb(st, "l_a%d" % i, [128, NB]) for i in range(2)]
                u_ = [sb(st, "l_u%d" % i, [128, NB]) for i in range(2)]
                hh = [sb(st, "l_h%d" % i, [128, 4, NB]) for i in range(2)]
                hf = sb(st, "l_hf", [128, 4, NB])
                cg = sb(st, "l_cg", [128, 4, NB])
                ob = [sb(st, "l_ob%d" % i, [128, 4, NB], BF16) for i in range(2)]
                carry = sb(st, "l_carry", [128, 4])
                kb.dma(bdf[:], View([], bd_in[l]))
                E(DVE, 'tensor_copy', out=bdb[:], in_=bdf[:])
                o_, w_ = CP['lru_lam']
                E(ACT, 'activation', out=ccol[:], in_=cpt[l][:, o_:o_ + 8], func=AF.Sigmoid)
                E(ACT, 'activation', out=ccol[:], in_=ccol[:], func=AF.Ln)
                E(DVE, 'tensor_scalar', out=ccol[:], in0=ccol[:], scalar1=8.0, scalar2=None, op0=ALU.mult)
                for d in range(2):
                    for bi, b in enumerate(blk_order(d)):
                        t0 = b * NB
                        rw = raw[bi % 2]
                        h = hh[bi % 2]
                        load_halo(rw, CX, 4, b)
                        if d == 1:
                            kb.dma(hf[:], View([kb.db('ofT%d' % (8 + c), b) for c in range(4)],
                                               ofT[8:12, :, t0:t0 + NB].rearrange("c p t -> p c t")))
                            kb.dma(cg[:], View([kb.db('hT%d' % (CG + c), b) for c in range(4)],
                                               hT[CG:CG + 4, :, t0:t0 + NB].rearrange("c p t -> p c t")))
                        if bi == 0:
                            E(DVE, 'memset', ap=carry[:], constant=0.0)
                        elif crosses(b, d):
                            E(DVE, 'tensor_scalar', out=carry[:], in0=carry[:], scalar1=flag[:, 0:1], scalar2=None, op0=ALU.mult)
                        for t in range(4):
                            conv4(POOL, xc[:, t, :], rw, t, lambda k: cpv(l, 'lru_conv', t * 4 + k), bias=cpv(l, 'lru_cb', t))
                            E(ACT, 'copy', out=xcb[:, t, :], in_=xc[:, t, :])
                            pa, px = ps[(2 * t) % 4], ps[(2 * t + 1) % 4]
                            kb.mm(pa[:, :], bdb[:, d * 4 + t, :], xcb[:, t, :])
                            kb.mm(px[:, :], bdb[:, 8 + d * 4 + t, :], xcb[:, t, :])
                            rr, ii, aa, uu = r_[t % 2], i_[t % 2], a_[t % 2], u_[t % 2]
                            E(ACT, 'activation', out=rr[:, :], in_=pa[:, :], func=AF.Sigmoid, bias=cpv(l, 'lru_ba', d * 4 + t))
                            E(ACT, 'activation', out=ii[:, :], in_=px[:, :], func=AF.Sigmoid, bias=cpv(l, 'lru_bx', d * 4 + t))
                            E(ACT, 'activation', out=aa[:, :], in_=rr[:, :], func=AF.Exp, scale=ccol[:, d * 4 + t:d * 4 + t + 1])
                            E(POOL, 'tensor_tensor', out=rr[:, :], in0=aa[:, :], in1=aa[:, :], op=ALU.mult)
                            E(DVE, 'tensor_scalar', out=rr[:, :], in0=rr[:, :], scalar1=-1.0, scalar2=1.0, op0=ALU.mult, op1=ALU.add)
                            E(ACT, 'activation', out=rr[:, :], in_=rr[:, :], func=AF.Sqrt)
                            E(POOL, 'tensor_tensor', out=ii[:, :], in0=ii[:, :], in1=xc[:, t, :], op=ALU.mult)
                            E(DVE, 'tensor_tensor', out=uu[:, :], in0=rr[:, :], in1=ii[:, :], op=ALU.mult)
                            if d == 0:
                                E(DVE, 'tensor_tensor_scan', out=h[:, t, :], data0=aa[:, :], data1=uu[:, :],
                                  initial=carry[:, t:t + 1], op0=ALU.mult, op1=ALU.add)
                            else:
                                E(DVE, 'tensor_tensor_scan', out=h[:, t, ::-1], data0=aa[:, ::-1], data1=uu[:, ::-1],
                                  initial=carry[:, t:t + 1], op0=ALU.mult, op1=ALU.add)
                        E(DVE, 'tensor_copy', out=carry[:], in_=h[:, :, NB - 1] if d == 0 else h[:, :, 0])
                        if d == 0:
                            kb.dma(View([kb.db('ofT%d' % (8 + c), b) for c in range(4)],
                                        ofT[8:12, :, t0:t0 + NB].rearrange("c p t -> p c t")), h[:])
                        else:
                            o2 = ob[bi % 2]
                            E(ACT, 'activation', out=cg[:], in_=cg[:], func=AF.Gelu_apprx_tanh)
                            E(POOL, 'tensor_tensor', out=hf[:], in0=hf[:], in1=h[:], op=ALU.add)
                            E(DVE, 'tensor_tensor', out=o2[:], in0=hf[:], in1=cg[:], op=ALU.mult)
                            kb.dma(View([kb.db('obT', b)], obT[8:12, :, t0:t0 + NB].rearrange("c p t -> p c t")), o2[:])
            kb.barrier()

        def phase_hg(l):
            with ExitStack() as st:
                lbc = sb(st, "g_lbc", [128, 4])
                omlb = sb(st, "g_omlb", [128, 4])
                nomlb = sb(st, "g_nomlb", [128, 4])
                qT = [sb(st, "g_qT%d" % i, [128, 4, NB]) for i in range(2)]
                fT = [sb(st, "g_fT%d" % i, [128, 4, NB]) for i in range(2)]
                vT = [sb(st, "g_vT%d" % i, [128, 4, NB]) for i in range(2)]
                s_ = [sb(st, "g_s%d" % i, [128, NB]) for i in range(2)]
                f_ = [sb(st, "g_f%d" % i, [128, NB]) for i in range(2)]
                kk = [sb(st, "g_kk%d" % i, [128, NB]) for i in range(2)]
                bq = [sb(st, "g_bq%d" % i, [128, NB]) for i in range(2)]
                dq_ = [sb(st, "g_dq%d" % i, [128, NB]) for i in range(2)]
                e1 = [sb(st, "g_e1%d" % i, [128, NB]) for i in range(2)]
                e2 = [sb(st, "g_e2%d" % i, [128, NB]) for i in range(2)]
                qt = sb(st, "g_qt", [128, 4, NB], BF16)
                kdT = sb(st, "g_kdT", [128, 4, NB], BF16)
                ebl = sb(st, "g_ebl", [128, 4, 8])
                vtok = sb(st, "g_vtok", [128, 4, 512], BF16)
                ktok = sb(st, "g_ktok", [128, 4, 512], BF16)
                vTb = sb(st, "g_vTb", [128, 4, NB], BF16)
                AT = [sb(st, "g_AT%d" % i, [128, 4, 64], BF16) for i in range(2)]
                S = sb(st, "g_S", [128, 4, 128])
                Sp = sb(st, "g_Sp", [128, 4, 128])
                Spb = sb(st, "g_Spb", [128, 4, 128], BF16)
                ost = [sb(st, "g_ost%d" % i, [128, 4, NB]) for i in range(2)]
                of_ = sb(st, "g_of", [128, 4, NB])
                gz = sb(st, "g_gz", [128, 4, NB])
                oh = [sb(st, "g_oh%d" % i, [128, NB]) for i in range(2)]
                sqb = [sb(st, "g_sqb%d" % i, [128, NB], BF16) for i in range(2)]
                sd = [sb(st, "g_sd%d" % i, [128, NB]) for i in range(2)]
                obs = [sb(st, "g_obs%d" % i, [128, 4, NB], BF16) for i in range(2)]
                if l == 0:
                    E(DVE, 'memset', ap=lbc[:], constant=0.0)
                else:
                    o0, o1 = CP['lb0'][0], CP['lb1'][0]
                    E(DVE, 'tensor_tensor', out=lbc[:], in0=cpt[l][:, o1:o1 + 4], in1=cpt[l][:, o0:o0 + 4], op=ALU.subtract)
                    E(ACT, 'activation', out=lbc[:], in_=lbc[:], func=AF.Sigmoid)
                E(DVE, 'tensor_scalar', out=omlb[:], in0=lbc[:], scalar1=-1.0, scalar2=1.0, op0=ALU.mult, op1=ALU.add)
                E(DVE, 'tensor_scalar', out=nomlb[:], in0=omlb[:], scalar1=-1.0, scalar2=None, op0=ALU.mult)
                pso = [ps[0], ps[1], ps[2], ps[3]]
                psA, psS, psT, psN = ps[4], ps[5], ps[6], ps[7]
                psTb = psT.v(psT.t[:, :].bitcast(BF16))
                for d in range(2):
                    HF = HFF if d == 0 else HFB
                    mask4 = CB('hgm_f4') if d == 0 else CB('hgm_b4')
                    for bi, b in enumerate(blk_order(d)):
                        t0 = b * NB
                        q_, fz, v_ = qT[bi % 2], fT[bi % 2], vT[bi % 2]
                        for (dst, tl) in ((q_, HQ), (fz, HF), (v_, HI)):
                            kb.dma(dst[:], View([kb.db('hT%d' % (tl + c), b) for c in range(4)],
                                                hT[tl:tl + 4, :, t0:t0 + NB].rearrange("c p t -> p c t")))
                        if d == 1:
                            kb.dma(of_[:], View([kb.db('ofT%d' % (4 + c), b) for c in range(4)],
                                                ofT[4:8, :, t0:t0 + NB].rearrange("c p t -> p c t")))
                            kb.dma(gz[:], View([kb.db('hT%d' % (HGT + c), b) for c in range(4)],
                                               hT[HGT:HGT + 4, :, t0:t0 + NB].rearrange("c p t -> p c t")))
                        if bi == 0:
                            E(DVE, 'memset', ap=S[:], constant=0.0)
                        elif crosses(b, d):
                            E(DVE, 'tensor_scalar', out=S[:], in0=S[:], scalar1=flag[:, 0:1], scalar2=None, op0=ALU.mult)
                        E(POOL, 'tensor_copy', out=vTb[:], in_=v_[:])
                        for j in range(4):
                            for h in range(4):
                                kb.tr(psTb.bufs[0].v(psTb.ap[:, h * 128:(h + 1) * 128]), vTb[:, h, j * 128:(j + 1) * 128], CB('ident'))
                            E(ACT, 'copy', out=vtok[:, j, :], in_=psT.v(psTb.ap[:, 0:512]))
                        for h in range(4):
                            ss, ff, k2, bb, dd, x1_, x2_ = s_[h % 2], f_[h % 2], kk[h % 2], bq[h % 2], dq_[h % 2], e1[h % 2], e2[h % 2]
                            E(ACT, 'activation', out=ss[:, :], in_=fz[:, h, :], func=AF.Sigmoid)
                            E(DVE, 'tensor_scalar', out=ff[:, :], in0=ss[:, :], scalar1=omlb[:, h:h + 1], scalar2=lbc[:, h:h + 1],
                              op0=ALU.mult, op1=ALU.add)
                            E(ACT, 'activation', out=ff[:, :], in_=ff[:, :], func=AF.Ln)
                            E(POOL, 'tensor_scalar', out=k2[:, :], in0=ss[:, :], scalar1=nomlb[:, h:h + 1], scalar2=omlb[:, h:h + 1],
                              op0=ALU.mult, op1=ALU.add)
                            if d == 0:
                                E(DVE, 'tensor_tensor_scan', out=bb[:, :], data0=C('rst_f'), data1=ff[:, :], initial=0.0,
                                  op0=ALU.mult, op1=ALU.add)
                            else:
                                rb = C('rst_b')
                                E(DVE, 'tensor_tensor_scan', out=bb[:, ::-1], data0=View(rb.bufs, rb.ap[:, ::-1]), data1=ff[:, ::-1],
                                  initial=0.0, op0=ALU.mult, op1=ALU.add)
                            b3 = bb.t[:, :].rearrange("p (c j) -> p c j", j=64)
                            blv = b3[:, :, 63] if d == 0 else b3[:, :, 0]
                            E(ACT, 'activation', out=ebl[:, h, :], in_=bb.v(blv), func=AF.Exp)
                            E(DVE, 'tensor_tensor', out=dd.v(dd.t[:, :].rearrange("p (c j) -> p c j", j=64)), in0=bb.v(b3),
                              in1=bb.v(bc_last(blv, 64)), op=ALU.subtract)
                            E(ACT, 'activation', out=x1_[:, :], in_=dd[:, :], func=AF.Exp)
                            E(ACT, 'activation', out=x2_[:, :], in_=dd[:, :], func=AF.Exp, scale=-1.0)
                            E(POOL, 'tensor_tensor', out=qt[:, h, :], in0=q_[:, h, :], in1=x1_[:, :], op=ALU.mult)
                            E(DVE, 'tensor_tensor', out=kdT[:, h, :], in0=k2[:, :], in1=x2_[:, :], op=ALU.mult)
                        for j in range(4):
                            for h in range(4):
                                kb.tr(psTb.bufs[0].v(psTb.ap[:, h * 128:(h + 1) * 128]), kdT[:, h, j * 128:(j + 1) * 128], CB('ident'))
                            E(ACT, 'copy', out=ktok[:, j, :], in_=psT.v(psTb.ap[:, 0:512]))
                        jl = range(4) if d == 0 else range(3, -1, -1)
                        for j in jl:
                            at = AT[j % 2]
                            for hfh in range(2):
                                c = 2 * j + hfh
                                for h in range(4):
                                    kb.mm(psA[hfh * 64:(hfh + 1) * 64, h * 64:(h + 1) * 64], kdT[:, h, c * 64:(c + 1) * 64],
                                          qt[:, h, c * 64:(c + 1) * 64])
                            E(DVE, 'tensor_tensor', out=at.v(at.t[:, :, :].rearrange("p h i -> p (h i)")), in0=psA[:, 0:256], in1=mask4, op=ALU.mult)
                            for hfh in (range(2) if d == 0 else range(1, -1, -1)):
                                c = 2 * j + hfh
                                p0 = hfh * 64
                                eb = ebl.v(bc_last(ebl.t[:, :, c], 128))
                                E(DVE, 'tensor_tensor', out=Spb[:], in0=S[:], in1=eb, op=ALU.mult)
                                E(POOL, 'tensor_tensor', out=Sp[:], in0=S[:], in1=eb, op=ALU.mult)
                                for h in range(4):
                                    kb.mm(pso[h][:, c * 64:(c + 1) * 64], Spb[:, h, :], qt[:, h, c * 64:(c + 1) * 64], start=True, stop=False)
                                    kb.mm(pso[h][:, c * 64:(c + 1) * 64], vtok[p0:p0 + 64, j, h * 128:(h + 1) * 128], at[p0:p0 + 64, h, :],
                                          start=False, stop=True)
                                for h in range(4):
                                    kb.mm(psS[:, h * 128:(h + 1) * 128], ktok[p0:p0 + 64, j, h * 128:(h + 1) * 128],
                                          vtok[p0:p0 + 64, j, h * 128:(h + 1) * 128])
                                E(DVE, 'tensor_tensor', out=S.v(S.t[:, :, :].rearrange("p h d -> p (h d)")),
                                  in0=Sp.v(Sp.t[:, :, :].rearrange("p h d -> p (h d)")), in1=psS[:, :], op=ALU.add)
                        if d == 0:
                            o1 = ost[bi % 2]
                            for h in range(4):
                                E(ACT, 'copy', out=o1[:, h, :], in_=pso[h][:, :])
                            kb.dma(View([kb.db('ofT%d' % (4 + c), b) for c in range(4)],
                                        ofT[4:8, :, t0:t0 + NB].rearrange("c p t -> p c t")), o1[:])
                        else:
                            o2 = obs[bi % 2]
                            E(ACT, 'activation', out=gz[:], in_=gz[:], func=AF.Silu)
                            for h in range(4):
                                oo, sq_, sd_ = oh[h % 2], sqb[h % 2], sd[h % 2]
                                E(DVE, 'tensor_tensor', out=oo[:, :], in0=pso[h][:, :], in1=of_[:, h, :], op=ALU.add)
                                E(ACT, 'activation', out=sq_[:, :], in_=oo[:, :], func=AF.Square)
                                kb.mm(psN[:, :], CB('ones'), sq_[:, :])
                                E(DVE, 'tensor_scalar', out=sd_[:, :], in0=psN[:, :], scalar1=1.0 / 128, scalar2=1e-6, op0=ALU.mult, op1=ALU.add)
                                E(ACT, 'activation', out=sd_[:, :], in_=sd_[:, :], func=AF.Sqrt)
                                E(DVE, 'reciprocal', out=sd_[:, :], in_=sd_[:, :])
                                E(POOL, 'tensor_tensor', out=oo[:, :], in0=oo[:, :], in1=sd_[:, :], op=ALU.mult)
                                E(DVE, 'scalar_tensor_tensor', out=o2[:, h, :], in0=oo[:, :], scalar=cpv(l, 'hg_nw'), in1=gz[:, h, :],
                                  op0=ALU.mult, op1=ALU.mult)
                            kb.dma(View([kb.db('obT', b)], obT[4:8, :, t0:t0 + NB].rearrange("c p t -> p c t")), o2[:])
            kb.barrier()

        def phase_dn(l):
            with ExitStack() as st:
                raw = sb(st, "d_raw", [128, 12, NB + 3])
                cv = sb(st, "d_cv", [128, 12, NB])
                abT = sb(st, "d_abT", [16, NB])
                qn = sb(st, "d_qn", [128, 4, NB], BF16)
                kn = sb(st, "d_kn", [128, 4, NB], BF16)
                vTb = sb(st, "d_vTb", [128, 4, NB], BF16)
                sqb = [sb(st, "d_sqb%d" % i, [128, NB], BF16) for i in range(2)]
                sd = [sb(st, "d_sd%d" % i, [128, NB]) for i in range(2)]
                negA = sb(st, "d_negA", [128, 8])
                gt = sb(st, "d_gt", [128, 4, 16])
                g_ = sb(st, "d_g", [128, 4, 4])
                gR = sb(st, "d_gR", [128, 4, 4], F32R)
                beta = sb(st, "d_beta", [128, 4, 4])
                nbeta = sb(st, "d_nbeta", [128, 4, 4])
                Mh = sb(st, "d_Mh", [128, 4, 128], F32R)
                Ecol = sb(st, "d_Ecol", [128, 12])
                sc1 = sb(st, "d_sc1", [128, 4])
                Dc = sb(st, "d_Dc", [128, 4, 128], BF16)
                Dm = sb(st, "d_Dm", [128, 4, 128], BF16)
                Nm = sb(st, "d_Nm", [128, 4, 128], F32R)
                attn = sb(st, "d_attn", [128, 4, 128], BF16)
                NT = sb(st, "d_NT", [128, 4, 128], F32R)
                AT = sb(st, "d_AT", [128, 4, 128], BF16)
                Pb = [sb(st, "d_P%d" % i, [128, 4, 128], F32R) for i in range(2)]
                PTb = [sb(st, "d_PT%d" % i, [128, 4, 128], F32R) for i in range(2)]
                Rb = [sb(st, "d_R%d" % i, [128, 4, 128], F32R) for i in range(2)]
                kbg = sb(st, "d_kbg", [128, 4, 128], F32R)
                kd = sb(st, "d_kd", [128, 4, 128], BF16)
                vb = sb(st, "d_vb", [128, 4, 128], F32R)
                nwT = sb(st, "d_nwT", [128, 4, 128], F32R)
                EG = sb(st, "d_EG", [128, 4, 128])
                qd = sb(st, "d_qd", [128, 4, 128], BF16)
                vnew = sb(st, "d_vnew", [128, 4, 128], BF16)
                S = sb(st, "d_S", [128, 4, 128], F32R)
                Stmp = sb(st, "d_Stmp", [128, 4, 128])
                Sb = sb(st, "d_Sb", [128, 4, 128], BF16)
                ost = [sb(st, "d_ost%d" % i, [128, 4, NB]) for i in range(2)]
                of_ = sb(st, "d_of", [128, 4, NB])
                gz = sb(st, "d_gz", [128, 4, NB])
                oh = [sb(st, "d_oh%d" % i, [128, NB]) for i in range(2)]
                obs = [sb(st, "d_obs%d" % i, [128, 4, NB], BF16) for i in range(2)]
                psD, psG, psA, psT, psM, psKV, psE, psS = ps
                psW, psP, psQ, psR, psV, psO, psN = psD, psG, psA, psT, psKV, psE, psS
                f3 = lambda bf: bf.v(bf.t[:, :].rearrange("p (h i) -> p h i", h=4))
                bfv = lambda bf, a, b_: bf.v(bf.t[:, :].bitcast(BF16)[:, a:b_])
                flat = lambda bf: bf.v(bf.t[:, :, :].rearrange("p h i -> p (h i)"))
                o_ = CP['alog'][0]
                E(ACT, 'activation', out=negA[:], in_=cpt[l][:, o_:o_ + 8], func=AF.Exp)
                E(DVE, 'tensor_scalar', out=negA[:], in0=negA[:], scalar1=-1.0, scalar2=None, op0=ALU.mult)
                for d in range(2):
                    TRI = C('tri_le') if d == 0 else C('tri_ge')
                    TRIC = C('tri_gt') if d == 0 else C('tri_lt')
                    NMR = CR('nm_f') if d == 0 else CR('nm_b')
                    TRIR = CR('tri_le') if d == 0 else CR('tri_ge')
                    TRICR = CR('tri_gt') if d == 0 else CR('tri_lt')
                    SM = CB('sm_f4') if d == 0 else CB('sm_b4')
                    o_ = CP['dtb'][0]
                    dtb = cpt[l][:, o_ + d * 4:o_ + d * 4 + 4]
                    nAd = negA[:, d * 4:d * 4 + 4]
                    for bi, b in enumerate(blk_order(d)):
                        t0 = b * NB
                        load_halo(raw, DQ, 12, b)
                        kb.dma(abT[:], View([kb.db('hT%d' % AB, b)], hT[AB, 0:16, t0:t0 + NB]))
                        if d == 1:
                            kb.dma(of_[:], View([kb.db('ofT%d' % c, b) for c in range(4)],
                                                ofT[0:4, :, t0:t0 + NB].rearrange("c p t -> p c t")))
                            kb.dma(gz[:], View([kb.db('hT%d' % (DZ + c), b) for c in range(4)],
                                               hT[DZ:DZ + 4, :, t0:t0 + NB].rearrange("c p t -> p c t")))
                        if bi == 0:
                            E(DVE, 'memset', ap=asf(S[:]), constant=0.0)
                            E(POOL, 'memset', ap=Sb[:], constant=0.0)
                        elif crosses(b, d):
                            E(DVE, 'tensor_scalar', out=S[:], in0=asf(S[:]), scalar1=flag[:, 0:1], scalar2=None, op0=ALU.mult)
                            E(DVE, 'tensor_copy', out=Sb[:], in_=asf(S[:]))
                        for t in range(12):
                            conv4(POOL if t % 2 else DVE, cv[:, t, :], raw, t, lambda k: cpv(l, 'dn_conv', t * 4 + k))
                        for t3 in range(3):
                            E(ACT, 'activation', out=cv[:, t3 * 4:t3 * 4 + 4, :], in_=cv[:, t3 * 4:t3 * 4 + 4, :], func=AF.Silu)
                        for t in range(8):
                            sq_, sd_ = sqb[t % 2], sd[t % 2]
                            E(ACT, 'activation', out=sq_[:, :], in_=cv[:, t, :], func=AF.Square)
                            kb.mm(psN[:, :], CB('ones'), sq_[:, :])
                            E(DVE, 'tensor_scalar', out=sd_[:, :], in0=psN[:, :], scalar1=1e-6, scalar2=None, op0=ALU.add)
                            E(ACT, 'activation', out=sd_[:, :], in_=sd_[:, :], func=AF.Sqrt)
                            E(DVE, 'reciprocal', out=sd_[:, :], in_=sd_[:, :])
                            dst = qn[:, t, :] if t < 4 else kn[:, t - 4, :]
                            E(DVE, 'scalar_tensor_tensor', out=dst, in0=cv[:, t, :], scalar=(128 ** -0.5 if t < 4 else 1.0),
                              in1=sd_[:, :], op0=ALU.mult, op1=ALU.mult)
                        E(POOL, 'tensor_copy', out=vTb[:], in_=cv[:, 8:12, :])
                        for j in range(4):
                            idt = C('ident')
                            kb.tr(psM[:, j * 16:(j + 1) * 16], abT[0:16, j * 128:(j + 1) * 128], View(idt.bufs, idt.ap[0:16, 0:16]))
                        E(ACT, 'copy', out=gt.v(gt.t[:, :, :].rearrange("p j c -> p (j c)")), in_=psM[:, 0:64])
                        E(DVE, 'tensor_tensor', out=g_[:], in0=gt[:, :, d * 4:d * 4 + 4], in1=View(dtb.bufs, bc_mid(dtb.ap, 4)), op=ALU.add)
                        E(ACT, 'activation', out=g_[:], in_=g_[:], func=AF.Exp)
                        E(DVE, 'tensor_scalar', out=g_[:], in0=g_[:], scalar1=1.0, scalar2=None, op0=ALU.add)
                        E(ACT, 'activation', out=g_[:], in_=g_[:], func=AF.Ln)
                        E(DVE, 'tensor_tensor', out=gR[:], in0=g_[:], in1=View(nAd.bufs, bc_mid(nAd.ap, 4)), op=ALU.mult)
                        E(ACT, 'activation', out=beta[:], in_=gt[:, :, 8 + d * 4:12 + d * 4], func=AF.Sigmoid)
                        E(DVE, 'tensor_scalar', out=nbeta[:], in0=beta[:], scalar1=-1.0, scalar2=None, op0=ALU.mult)
                        o1 = ost[bi % 2]
                        for j in (range(4) if d == 0 else range(3, -1, -1)):
                            cs = slice(j * 128, (j + 1) * 128)
                            gj = gR[:, j, :]
                            gjf = asf(gj)
                            E(DVE, 'tensor_tensor', out=Mh[:], in0=View(TRI.bufs, bc_mid(TRI.ap, 4)), in1=View(gjf.bufs, bc_last(gjf.ap, 128)),
                              op=ALU.mult)
                            for h in range(4):
                                hs = slice(h * 128, (h + 1) * 128)
                                kb.mm(psD[:, hs], Mh[:, h, :], CR('ones'), start=True, stop=False)
                                kb.mm(psD[:, hs], CR('negones'), Mh[:, h, :], start=False, stop=False)
                                kb.mm(psD[:, hs], CR('ident'), NMR, start=False, stop=True)
                            kb.mm(psM[:, 64:68], TRIR, gj)
                            kb.mm(psM[:, 68:72], TRICR, gj)
                            kb.mm(psM[:, 72:76], CR('ones'), gj)
                            E(ACT, 'activation', out=Ecol[:], in_=psM[:, 64:76], func=AF.Exp)
                            E(ACT, 'activation', out=flat(Dc), in_=psD[:, :], func=AF.Exp)
                            E(POOL, 'tensor_tensor', out=flat(Dm), in0=flat(Dc), in1=SM, op=ALU.mult)
                            for h in range(4):
                                hs = slice(h * 128, (h + 1) * 128)
                                kb.mm(psG[:, hs], kn[:, h, cs], kn[:, h, cs])
                                kb.mm(psA[:, hs], qn[:, h, cs], kn[:, h, cs])
                            for h in range(4):
                                hs = slice(h * 128, (h + 1) * 128)
                                E(DVE, 'scalar_tensor_tensor', out=Nm[:, h, :], in0=psG[:, hs], scalar=nbeta[:, j, h:h + 1], in1=Dm[:, h, :],
                                  op0=ALU.mult, op1=ALU.mult)
                            E(DVE, 'tensor_tensor', out=flat(attn), in0=psA[:, :], in1=flat(Dc), op=ALU.mult)
                            for h in range(4):
                                kb.tr(View(psT.bufs if False else [psT], psT.t[:, h * 128:(h + 1) * 128].bitcast(F32R)), Nm[:, h, :], CR('ident'))
                                kb.tr(bfv(psM, 512 + h * 128, 512 + (h + 1) * 128), attn[:, h, :], CB('ident'))
                            E(ACT, 'copy', out=flat(NT), in_=psT[:, :])
                            E(DVE, 'tensor_copy', out=flat(AT), in_=bfv(psM, 512, 1024))
                            P, PT = Nm, NT
                            R = Rb[0]
                            E(DVE, 'tensor_tensor', out=flat(R), in0=asf(flat(NT)), in1=C('ident4'),
                              op=ALU.add)
                            for k in range(6):
                                Pn, PTn, Rn = Pb[k % 2], PTb[k % 2], Rb[(k + 1) % 2]
                                for h in range(4):
                                    hs = slice(h * 128, (h + 1) * 128)
                                    kb.mm(psP[:, hs], PT[:, h, :], P[:, h, :])
                                E(ACT, 'copy', out=flat(Pn), in_=psP[:, :])
                                if k < 5:
                                    for h in range(4):
                                        hs = slice(h * 128, (h + 1) * 128)
                                        kb.mm(psQ[:, hs], P[:, h, :], PT[:, h, :])
                                    E(DVE, 'tensor_copy', out=flat(PTn), in_=psQ[:, :])
                                for h in range(4):
                                    hs = slice(h * 128, (h + 1) * 128)
                                    kb.mm(psR[:, hs], Pn[:, h, :], R[:, h, :])
                                E(DVE, 'tensor_tensor', out=flat(Rn), in0=psR[:, :], in1=asf(flat(R)), op=ALU.add)
                                P, PT, R = Pn, PTn, Rn
                            TT = R
                            for h in range(4):
                                kb.tr(bfv(psKV, h * 128, (h + 1) * 128), kn[:, h, cs], CB('ident'))
                                kb.tr(bfv(psKV, 512 + h * 128, 512 + (h + 1) * 128), vTb[:, h, cs], CB('ident'))
                            E(DVE, 'tensor_tensor', out=sc1[:], in0=beta[:, j, :], in1=Ecol[:, 0:4], op=ALU.mult)
                            kv3 = lambda a: psKV.v(psKV.t[:, :].bitcast(BF16)[:, a:a + 512].rearrange("p (h i) -> p h i", h=4))
                            E(DVE, 'tensor_tensor', out=kbg[:], in0=kv3(0), in1=sc1.v(bc_last(sc1.t[:, 0:4], 128)), op=ALU.mult)
                            E(DVE, 'tensor_tensor', out=kd[:], in0=kv3(0), in1=Ecol.v(bc_last(Ecol.t[:, 4:8], 128)), op=ALU.mult)
                            E(DVE, 'tensor_tensor', out=vb[:], in0=kv3(512), in1=beta.v(bc_last(beta.t[:, j, :], 128)), op=ALU.mult)
                            for h in range(4):
                                hs = slice(h * 128, (h + 1) * 128)
                                kb.mm(psW[:, hs], kbg[:, h, :], TT[:, h, :])
                            E(ACT, 'activation', out=flat(nwT), in_=psW[:, :], func=AF.Copy, scale=-1.0)
                            for h in range(4):
                                hs = slice(h * 128, (h + 1) * 128)
                                kb.mm(psE[:, hs], CR('ones'), Mh[:, h, :])
                            E(ACT, 'activation', out=flat(EG), in_=psE[:, :], func=AF.Exp)
                            E(POOL, 'tensor_tensor', out=qd[:], in0=qn[:, :, cs], in1=EG[:], op=ALU.mult)
                            for h in range(4):
                                hs = slice(h * 128, (h + 1) * 128)
                                kb.mm(psV[:, hs], TT[:, h, :], vb[:, h, :], start=True, stop=False)
                                kb.mm(psV[:, hs], nwT[:, h, :], S[:, h, :], start=False, stop=True)
                            E(ACT, 'copy', out=flat(vnew), in_=psV[:, :])
                            for h in range(4):
                                hs = slice(h * 128, (h + 1) * 128)
                                kb.mm(psO[:, hs], Sb[:, h, :], qd[:, h, :], start=True, stop=False)
                                kb.mm(psO[:, hs], vnew[:, h, :], AT[:, h, :], start=False, stop=True)
                            for h in range(4):
                                hs = slice(h * 128, (h + 1) * 128)
                                kb.mm(psS[:, hs], kd[:, h, :], vnew[:, h, :])
                            E(ACT, 'copy', out=o1[:, :, cs], in_=f3(psO))
                            E(DVE, 'tensor_tensor', out=Stmp[:], in0=asf(S[:]), in1=Ecol.v(bc_last(Ecol.t[:, 8:12], 128)), op=ALU.mult)
                            E(DVE, 'tensor_tensor', out=flat(S), in0=flat(Stmp), in1=psS[:, :], op=ALU.add)
                            E(ACT, 'copy', out=Sb[:], in_=asf(S[:]))
                        if d == 0:
                            kb.dma(View([kb.db('ofT%d' % c, b) for c in range(4)],
                                        ofT[0:4, :, t0:t0 + NB].rearrange("c p t -> p c t")), o1[:])
                        else:
                            o2 = obs[bi % 2]
                            E(ACT, 'activation', out=gz[:], in_=gz[:], func=AF.Silu)
                            for h in range(4):
                                oo, sq_, sd_ = oh[h % 2], sqb[h % 2], sd[h % 2]
                                E(DVE, 'tensor_tensor', out=oo[:, :], in0=o1[:, h, :], in1=of_[:, h, :], op=ALU.add)
                                E(ACT, 'activation', out=sq_[:, :], in_=oo[:, :], func=AF.Square)
                                kb.mm(psN[:, :], CB('ones'), sq_[:, :])
                                E(DVE, 'tensor_scalar', out=sd_[:, :], in0=psN[:, :], scalar1=1.0 / 128, scalar2=1e-6, op0=ALU.mult, op1=ALU.add)
                                E(ACT, 'activation', out=sd_[:, :], in_=sd_[:, :], func=AF.Sqrt)
                                E(DVE, 'reciprocal', out=sd_[:, :], in_=sd_[:, :])
                                E(POOL, 'tensor_tensor', out=oo[:, :], in0=oo[:, :], in1=sd_[:, :], op=ALU.mult)
                                E(DVE, 'scalar_tensor_tensor', out=o2[:, h, :], in0=oo[:, :], scalar=cpv(l, 'dn_nw'), in1=gz[:, h, :],
                                  op0=ALU.mult, op1=ALU.mult)
                            kb.dma(View([kb.db('obT', b)], obT[0:4, :, t0:t0 + NB].rearrange("c p t -> p c t")), o2[:])
            kb.barrier()

        MIX = {'lru': phase_lru, 'hg': phase_hg, 'dn': phase_dn}
        run = phases if phases is not None else ['wprep', 'embed', 'proj', 'lru', 'hg', 'dn', 'post']
        if 'wprep' in run:
            phase_wprep()
        if 'embed' in run:
            phase_embed()
        for l in range(nlayers):
            if 'proj' in run:
                phase_proj(l)
            for m in ['lru', 'hg', 'dn']:
                if m in run and m in MIX:
                    MIX[m](l)
            if 'post' in run:
                phase_post(l, l == nlayers - 1)
        kb.barrier()
        print("instructions:", kb.ninst)
    return nc


def _colparams(W):
    L = DEPTH
    cp = np.zeros((L, 128, NCP), np.float32)

    def put(l, name, arr):
        o, w = CP[name]
        cp[l, :, o:o + w] = arr

    def chan(v, nt):
        return np.asarray(v).reshape(nt, 128).T
    for l in range(L):
        put(l, 'dn_conv', np.asarray(W['dn_conv_w'][l]).reshape(4, 12, 128).transpose(2, 1, 0).reshape(128, 48))
        put(l, 'lru_conv', np.asarray(W['lru_conv_w'][l]).reshape(4, 4, 128).transpose(2, 1, 0).reshape(128, 16))
        put(l, 'lru_cb', chan(W['lru_conv_b'][l], 4))
        put(l, 'lru_ba', np.asarray(W['lru_ba'][l]).reshape(2, 4, 128).transpose(2, 0, 1).reshape(128, 8))
        put(l, 'lru_bx', np.asarray(W['lru_bx'][l]).reshape(2, 4, 128).transpose(2, 0, 1).reshape(128, 8))
        put(l, 'lru_lam', np.asarray(W['lru_lambda'][l]).reshape(2, 4, 128).transpose(2, 0, 1).reshape(128, 8))
        put(l, 'dn_nw', np.asarray(W['dn_norm_w'][l]).reshape(128, 1))
        put(l, 'hg_nw', np.asarray(W['hg_norm_w'][l]).reshape(128, 1))
        for n in ['ln1_g', 'ln1_b', 'ln2_g', 'ln2_b']:
            put(l, n, chan(W[n][l], 8))
        put(l, 'lb0', chan(W['hg_lb_logits'][0], 4))
        put(l, 'lb1', chan(W['hg_lb_logits'][1], 4))
        put(l, 'emb_g', chan(W['emb_ln_g'], 8))
        put(l, 'emb_b', chan(W['emb_ln_b'], 8))
        put(l, 'alog', np.tile(np.asarray(W['dn_A_log'][l]).reshape(1, 8), (128, 1)))
        put(l, 'dtb', np.tile(np.asarray(W['dn_dt_bias'][l]).reshape(1, 8), (128, 1)))
    return cp


def _blockdiag(W):
    bd = np.zeros((DEPTH, 128, 16, 128), np.float32)
    for l in range(DEPTH):
        for ai, nm in enumerate(['lru_wa', 'lru_wx']):
            w = np.asarray(W[nm][l])
            for d in range(2):
                for t in range(4):
                    idx = ai * 8 + d * 4 + t
                    for s in range(2):
                        bd[l, s * 64:(s + 1) * 64, idx, s * 64:(s + 1) * 64] = w[d, 2 * t + s]
    return bd


def make_in_maps(W, xs, ps_, flags):
    cp = _colparams(W)
    bd = _blockdiag(W)
    common = dict(cst=CONST_ARR, cp=cp, bd=bd,
                  w_in=np.ascontiguousarray(W['w_in'], dtype=np.float32), w_branch=np.asarray(W['w_branch'], np.float32),
                  w_out=np.asarray(W['w_out'], np.float32), w_mlp1=np.asarray(W['w_mlp1'], np.float32),
                  w_mlp2=np.asarray(W['w_mlp2'], np.float32), w_ple_gate=np.asarray(W['w_ple_gate'], np.float32),
                  w_ple_proj=np.asarray(W['w_ple_proj'], np.float32))
    maps = []
    for x, p, f in zip(xs, ps_, flags):
        m = dict(common)
        m['x'] = np.ascontiguousarray(x, dtype=np.float32)
        m['p'] = np.ascontiguousarray(p, dtype=np.float32)
        m['flag'] = np.full((128, 1), f, np.float32)
        maps.append(m)
    return maps


_NC_CACHE = {}


def kernel(x_prompt, x_sample, p_prompt, p_sample, **W):
    x_prompt = np.asarray(x_prompt)
    x_sample = np.asarray(x_sample)
    p_prompt = np.asarray(p_prompt)
    p_sample = np.asarray(p_sample)
    T, SEG = 8192, 4096
    assign = [(0, 1), (2, 3), (4, 4), (5, 5), (6, 6), (7, 7)]
    xs, ps_, flags = [], [], []
    for c in range(2):
        xs.append(x_sample[c])
        ps_.append(p_sample[:, c])
        flags.append(1.0)
    for a, b in assign:
        xs.append(np.concatenate([x_prompt[a], x_prompt[b]], axis=0))
        ps_.append(np.concatenate([p_prompt[:, a], p_prompt[:, b]], axis=1))
        flags.append(0.0)
    if 'nc' not in _NC_CACHE:
        _NC_CACHE['nc'] = build(T, SEG)
    nc = _NC_CACHE['nc']
    maps = make_in_maps(W, xs, ps_, flags)
    res = run_bass_kernel_spmd(nc, maps, core_ids=list(range(NCORES)))
    y_prompt = np.zeros((8, 4096, D), np.float32)
    y_sample = np.zeros((2, 8192, D), np.float32)
    for c in range(2):
        y_sample[c] = res.results[c]['y']
    for i, (a, b) in enumerate(assign):
        y = res.results[2 + i]['y']
        y_prompt[a] = y[:4096]
        if b != a:
            y_prompt[b] = y[4096:]
    return (y_prompt, y_sample)
```

```python
import numpy as np
from contextlib import ExitStack
import concourse.bass as bass
import concourse.mybir as mybir
from concourse.bass_utils import run_bass_kernel_spmd

F32 = mybir.dt.float32
BF16 = mybir.dt.bfloat16
F32R = mybir.dt.float32r
AF = mybir.ActivationFunctionType
ALU = mybir.AluOpType

D = 1024
NIN = 8720
DFF = 4096
DPLE = 256
DEPTH = 2
ALPHA = (2.0 * DEPTH) ** 0.25
NB = 512
NCORES = 8
DQ, DK, DV, DZ, HQ, HFF, HFB, HI, HGT, CX, CG, GA, GB, GC, AB = 0, 4, 8, 12, 16, 20, 24, 28, 32, 36, 40, 44, 52, 60, 68
NHT = 69
WIN_TILES = [(i * 128, 128) for i in range(16)] + [(2064 + 128 * i, 128) for i in range(52)] + [(2048, 16)]

def _consts():
    i = np.arange(128)
    c = {}
    c['ident'] = np.eye(128)
    c['ones'] = np.ones((128, 128))
    c['negones'] = -np.ones((128, 128))
    c['tri_le'] = (i[:, None] <= i[None, :]) * 1.0
    c['tri_ge'] = (i[:, None] >= i[None, :]) * 1.0
    c['tri_gt'] = (i[:, None] > i[None, :]) * 1.0
    c['tri_lt'] = (i[:, None] < i[None, :]) * 1.0
    c['nm_f'] = np.where(i[:, None] < i[None, :], -30000.0, 0.0)
    c['nm_b'] = np.where(i[:, None] > i[None, :], -30000.0, 0.0)
    c['sm_f4'] = np.tile((i[:, None] > i[None, :]) * 1.0, (1, 4))
    c['sm_b4'] = np.tile((i[:, None] < i[None, :]) * 1.0, (1, 4))
    c['ident4'] = np.tile(np.eye(128), (1, 4))
    j = np.arange(64)
    c['hgm_f4'] = np.tile((j[None, :] >= (i[:, None] % 64)) * 1.0, (1, 4))
    c['hgm_b4'] = np.tile((j[None, :] <= (i[:, None] % 64)) * 1.0, (1, 4))
    t = np.arange(NB)
    c['rst_f'] = np.tile(((t % 64) != 0) * 1.0, (128, 1))
    c['rst_b'] = np.tile(((t % 64) != 63) * 1.0, (128, 1))
    offs = {}
    o = 0
    arrs = []
    for k, v in c.items():
        offs[k] = (o, v.shape[1])
        o += v.shape[1]
        arrs.append(v.astype(np.float32))
    return np.ascontiguousarray(np.concatenate(arrs, axis=1)), offs


CONST_ARR, CONST_OFF = _consts()
CP = {}
_o = 0
for _n, _w in [('dn_conv', 48), ('lru_conv', 16), ('lru_cb', 4), ('lru_ba', 8), ('lru_bx', 8), ('lru_lam', 8),
               ('dn_nw', 1), ('hg_nw', 1), ('ln1_g', 8), ('ln1_b', 8), ('ln2_g', 8), ('ln2_b', 8),
               ('lb0', 4), ('lb1', 4), ('emb_g', 8), ('emb_b', 8), ('alog', 8), ('dtb', 8)]:
    CP[_n] = (_o, _w)
    _o += _w
NCP = _o


class View:
    def __init__(self, bufs, ap):
        self.bufs = bufs
        self.ap = ap

    def __getitem__(self, idx):
        return View(self.bufs, self.ap[idx])


class Buf:
    def __init__(self, t=None, name=""):
        self.t = t
        self.w = {}
        self.r = {}
        self.name = name

    def __getitem__(self, idx):
        return View([self], self.t[idx])

    def v(self, ap):
        return View([self], ap)


class Eng:
    def __init__(self, h, sid, name):
        self.h = h
        self.sid = sid
        self.cnt = 0
        self.waited = {}
        self.name = name
        self.old = []


class KB:
    NDS = 48
    EPOCH = 20000

    def __init__(self, nc, stack):
        self.nc = nc
        self.sems = {}
        self.nsid = 0

        def mk(n):
            s = stack.enter_context(nc.semaphore(n))
            sid = self.nsid
            self.nsid += 1
            self.sems[sid] = s
            return sid
        self.mk = mk
        self.PE = Eng(nc.tensor, mk("s_pe"), "pe")
        self.DVE = Eng(nc.vector, mk("s_dve"), "dve")
        self.ACT = Eng(nc.scalar, mk("s_act"), "act")
        self.POOL = Eng(nc.gpsimd, mk("s_pool"), "pool")
        self.SP = Eng(nc.sync, None, "sp")
        self.engs = [self.PE, self.DVE, self.ACT, self.POOL]
        self.dsid = [mk("s_dma%d" % i) for i in range(self.NDS)]
        self.dval = [0] * self.NDS
        self.dnext = 0
        self.dbufs = {}
        self.ninst = 0

    def db(self, name, blk):
        k = (name, blk)
        if k not in self.dbufs:
            self.dbufs[k] = Buf(None, "%s_%s" % (name, blk))
        return self.dbufs[k]

    def _wait(self, eng, sid, val):
        if val <= 0 or eng.waited.get(sid, 0) >= val:
            return
        eng.h.wait_ge(self.sems[sid], val)
        eng.waited[sid] = val

    def _sync(self, eng, reads, writes):
        own = eng.sid
        for b in reads:
            for sid, v in b.w.items():
                self._wait(eng, sid, v)
        for b in writes:
            for sid, v in b.w.items():
                if sid != own:
                    self._wait(eng, sid, v)
            for sid, v in b.r.items():
                if sid != own:
                    self._wait(eng, sid, v)

    def _mark(self, reads, writes, sid, val):
        for b in reads:
            b.r[sid] = max(b.r.get(sid, 0), val)
        for b in writes:
            b.w = {sid: val}
            b.r = {}

    def E(self, eng, meth, **kw):
        reads, writes, args = [], [], {}
        for k_, v in kw.items():
            if isinstance(v, View):
                if k_ in ('out', 'accum_out', 'ap') or any(getattr(b_, 'excl', False) for b_ in v.bufs):
                    writes.extend(v.bufs)
                else:
                    reads.extend(v.bufs)
                args[k_] = v.ap
            else:
                args[k_] = v
        if eng.cnt >= self.EPOCH:
            eng.old.append((eng.sid, eng.cnt))
            eng.sid = self.mk("s_%s_e%d" % (eng.name, len(eng.old)))
            eng.cnt = 0
        self._sync(eng, reads, writes)
        inst = getattr(eng.h, meth)(**args)
        inst.then_inc(self.sems[eng.sid], 1)
        eng.cnt += 1
        self.ninst += 1
        self._mark(reads, writes, eng.sid, eng.cnt)
        return inst

    def mm(self, out, lhsT, rhs, start=True, stop=True, extra_w=()):
        return self.E(self.PE, 'matmul', out=out, lhsT=lhsT, rhs=rhs, start=start, stop=stop)

    def tr(self, out, in_, ident):
        return self.E(self.PE, 'transpose', out=out, in_=in_, identity=ident)

    def dma(self, out, in_, q=None):
        eng = self.SP
        s = self.dnext
        self.dnext = (self.dnext + 1) % self.NDS
        sid = self.dsid[s]
        self._wait(eng, sid, self.dval[s])
        self._sync(eng, in_.bufs, out.bufs)
        self.dval[s] += 16
        eng.h.dma_start(out=out.ap, in_=in_.ap).then_inc(self.sems[sid], 16)
        self.ninst += 1
        self._mark(in_.bufs, out.bufs, sid, self.dval[s])

    def barrier(self):
        for e in self.engs + [self.SP]:
            for o in self.engs:
                if o is not e:
                    if o.cnt > 0:
                        self._wait(e, o.sid, o.cnt)
                    elif o.old:
                        self._wait(e, o.old[-1][0], o.old[-1][1])
            for s in range(self.NDS):
                self._wait(e, self.dsid[s], self.dval[s])


def bc_last(ap, n):
    l = [list(x) for x in ap.ap]
    return bass.AP(ap.tensor, ap.offset, l + [[0, n]])


def bc_mid(ap, n):
    l = [list(x) for x in ap.ap]
    return bass.AP(ap.tensor, ap.offset, [l[0], [0, n]] + l[1:])


def build(T, SEG, phases=None, debug_out=(), nlayers=DEPTH):
    NBLK = T // NB
    nc = bass.Bass("TRN2", target_bir_lowering=False)
    dt = nc.dram_tensor
    x_in = dt("x", [T, D], F32, kind="ExternalInput").ap()
    p_in = dt("p", [DEPTH, T, DPLE], F32, kind="ExternalInput").ap()
    cst_in = dt("cst", list(CONST_ARR.shape), F32, kind="ExternalInput").ap()
    cp_in = dt("cp", [DEPTH, 128, NCP], F32, kind="ExternalInput").ap()
    bd_in = dt("bd", [DEPTH, 128, 16, 128], F32, kind="ExternalInput").ap()
    flag_in = dt("flag", [128, 1], F32, kind="ExternalInput").ap()
    w_in = dt("w_in", [DEPTH, D, NIN], F32, kind="ExternalInput").ap()
    w_branch = dt("w_branch", [DEPTH, 3, 512, D], F32, kind="ExternalInput").ap()
    w_out = dt("w_out", [DEPTH, D, D], F32, kind="ExternalInput").ap()
    w_mlp1 = dt("w_mlp1", [DEPTH, D, DFF], F32, kind="ExternalInput").ap()
    w_mlp2 = dt("w_mlp2", [DEPTH, DFF, D], F32, kind="ExternalInput").ap()
    w_pg = dt("w_ple_gate", [DEPTH, D, D], F32, kind="ExternalInput").ap()
    w_pp = dt("w_ple_proj", [DEPTH, DPLE, D], F32, kind="ExternalInput").ap()
    y_out = dt("y", [T, D], F32, kind="ExternalOutput").ap()
    dbg = {}
    okind = lambda n: "ExternalOutput" if n in debug_out else "Internal"
    xT = dt("xT", [8, 128, T], F32, kind=okind("xT")).ap()
    class _HT:
        SPLIT = 36

        def __init__(self):
            self.a = dt("hT", [self.SPLIT, 128, T], F32, kind=okind("hT")).ap()
            self.b = dt("hTb", [NHT - self.SPLIT, 128, T], F32, kind=okind("hT")).ap()

        def __getitem__(self, idx):
            f = idx[0]
            rest = tuple(idx[1:])
            if isinstance(f, slice):
                if f.start < self.SPLIT:
                    assert f.stop <= self.SPLIT
                    return self.a[(f,) + rest]
                return self.b[(slice(f.start - self.SPLIT, f.stop - self.SPLIT),) + rest]
            if f < self.SPLIT:
                return self.a[(f,) + rest]
            return self.b[(f - self.SPLIT,) + rest]
    hT = _HT()
    ofT = dt("ofT", [12, 128, T], F32, kind=okind("ofT")).ap()
    obT = dt("obT", [12, 128, T], BF16, kind=okind("obT")).ap()
    wspec = {
        'win': (8, WIN_TILES, lambda l: w_in[l]),
        'wb0': (4, [(i * 128, 128) for i in range(8)], lambda l: w_branch[l, 0]),
        'wb1': (4, [(i * 128, 128) for i in range(8)], lambda l: w_branch[l, 1]),
        'wb2': (4, [(i * 128, 128) for i in range(8)], lambda l: w_branch[l, 2]),
        'wout': (8, [(i * 128, 128) for i in range(8)], lambda l: w_out[l]),
        'wm1': (8, [(i * 128, 128) for i in range(32)], lambda l: w_mlp1[l]),
        'wm2': (32, [(i * 128, 128) for i in range(8)], lambda l: w_mlp2[l]),
        'wpg': (8, [(i * 128, 128) for i in range(8)], lambda l: w_pg[l]),
        'wpp': (2, [(i * 128, 128) for i in range(8)], lambda l: w_pp[l]),
    }
    ws = {n: dt("ws_" + n, [DEPTH, len(s[1]), 128, s[0], 128], BF16, kind="Internal").ap() for n, s in wspec.items()}

    with ExitStack() as st0:
        kb = KB(nc, st0)
        PE, DVE, ACT, POOL = kb.PE, kb.DVE, kb.ACT, kb.POOL
        E = kb.E

        uniq = [0]

        def sb(st, name, shape, dtype=F32):
            uniq[0] += 1
            return Buf(st.enter_context(nc.sbuf_tensor("sb%d_%s" % (uniq[0], name), shape, dtype)), name)

        ps = [Buf(st0.enter_context(nc.psum_tensor("ps%d" % i, [128, 512], F32)), "ps%d" % i) for i in range(8)]
        for b_ in ps:
            b_.excl = True
        cst = sb(st0, "cst", list(CONST_ARR.shape))
        cpt = [sb(st0, "cp%d" % l, [128, NCP]) for l in range(DEPTH)]
        flag = sb(st0, "flag", [128, 1])
        cbf = sb(st0, "cbf", [128, 128 * 2 + 512 * 3 + 256 * 2], BF16)
        kb.dma(cst[:], View([], cst_in[:, :]))
        for l in range(DEPTH):
            kb.dma(cpt[l][:], View([], cp_in[l]))
        kb.dma(flag[:], View([], flag_in[:, :]))

        def C(name):
            o, w = CONST_OFF[name]
            return cst[:, o:o + w]

        cb_off = {}
        o = 0
        for n in ['ident', 'ones', 'sm_f4', 'sm_b4', 'ident4', 'hgm_f4', 'hgm_b4']:
            w = CONST_OFF[n][1]
            cb_off[n] = (o, w)
            E(DVE, 'tensor_copy', out=cbf[:, o:o + w], in_=C(n))
            o += w

        def CB(name):
            o, w = cb_off[name]
            return cbf[:, o:o + w]

        crn = ['ident', 'ones', 'negones', 'tri_le', 'tri_ge', 'tri_gt', 'tri_lt', 'nm_f', 'nm_b']
        crt = sb(st0, "crt", [128, 128 * len(crn)], F32R)
        for i_, n in enumerate(crn):
            E(DVE, 'tensor_copy', out=crt[:, i_ * 128:(i_ + 1) * 128], in_=C(n))

        def CR(name):
            i_ = crn.index(name)
            return crt[:, i_ * 128:(i_ + 1) * 128]

        def asf(v):
            return View(v.bufs, v.ap.bitcast(F32))

        def cpv(l, name, i=0, n=1):
            o, w = CP[name]
            return cpt[l][:, o + i:o + i + n]

        def phase_wprep():
            with ExitStack() as st:
                sf = [sb(st, "wpf%d" % i, [128, 4096]) for i in range(2)]
                sbf = [sb(st, "wpb%d" % i, [128, 4096], BF16) for i in range(2)]
                it = 0
                for l in range(DEPTH):
                    for name, (n_k, tiles, srcf) in wspec.items():
                        src = srcf(l).rearrange("(k p) n -> p k n", p=128)
                        gmax = 4 if n_k <= 8 else 1
                        ti = 0
                        while ti < len(tiles):
                            g = 1
                            while (g < gmax and ti + g < len(tiles) and tiles[ti + g][1] == 128 and tiles[ti][1] == 128
                                   and tiles[ti + g][0] == tiles[ti][0] + 128 * g):
                                g += 1
                            c0 = tiles[ti][0]
                            gw = sum(tiles[ti + j][1] for j in range(g))
                            s = it % 2
                            it += 1
                            fv = sf[s].t[:, 0:n_k * gw].rearrange("p (k c) -> p k c", k=n_k)
                            bv = sbf[s].t[:, 0:n_k * gw].rearrange("p (k c) -> p k c", k=n_k)
                            kb.dma(sf[s].v(fv), View([], src[:, :, c0:c0 + gw]))
                            E(POOL if it % 2 else ACT, 'tensor_copy' if it % 2 else 'copy', out=sbf[s].v(bv), in_=sf[s].v(fv))
                            for j in range(g):
                                wdt = tiles[ti + j][1]
                                kb.dma(View([kb.db('ws_' + name, l)], ws[name][l, ti + j, :, :, 0:wdt]),
                                       sbf[s].v(bv[:, :, j * 128:j * 128 + wdt]))
                            ti += g
            kb.barrier()

        wctr = [0]
        pctr = [0]

        def dense(wb, l, name, tile_ids, rhs_fn, N, evac, psbanks=(0, 1, 2)):
            n_k = wspec[name][0]
            tiles = wspec[name][1]
            tile_ids = list(tile_ids)
            depth = len(wb) - 1
            base = wctr[0]
            wctr[0] += len(tile_ids)

            def wload(i):
                kb.dma(wb[(base + i) % len(wb)][:, 0:n_k, :], View([kb.db('ws_' + name, l)], ws[name][l, tile_ids[i]]))
            for i in range(min(depth, len(tile_ids))):
                wload(i)
            for i, ti in enumerate(tile_ids):
                wdt = tiles[ti][1]
                s = (base + i) % len(wb)
                if i + depth < len(tile_ids):
                    wload(i + depth)
                pb = ps[psbanks[pctr[0] % len(psbanks)]]
                pctr[0] += 1
                for k in range(n_k):
                    kb.mm(pb[:, 0:N], wb[s][:, k, :], rhs_fn(k), start=(k == 0), stop=(k == n_k - 1))
                evac(ti, pb[0:wdt, 0:N], wdt)

        def layer_norm(st_bufs, src, g_fn, b_fn, dst_f=None, dst_b=None, N=NB):
            sq, xb_, mean, m2, rstd, tmp = st_bufs
            pm, pq = ps[3], ps[4]
            for c in range(8):
                E(ACT, 'activation', out=sq[c % 2][:, 0:N], in_=src[:, c, 0:N], func=AF.Square)
                E(POOL, 'tensor_copy', out=xb_[c % 2][:, 0:N], in_=src[:, c, 0:N])
                kb.mm(pm[:, 0:N], CB('ones'), xb_[c % 2][:, 0:N], start=(c == 0), stop=(c == 7))
                kb.mm(pq[:, 0:N], CB('ones'), sq[c % 2][:, 0:N], start=(c == 0), stop=(c == 7))
            E(ACT, 'activation', out=mean[:, 0:N], in_=pm[:, 0:N], func=AF.Copy, scale=1.0 / D)
            E(DVE, 'tensor_tensor', out=m2[:, 0:N], in0=mean[:, 0:N], in1=mean[:, 0:N], op=ALU.mult)
            E(DVE, 'scalar_tensor_tensor', out=m2[:, 0:N], in0=pq[:, 0:N], scalar=1.0 / D, in1=m2[:, 0:N],
              op0=ALU.mult, op1=ALU.subtract)
            E(DVE, 'tensor_scalar', out=m2[:, 0:N], in0=m2[:, 0:N], scalar1=0.0, scalar2=1e-5, op0=ALU.max, op1=ALU.add)
            E(ACT, 'activation', out=m2[:, 0:N], in_=m2[:, 0:N], func=AF.Sqrt)
            E(DVE, 'reciprocal', out=rstd[:, 0:N], in_=m2[:, 0:N])
            for c in range(8):
                t_ = tmp[c % 2]
                E(DVE, 'tensor_tensor', out=t_[:, 0:N], in0=src[:, c, 0:N], in1=mean[:, 0:N], op=ALU.subtract)
                E(DVE, 'tensor_tensor', out=t_[:, 0:N], in0=t_[:, 0:N], in1=rstd[:, 0:N], op=ALU.mult)
                if dst_f is not None:
                    E(ACT, 'activation', out=dst_f[:, c, 0:N], in_=t_[:, 0:N], func=AF.Identity, scale=g_fn(c), bias=b_fn(c))
                if dst_b is not None:
                    E(ACT, 'activation', out=dst_b[:, c, 0:N], in_=t_[:, 0:N], func=AF.Identity, scale=g_fn(c), bias=b_fn(c))

        def ln_bufs(st):
            return ([sb(st, "ln_sq%d" % i, [128, NB], BF16) for i in range(2)],
                    [sb(st, "ln_xb%d" % i, [128, NB], BF16) for i in range(2)],
                    sb(st, "ln_mean", [128, NB]), sb(st, "ln_m2", [128, NB]), sb(st, "ln_rstd", [128, NB]),
                    [sb(st, "ln_tmp%d" % i, [128, NB]) for i in range(2)])

        def phase_embed():
            with ExitStack() as st:
                xin = [sb(st, "e_xin%d" % i, [128, 4, D]) for i in range(2)]
                xf = sb(st, "e_xf", [128, 8, NB])
                xo = sb(st, "e_xo", [128, 8, NB])
                lb = ln_bufs(st)
                for b in range(NBLK):
                    t0 = b * NB
                    xi = xin[b % 2]
                    kb.dma(xi[:], View([], x_in[t0:t0 + NB, :].rearrange("(j p) d -> p j d", p=128)))
                    for c in range(8):
                        pb = ps[c % 3]
                        for j in range(4):
                            kb.tr(pb[:, j * 128:(j + 1) * 128], xi[:, j, c * 128:(c + 1) * 128], C('ident'))
                        E(ACT if c % 2 else DVE, 'copy' if c % 2 else 'tensor_copy', out=xf[:, c, :], in_=pb[:, :])
                    layer_norm(lb, xf, lambda c: cpv(0, 'emb_g', c), lambda c: cpv(0, 'emb_b', c), dst_f=xo)
                    kb.dma(View([kb.db('xT', b)], xT[:, :, t0:t0 + NB].rearrange("c p t -> p c t")), xo[:])
            kb.barrier()

        def phase_proj(l):
            with ExitStack() as st:
                xin = [sb(st, "p_xin%d" % i, [128, 8, NB]) for i in range(2)]
                xb_ = [sb(st, "p_xb%d" % i, [128, 8, NB], BF16) for i in range(2)]
                stg = [sb(st, "p_stg%d" % i, [128, NB]) for i in range(6)]
                wb = [sb(st, "p_wb%d" % i, [128, 8, 128], BF16) for i in range(4)]
                sctr = [0]
                kb.dma(xin[0][:], View([kb.db('xT', 0)], xT[:, :, 0:NB].rearrange("c p t -> p c t")))
                for b in range(NBLK):
                    t0 = b * NB
                    xi, xbb = xin[b % 2], xb_[b % 2]
                    if b + 1 < NBLK:
                        kb.dma(xin[(b + 1) % 2][:], View([kb.db('xT', b + 1)], xT[:, :, t0 + NB:t0 + 2 * NB].rearrange("c p t -> p c t")))
                    E(POOL, 'tensor_copy', out=xbb[:, 0:4, :], in_=xi[:, 0:4, :])
                    E(DVE, 'tensor_copy', out=xbb[:, 4:8, :], in_=xi[:, 4:8, :])

                    def evac(ti, pv, wdt):
                        sg_ = stg[sctr[0] % len(stg)]
                        sctr[0] += 1
                        if GA <= ti < AB:
                            E(ACT, 'activation', out=sg_[0:wdt, :], in_=pv, func=AF.Sigmoid)
                        elif sctr[0] % 2:
                            E(DVE, 'tensor_copy', out=sg_[0:wdt, :], in_=pv)
                        else:
                            E(ACT, 'copy', out=sg_[0:wdt, :], in_=pv)
                        kb.dma(View([kb.db('hT%d' % ti, b)], hT[ti, 0:wdt, t0:t0 + NB]), sg_[0:wdt, :])
                    dense(wb, l, 'win', range(NHT), lambda k: xbb[:, k, :], NB, evac)
            kb.barrier()

        def phase_post(l, last):
            with ExitStack() as st:
                xres = sb(st, "q_xres", [128, 8, NB])
                hid = sb(st, "q_hid", [128, 32, NB], BF16)
                sig = sb(st, "q_sig", [128, 8, NB])
                acc = sb(st, "q_acc", [128, 8 * NB])
                accb = sb(st, "q_accb", [128, 8, NB], BF16)
                x1 = sb(st, "q_x1", [128, 8, NB])
                x1b = sb(st, "q_x1b", [128, 8, NB], BF16)
                rl = [sb(st, "q_rl%d" % i, [128, NB]) for i in range(2)]
                pin = sb(st, "q_pin", [128, 4, DPLE])
                pT = sb(st, "q_pT", [128, 2, NB], BF16)
                wb = [sb(st, "q_wb%d" % i, [128, 32, 128], BF16) for i in range(3)]
                lb = ln_bufs(st)
                acc3 = acc.v(acc.t[:, :].rearrange("p (c t) -> p c t", c=8))
                accv = lambda c: acc.v(acc.t[:, c * NB:(c + 1) * NB])
                for b in range(NBLK):
                    t0 = b * NB
                    kb.dma(xres[:], View([kb.db('xT', b)], xT[:, :, t0:t0 + NB].rearrange("c p t -> p c t")))
                    kb.dma(hid[:, 0:12, :], View([kb.db('obT', b)], obT[:, :, t0:t0 + NB].rearrange("c p t -> p c t")))
                    kb.dma(pin[:], View([], p_in[l, t0:t0 + NB, :].rearrange("(j p) d -> p j d", p=128)))
                    for c in range(2):
                        pb = ps[5 + c]
                        for j in range(4):
                            kb.tr(pb[:, j * 128:(j + 1) * 128], pin[:, j, c * 128:(c + 1) * 128], C('ident'))
                        E(ACT, 'copy', out=pT[:, c, :], in_=pb[:, :])
                    for n in range(3):
                        gt0 = [GA, GB, GC][n]
                        kb.dma(sig[:], View([kb.db('hT%d' % (gt0 + c), b) for c in range(8)],
                                            hT[gt0:gt0 + 8, :, t0:t0 + NB].rearrange("c p t -> p c t")))

                        def evac(ti, pv, wdt, n=n):
                            if n == 0:
                                E(DVE, 'tensor_tensor', out=accv(ti), in0=pv, in1=sig[:, ti, :], op=ALU.mult)
                            else:
                                r_ = rl[ti % 2]
                                E(DVE, 'tensor_tensor', out=r_[:, :], in0=pv, in1=sig[:, ti, :], op=ALU.mult)
                                if n == 1:
                                    E(POOL, 'tensor_tensor', out=accv(ti), in0=accv(ti), in1=r_[:, :], op=ALU.add)
                                else:
                                    E(POOL, 'tensor_tensor', out=accb[:, ti, :], in0=accv(ti), in1=r_[:, :], op=ALU.add)
                        dense(wb, l, 'wb%d' % n, range(8), lambda k, n=n: hid[:, n * 4 + k, :], NB, evac)

                    def evac(ti, pv, wdt):
                        E(DVE, 'scalar_tensor_tensor', out=accv(ti), in0=xres[:, ti, :], scalar=ALPHA, in1=pv,
                          op0=ALU.mult, op1=ALU.add)
                    dense(wb, l, 'wout', range(8), lambda k: accb[:, k, :], NB, evac)
                    layer_norm(lb, acc3, lambda c: cpv(l, 'ln1_g', c), lambda c: cpv(l, 'ln1_b', c), dst_f=x1, dst_b=x1b)

                    def evac(ti, pv, wdt):
                        r_ = rl[ti % 2]
                        E(ACT, 'activation', out=r_[:, :], in_=pv, func=AF.Relu)
                        E(POOL if ti % 2 else DVE, 'tensor_tensor', out=hid[:, ti, :], in0=r_[:, :], in1=r_[:, :], op=ALU.mult)
                    dense(wb, l, 'wm1', range(32), lambda k: x1b[:, k, :], NB, evac)

                    def evac(ti, pv, wdt):
                        E(ACT, 'activation', out=sig[:, ti, :], in_=pv, func=AF.Sigmoid)
                    dense(wb, l, 'wpg', range(8), lambda k: x1b[:, k, :], NB, evac)

                    def evac(ti, pv, wdt):
                        E(DVE, 'tensor_tensor', out=sig[:, ti, :], in0=pv, in1=sig[:, ti, :], op=ALU.mult)
                    dense(wb, l, 'wpp', range(8), lambda k: pT[:, k, :], NB, evac)

                    def evac(ti, pv, wdt):
                        E(DVE, 'scalar_tensor_tensor', out=accv(ti), in0=x1[:, ti, :], scalar=ALPHA, in1=pv,
                          op0=ALU.mult, op1=ALU.add)
                        E(POOL, 'tensor_tensor', out=accv(ti), in0=accv(ti), in1=sig[:, ti, :], op=ALU.add)
                    dense(wb, l, 'wm2', range(8), lambda k: hid[:, k, :], NB, evac)
                    layer_norm(lb, acc3, lambda c: cpv(l, 'ln2_g', c), lambda c: cpv(l, 'ln2_b', c), dst_f=xres)
                    if not last:
                        kb.dma(View([kb.db('xT', b)], xT[:, :, t0:t0 + NB].rearrange("c p t -> p c t")), xres[:])
                    else:
                        ov = x1.v(x1.t[:, :, :].rearrange("p c t -> p (c t)").rearrange("p (j d) -> p j d", j=4))
                        for j in range(4):
                            for c2 in range(2):
                                pb = ps[5 + c2]
                                for cc in range(4):
                                    c = c2 * 4 + cc
                                    kb.tr(pb[:, cc * 128:(cc + 1) * 128], xres[:, c, j * 128:(j + 1) * 128], C('ident'))
                                E(ACT if c2 else DVE, 'copy' if c2 else 'tensor_copy',
                                  out=x1.v(ov.ap[:, j, c2 * 512:(c2 + 1) * 512]), in_=pb[:, :])
                        kb.dma(View([], y_out[t0:t0 + NB, :].rearrange("(j p) d -> p j d", p=128)), ov)
            kb.barrier()

        SEGB = SEG // NB

        def blk_order(d):
            return list(range(NBLK)) if d == 0 else list(range(NBLK - 1, -1, -1))

        def load_halo(raw, tile0, nt, b):
            t0 = b * NB
            lo, hi = max(t0 - 1, 0), min(t0 + NB + 2, T)
            if lo > t0 - 1:
                E(POOL, 'memset', ap=raw[:, :, 0:1], constant=0.0)
            if hi < t0 + NB + 2:
                E(POOL, 'memset', ap=raw[:, :, NB + 1:NB + 3], constant=0.0)
            bl = [kb.db('hT%d' % (tile0 + c), bb) for c in range(nt) for bb in (b - 1, b, b + 1) if 0 <= bb < NBLK]
            kb.dma(raw[:, :, lo - (t0 - 1):hi - (t0 - 1)], View(bl, hT[tile0:tile0 + nt, :, lo:hi].rearrange("c p t -> p c t")))
            if b % SEGB == 0 and b > 0:
                E(DVE, 'tensor_scalar', out=raw[:, :, 0:1], in0=raw[:, :, 0:1], scalar1=flag[:, 0:1], scalar2=None, op0=ALU.mult)
            if b % SEGB == SEGB - 1 and b < NBLK - 1:
                E(DVE, 'tensor_scalar', out=raw[:, :, NB + 1:NB + 3], in0=raw[:, :, NB + 1:NB + 3], scalar1=flag[:, 0:1],
                  scalar2=None, op0=ALU.mult)

        def conv4(eng, out, raw, t, wcol, bias=None):
            if bias is None:
                E(eng, 'tensor_scalar', out=out, in0=raw[:, t, 0:NB], scalar1=wcol(0), scalar2=None, op0=ALU.mult)
            else:
                E(eng, 'tensor_scalar', out=out, in0=raw[:, t, 0:NB], scalar1=wcol(0), scalar2=bias, op0=ALU.mult, op1=ALU.add)
            for k in range(1, 4):
                E(DVE, 'scalar_tensor_tensor', out=out, in0=raw[:, t, k:k + NB], scalar=wcol(k), in1=out, op0=ALU.mult, op1=ALU.add)

        def crosses(b, d):
            return (d == 0 and b > 0 and b % SEGB == 0) or (d == 1 and b < NBLK - 1 and b % SEGB == SEGB - 1)

        def phase_lru(l):
            with ExitStack() as st:
                bdf = sb(st, "l_bdf", [128, 16, 128])
                bdb = sb(st, "l_bdb", [128, 16, 128], BF16)
                ccol = sb(st, "l_ccol", [128, 8])
                raw = [sb(st, "l_raw%d" % i, [128, 4, NB + 3]) for i in range(2)]
                xc = sb(st, "l_xc", [128, 4, NB])
                xcb = sb(st, "l_xcb", [128, 4, NB], BF16)
                r_ = [sb(st, "l_r%d" % i, [128, NB]) for i in range(2)]
                i_ = [sb(st, "l_i%d" % i, [128, NB]) for i in range(2)]
                a_ = [sb(st, "l_a%d" % i, [128, NB]) for i in range(2)]
                u_ = [sb(st, "l_u%d" % i, [128, NB]) for i in range(2)]
                hh = [sb(st, "l_h%d" % i, [128, 4, NB]) for i in range(2)]
                hf = sb(st, "l_hf", [128, 4, NB])
                cg = sb(st, "l_cg", [128, 4, NB])
                ob = [sb(st, "l_ob%d" % i, [128, 4, NB], BF16) for i in range(2)]
                carry = sb(st, "l_carry", [128, 4])
                kb.dma(bdf[:], View([], bd_in[l]))
                E(DVE, 'tensor_copy', out=bdb[:], in_=bdf[:])
                o_, w_ = CP['lru_lam']
                E(ACT, 'activation', out=ccol[:], in_=cpt[l][:, o_:o_ + 8], func=AF.Sigmoid)
                E(ACT, 'activation', out=ccol[:], in_=ccol[:], func=AF.Ln)
                E(DVE, 'tensor_scalar', out=ccol[:], in0=ccol[:], scalar1=8.0, scalar2=None, op0=ALU.mult)
                for d in range(2):
                    for bi, b in enumerate(blk_order(d)):
                        t0 = b * NB
                        rw = raw[bi % 2]
                        h = hh[bi % 2]
                        load_halo(rw, CX, 4, b)
                        if d == 1:
                            kb.dma(hf[:], View([kb.db('ofT%d' % (8 + c), b) for c in range(4)],
                                               ofT[8:12, :, t0:t0 + NB].rearrange("c p t -> p c t")))
                            kb.dma(cg[:], View([kb.db('hT%d' % (CG + c), b) for c in range(4)],
                                               hT[CG:CG + 4, :, t0:t0 + NB].rearrange("c p t -> p c t")))
                        if bi == 0:
                            E(DVE, 'memset', ap=carry[:], constant=0.0)
                        elif crosses(b, d):
                            E(DVE, 'tensor_scalar', out=carry[:], in0=carry[:], scalar1=flag[:, 0:1], scalar2=None, op0=ALU.mult)
                        for t in range(4):
                            conv4(POOL, xc[:, t, :], rw, t, lambda k: cpv(l, 'lru_conv', t * 4 + k), bias=cpv(l, 'lru_cb', t))
                            E(ACT, 'copy', out=xcb[:, t, :], in_=xc[:, t, :])
                            pa, px = ps[(2 * t) % 4], ps[(2 * t + 1) % 4]
                            kb.mm(pa[:, :], bdb[:, d * 4 + t, :], xcb[:, t, :])
                            kb.mm(px[:, :], bdb[:, 8 + d * 4 + t, :], xcb[:, t, :])
                            rr, ii, aa, uu = r_[t % 2], i_[t % 2], a_[t % 2], u_[t % 2]
                            E(ACT, 'activation', out=rr[:, :], in_=pa[:, :], func=AF.Sigmoid, bias=cpv(l, 'lru_ba', d * 4 + t))
                            E(ACT, 'activation', out=ii[:, :], in_=px[:, :], func=AF.Sigmoid, bias=cpv(l, 'lru_bx', d * 4 + t))
                            E(ACT, 'activation', out=aa[:, :], in_=rr[:, :], func=AF.Exp, scale=ccol[:, d * 4 + t:d * 4 + t + 1])
                            E(POOL, 'tensor_tensor', out=rr[:, :], in0=aa[:, :], in1=aa[:, :], op=ALU.mult)
                            E(DVE, 'tensor_scalar', out=rr[:, :], in0=rr[:, :], scalar1=-1.0, scalar2=1.0, op0=ALU.mult, op1=ALU.add)
                            E(ACT, 'activation', out=rr[:, :], in_=rr[:, :], func=AF.Sqrt)
                            E(POOL, 'tensor_tensor', out=ii[:, :], in0=ii[:, :], in1=xc[:, t, :], op=ALU.mult)
                            E(DVE, 'tensor_tensor', out=uu[:, :], in0=rr[:, :], in1=ii[:, :], op=ALU.mult)
                            if d == 0:
                                E(DVE, 'tensor_tensor_scan', out=h[:, t, :], data0=aa[:, :], data1=uu[:, :],
                                  initial=carry[:, t:t + 1], op0=ALU.mult, op1=ALU.add)
                            else:
                                E(DVE, 'tensor_tensor_scan', out=h[:, t, ::-1], data0=aa[:, ::-1], data1=uu[:, ::-1],
                                  initial=carry[:, t:t + 1], op0=ALU.mult, op1=ALU.add)
                        E(DVE, 'tensor_copy', out=carry[:], in_=h[:, :, NB - 1] if d == 0 else h[:, :, 0])
                        if d == 0:
                            kb.dma(View([kb.db('ofT%d' % (8 + c), b) for c in range(4)],
                                        ofT[8:12, :, t0:t0 + NB].rearrange("c p t -> p c t")), h[:])
                        else:
                            o2 = ob[bi % 2]
                            E(ACT, 'activation', out=cg[:], in_=cg[:], func=AF.Gelu_apprx_tanh)
                            E(POOL, 'tensor_tensor', out=hf[:], in0=hf[:], in1=h[:], op=ALU.add)
                            E(DVE, 'tensor_tensor', out=o2[:], in0=hf[:], in1=cg[:], op=ALU.mult)
                            kb.dma(View([kb.db('obT', b)], obT[8:12, :, t0:t0 + NB].rearrange("c p t -> p c t")), o2[:])
            kb.barrier()

        def phase_hg(l):
            with ExitStack() as st:
                lbc = sb(st, "g_lbc", [128, 4])
                omlb = sb(st, "g_omlb", [128, 4])
                nomlb = sb(st, "g_nomlb", [128, 4])
                qT = [sb(st, "g_qT%d" % i, [128, 4, NB]) for i in range(2)]
                fT = [sb(st, "g_fT%d" % i, [128, 4, NB]) for i in range(2)]
                vT = [sb(st, "g_vT%d" % i, [128, 4, NB]) for i in range(2)]
                s_ = [sb(st, "g_s%d" % i, [128, NB]) for i in range(2)]
                f_ = [sb(st, "g_f%d" % i, [128, NB]) for i in range(2)]
                kk = [sb(st, "g_kk%d" % i, [128, NB]) for i in range(2)]
                bq = [sb(st, "g_bq%d" % i, [128, NB]) for i in range(2)]
                dq_ = [sb(st, "g_dq%d" % i, [128, NB]) for i in range(2)]
                e1 = [sb(st, "g_e1%d" % i, [128, NB]) for i in range(2)]
                e2 = [sb(st, "g_e2%d" % i, [128, NB]) for i in range(2)]
                qt = sb(st, "g_qt", [128, 4, NB], BF16)
                kdT = sb(st, "g_kdT", [128, 4, NB], BF16)
                ebl = sb(st, "g_ebl", [128, 4, 8])
                vtok = sb(st, "g_vtok", [128, 4, 512], BF16)
                ktok = sb(st, "g_ktok", [128, 4, 512], BF16)
                vTb = sb(st, "g_vTb", [128, 4, NB], BF16)
                AT = [sb(st, "g_AT%d" % i, [128, 4, 64], BF16) for i in range(2)]
                S = sb(st, "g_S", [128, 4, 128])
                Sp = sb(st, "g_Sp", [128, 4, 128])
                Spb = sb(st, "g_Spb", [128, 4, 128], BF16)
                ost = [sb(st, "g_ost%d" % i, [128, 4, NB]) for i in range(2)]
                of_ = sb(st, "g_of", [128, 4, NB])
                gz = sb(st, "g_gz", [128, 4, NB])
                oh = [sb(st, "g_oh%d" % i, [128, NB]) for i in range(2)]
                sqb = [sb(st, "g_sqb%d" % i, [128, NB], BF16) for i in range(2)]
                sd = [sb(st, "g_sd%d" % i, [128, NB]) for i in range(2)]
                obs = [sb(st, "g_obs%d" % i, [128, 4, NB], BF16) for i in range(2)]
                if l == 0:
                    E(DVE, 'memset', ap=lbc[:], constant=0.0)
                else:
                    o0, o1 = CP['lb0'][0], CP['lb1'][0]
                    E(DVE, 'tensor_tensor', out=lbc[:], in0=cpt[l][:, o1:o1 + 4], in1=cpt[l][:, o0:o0 + 4], op=ALU.subtract)
                    E(ACT, 'activation', out=lbc[:], in_=lbc[:], func=AF.Sigmoid)
                E(DVE, 'tensor_scalar', out=omlb[:], in0=lbc[:], scalar1=-1.0, scalar2=1.0, op0=ALU.mult, op1=ALU.add)
                E(DVE, 'tensor_scalar', out=nomlb[:], in0=omlb[:], scalar1=-1.0, scalar2=None, op0=ALU.mult)
                pso = [ps[0], ps[1], ps[2], ps[3]]
                psA, psS, psT, psN = ps[4], ps[5], ps[6], ps[7]
                psTb = psT.v(psT.t[:, :].bitcast(BF16))
                for d in range(2):
                    HF = HFF if d == 0 else HFB
                    mask4 = CB('hgm_f4') if d == 0 else CB('hgm_b4')
                    for bi, b in enumerate(blk_order(d)):
                        t0 = b * NB
                        q_, fz, v_ = qT[bi % 2], fT[bi % 2], vT[bi % 2]
                        for (dst, tl) in ((q_, HQ), (fz, HF), (v_, HI)):
                            kb.dma(dst[:], View([kb.db('hT%d' % (tl + c), b) for c in range(4)],
                                                hT[tl:tl + 4, :, t0:t0 + NB].rearrange("c p t -> p c t")))
                        if d == 1:
                            kb.dma(of_[:], View([kb.db('ofT%d' % (4 + c), b) for c in range(4)],
                                                ofT[4:8, :, t0:t0 + NB].rearrange("c p t -> p c t")))
                            kb.dma(gz[:], View([kb.db('hT%d' % (HGT + c), b) for c in range(4)],
                                               hT[HGT:HGT + 4, :, t0:t0 + NB].rearrange("c p t -> p c t")))
                        if bi == 0:
                            E(DVE, 'memset', ap=S[:], constant=0.0)
                        elif crosses(b, d):
                            E(DVE, 'tensor_scalar', out=S[:], in0=S[:], scalar1=flag[:, 0:1], scalar2=None, op0=ALU.mult)
                        E(POOL, 'tensor_copy', out=vTb[:], in_=v_[:])
                        for j in range(4):
                            for h in range(4):
                                kb.tr(psTb.bufs[0].v(psTb.ap[:, h * 128:(h + 1) * 128]), vTb[:, h, j * 128:(j + 1) * 128], CB('ident'))
                            E(ACT, 'copy', out=vtok[:, j, :], in_=psT.v(psTb.ap[:, 0:512]))
                        for h in range(4):
                            ss, ff, k2, bb, dd, x1_, x2_ = s_[h % 2], f_[h % 2], kk[h % 2], bq[h % 2], dq_[h % 2], e1[h % 2], e2[h % 2]
                            E(ACT, 'activation', out=ss[:, :], in_=fz[:, h, :], func=AF.Sigmoid)
                            E(DVE, 'tensor_scalar', out=ff[:, :], in0=ss[:, :], scalar1=omlb[:, h:h + 1], scalar2=lbc[:, h:h + 1],
                              op0=ALU.mult, op1=ALU.add)
                            E(ACT, 'activation', out=ff[:, :], in_=ff[:, :], func=AF.Ln)
                            E(POOL, 'tensor_scalar', out=k2[:, :], in0=ss[:, :], scalar1=nomlb[:, h:h + 1], scalar2=omlb[:, h:h + 1],
                              op0=ALU.mult, op1=ALU.add)
                            if d == 0:
                                E(DVE, 'tensor_tensor_scan', out=bb[:, :], data0=C('rst_f'), data1=ff[:, :], initial=0.0,
                                  op0=ALU.mult, op1=ALU.add)
                            else:
                                rb = C('rst_b')
                                E(DVE, 'tensor_tensor_scan', out=bb[:, ::-1], data0=View(rb.bufs, rb.ap[:, ::-1]), data1=ff[:, ::-1],
                                  initial=0.0, op0=ALU.mult, op1=ALU.add)
                            b3 = bb.t[:, :].rearrange("p (c j) -> p c j", j=64)
                            blv = b3[:, :, 63] if d == 0 else b3[:, :, 0]
                            E(ACT, 'activation', out=ebl[:, h, :], in_=bb.v(blv), func=AF.Exp)
                            E(DVE, 'tensor_tensor', out=dd.v(dd.t[:, :].rearrange("p (c j) -> p c j", j=64)), in0=bb.v(b3),
                              in1=bb.v(bc_last(blv, 64)), op=ALU.subtract)
                            E(ACT, 'activation', out=x1_[:, :], in_=dd[:, :], func=AF.Exp)
                            E(ACT, 'activation', out=x2_[:, :], in_=dd[:, :], func=AF.Exp, scale=-1.0)
                            E(POOL, 'tensor_tensor', out=qt[:, h, :], in0=q_[:, h, :], in1=x1_[:, :], op=ALU.mult)
                            E(DVE, 'tensor_tensor', out=kdT[:, h, :], in0=k2[:, :], in1=x2_[:, :], op=ALU.mult)
                        for j in range(4):
                            for h in range(4):
                                kb.tr(psTb.bufs[0].v(psTb.ap[:, h * 128:(h + 1) * 128]), kdT[:, h, j * 128:(j + 1) * 128], CB('ident'))
                            E(ACT, 'copy', out=ktok[:, j, :], in_=psT.v(psTb.ap[:, 0:512]))
                        jl = range(4) if d == 0 else range(3, -1, -1)
                        for j in jl:
                            at = AT[j % 2]
                            for hfh in range(2):
                                c = 2 * j + hfh
                                for h in range(4):
                                    kb.mm(psA[hfh * 64:(hfh + 1) * 64, h * 64:(h + 1) * 64], kdT[:, h, c * 64:(c + 1) * 64],
                                          qt[:, h, c * 64:(c + 1) * 64])
                            E(DVE, 'tensor_tensor', out=at.v(at.t[:, :, :].rearrange("p h i -> p (h i)")), in0=psA[:, 0:256], in1=mask4, op=ALU.mult)
                            for hfh in (range(2) if d == 0 else range(1, -1, -1)):
                                c = 2 * j + hfh
                                p0 = hfh * 64
                                eb = ebl.v(bc_last(ebl.t[:, :, c], 128))
                                E(DVE, 'tensor_tensor', out=Spb[:], in0=S[:], in1=eb, op=ALU.mult)
                                E(POOL, 'tensor_tensor', out=Sp[:], in0=S[:], in1=eb, op=ALU.mult)
                                for h in range(4):
                                    kb.mm(pso[h][:, c * 64:(c + 1) * 64], Spb[:, h, :], qt[:, h, c * 64:(c + 1) * 64], start=True, stop=False)
                                    kb.mm(pso[h][:, c * 64:(c + 1) * 64], vtok[p0:p0 + 64, j, h * 128:(h + 1) * 128], at[p0:p0 + 64, h, :],
                                          start=False, stop=True)
                                for h in range(4):
                                    kb.mm(psS[:, h * 128:(h + 1) * 128], ktok[p0:p0 + 64, j, h * 128:(h + 1) * 128],
                                          vtok[p0:p0 + 64, j, h * 128:(h + 1) * 128])
                                E(DVE, 'tensor_tensor', out=S.v(S.t[:, :, :].rearrange("p h d -> p (h d)")),
                                  in0=Sp.v(Sp.t[:, :, :].rearrange("p h d -> p (h d)")), in1=psS[:, :], op=ALU.add)
                        if d == 0:
                            o1 = ost[bi % 2]
                            for h in range(4):
                                E(ACT, 'copy', out=o1[:, h, :], in_=pso[h][:, :])
                            kb.dma(View([kb.db('ofT%d' % (4 + c), b) for c in range(4)],
                                        ofT[4:8, :, t0:t0 + NB].rearrange("c p t -> p c t")), o1[:])
                        else:
                            o2 = obs[bi % 2]
                            E(ACT, 'activation', out=gz[:], in_=gz[:], func=AF.Silu)
                            for h in range(4):
                                oo, sq_, sd_ = oh[h % 2], sqb[h % 2], sd[h % 2]
                                E(DVE, 'tensor_tensor', out=oo[:, :], in0=pso[h][:, :], in1=of_[:, h, :], op=ALU.add)
                                E(ACT, 'activation', out=sq_[:, :], in_=oo[:, :], func=AF.Square)
                                kb.mm(psN[:, :], CB('ones'), sq_[:, :])
                                E(DVE, 'tensor_scalar', out=sd_[:, :], in0=psN[:, :], scalar1=1.0 / 128, scalar2=1e-6, op0=ALU.mult, op1=ALU.add)
                                E(ACT, 'activation', out=sd_[:, :], in_=sd_[:, :], func=AF.Sqrt)
                                E(DVE, 'reciprocal', out=sd_[:, :], in_=sd_[:, :])
                                E(POOL, 'tensor_tensor', out=oo[:, :], in0=oo[:, :], in1=sd_[:, :], op=ALU.mult)
                                E(DVE, 'scalar_tensor_tensor', out=o2[:, h, :], in0=oo[:, :], scalar=cpv(l, 'hg_nw'), in1=gz[:, h, :],
                                  op0=ALU.mult, op1=ALU.mult)
                            kb.dma(View([kb.db('obT', b)], obT[4:8, :, t0:t0 + NB].rearrange("c p t -> p c t")), o2[:])
            kb.barrier()

        class ColBuf:
            def __init__(self, bank, c0, c1, name=""):
                self.bank = bank
                self.t = bank.t
                self.c0, self.c1 = c0, c1

            def cols(self, a=None, b=None, rows=slice(None)):
                a = 0 if a is None else a
                b = (self.c1 - self.c0) if b is None else b
                return View([self.bank], self.t[rows, self.c0 + a:self.c0 + b])

            def bf(self, a, b):
                return View([self.bank], self.t[:, :].bitcast(BF16)[:, 2 * self.c0 + a:2 * self.c0 + b])

        def rr(gens):
            gens = list(gens)
            while gens:
                for g in list(gens):
                    try:
                        next(g)
                    except StopIteration:
                        gens.remove(g)

        NCH = 2
        NHC = 4 // NCH

        def phase_dn(l):
            with ExitStack() as st:
                raw = sb(st, "d_raw", [128, 12, NB + 3])
                cv = sb(st, "d_cv", [128, 12, NB])
                abT = sb(st, "d_abT", [16, NB])
                qn = sb(st, "d_qn", [128, 4, NB], BF16)
                kn = sb(st, "d_kn", [128, 4, NB], BF16)
                vTb = sb(st, "d_vTb", [128, 4, NB], BF16)
                sqb = [sb(st, "d_sqb%d" % i, [128, NB], BF16) for i in range(2)]
                sd = [sb(st, "d_sd%d" % i, [128, NB]) for i in range(2)]
                negA = sb(st, "d_negA", [128, 8])
                gt = sb(st, "d_gt", [128, 4, 16])
                g_ = sb(st, "d_g", [128, 4, 4])
                gR = sb(st, "d_gR", [128, 4, 4], F32R)
                beta = sb(st, "d_beta", [128, 4, 4])
                nbeta = sb(st, "d_nbeta", [128, 4, 4])
                of_ = sb(st, "d_of", [128, 4, NB])
                gz = sb(st, "d_gz", [128, 4, NB])
                oh = [sb(st, "d_oh%d" % i, [128, NB]) for i in range(2)]
                obs = sb(st, "d_obs", [128, 4, NB], BF16)

                class CH:
                    pass
                chs = []
                for c in range(NCH):
                    ch = CH()
                    ch.c = c
                    ch.heads = list(range(c * NHC, (c + 1) * NHC))
                    n3 = [128, NHC, 128]
                    mk_ = lambda nm, dt_=F32, c=c, n3=n3: sb(st, "d%d_%s" % (c, nm), n3, dt_)
                    ch.Mh = mk_("Mh", F32R)
                    ch.Ecol = sb(st, "d%d_Ecol" % c, [128, 3, NHC])
                    ch.sc1 = sb(st, "d%d_sc1" % c, [128, NHC])
                    ch.Dc, ch.Dm, ch.attn, ch.AT = mk_("Dc", BF16), mk_("Dm", BF16), mk_("attn", BF16), mk_("AT", BF16)
                    ch.Nm, ch.NT = mk_("Nm", F32R), mk_("NT", F32R)
                    ch.Pb = [mk_("P%d" % i, F32R) for i in range(2)]
                    ch.PTb = [mk_("PT%d" % i, F32R) for i in range(2)]
                    ch.Rb = [mk_("R%d" % i, F32R) for i in range(2)]
                    ch.kbg, ch.vb, ch.nwT = mk_("kbg", F32R), mk_("vb", F32R), mk_("nwT", F32R)
                    ch.kd, ch.qd, ch.vnew = mk_("kd", BF16), mk_("qd", BF16), mk_("vnew", BF16)
                    ch.EG, ch.Stmp = mk_("EG"), mk_("Stmp")
                    ch.S, ch.Sb = mk_("S", F32R), mk_("Sb", BF16)
                    ch.ost = sb(st, "d%d_ost" % c, [128, NHC, NB])
                    W_ = NHC * 128
                    assert NCH == 2 and W_ == 256
                    B0, B1, B2, B3 = [ps[4 * c + i] for i in range(4)]
                    ch.psD = ColBuf(B0, 0, 256)
                    ch.psPT = ColBuf(B0, 0, 256)
                    ch.psGc = ColBuf(B0, 256, 272)
                    ch.psAT = ColBuf(B0, 272, 400)
                    ch.psE = ColBuf(B1, 0, 256)
                    ch.psKV = ColBuf(B1, 256, 512)
                    ch.psG = ColBuf(B2, 0, 256)
                    ch.psA = ColBuf(B2, 256, 512)
                    ch.psT = ColBuf(B3, 0, 256)
                    ch.psS = ColBuf(B3, 256, 512)
                    chs.append(ch)
                psGT = ColBuf(ps[0], 400, 464)
                psNv = ps[7][:, :]
                flat = lambda bf: bf.v(bf.t[:, :, :].rearrange("p h i -> p (h i)"))
                o_ = CP['alog'][0]
                E(ACT, 'activation', out=negA[:], in_=cpt[l][:, o_:o_ + 8], func=AF.Exp)
                E(DVE, 'tensor_scalar', out=negA[:], in0=negA[:], scalar1=-1.0, scalar2=None, op0=ALU.mult)

                def chain(ch, d, order, TRI, TRIR, TRICR, NMR, SM):
                    hs_ = ch.heads
                    h0 = hs_[0]
                    W_ = NHC * 128
                    for j in order:
                        cs = slice(j * 128, (j + 1) * 128)
                        gj = gR[:, j, h0:h0 + NHC]
                        gjf = asf(gj)
                        E(DVE, 'tensor_tensor', out=ch.Mh[:], in0=View(TRI.bufs, bc_mid(TRI.ap, NHC)),
                          in1=View(gjf.bufs, bc_last(gjf.ap, 128)), op=ALU.mult)
                        yield
                        for hi in range(NHC):
                            o2 = ch.psD.cols(hi * 128, (hi + 1) * 128)
                            kb.mm(o2, ch.Mh[:, hi, :], CR('ones'), start=True, stop=False)
                            kb.mm(o2, CR('negones'), ch.Mh[:, hi, :], start=False, stop=False)
                            kb.mm(o2, CR('ident'), NMR, start=False, stop=True)
                        kb.mm(ch.psGc.cols(0, NHC), TRIR, gj)
                        kb.mm(ch.psGc.cols(NHC, 2 * NHC), TRICR, gj)
                        kb.mm(ch.psGc.cols(2 * NHC, 3 * NHC), CR('ones'), gj)
                        for hi, h in enumerate(hs_):
                            kb.mm(ch.psG.cols(hi * 128, (hi + 1) * 128), kn[:, h, cs], kn[:, h, cs])
                            kb.mm(ch.psA.cols(hi * 128, (hi + 1) * 128), qn[:, h, cs], kn[:, h, cs])
                        for hi in range(NHC):
                            kb.mm(ch.psE.cols(hi * 128, (hi + 1) * 128), CR('ones'), ch.Mh[:, hi, :])
                        yield
                        E(ACT, 'activation', out=ch.Ecol.v(ch.Ecol.t[:, :, :].rearrange("p a h -> p (a h)")), in_=ch.psGc.cols(0, 3 * NHC), func=AF.Exp)
                        E(ACT, 'activation', out=flat(ch.Dc), in_=ch.psD.cols(), func=AF.Exp)
                        E(ACT, 'activation', out=flat(ch.EG), in_=ch.psE.cols(), func=AF.Exp)
                        yield
                        E(POOL, 'tensor_tensor', out=flat(ch.Dm), in0=flat(ch.Dc), in1=View(SM.bufs, SM.ap[:, 0:W_]), op=ALU.mult)
                        E(POOL, 'tensor_tensor', out=ch.qd[:], in0=qn[:, h0:h0 + NHC, cs], in1=ch.EG[:], op=ALU.mult)
                        yield
                        for hi, h in enumerate(hs_):
                            E(DVE, 'scalar_tensor_tensor', out=ch.Nm[:, hi, :], in0=ch.psG.cols(hi * 128, (hi + 1) * 128),
                              scalar=nbeta[:, j, h:h + 1], in1=ch.Dm[:, hi, :], op0=ALU.mult, op1=ALU.mult)
                        E(DVE, 'tensor_tensor', out=flat(ch.attn), in0=ch.psA.cols(), in1=flat(ch.Dc), op=ALU.mult)
                        yield
                        for hi in range(NHC):
                            kb.tr(View([ch.psT.bank], ch.psT.t[:, ch.psT.c0 + hi * 128:ch.psT.c0 + (hi + 1) * 128].bitcast(F32R)), ch.Nm[:, hi, :], CR('ident'))
                            kb.tr(ch.psAT.bf(hi * 128, (hi + 1) * 128), ch.attn[:, hi, :], CB('ident'))
                        for hi, h in enumerate(hs_):
                            kb.tr(ch.psKV.bf(hi * 128, (hi + 1) * 128), kn[:, h, cs], CB('ident'))
                            kb.tr(ch.psKV.bf(W_ + hi * 128, W_ + (hi + 1) * 128), vTb[:, h, cs], CB('ident'))
                        yield
                        E(ACT, 'copy', out=flat(ch.NT), in_=ch.psT.cols())
                        E(DVE, 'tensor_copy', out=flat(ch.AT), in_=ch.psAT.bf(0, W_))
                        E(DVE, 'tensor_tensor', out=ch.sc1[:], in0=beta[:, j, h0:h0 + NHC], in1=ch.Ecol[:, 0, :], op=ALU.mult)
                        kv3 = lambda a: View([ch.psKV.bank], ch.psKV.bf(a, a + W_).ap.rearrange("p (h i) -> p h i", h=NHC))
                        E(DVE, 'tensor_tensor', out=ch.kbg[:], in0=kv3(0), in1=ch.sc1.v(bc_last(ch.sc1.t[:, 0:NHC], 128)), op=ALU.mult)
                        E(DVE, 'tensor_tensor', out=ch.kd[:], in0=kv3(0), in1=ch.Ecol.v(bc_last(ch.Ecol.t[:, 1, :], 128)), op=ALU.mult)
                        E(DVE, 'tensor_tensor', out=ch.vb[:], in0=kv3(W_), in1=beta.v(bc_last(beta.t[:, j, h0:h0 + NHC], 128)), op=ALU.mult)
                        yield
                        P, PT = ch.Nm, ch.NT
                        R = ch.Rb[0]
                        ident_n = C('ident4')
                        E(DVE, 'tensor_tensor', out=flat(R), in0=asf(flat(ch.NT)), in1=View(ident_n.bufs, ident_n.ap[:, 0:W_]), op=ALU.add)
                        yield
                        for k in range(6):
                            Pn, PTn, Rn = ch.Pb[k % 2], ch.PTb[k % 2], ch.Rb[(k + 1) % 2]
                            for hi in range(NHC):
                                kb.mm(ch.psG.cols(hi * 128, (hi + 1) * 128), PT[:, hi, :], P[:, hi, :])
                            if k < 5:
                                for hi in range(NHC):
                                    kb.mm(ch.psPT.cols(hi * 128, (hi + 1) * 128), P[:, hi, :], PT[:, hi, :])
                            yield
                            E(ACT, 'copy', out=flat(Pn), in_=ch.psG.cols())
                            if k < 5:
                                E(DVE, 'tensor_copy', out=flat(PTn), in_=ch.psPT.cols())
                            yield
                            for hi in range(NHC):
                                kb.mm(ch.psT.cols(hi * 128, (hi + 1) * 128), Pn[:, hi, :], R[:, hi, :])
                            yield
                            E(DVE, 'tensor_tensor', out=flat(Rn), in0=ch.psT.cols(), in1=asf(flat(R)), op=ALU.add)
                            yield
                            P, PT, R = Pn, PTn, Rn
                        TT = R
                        for hi in range(NHC):
                            kb.mm(ch.psD.cols(hi * 128, (hi + 1) * 128), ch.kbg[:, hi, :], TT[:, hi, :])
                        yield
                        E(ACT, 'activation', out=flat(ch.nwT), in_=ch.psD.cols(), func=AF.Copy, scale=-1.0)
                        yield
                        for hi in range(NHC):
                            o2 = ch.psKV.cols(hi * 128, (hi + 1) * 128)
                            kb.mm(o2, TT[:, hi, :], ch.vb[:, hi, :], start=True, stop=False)
                            kb.mm(o2, ch.nwT[:, hi, :], ch.S[:, hi, :], start=False, stop=True)
                        yield
                        E(ACT, 'copy', out=flat(ch.vnew), in_=ch.psKV.cols())
                        yield
                        for hi in range(NHC):
                            o2 = ch.psE.cols(hi * 128, (hi + 1) * 128)
                            kb.mm(o2, ch.Sb[:, hi, :], ch.qd[:, hi, :], start=True, stop=False)
                            kb.mm(o2, ch.vnew[:, hi, :], ch.AT[:, hi, :], start=False, stop=True)
                        for hi in range(NHC):
                            kb.mm(ch.psS.cols(hi * 128, (hi + 1) * 128), ch.kd[:, hi, :], ch.vnew[:, hi, :])
                        yield
                        E(DVE, 'tensor_tensor', out=ch.Stmp[:], in0=asf(ch.S[:]), in1=ch.Ecol.v(bc_last(ch.Ecol.t[:, 2, :], 128)), op=ALU.mult)
                        E(DVE, 'tensor_tensor', out=flat(ch.S), in0=flat(ch.Stmp), in1=ch.psS.cols(), op=ALU.add)
                        E(ACT, 'copy', out=ch.Sb[:], in_=asf(ch.S[:]))
                        E(ACT, 'copy', out=ch.ost[:, :, cs], in_=View([ch.psE.bank], ch.psE.cols().ap.rearrange("p (h i) -> p h i", h=NHC)))
                        yield

                for d in range(2):
                    TRI = C('tri_le') if d == 0 else C('tri_ge')
                    NMR = CR('nm_f') if d == 0 else CR('nm_b')
                    TRIR = CR('tri_le') if d == 0 else CR('tri_ge')
                    TRICR = CR('tri_gt') if d == 0 else CR('tri_lt')
                    SM = CB('sm_f4') if d == 0 else CB('sm_b4')
                    o_ = CP['dtb'][0]
                    dtb = cpt[l][:, o_ + d * 4:o_ + d * 4 + 4]
                    nAd = negA[:, d * 4:d * 4 + 4]
                    for bi, b in enumerate(blk_order(d)):
                        t0 = b * NB
                        load_halo(raw, DQ, 12, b)
                        kb.dma(abT[:], View([kb.db('hT%d' % AB, b)], hT[AB, 0:16, t0:t0 + NB]))
                        if d == 1:
                            kb.dma(of_[:], View([kb.db('ofT%d' % c, b) for c in range(4)],
                                                ofT[0:4, :, t0:t0 + NB].rearrange("c p t -> p c t")))
                            kb.dma(gz[:], View([kb.db('hT%d' % (DZ + c), b) for c in range(4)],
                                               hT[DZ:DZ + 4, :, t0:t0 + NB].rearrange("c p t -> p c t")))
                        for ch in chs:
                            if bi == 0:
                                E(DVE, 'memset', ap=asf(ch.S[:]), constant=0.0)
                                E(POOL, 'memset', ap=ch.Sb[:], constant=0.0)
                            elif crosses(b, d):
                                E(DVE, 'tensor_scalar', out=ch.S[:], in0=asf(ch.S[:]), scalar1=flag[:, 0:1], scalar2=None, op0=ALU.mult)
                                E(DVE, 'tensor_copy', out=ch.Sb[:], in_=asf(ch.S[:]))
                        for t in range(12):
                            conv4(POOL if t % 2 else DVE, cv[:, t, :], raw, t, lambda k: cpv(l, 'dn_conv', t * 4 + k))
                        for t3 in range(3):
                            E(ACT, 'activation', out=cv[:, t3 * 4:t3 * 4 + 4, :], in_=cv[:, t3 * 4:t3 * 4 + 4, :], func=AF.Silu)
                        for t in range(8):
                            sq_, sd_ = sqb[t % 2], sd[t % 2]
                            E(ACT, 'activation', out=sq_[:, :], in_=cv[:, t, :], func=AF.Square)
                            kb.mm(psNv, CB('ones'), sq_[:, :])
                            E(DVE, 'tensor_scalar', out=sd_[:, :], in0=psNv, scalar1=1e-6, scalar2=None, op0=ALU.add)
                            E(ACT, 'activation', out=sd_[:, :], in_=sd_[:, :], func=AF.Sqrt)
                            E(DVE, 'reciprocal', out=sd_[:, :], in_=sd_[:, :])
                            dst = qn[:, t, :] if t < 4 else kn[:, t - 4, :]
                            E(DVE, 'scalar_tensor_tensor', out=dst, in0=cv[:, t, :], scalar=(128 ** -0.5 if t < 4 else 1.0),
                              in1=sd_[:, :], op0=ALU.mult, op1=ALU.mult)
                        E(POOL, 'tensor_copy', out=vTb[:], in_=cv[:, 8:12, :])
                        for j in range(4):
                            idt = C('ident')
                            kb.tr(psGT.cols(j * 16, (j + 1) * 16), abT[0:16, j * 128:(j + 1) * 128], View(idt.bufs, idt.ap[0:16, 0:16]))
                        E(ACT, 'copy', out=gt.v(gt.t[:, :, :].rearrange("p j c -> p (j c)")), in_=psGT.cols())
                        E(DVE, 'tensor_tensor', out=g_[:], in0=gt[:, :, d * 4:d * 4 + 4], in1=View(dtb.bufs, bc_mid(dtb.ap, 4)), op=ALU.add)
                        E(ACT, 'activation', out=g_[:], in_=g_[:], func=AF.Exp)
                        E(DVE, 'tensor_scalar', out=g_[:], in0=g_[:], scalar1=1.0, scalar2=None, op0=ALU.add)
                        E(ACT, 'activation', out=g_[:], in_=g_[:], func=AF.Ln)
                        E(DVE, 'tensor_tensor', out=gR[:], in0=g_[:], in1=View(nAd.bufs, bc_mid(nAd.ap, 4)), op=ALU.mult)
                        E(ACT, 'activation', out=beta[:], in_=gt[:, :, 8 + d * 4:12 + d * 4], func=AF.Sigmoid)
                        E(DVE, 'tensor_scalar', out=nbeta[:], in0=beta[:], scalar1=-1.0, scalar2=None, op0=ALU.mult)
                        order = list(range(4)) if d == 0 else list(range(3, -1, -1))
                        rr([chain(ch, d, order, TRI, TRIR, TRICR, NMR, SM) for ch in chs])
                        if d == 0:
                            for ch in chs:
                                h0 = ch.heads[0]
                                kb.dma(View([kb.db('ofT%d' % c, b) for c in ch.heads],
                                            ofT[h0:h0 + NHC, :, t0:t0 + NB].rearrange("c p t -> p c t")), ch.ost[:])
                        else:
                            E(ACT, 'activation', out=gz[:], in_=gz[:], func=AF.Silu)
                            for h in range(4):
                                ch = chs[h // NHC]
                                hi = h % NHC
                                oo, sq_, sd_ = oh[h % 2], sqb[h % 2], sd[h % 2]
                                E(DVE, 'tensor_tensor', out=oo[:, :], in0=ch.ost[:, hi, :], in1=of_[:, h, :], op=ALU.add)
                                E(ACT, 'activation', out=sq_[:, :], in_=oo[:, :], func=AF.Square)
                                kb.mm(psNv, CB('ones'), sq_[:, :])
                                E(DVE, 'tensor_scalar', out=sd_[:, :], in0=psNv, scalar1=1.0 / 128, scalar2=1e-6, op0=ALU.mult, op1=ALU.add)
                                E(ACT, 'activation', out=sd_[:, :], in_=sd_[:, :], func=AF.Sqrt)
                                E(DVE, 'reciprocal', out=sd_[:, :], in_=sd_[:, :])
                                E(POOL, 'tensor_tensor', out=oo[:, :], in0=oo[:, :], in1=sd_[:, :], op=ALU.mult)
                                E(DVE, 'scalar_tensor_tensor', out=obs[:, h, :], in0=oo[:, :], scalar=cpv(l, 'dn_nw'), in1=gz[:, h, :],
                                  op0=ALU.mult, op1=ALU.mult)
                            kb.dma(View([kb.db('obT', b)], obT[0:4, :, t0:t0 + NB].rearrange("c p t -> p c t")), obs[:])
            kb.barrier()

        MIX = {'lru': phase_lru, 'hg': phase_hg, 'dn': phase_dn}
        run = phases if phases is not None else ['wprep', 'embed', 'proj', 'lru', 'hg', 'dn', 'post']
        if 'wprep' in run:
            phase_wprep()
        if 'embed' in run:
            phase_embed()
        for l in range(nlayers):
            if 'proj' in run:
                phase_proj(l)
            for m in ['lru', 'hg', 'dn']:
                if m in run and m in MIX:
                    MIX[m](l)
            if 'post' in run:
                phase_post(l, l == nlayers - 1)
        kb.barrier()
        print("instructions:", kb.ninst)
    return nc


def _colparams(W):
    L = DEPTH
    cp = np.zeros((L, 128, NCP), np.float32)

    def put(l, name, arr):
        o, w = CP[name]
        cp[l, :, o:o + w] = arr

    def chan(v, nt):
        return np.asarray(v).reshape(nt, 128).T
    for l in range(L):
        put(l, 'dn_conv', np.asarray(W['dn_conv_w'][l]).reshape(4, 12, 128).transpose(2, 1, 0).reshape(128, 48))
        put(l, 'lru_conv', np.asarray(W['lru_conv_w'][l]).reshape(4, 4, 128).transpose(2, 1, 0).reshape(128, 16))
        put(l, 'lru_cb', chan(W['lru_conv_b'][l], 4))
        put(l, 'lru_ba', np.asarray(W['lru_ba'][l]).reshape(2, 4, 128).transpose(2, 0, 1).reshape(128, 8))
        put(l, 'lru_bx', np.asarray(W['lru_bx'][l]).reshape(2, 4, 128).transpose(2, 0, 1).reshape(128, 8))
        put(l, 'lru_lam', np.asarray(W['lru_lambda'][l]).reshape(2, 4, 128).transpose(2, 0, 1).reshape(128, 8))
        put(l, 'dn_nw', np.asarray(W['dn_norm_w'][l]).reshape(128, 1))
        put(l, 'hg_nw', np.asarray(W['hg_norm_w'][l]).reshape(128, 1))
        for n in ['ln1_g', 'ln1_b', 'ln2_g', 'ln2_b']:
            put(l, n, chan(W[n][l], 8))
        put(l, 'lb0', chan(W['hg_lb_logits'][0], 4))
        put(l, 'lb1', chan(W['hg_lb_logits'][1], 4))
        put(l, 'emb_g', chan(W['emb_ln_g'], 8))
        put(l, 'emb_b', chan(W['emb_ln_b'], 8))
        put(l, 'alog', np.tile(np.asarray(W['dn_A_log'][l]).reshape(1, 8), (128, 1)))
        put(l, 'dtb', np.tile(np.asarray(W['dn_dt_bias'][l]).reshape(1, 8), (128, 1)))
    return cp


def _blockdiag(W):
    bd = np.zeros((DEPTH, 128, 16, 128), np.float32)
    for l in range(DEPTH):
        for ai, nm in enumerate(['lru_wa', 'lru_wx']):
            w = np.asarray(W[nm][l])
            for d in range(2):
                for t in range(4):
                    idx = ai * 8 + d * 4 + t
                    for s in range(2):
                        bd[l, s * 64:(s + 1) * 64, idx, s * 64:(s + 1) * 64] = w[d, 2 * t + s]
    return bd


def make_in_maps(W, xs, ps_, flags):
    cp = _colparams(W)
    bd = _blockdiag(W)
    common = dict(cst=CONST_ARR, cp=cp, bd=bd,
                  w_in=np.ascontiguousarray(W['w_in'], dtype=np.float32), w_branch=np.asarray(W['w_branch'], np.float32),
                  w_out=np.asarray(W['w_out'], np.float32), w_mlp1=np.asarray(W['w_mlp1'], np.float32),
                  w_mlp2=np.asarray(W['w_mlp2'], np.float32), w_ple_gate=np.asarray(W['w_ple_gate'], np.float32),
                  w_ple_proj=np.asarray(W['w_ple_proj'], np.float32))
    maps = []
    for x, p, f in zip(xs, ps_, flags):
        m = dict(common)
        m['x'] = np.ascontiguousarray(x, dtype=np.float32)
        m['p'] = np.ascontiguousarray(p, dtype=np.float32)
        m['flag'] = np.full((128, 1), f, np.float32)
        maps.append(m)
    return maps


_NC_CACHE = {}


def kernel(x_prompt, x_sample, p_prompt, p_sample, **W):
    x_prompt = np.asarray(x_prompt)
    x_sample = np.asarray(x_sample)
    p_prompt = np.asarray(p_prompt)
    p_sample = np.asarray(p_sample)
    T, SEG = 8192, 4096
    assign = [(0, 1), (2, 3), (4, 4), (5, 5), (6, 6), (7, 7)]
    xs, ps_, flags = [], [], []
    for c in range(2):
        xs.append(x_sample[c])
        ps_.append(p_sample[:, c])
        flags.append(1.0)
    for a, b in assign:
        xs.append(np.concatenate([x_prompt[a], x_prompt[b]], axis=0))
        ps_.append(np.concatenate([p_prompt[:, a], p_prompt[:, b]], axis=1))
        flags.append(0.0)
    if 'nc' not in _NC_CACHE:
        _NC_CACHE['nc'] = build(T, SEG)
    nc = _NC_CACHE['nc']
    maps = make_in_maps(W, xs, ps_, flags)
    res = run_bass_kernel_spmd(nc, maps, core_ids=list(range(NCORES)))
    y_prompt = np.zeros((8, 4096, D), np.float32)
    y_sample = np.zeros((2, 8192, D), np.float32)
    for c in range(2):
        y_sample[c] = res.results[c]['y']
    for i, (a, b) in enumerate(assign):
        y = res.results[2 + i]['y']
        y_prompt[a] = y[:4096]
        if b != a:
            y_prompt[b] = y[4096:]
    return (y_prompt, y_sample)
```

```python
import numpy as np
from contextlib import ExitStack
import concourse.bass as bass
import concourse.mybir as mybir
from concourse.bass_utils import run_bass_kernel_spmd

F32 = mybir.dt.float32
BF16 = mybir.dt.bfloat16
F32R = mybir.dt.float32r
AF = mybir.ActivationFunctionType
ALU = mybir.AluOpType

D = 1024
NIN = 8720
DFF = 4096
DPLE = 256
DEPTH = 2
ALPHA = (2.0 * DEPTH) ** 0.25
NB = 512
NCORES = 8
DQ, DK, DV, DZ, HQ, HFF, HFB, HI, HGT, CX, CG, GA, GB, GC, AB = 0, 4, 8, 12, 16, 20, 24, 28, 32, 36, 40, 44, 52, 60, 68
NHT = 69
WIN_TILES = [(i * 128, 128) for i in range(16)] + [(2064 + 128 * i, 128) for i in range(52)] + [(2048, 16)]

def _consts():
    i = np.arange(128)
    c = {}
    c['ident'] = np.eye(128)
    c['ones'] = np.ones((128, 128))
    c['negones'] = -np.ones((128, 128))
    c['tri_le'] = (i[:, None] <= i[None, :]) * 1.0
    c['tri_ge'] = (i[:, None] >= i[None, :]) * 1.0
    c['tri_gt'] = (i[:, None] > i[None, :]) * 1.0
    c['tri_lt'] = (i[:, None] < i[None, :]) * 1.0
    c['nm_f'] = np.where(i[:, None] < i[None, :], -30000.0, 0.0)
    c['nm_b'] = np.where(i[:, None] > i[None, :], -30000.0, 0.0)
    c['sm_f4'] = np.tile((i[:, None] > i[None, :]) * 1.0, (1, 4))
    c['sm_b4'] = np.tile((i[:, None] < i[None, :]) * 1.0, (1, 4))
    c['ident4'] = np.tile(np.eye(128), (1, 4))
    j = np.arange(64)
    c['hgm_f4'] = np.tile((j[None, :] >= (i[:, None] % 64)) * 1.0, (1, 4))
    c['hgm_b4'] = np.tile((j[None, :] <= (i[:, None] % 64)) * 1.0, (1, 4))
    t = np.arange(NB)
    c['rst_f'] = np.tile(((t % 64) != 0) * 1.0, (128, 1))
    c['rst_b'] = np.tile(((t % 64) != 63) * 1.0, (128, 1))
    offs = {}
    o = 0
    arrs = []
    for k, v in c.items():
        offs[k] = (o, v.shape[1])
        o += v.shape[1]
        arrs.append(v.astype(np.float32))
    return np.ascontiguousarray(np.concatenate(arrs, axis=1)), offs


CONST_ARR, CONST_OFF = _consts()
CP = {}
_o = 0
for _n, _w in [('dn_conv', 48), ('lru_conv', 16), ('lru_cb', 4), ('lru_ba', 8), ('lru_bx', 8), ('lru_lam', 8),
               ('dn_nw', 1), ('hg_nw', 1), ('ln1_g', 8), ('ln1_b', 8), ('ln2_g', 8), ('ln2_b', 8),
               ('lb0', 4), ('lb1', 4), ('emb_g', 8), ('emb_b', 8), ('alog', 8), ('dtb', 8)]:
    CP[_n] = (_o, _w)
    _o += _w
NCP = _o


class View:
    def __init__(self, bufs, ap):
        self.bufs = bufs
        self.ap = ap

    def __getitem__(self, idx):
        return View(self.bufs, self.ap[idx])


class Buf:
    def __init__(self, t=None, name=""):
        self.t = t
        self.w = {}
        self.r = {}
        self.name = name

    def __getitem__(self, idx):
        return View([self], self.t[idx])

    def v(self, ap):
        return View([self], ap)


class Eng:
    def __init__(self, h, sid, name):
        self.h = h
        self.sid = sid
        self.cnt = 0
        self.waited = {}
        self.name = name
        self.old = []


class KB:
    NDS = 48
    EPOCH = 20000

    def __init__(self, nc, stack):
        self.nc = nc
        self.sems = {}
        self.nsid = 0

        def mk(n):
            s = stack.enter_context(nc.semaphore(n))
            sid = self.nsid
            self.nsid += 1
            self.sems[sid] = s
            return sid
        self.mk = mk
        self.PE = Eng(nc.tensor, mk("s_pe"), "pe")
        self.DVE = Eng(nc.vector, mk("s_dve"), "dve")
        self.ACT = Eng(nc.scalar, mk("s_act"), "act")
        self.POOL = Eng(nc.gpsimd, mk("s_pool"), "pool")
        self.SP = Eng(nc.sync, None, "sp")
        self.engs = [self.PE, self.DVE, self.ACT, self.POOL]
        self.dsid = [mk("s_dma%d" % i) for i in range(self.NDS)]
        self.dval = [0] * self.NDS
        self.dnext = 0
        self.dbufs = {}
        self.ninst = 0

    def db(self, name, blk):
        k = (name, blk)
        if k not in self.dbufs:
            self.dbufs[k] = Buf(None, "%s_%s" % (name, blk))
        return self.dbufs[k]

    def _wait(self, eng, sid, val):
        if val <= 0 or eng.waited.get(sid, 0) >= val:
            return
        eng.h.wait_ge(self.sems[sid], val)
        eng.waited[sid] = val

    def _sync(self, eng, reads, writes):
        own = eng.sid
        for b in reads:
            for sid, v in b.w.items():
                self._wait(eng, sid, v)
        for b in writes:
            for sid, v in b.w.items():
                if sid != own:
                    self._wait(eng, sid, v)
            for sid, v in b.r.items():
                if sid != own:
                    self._wait(eng, sid, v)

    def _mark(self, reads, writes, sid, val):
        for b in reads:
            b.r[sid] = max(b.r.get(sid, 0), val)
        for b in writes:
            b.w = {sid: val}
            b.r = {}

    def E(self, eng, meth, **kw):
        reads, writes, args = [], [], {}
        for k_, v in kw.items():
            if isinstance(v, View):
                if k_ in ('out', 'accum_out', 'ap') or any(getattr(b_, 'excl', False) for b_ in v.bufs):
                    writes.extend(v.bufs)
                else:
                    reads.extend(v.bufs)
                args[k_] = v.ap
            else:
                args[k_] = v
        if eng.cnt >= self.EPOCH:
            eng.old.append((eng.sid, eng.cnt))
            eng.sid = self.mk("s_%s_e%d" % (eng.name, len(eng.old)))
            eng.cnt = 0
        self._sync(eng, reads, writes)
        inst = getattr(eng.h, meth)(**args)
        inst.then_inc(self.sems[eng.sid], 1)
        eng.cnt += 1
        self.ninst += 1
        self._mark(reads, writes, eng.sid, eng.cnt)
        return inst

    def mm(self, out, lhsT, rhs, start=True, stop=True, extra_w=()):
        return self.E(self.PE, 'matmul', out=out, lhsT=lhsT, rhs=rhs, start=start, stop=stop)

    def tr(self, out, in_, ident):
        return self.E(self.PE, 'transpose', out=out, in_=in_, identity=ident)

    def dma(self, out, in_, q=None):
        eng = self.SP
        s = self.dnext
        self.dnext = (self.dnext + 1) % self.NDS
        sid = self.dsid[s]
        self._wait(eng, sid, self.dval[s])
        self._sync(eng, in_.bufs, out.bufs)
        self.dval[s] += 16
        eng.h.dma_start(out=out.ap, in_=in_.ap).then_inc(self.sems[sid], 16)
        self.ninst += 1
        self._mark(in_.bufs, out.bufs, sid, self.dval[s])

    def barrier(self):
        for e in self.engs + [self.SP]:
            for o in self.engs:
                if o is not e:
                    if o.cnt > 0:
                        self._wait(e, o.sid, o.cnt)
                    elif o.old:
                        self._wait(e, o.old[-1][0], o.old[-1][1])
            for s in range(self.NDS):
                self._wait(e, self.dsid[s], self.dval[s])


def bc_last(ap, n):
    l = [list(x) for x in ap.ap]
    return bass.AP(ap.tensor, ap.offset, l + [[0, n]])


def bc_mid(ap, n):
    l = [list(x) for x in ap.ap]
    return bass.AP(ap.tensor, ap.offset, [l[0], [0, n]] + l[1:])


def build(T, SEG, phases=None, debug_out=(), nlayers=DEPTH):
    NBLK = T // NB
    nc = bass.Bass("TRN2", target_bir_lowering=False)
    dt = nc.dram_tensor
    x_in = dt("x", [T, D], F32, kind="ExternalInput").ap()
    p_in = dt("p", [DEPTH, T, DPLE], F32, kind="ExternalInput").ap()
    cst_in = dt("cst", list(CONST_ARR.shape), F32, kind="ExternalInput").ap()
    cp_in = dt("cp", [DEPTH, 128, NCP], F32, kind="ExternalInput").ap()
    bd_in = dt("bd", [DEPTH, 128, 16, 128], F32, kind="ExternalInput").ap()
    flag_in = dt("flag", [128, 1], F32, kind="ExternalInput").ap()
    w_in = dt("w_in", [DEPTH, D, NIN], F32, kind="ExternalInput").ap()
    w_branch = dt("w_branch", [DEPTH, 3, 512, D], F32, kind="ExternalInput").ap()
    w_out = dt("w_out", [DEPTH, D, D], F32, kind="ExternalInput").ap()
    w_mlp1 = dt("w_mlp1", [DEPTH, D, DFF], F32, kind="ExternalInput").ap()
    w_mlp2 = dt("w_mlp2", [DEPTH, DFF, D], F32, kind="ExternalInput").ap()
    w_pg = dt("w_ple_gate", [DEPTH, D, D], F32, kind="ExternalInput").ap()
    w_pp = dt("w_ple_proj", [DEPTH, DPLE, D], F32, kind="ExternalInput").ap()
    y_out = dt("y", [T, D], F32, kind="ExternalOutput").ap()
    dbg = {}
    okind = lambda n: "ExternalOutput" if n in debug_out else "Internal"
    xT = dt("xT", [8, 128, T], F32, kind=okind("xT")).ap()
    class _HT:
        SPLIT = 36

        def __init__(self):
            self.a = dt("hT", [self.SPLIT, 128, T], F32, kind=okind("hT")).ap()
            self.b = dt("hTb", [NHT - self.SPLIT, 128, T], F32, kind=okind("hT")).ap()

        def __getitem__(self, idx):
            f = idx[0]
            rest = tuple(idx[1:])
            if isinstance(f, slice):
                if f.start < self.SPLIT:
                    assert f.stop <= self.SPLIT
                    return self.a[(f,) + rest]
                return self.b[(slice(f.start - self.SPLIT, f.stop - self.SPLIT),) + rest]
            if f < self.SPLIT:
                return self.a[(f,) + rest]
            return self.b[(f - self.SPLIT,) + rest]
    hT = _HT()
    ofT = dt("ofT", [12, 128, T], F32, kind=okind("ofT")).ap()
    obT = dt("obT", [12, 128, T], BF16, kind=okind("obT")).ap()
    wspec = {
        'win': (8, WIN_TILES, lambda l: w_in[l]),
        'wb0': (4, [(i * 128, 128) for i in range(8)], lambda l: w_branch[l, 0]),
        'wb1': (4, [(i * 128, 128) for i in range(8)], lambda l: w_branch[l, 1]),
        'wb2': (4, [(i * 128, 128) for i in range(8)], lambda l: w_branch[l, 2]),
        'wout': (8, [(i * 128, 128) for i in range(8)], lambda l: w_out[l]),
        'wm1': (8, [(i * 128, 128) for i in range(32)], lambda l: w_mlp1[l]),
        'wm2': (32, [(i * 128, 128) for i in range(8)], lambda l: w_mlp2[l]),
        'wpg': (8, [(i * 128, 128) for i in range(8)], lambda l: w_pg[l]),
        'wpp': (2, [(i * 128, 128) for i in range(8)], lambda l: w_pp[l]),
    }
    ws = {n: dt("ws_" + n, [DEPTH, len(s[1]), 128, s[0], 128], BF16, kind="Internal").ap() for n, s in wspec.items()}

    with ExitStack() as st0:
        kb = KB(nc, st0)
        PE, DVE, ACT, POOL = kb.PE, kb.DVE, kb.ACT, kb.POOL
        E = kb.E

        uniq = [0]

        def sb(st, name, shape, dtype=F32):
            uniq[0] += 1
            return Buf(st.enter_context(nc.sbuf_tensor("sb%d_%s" % (uniq[0], name), shape, dtype)), name)

        ps = [Buf(st0.enter_context(nc.psum_tensor("ps%d" % i, [128, 512], F32)), "ps%d" % i) for i in range(8)]
        for b_ in ps:
            b_.excl = True
        cst = sb(st0, "cst", list(CONST_ARR.shape))
        cpt = [sb(st0, "cp%d" % l, [128, NCP]) for l in range(DEPTH)]
        flag = sb(st0, "flag", [128, 1])
        cbf = sb(st0, "cbf", [128, 128 * 2 + 512 * 3 + 256 * 2], BF16)
        kb.dma(cst[:], View([], cst_in[:, :]))
        for l in range(DEPTH):
            kb.dma(cpt[l][:], View([], cp_in[l]))
        kb.dma(flag[:], View([], flag_in[:, :]))

        def C(name):
            o, w = CONST_OFF[name]
            return cst[:, o:o + w]

        cb_off = {}
        o = 0
        for n in ['ident', 'ones', 'sm_f4', 'sm_b4', 'ident4', 'hgm_f4', 'hgm_b4']:
            w = CONST_OFF[n][1]
            cb_off[n] = (o, w)
            E(DVE, 'tensor_copy', out=cbf[:, o:o + w], in_=C(n))
            o += w

        def CB(name):
            o, w = cb_off[name]
            return cbf[:, o:o + w]

        crn = ['ident', 'ones', 'negones', 'tri_le', 'tri_ge', 'tri_gt', 'tri_lt', 'nm_f', 'nm_b']
        crt = sb(st0, "crt", [128, 128 * len(crn)], F32R)
        for i_, n in enumerate(crn):
            E(DVE, 'tensor_copy', out=crt[:, i_ * 128:(i_ + 1) * 128], in_=C(n))

        def CR(name):
            i_ = crn.index(name)
            return crt[:, i_ * 128:(i_ + 1) * 128]

        def asf(v):
            return View(v.bufs, v.ap.bitcast(F32))

        def cpv(l, name, i=0, n=1):
            o, w = CP[name]
            return cpt[l][:, o + i:o + i + n]

        def phase_wprep():
            with ExitStack() as st:
                sf = [sb(st, "wpf%d" % i, [128, 4096]) for i in range(2)]
                sbf = [sb(st, "wpb%d" % i, [128, 4096], BF16) for i in range(2)]
                it = 0
                for l in range(DEPTH):
                    for name, (n_k, tiles, srcf) in wspec.items():
                        src = srcf(l).rearrange("(k p) n -> p k n", p=128)
                        gmax = 4 if n_k <= 8 else 1
                        ti = 0
                        while ti < len(tiles):
                            g = 1
                            while (g < gmax and ti + g < len(tiles) and tiles[ti + g][1] == 128 and tiles[ti][1] == 128
                                   and tiles[ti + g][0] == tiles[ti][0] + 128 * g):
                                g += 1
                            c0 = tiles[ti][0]
                            gw = sum(tiles[ti + j][1] for j in range(g))
                            s = it % 2
                            it += 1
                            fv = sf[s].t[:, 0:n_k * gw].rearrange("p (k c) -> p k c", k=n_k)
                            bv = sbf[s].t[:, 0:n_k * gw].rearrange("p (k c) -> p k c", k=n_k)
                            kb.dma(sf[s].v(fv), View([], src[:, :, c0:c0 + gw]))
                            E(POOL if it % 2 else ACT, 'tensor_copy' if it % 2 else 'copy', out=sbf[s].v(bv), in_=sf[s].v(fv))
                            for j in range(g):
                                wdt = tiles[ti + j][1]
                                kb.dma(View([kb.db('ws_' + name, l)], ws[name][l, ti + j, :, :, 0:wdt]),
                                       sbf[s].v(bv[:, :, j * 128:j * 128 + wdt]))
                            ti += g
            kb.barrier()

        wctr = [0]
        pctr = [0]

        def dense(wb, l, name, tile_ids, rhs_fn, N, evac, psbanks=(0, 1, 2)):
            n_k = wspec[name][0]
            tiles = wspec[name][1]
            tile_ids = list(tile_ids)
            depth = len(wb) - 1
            base = wctr[0]
            wctr[0] += len(tile_ids)

            def wload(i):
                kb.dma(wb[(base + i) % len(wb)][:, 0:n_k, :], View([kb.db('ws_' + name, l)], ws[name][l, tile_ids[i]]))
            for i in range(min(depth, len(tile_ids))):
                wload(i)
            for i, ti in enumerate(tile_ids):
                wdt = tiles[ti][1]
                s = (base + i) % len(wb)
                if i + depth < len(tile_ids):
                    wload(i + depth)
                pb = ps[psbanks[pctr[0] % len(psbanks)]]
                pctr[0] += 1
                for k in range(n_k):
                    kb.mm(pb[:, 0:N], wb[s][:, k, :], rhs_fn(k), start=(k == 0), stop=(k == n_k - 1))
                evac(ti, pb[0:wdt, 0:N], wdt)

        def layer_norm(st_bufs, src, g_fn, b_fn, dst_f=None, dst_b=None, N=NB):
            sq, xb_, mean, m2, rstd, tmp = st_bufs
            pm, pq = ps[3], ps[4]
            for c in range(8):
                E(ACT, 'activation', out=sq[c % 2][:, 0:N], in_=src[:, c, 0:N], func=AF.Square)
                E(DVE, 'tensor_copy', out=xb_[c % 2][:, 0:N], in_=src[:, c, 0:N])
                kb.mm(pm[:, 0:N], CB('ones'), xb_[c % 2][:, 0:N], start=(c == 0), stop=(c == 7))
                kb.mm(pq[:, 0:N], CB('ones'), sq[c % 2][:, 0:N], start=(c == 0), stop=(c == 7))
            E(ACT, 'activation', out=mean[:, 0:N], in_=pm[:, 0:N], func=AF.Copy, scale=1.0 / D)
            E(DVE, 'tensor_tensor', out=m2[:, 0:N], in0=mean[:, 0:N], in1=mean[:, 0:N], op=ALU.mult)
            E(DVE, 'scalar_tensor_tensor', out=m2[:, 0:N], in0=pq[:, 0:N], scalar=1.0 / D, in1=m2[:, 0:N],
              op0=ALU.mult, op1=ALU.subtract)
            E(DVE, 'tensor_scalar', out=m2[:, 0:N], in0=m2[:, 0:N], scalar1=0.0, scalar2=1e-5, op0=ALU.max, op1=ALU.add)
            E(ACT, 'activation', out=m2[:, 0:N], in_=m2[:, 0:N], func=AF.Ln)
            E(ACT, 'activation', out=rstd[:, 0:N], in_=m2[:, 0:N], func=AF.Exp, scale=-0.5)
            for c in range(8):
                t_ = tmp[c % 2]
                E(DVE, 'tensor_tensor', out=t_[:, 0:N], in0=src[:, c, 0:N], in1=mean[:, 0:N], op=ALU.subtract)
                E(DVE, 'tensor_tensor', out=t_[:, 0:N], in0=t_[:, 0:N], in1=rstd[:, 0:N], op=ALU.mult)
                if dst_f is not None:
                    E(ACT, 'activation', out=dst_f[:, c, 0:N], in_=t_[:, 0:N], func=AF.Identity, scale=g_fn(c), bias=b_fn(c))
                if dst_b is not None:
                    E(ACT, 'activation', out=dst_b[:, c, 0:N], in_=t_[:, 0:N], func=AF.Identity, scale=g_fn(c), bias=b_fn(c))

        def ln_bufs(st):
            return ([sb(st, "ln_sq%d" % i, [128, NB], BF16) for i in range(2)],
                    [sb(st, "ln_xb%d" % i, [128, NB], BF16) for i in range(2)],
                    sb(st, "ln_mean", [128, NB]), sb(st, "ln_m2", [128, NB]), sb(st, "ln_rstd", [128, NB]),
                    [sb(st, "ln_tmp%d" % i, [128, NB]) for i in range(2)])

        def phase_embed():
            with ExitStack() as st:
                xin = [sb(st, "e_xin%d" % i, [128, 4, D]) for i in range(2)]
                xf = sb(st, "e_xf", [128, 8, NB])
                xo = sb(st, "e_xo", [128, 8, NB])
                lb = ln_bufs(st)
                for b in range(NBLK):
                    t0 = b * NB
                    xi = xin[b % 2]
                    kb.dma(xi[:], View([], x_in[t0:t0 + NB, :].rearrange("(j p) d -> p j d", p=128)))
                    for c in range(8):
                        pb = ps[c % 3]
                        for j in range(4):
                            kb.tr(pb[:, j * 128:(j + 1) * 128], xi[:, j, c * 128:(c + 1) * 128], C('ident'))
                        E(ACT if c % 2 else DVE, 'copy' if c % 2 else 'tensor_copy', out=xf[:, c, :], in_=pb[:, :])
                    layer_norm(lb, xf, lambda c: cpv(0, 'emb_g', c), lambda c: cpv(0, 'emb_b', c), dst_f=xo)
                    kb.dma(View([kb.db('xT', b)], xT[:, :, t0:t0 + NB].rearrange("c p t -> p c t")), xo[:])
            kb.barrier()

        def phase_proj(l):
            with ExitStack() as st:
                xin = [sb(st, "p_xin%d" % i, [128, 8, NB]) for i in range(2)]
                xb_ = [sb(st, "p_xb%d" % i, [128, 8, NB], BF16) for i in range(2)]
                stg = [sb(st, "p_stg%d" % i, [128, NB]) for i in range(6)]
                wb = [sb(st, "p_wb%d" % i, [128, 8, 128], BF16) for i in range(4)]
                sctr = [0]
                kb.dma(xin[0][:], View([kb.db('xT', 0)], xT[:, :, 0:NB].rearrange("c p t -> p c t")))
                for b in range(NBLK):
                    t0 = b * NB
                    xi, xbb = xin[b % 2], xb_[b % 2]
                    if b + 1 < NBLK:
                        kb.dma(xin[(b + 1) % 2][:], View([kb.db('xT', b + 1)], xT[:, :, t0 + NB:t0 + 2 * NB].rearrange("c p t -> p c t")))
                    E(ACT, 'copy', out=xbb[:, 0:4, :], in_=xi[:, 0:4, :])
                    E(DVE, 'tensor_copy', out=xbb[:, 4:8, :], in_=xi[:, 4:8, :])

                    def evac(ti, pv, wdt):
                        sg_ = stg[sctr[0] % len(stg)]
                        sctr[0] += 1
                        if GA <= ti < AB:
                            E(ACT, 'activation', out=sg_[0:wdt, :], in_=pv, func=AF.Sigmoid)
                        elif sctr[0] % 2:
                            E(DVE, 'tensor_copy', out=sg_[0:wdt, :], in_=pv)
                        else:
                            E(ACT, 'copy', out=sg_[0:wdt, :], in_=pv)
                        kb.dma(View([kb.db('hT%d' % ti, b)], hT[ti, 0:wdt, t0:t0 + NB]), sg_[0:wdt, :])
                    dense(wb, l, 'win', range(NHT), lambda k: xbb[:, k, :], NB, evac)
            kb.barrier()

        def phase_post(l, last):
            with ExitStack() as st:
                xres = sb(st, "q_xres", [128, 8, NB])
                hid = sb(st, "q_hid", [128, 32, NB], BF16)
                sig = sb(st, "q_sig", [128, 8, NB])
                acc = sb(st, "q_acc", [128, 8 * NB])
                accb = sb(st, "q_accb", [128, 8, NB], BF16)
                x1 = sb(st, "q_x1", [128, 8, NB])
                x1b = sb(st, "q_x1b", [128, 8, NB], BF16)
                rl = [sb(st, "q_rl%d" % i, [128, NB]) for i in range(2)]
                pin = sb(st, "q_pin", [128, 4, DPLE])
                pT = sb(st, "q_pT", [128, 2, NB], BF16)
                wb = [sb(st, "q_wb%d" % i, [128, 32, 128], BF16) for i in range(3)]
                lb = ln_bufs(st)
                acc3 = acc.v(acc.t[:, :].rearrange("p (c t) -> p c t", c=8))
                accv = lambda c: acc.v(acc.t[:, c * NB:(c + 1) * NB])
                for b in range(NBLK):
                    t0 = b * NB
                    kb.dma(xres[:], View([kb.db('xT', b)], xT[:, :, t0:t0 + NB].rearrange("c p t -> p c t")))
                    kb.dma(hid[:, 0:12, :], View([kb.db('obT', b)], obT[:, :, t0:t0 + NB].rearrange("c p t -> p c t")))
                    kb.dma(pin[:], View([], p_in[l, t0:t0 + NB, :].rearrange("(j p) d -> p j d", p=128)))
                    for c in range(2):
                        pb = ps[5 + c]
                        for j in range(4):
                            kb.tr(pb[:, j * 128:(j + 1) * 128], pin[:, j, c * 128:(c + 1) * 128], C('ident'))
                        E(ACT, 'copy', out=pT[:, c, :], in_=pb[:, :])
                    for n in range(3):
                        gt0 = [GA, GB, GC][n]
                        kb.dma(sig[:], View([kb.db('hT%d' % (gt0 + c), b) for c in range(8)],
                                            hT[gt0:gt0 + 8, :, t0:t0 + NB].rearrange("c p t -> p c t")))

                        def evac(ti, pv, wdt, n=n):
                            if n == 0:
                                E(DVE, 'tensor_tensor', out=accv(ti), in0=pv, in1=sig[:, ti, :], op=ALU.mult)
                            else:
                                r_ = rl[ti % 2]
                                E(DVE, 'tensor_tensor', out=r_[:, :], in0=pv, in1=sig[:, ti, :], op=ALU.mult)
                                if n == 1:
                                    E(POOL, 'tensor_tensor', out=accv(ti), in0=accv(ti), in1=r_[:, :], op=ALU.add)
                                else:
                                    E(POOL, 'tensor_tensor', out=accb[:, ti, :], in0=accv(ti), in1=r_[:, :], op=ALU.add)
                        dense(wb, l, 'wb%d' % n, range(8), lambda k, n=n: hid[:, n * 4 + k, :], NB, evac)

                    def evac(ti, pv, wdt):
                        E(DVE, 'scalar_tensor_tensor', out=accv(ti), in0=xres[:, ti, :], scalar=ALPHA, in1=pv,
                          op0=ALU.mult, op1=ALU.add)
                    dense(wb, l, 'wout', range(8), lambda k: accb[:, k, :], NB, evac)
                    layer_norm(lb, acc3, lambda c: cpv(l, 'ln1_g', c), lambda c: cpv(l, 'ln1_b', c), dst_f=x1, dst_b=x1b)

                    def evac(ti, pv, wdt):
                        r_ = rl[ti % 2]
                        E(ACT, 'activation', out=r_[:, :], in_=pv, func=AF.Relu)
                        E(POOL if ti % 2 else DVE, 'tensor_tensor', out=hid[:, ti, :], in0=r_[:, :], in1=r_[:, :], op=ALU.mult)
                    dense(wb, l, 'wm1', range(32), lambda k: x1b[:, k, :], NB, evac)

                    def evac(ti, pv, wdt):
                        E(ACT, 'activation', out=sig[:, ti, :], in_=pv, func=AF.Sigmoid)
                    dense(wb, l, 'wpg', range(8), lambda k: x1b[:, k, :], NB, evac)

                    def evac(ti, pv, wdt):
                        E(DVE, 'tensor_tensor', out=sig[:, ti, :], in0=pv, in1=sig[:, ti, :], op=ALU.mult)
                    dense(wb, l, 'wpp', range(8), lambda k: pT[:, k, :], NB, evac)

                    def evac(ti, pv, wdt):
                        E(DVE, 'scalar_tensor_tensor', out=accv(ti), in0=x1[:, ti, :], scalar=ALPHA, in1=pv,
                          op0=ALU.mult, op1=ALU.add)
                        E(POOL, 'tensor_tensor', out=accv(ti), in0=accv(ti), in1=sig[:, ti, :], op=ALU.add)
                    dense(wb, l, 'wm2', range(8), lambda k: hid[:, k, :], NB, evac)
                    layer_norm(lb, acc3, lambda c: cpv(l, 'ln2_g', c), lambda c: cpv(l, 'ln2_b', c), dst_f=xres)
                    if not last:
                        kb.dma(View([kb.db('xT', b)], xT[:, :, t0:t0 + NB].rearrange("c p t -> p c t")), xres[:])
                    else:
                        ov = x1.v(x1.t[:, :, :].rearrange("p c t -> p (c t)").rearrange("p (j d) -> p j d", j=4))
                        for j in range(4):
                            for c2 in range(2):
                                pb = ps[5 + c2]
                                for cc in range(4):
                                    c = c2 * 4 + cc
                                    kb.tr(pb[:, cc * 128:(cc + 1) * 128], xres[:, c, j * 128:(j + 1) * 128], C('ident'))
                                E(ACT if c2 else DVE, 'copy' if c2 else 'tensor_copy',
                                  out=x1.v(ov.ap[:, j, c2 * 512:(c2 + 1) * 512]), in_=pb[:, :])
                        kb.dma(View([], y_out[t0:t0 + NB, :].rearrange("(j p) d -> p j d", p=128)), ov)
            kb.barrier()

        SEGB = SEG // NB

        def blk_order(d):
            return list(range(NBLK)) if d == 0 else list(range(NBLK - 1, -1, -1))

        def load_halo(raw, tile0, nt, b):
            t0 = b * NB
            lo, hi = max(t0 - 1, 0), min(t0 + NB + 2, T)
            if lo > t0 - 1:
                E(POOL, 'memset', ap=raw[:, :, 0:1], constant=0.0)
            if hi < t0 + NB + 2:
                E(POOL, 'memset', ap=raw[:, :, NB + 1:NB + 3], constant=0.0)
            bl = [kb.db('hT%d' % (tile0 + c), bb) for c in range(nt) for bb in (b - 1, b, b + 1) if 0 <= bb < NBLK]
            kb.dma(raw[:, :, lo - (t0 - 1):hi - (t0 - 1)], View(bl, hT[tile0:tile0 + nt, :, lo:hi].rearrange("c p t -> p c t")))
            if b % SEGB == 0 and b > 0:
                E(DVE, 'tensor_scalar', out=raw[:, :, 0:1], in0=raw[:, :, 0:1], scalar1=flag[:, 0:1], scalar2=None, op0=ALU.mult)
            if b % SEGB == SEGB - 1 and b < NBLK - 1:
                E(DVE, 'tensor_scalar', out=raw[:, :, NB + 1:NB + 3], in0=raw[:, :, NB + 1:NB + 3], scalar1=flag[:, 0:1],
                  scalar2=None, op0=ALU.mult)

        def conv4(eng, out, raw, t, wcol, bias=None):
            if bias is None:
                E(eng, 'tensor_scalar', out=out, in0=raw[:, t, 0:NB], scalar1=wcol(0), scalar2=0.0, op0=ALU.mult, op1=ALU.add)
            else:
                E(eng, 'tensor_scalar', out=out, in0=raw[:, t, 0:NB], scalar1=wcol(0), scalar2=bias, op0=ALU.mult, op1=ALU.add)
            for k in range(1, 4):
                E(DVE, 'scalar_tensor_tensor', out=out, in0=raw[:, t, k:k + NB], scalar=wcol(k), in1=out, op0=ALU.mult, op1=ALU.add)

        def crosses(b, d):
            return (d == 0 and b > 0 and b % SEGB == 0) or (d == 1 and b < NBLK - 1 and b % SEGB == SEGB - 1)

        def phase_lru(l):
            with ExitStack() as st:
                bdf = sb(st, "l_bdf", [128, 16, 128])
                bdb = sb(st, "l_bdb", [128, 16, 128], BF16)
                ccol = sb(st, "l_ccol", [128, 8])
                raw = [sb(st, "l_raw%d" % i, [128, 4, NB + 3]) for i in range(2)]
                xc = sb(st, "l_xc", [128, 4, NB])
                xcb = sb(st, "l_xcb", [128, 4, NB], BF16)
                r_ = [sb(st, "l_r%d" % i, [128, NB]) for i in range(2)]
                i_ = [sb(st, "l_i%d" % i, [128, NB]) for i in range(2)]
                a_ = [sb(st, "l_a%d" % i, [128, NB]) for i in range(2)]
                u_ = [sb(st, "l_u%d" % i, [128, NB]) for i in range(2)]
                hh = [sb(st, "l_h%d" % i, [128, 4, NB]) for i in range(2)]
                hf = sb(st, "l_hf", [128, 4, NB])
                cg = sb(st, "l_cg", [128, 4, NB])
                ob = [sb(st, "l_ob%d" % i, [128, 4, NB], BF16) for i in range(2)]
                carry = sb(st, "l_carry", [128, 4])
                kb.dma(bdf[:], View([], bd_in[l]))
                E(DVE, 'tensor_copy', out=bdb[:], in_=bdf[:])
                o_, w_ = CP['lru_lam']
                E(ACT, 'activation', out=ccol[:], in_=cpt[l][:, o_:o_ + 8], func=AF.Sigmoid)
                E(ACT, 'activation', out=ccol[:], in_=ccol[:], func=AF.Ln)
                E(DVE, 'tensor_scalar', out=ccol[:], in0=ccol[:], scalar1=8.0, scalar2=None, op0=ALU.mult)
                for d in range(2):
                    for bi, b in enumerate(blk_order(d)):
                        t0 = b * NB
                        rw = raw[bi % 2]
                        h = hh[bi % 2]
                        load_halo(rw, CX, 4, b)
                        if d == 1:
                            kb.dma(hf[:], View([kb.db('ofT%d' % (8 + c), b) for c in range(4)],
                                               ofT[8:12, :, t0:t0 + NB].rearrange("c p t -> p c t")))
                            kb.dma(cg[:], View([kb.db('hT%d' % (CG + c), b) for c in range(4)],
                                               hT[CG:CG + 4, :, t0:t0 + NB].rearrange("c p t -> p c t")))
                        if bi == 0:
                            E(DVE, 'memset', ap=carry[:], constant=0.0)
                        elif crosses(b, d):
                            E(DVE, 'tensor_scalar', out=carry[:], in0=carry[:], scalar1=flag[:, 0:1], scalar2=None, op0=ALU.mult)
                        for t in range(4):
                            conv4(POOL, xc[:, t, :], rw, t, lambda k: cpv(l, 'lru_conv', t * 4 + k), bias=cpv(l, 'lru_cb', t))
                            E(ACT, 'copy', out=xcb[:, t, :], in_=xc[:, t, :])
                            pa, px = ps[(2 * t) % 4], ps[(2 * t + 1) % 4]
                            kb.mm(pa[:, :], bdb[:, d * 4 + t, :], xcb[:, t, :])
                            kb.mm(px[:, :], bdb[:, 8 + d * 4 + t, :], xcb[:, t, :])
                            rr, ii, aa, uu = r_[t % 2], i_[t % 2], a_[t % 2], u_[t % 2]
                            E(ACT, 'activation', out=rr[:, :], in_=pa[:, :], func=AF.Sigmoid, bias=cpv(l, 'lru_ba', d * 4 + t))
                            E(ACT, 'activation', out=ii[:, :], in_=px[:, :], func=AF.Sigmoid, bias=cpv(l, 'lru_bx', d * 4 + t))
                            E(ACT, 'activation', out=aa[:, :], in_=rr[:, :], func=AF.Exp, scale=ccol[:, d * 4 + t:d * 4 + t + 1])
                            E(POOL, 'tensor_tensor', out=rr[:, :], in0=aa[:, :], in1=aa[:, :], op=ALU.mult)
                            E(DVE, 'tensor_scalar', out=rr[:, :], in0=rr[:, :], scalar1=-1.0, scalar2=1.0, op0=ALU.mult, op1=ALU.add)
                            E(ACT, 'activation', out=rr[:, :], in_=rr[:, :], func=AF.Sqrt)
                            E(POOL, 'tensor_tensor', out=ii[:, :], in0=ii[:, :], in1=xc[:, t, :], op=ALU.mult)
                            E(DVE, 'tensor_tensor', out=uu[:, :], in0=rr[:, :], in1=ii[:, :], op=ALU.mult)
                            if d == 0:
                                E(DVE, 'tensor_tensor_scan', out=h[:, t, :], data0=aa[:, :], data1=uu[:, :],
                                  initial=carry[:, t:t + 1], op0=ALU.mult, op1=ALU.add)
                            else:
                                E(DVE, 'tensor_tensor_scan', out=h[:, t, ::-1], data0=aa[:, ::-1], data1=uu[:, ::-1],
                                  initial=carry[:, t:t + 1], op0=ALU.mult, op1=ALU.add)
                        E(DVE, 'tensor_copy', out=carry[:], in_=h[:, :, NB - 1] if d == 0 else h[:, :, 0])
                        if d == 0:
                            kb.dma(View([kb.db('ofT%d' % (8 + c), b) for c in range(4)],
                                        ofT[8:12, :, t0:t0 + NB].rearrange("c p t -> p c t")), h[:])
                        else:
                            o2 = ob[bi % 2]
                            E(ACT, 'activation', out=cg[:], in_=cg[:], func=AF.Gelu_apprx_tanh)
                            E(POOL, 'tensor_tensor', out=hf[:], in0=hf[:], in1=h[:], op=ALU.add)
                            E(DVE, 'tensor_tensor', out=o2[:], in0=hf[:], in1=cg[:], op=ALU.mult)
                            kb.dma(View([kb.db('obT', b)], obT[8:12, :, t0:t0 + NB].rearrange("c p t -> p c t")), o2[:])
            kb.barrier()

        def phase_hg(l):
            with ExitStack() as st:
                lbc = sb(st, "g_lbc", [128, 4])
                omlb = sb(st, "g_omlb", [128, 4])
                nomlb = sb(st, "g_nomlb", [128, 4])
                qT = [sb(st, "g_qT%d" % i, [128, 4, NB]) for i in range(2)]
                fT = [sb(st, "g_fT%d" % i, [128, 4, NB]) for i in range(2)]
                vT = [sb(st, "g_vT%d" % i, [128, 4, NB]) for i in range(2)]
                s_ = [sb(st, "g_s%d" % i, [128, NB]) for i in range(2)]
                f_ = [sb(st, "g_f%d" % i, [128, NB]) for i in range(2)]
                kk = [sb(st, "g_kk%d" % i, [128, NB]) for i in range(2)]
                bq = [sb(st, "g_bq%d" % i, [128, NB]) for i in range(2)]
                dq_ = [sb(st, "g_dq%d" % i, [128, NB]) for i in range(2)]
                e1 = [sb(st, "g_e1%d" % i, [128, NB]) for i in range(2)]
                e2 = [sb(st, "g_e2%d" % i, [128, NB]) for i in range(2)]
                qt = sb(st, "g_qt", [128, 4, NB], BF16)
                kdT = sb(st, "g_kdT", [128, 4, NB], BF16)
                ebl = sb(st, "g_ebl", [128, 4, 8])
                vtok = sb(st, "g_vtok", [128, 4, 512], BF16)
                ktok = sb(st, "g_ktok", [128, 4, 512], BF16)
                vTb = sb(st, "g_vTb", [128, 4, NB], BF16)
                AT = [sb(st, "g_AT%d" % i, [128, 4, 64], BF16) for i in range(2)]
                S = sb(st, "g_S", [128, 4, 128])
                Sp = sb(st, "g_Sp", [128, 4, 128])
                Spb = sb(st, "g_Spb", [128, 4, 128], BF16)
                ost = [sb(st, "g_ost%d" % i, [128, 4, NB]) for i in range(2)]
                of_ = sb(st, "g_of", [128, 4, NB])
                gz = sb(st, "g_gz", [128, 4, NB])
                oh = [sb(st, "g_oh%d" % i, [128, NB]) for i in range(2)]
                sqb = [sb(st, "g_sqb%d" % i, [128, NB], BF16) for i in range(2)]
                sd = [sb(st, "g_sd%d" % i, [128, NB]) for i in range(2)]
                obs = [sb(st, "g_obs%d" % i, [128, 4, NB], BF16) for i in range(2)]
                if l == 0:
                    E(DVE, 'memset', ap=lbc[:], constant=0.0)
                else:
                    o0, o1 = CP['lb0'][0], CP['lb1'][0]
                    E(DVE, 'tensor_tensor', out=lbc[:], in0=cpt[l][:, o1:o1 + 4], in1=cpt[l][:, o0:o0 + 4], op=ALU.subtract)
                    E(ACT, 'activation', out=lbc[:], in_=lbc[:], func=AF.Sigmoid)
                E(DVE, 'tensor_scalar', out=omlb[:], in0=lbc[:], scalar1=-1.0, scalar2=1.0, op0=ALU.mult, op1=ALU.add)
                E(DVE, 'tensor_scalar', out=nomlb[:], in0=omlb[:], scalar1=-1.0, scalar2=None, op0=ALU.mult)
                pso = [ps[0], ps[1], ps[2], ps[3]]
                psA, psS, psT, psN = ps[4], ps[5], ps[6], ps[7]
                psTb = psT.v(psT.t[:, :].bitcast(BF16))
                for d in range(2):
                    HF = HFF if d == 0 else HFB
                    mask4 = CB('hgm_f4') if d == 0 else CB('hgm_b4')
                    for bi, b in enumerate(blk_order(d)):
                        t0 = b * NB
                        q_, fz, v_ = qT[bi % 2], fT[bi % 2], vT[bi % 2]
                        for (dst, tl) in ((q_, HQ), (fz, HF), (v_, HI)):
                            kb.dma(dst[:], View([kb.db('hT%d' % (tl + c), b) for c in range(4)],
                                                hT[tl:tl + 4, :, t0:t0 + NB].rearrange("c p t -> p c t")))
                        if d == 1:
                            kb.dma(of_[:], View([kb.db('ofT%d' % (4 + c), b) for c in range(4)],
                                                ofT[4:8, :, t0:t0 + NB].rearrange("c p t -> p c t")))
                            kb.dma(gz[:], View([kb.db('hT%d' % (HGT + c), b) for c in range(4)],
                                               hT[HGT:HGT + 4, :, t0:t0 + NB].rearrange("c p t -> p c t")))
                        if bi == 0:
                            E(DVE, 'memset', ap=S[:], constant=0.0)
                        elif crosses(b, d):
                            E(DVE, 'tensor_scalar', out=S[:], in0=S[:], scalar1=flag[:, 0:1], scalar2=None, op0=ALU.mult)
                        E(ACT, 'copy', out=vTb[:], in_=v_[:])
                        for j in range(4):
                            for h in range(4):
                                kb.tr(psTb.bufs[0].v(psTb.ap[:, h * 128:(h + 1) * 128]), vTb[:, h, j * 128:(j + 1) * 128], CB('ident'))
                            E(ACT, 'copy', out=vtok[:, j, :], in_=psT.v(psTb.ap[:, 0:512]))
                        for h in range(4):
                            ss, ff, k2, bb, dd, x1_, x2_ = s_[h % 2], f_[h % 2], kk[h % 2], bq[h % 2], dq_[h % 2], e1[h % 2], e2[h % 2]
                            E(ACT, 'activation', out=ss[:, :], in_=fz[:, h, :], func=AF.Sigmoid)
                            E(DVE, 'tensor_scalar', out=ff[:, :], in0=ss[:, :], scalar1=omlb[:, h:h + 1], scalar2=lbc[:, h:h + 1],
                              op0=ALU.mult, op1=ALU.add)
                            E(ACT, 'activation', out=ff[:, :], in_=ff[:, :], func=AF.Ln)
                            E(POOL, 'tensor_scalar', out=k2[:, :], in0=ss[:, :], scalar1=nomlb[:, h:h + 1], scalar2=omlb[:, h:h + 1],
                              op0=ALU.mult, op1=ALU.add)
                            if d == 0:
                                E(DVE, 'tensor_tensor_scan', out=bb[:, :], data0=C('rst_f'), data1=ff[:, :], initial=0.0,
                                  op0=ALU.mult, op1=ALU.add)
                            else:
                                rb = C('rst_b')
                                E(DVE, 'tensor_tensor_scan', out=bb[:, ::-1], data0=View(rb.bufs, rb.ap[:, ::-1]), data1=ff[:, ::-1],
                                  initial=0.0, op0=ALU.mult, op1=ALU.add)
                            b3 = bb.t[:, :].rearrange("p (c j) -> p c j", j=64)
                            blv = b3[:, :, 63] if d == 0 else b3[:, :, 0]
                            E(ACT, 'activation', out=ebl[:, h, :], in_=bb.v(blv), func=AF.Exp)
                            E(DVE, 'tensor_tensor', out=dd.v(dd.t[:, :].rearrange("p (c j) -> p c j", j=64)), in0=bb.v(b3),
                              in1=bb.v(bc_last(blv, 64)), op=ALU.subtract)
                            E(ACT, 'activation', out=x1_[:, :], in_=dd[:, :], func=AF.Exp)
                            E(ACT, 'activation', out=x2_[:, :], in_=dd[:, :], func=AF.Exp, scale=-1.0)
                            E(POOL, 'tensor_tensor', out=qt[:, h, :], in0=q_[:, h, :], in1=x1_[:, :], op=ALU.mult)
                            E(DVE, 'tensor_tensor', out=kdT[:, h, :], in0=k2[:, :], in1=x2_[:, :], op=ALU.mult)
                        for j in range(4):
                            for h in range(4):
                                kb.tr(psTb.bufs[0].v(psTb.ap[:, h * 128:(h + 1) * 128]), kdT[:, h, j * 128:(j + 1) * 128], CB('ident'))
                            E(ACT, 'copy', out=ktok[:, j, :], in_=psT.v(psTb.ap[:, 0:512]))
                        jl = range(4) if d == 0 else range(3, -1, -1)
                        for j in jl:
                            at = AT[j % 2]
                            for hfh in range(2):
                                c = 2 * j + hfh
                                for h in range(4):
                                    kb.mm(psA[hfh * 64:(hfh + 1) * 64, h * 64:(h + 1) * 64], kdT[:, h, c * 64:(c + 1) * 64],
                                          qt[:, h, c * 64:(c + 1) * 64])
                            E(DVE, 'tensor_tensor', out=at.v(at.t[:, :, :].rearrange("p h i -> p (h i)")), in0=psA[:, 0:256], in1=mask4, op=ALU.mult)
                            for hfh in (range(2) if d == 0 else range(1, -1, -1)):
                                c = 2 * j + hfh
                                p0 = hfh * 64
                                eb = ebl.v(bc_last(ebl.t[:, :, c], 128))
                                E(DVE, 'tensor_tensor', out=Spb[:], in0=S[:], in1=eb, op=ALU.mult)
                                E(POOL, 'tensor_tensor', out=Sp[:], in0=S[:], in1=eb, op=ALU.mult)
                                for h in range(4):
                                    kb.mm(pso[h][:, c * 64:(c + 1) * 64], Spb[:, h, :], qt[:, h, c * 64:(c + 1) * 64], start=True, stop=False)
                                    kb.mm(pso[h][:, c * 64:(c + 1) * 64], vtok[p0:p0 + 64, j, h * 128:(h + 1) * 128], at[p0:p0 + 64, h, :],
                                          start=False, stop=True)
                                for h in range(4):
                                    kb.mm(psS[:, h * 128:(h + 1) * 128], ktok[p0:p0 + 64, j, h * 128:(h + 1) * 128],
                                          vtok[p0:p0 + 64, j, h * 128:(h + 1) * 128])
                                E(DVE, 'tensor_tensor', out=S.v(S.t[:, :, :].rearrange("p h d -> p (h d)")),
                                  in0=Sp.v(Sp.t[:, :, :].rearrange("p h d -> p (h d)")), in1=psS[:, :], op=ALU.add)
                        if d == 0:
                            o1 = ost[bi % 2]
                            for h in range(4):
                                E(ACT, 'copy', out=o1[:, h, :], in_=pso[h][:, :])
                            kb.dma(View([kb.db('ofT%d' % (4 + c), b) for c in range(4)],
                                        ofT[4:8, :, t0:t0 + NB].rearrange("c p t -> p c t")), o1[:])
                        else:
                            o2 = obs[bi % 2]
                            E(ACT, 'activation', out=gz[:], in_=gz[:], func=AF.Silu)
                            for h in range(4):
                                oo, sq_, sd_ = oh[h % 2], sqb[h % 2], sd[h % 2]
                                E(DVE, 'tensor_tensor', out=oo[:, :], in0=pso[h][:, :], in1=of_[:, h, :], op=ALU.add)
                                E(ACT, 'activation', out=sq_[:, :], in_=oo[:, :], func=AF.Square)
                                kb.mm(psN[:, :], CB('ones'), sq_[:, :])
                                E(DVE, 'tensor_scalar', out=sd_[:, :], in0=psN[:, :], scalar1=1.0 / 128, scalar2=1e-6, op0=ALU.mult, op1=ALU.add)
                                E(ACT, 'activation', out=sd_[:, :], in_=sd_[:, :], func=AF.Ln)
                                E(ACT, 'activation', out=sd_[:, :], in_=sd_[:, :], func=AF.Exp, scale=-0.5)
                                E(POOL, 'tensor_tensor', out=oo[:, :], in0=oo[:, :], in1=sd_[:, :], op=ALU.mult)
                                E(DVE, 'scalar_tensor_tensor', out=o2[:, h, :], in0=oo[:, :], scalar=cpv(l, 'hg_nw'), in1=gz[:, h, :],
                                  op0=ALU.mult, op1=ALU.mult)
                            kb.dma(View([kb.db('obT', b)], obT[4:8, :, t0:t0 + NB].rearrange("c p t -> p c t")), o2[:])
            kb.barrier()

        class ColBuf:
            def __init__(self, bank, c0, c1, name=""):
                self.bank = bank
                self.t = bank.t
                self.c0, self.c1 = c0, c1

            def cols(self, a=None, b=None, rows=slice(None)):
                a = 0 if a is None else a
                b = (self.c1 - self.c0) if b is None else b
                return View([self.bank], self.t[rows, self.c0 + a:self.c0 + b])

            def bf(self, a, b):
                return View([self.bank], self.t[:, :].bitcast(BF16)[:, 2 * self.c0 + a:2 * self.c0 + b])

        def rr(gens):
            gens = list(gens)
            while gens:
                for g in list(gens):
                    try:
                        next(g)
                    except StopIteration:
                        gens.remove(g)

        NCH = 2
        NHC = 4 // NCH

        def phase_dn(l):
            with ExitStack() as st:
                raw = sb(st, "d_raw", [128, 12, NB + 3])
                cv = sb(st, "d_cv", [128, 12, NB])
                abT = sb(st, "d_abT", [16, NB])
                qn = sb(st, "d_qn", [128, 4, NB], BF16)
                kn = sb(st, "d_kn", [128, 4, NB], BF16)
                vTb = sb(st, "d_vTb", [128, 4, NB], BF16)
                sqb = [sb(st, "d_sqb%d" % i, [128, NB], BF16) for i in range(2)]
                sd = [sb(st, "d_sd%d" % i, [128, NB]) for i in range(2)]
                negA = sb(st, "d_negA", [128, 8])
                gt = sb(st, "d_gt", [128, 4, 16])
                g_ = sb(st, "d_g", [128, 4, 4])
                gR = sb(st, "d_gR", [128, 4, 4], F32R)
                beta = sb(st, "d_beta", [128, 4, 4])
                nbeta = sb(st, "d_nbeta", [128, 4, 4])
                of_ = sb(st, "d_of", [128, 4, NB])
                gz = sb(st, "d_gz", [128, 4, NB])
                oh = [sb(st, "d_oh%d" % i, [128, NB]) for i in range(2)]
                obs = sb(st, "d_obs", [128, 4, NB], BF16)

                class CH:
                    pass
                chs = []
                for c in range(NCH):
                    ch = CH()
                    ch.c = c
                    ch.heads = list(range(c * NHC, (c + 1) * NHC))
                    n3 = [128, NHC, 128]
                    mk_ = lambda nm, dt_=F32, c=c, n3=n3: sb(st, "d%d_%s" % (c, nm), n3, dt_)
                    ch.Mh = mk_("Mh", F32R)
                    ch.Ecol = sb(st, "d%d_Ecol" % c, [128, 3, NHC])
                    ch.sc1 = sb(st, "d%d_sc1" % c, [128, NHC])
                    ch.Dc, ch.Dm, ch.attn, ch.AT = mk_("Dc", BF16), mk_("Dm", BF16), mk_("attn", BF16), mk_("AT", BF16)
                    ch.Nm, ch.NT = mk_("Nm", F32R), mk_("NT", F32R)
                    ch.Pb = [mk_("P%d" % i, F32R) for i in range(2)]
                    ch.PTb = [mk_("PT%d" % i, F32R) for i in range(2)]
                    ch.Rb = [mk_("R%d" % i, F32R) for i in range(2)]
                    ch.kbg, ch.vb, ch.nwT = mk_("kbg", F32R), mk_("vb", F32R), mk_("nwT", F32R)
                    ch.kd, ch.qd, ch.vnew = mk_("kd", BF16), mk_("qd", BF16), mk_("vnew", BF16)
                    ch.EG, ch.Stmp = mk_("EG"), mk_("Stmp")
                    ch.S, ch.Sb = mk_("S", F32R), mk_("Sb", BF16)
                    ch.ost = sb(st, "d%d_ost" % c, [128, NHC, NB])
                    W_ = NHC * 128
                    assert NCH == 2 and W_ == 256
                    B0, B1, B2, B3 = [ps[4 * c + i] for i in range(4)]
                    ch.psD = ColBuf(B0, 0, 256)
                    ch.psPT = ColBuf(B0, 0, 256)
                    ch.psGc = ColBuf(B0, 256, 272)
                    ch.psAT = ColBuf(B0, 272, 400)
                    ch.psE = ColBuf(B1, 0, 256)
                    ch.psKV = ColBuf(B1, 256, 512)
                    ch.psG = ColBuf(B2, 0, 256)
                    ch.psA = ColBuf(B2, 256, 512)
                    ch.psT = ColBuf(B3, 0, 256)
                    ch.psS = ColBuf(B3, 256, 512)
                    chs.append(ch)
                psGT = ColBuf(ps[0], 400, 464)
                psNv = ps[7][:, :]
                flat = lambda bf: bf.v(bf.t[:, :, :].rearrange("p h i -> p (h i)"))
                o_ = CP['alog'][0]
                E(ACT, 'activation', out=negA[:], in_=cpt[l][:, o_:o_ + 8], func=AF.Exp)
                E(DVE, 'tensor_scalar', out=negA[:], in0=negA[:], scalar1=-1.0, scalar2=None, op0=ALU.mult)

                def chain(ch, d, order, TRI, TRIR, TRICR, NMR, SM):
                    hs_ = ch.heads
                    h0 = hs_[0]
                    W_ = NHC * 128
                    for j in order:
                        cs = slice(j * 128, (j + 1) * 128)
                        gj = gR[:, j, h0:h0 + NHC]
                        gjf = asf(gj)
                        E(DVE, 'tensor_tensor', out=ch.Mh[:], in0=View(TRI.bufs, bc_mid(TRI.ap, NHC)),
                          in1=View(gjf.bufs, bc_last(gjf.ap, 128)), op=ALU.mult)
                        yield
                        for hi in range(NHC):
                            o2 = ch.psD.cols(hi * 128, (hi + 1) * 128)
                            kb.mm(o2, ch.Mh[:, hi, :], CR('ones'), start=True, stop=False)
                            kb.mm(o2, CR('negones'), ch.Mh[:, hi, :], start=False, stop=False)
                            kb.mm(o2, CR('ident'), NMR, start=False, stop=True)
                        kb.mm(ch.psGc.cols(0, NHC), TRIR, gj)
                        kb.mm(ch.psGc.cols(NHC, 2 * NHC), TRICR, gj)
                        kb.mm(ch.psGc.cols(2 * NHC, 3 * NHC), CR('ones'), gj)
                        for hi, h in enumerate(hs_):
                            kb.mm(ch.psG.cols(hi * 128, (hi + 1) * 128), kn[:, h, cs], kn[:, h, cs])
                            kb.mm(ch.psA.cols(hi * 128, (hi + 1) * 128), qn[:, h, cs], kn[:, h, cs])
                        for hi in range(NHC):
                            kb.mm(ch.psE.cols(hi * 128, (hi + 1) * 128), CR('ones'), ch.Mh[:, hi, :])
                        yield
                        E(ACT, 'activation', out=ch.Ecol.v(ch.Ecol.t[:, :, :].rearrange("p a h -> p (a h)")), in_=ch.psGc.cols(0, 3 * NHC), func=AF.Exp)
                        E(ACT, 'activation', out=flat(ch.Dc), in_=ch.psD.cols(), func=AF.Exp)
                        E(ACT, 'activation', out=flat(ch.EG), in_=ch.psE.cols(), func=AF.Exp)
                        yield
                        E(POOL, 'tensor_tensor', out=flat(ch.Dm), in0=flat(ch.Dc), in1=View(SM.bufs, SM.ap[:, 0:W_]), op=ALU.mult)
                        E(POOL, 'tensor_tensor', out=ch.qd[:], in0=qn[:, h0:h0 + NHC, cs], in1=ch.EG[:], op=ALU.mult)
                        yield
                        for hi, h in enumerate(hs_):
                            E(DVE, 'scalar_tensor_tensor', out=ch.Nm[:, hi, :], in0=ch.psG.cols(hi * 128, (hi + 1) * 128),
                              scalar=nbeta[:, j, h:h + 1], in1=ch.Dm[:, hi, :], op0=ALU.mult, op1=ALU.mult)
                        E(DVE, 'tensor_tensor', out=flat(ch.attn), in0=ch.psA.cols(), in1=flat(ch.Dc), op=ALU.mult)
                        yield
                        for hi in range(NHC):
                            kb.tr(View([ch.psT.bank], ch.psT.t[:, ch.psT.c0 + hi * 128:ch.psT.c0 + (hi + 1) * 128].bitcast(F32R)), ch.Nm[:, hi, :], CR('ident'))
                            kb.tr(ch.psAT.bf(hi * 128, (hi + 1) * 128), ch.attn[:, hi, :], CB('ident'))
                        for hi, h in enumerate(hs_):
                            kb.tr(ch.psKV.bf(hi * 128, (hi + 1) * 128), kn[:, h, cs], CB('ident'))
                            kb.tr(ch.psKV.bf(W_ + hi * 128, W_ + (hi + 1) * 128), vTb[:, h, cs], CB('ident'))
                        yield
                        E(ACT, 'copy', out=flat(ch.NT), in_=ch.psT.cols())
                        E(DVE, 'tensor_copy', out=flat(ch.AT), in_=ch.psAT.bf(0, W_))
                        E(DVE, 'tensor_tensor', out=ch.sc1[:], in0=beta[:, j, h0:h0 + NHC], in1=ch.Ecol[:, 0, :], op=ALU.mult)
                        kv3 = lambda a: View([ch.psKV.bank], ch.psKV.bf(a, a + W_).ap.rearrange("p (h i) -> p h i", h=NHC))
                        E(DVE, 'tensor_tensor', out=ch.kbg[:], in0=kv3(0), in1=ch.sc1.v(bc_last(ch.sc1.t[:, 0:NHC], 128)), op=ALU.mult)
                        E(DVE, 'tensor_tensor', out=ch.kd[:], in0=kv3(0), in1=ch.Ecol.v(bc_last(ch.Ecol.t[:, 1, :], 128)), op=ALU.mult)
                        E(DVE, 'tensor_tensor', out=ch.vb[:], in0=kv3(W_), in1=beta.v(bc_last(beta.t[:, j, h0:h0 + NHC], 128)), op=ALU.mult)
                        yield
                        P, PT = ch.Nm, ch.NT
                        R = ch.Rb[0]
                        ident_n = C('ident4')
                        E(DVE, 'tensor_tensor', out=flat(R), in0=asf(flat(ch.NT)), in1=View(ident_n.bufs, ident_n.ap[:, 0:W_]), op=ALU.add)
                        yield
                        for k in range(6):
                            Pn, PTn, Rn = ch.Pb[k % 2], ch.PTb[k % 2], ch.Rb[(k + 1) % 2]
                            for hi in range(NHC):
                                kb.mm(ch.psG.cols(hi * 128, (hi + 1) * 128), PT[:, hi, :], P[:, hi, :])
                            if k < 5:
                                for hi in range(NHC):
                                    kb.mm(ch.psPT.cols(hi * 128, (hi + 1) * 128), P[:, hi, :], PT[:, hi, :])
                            yield
                            E(ACT, 'copy', out=flat(Pn), in_=ch.psG.cols())
                            if k < 5:
                                E(DVE, 'tensor_copy', out=flat(PTn), in_=ch.psPT.cols())
                            yield
                            for hi in range(NHC):
                                kb.mm(ch.psT.cols(hi * 128, (hi + 1) * 128), Pn[:, hi, :], R[:, hi, :])
                            yield
                            E(DVE, 'tensor_tensor', out=flat(Rn), in0=ch.psT.cols(), in1=asf(flat(R)), op=ALU.add)
                            yield
                            P, PT, R = Pn, PTn, Rn
                        TT = R
                        for hi in range(NHC):
                            kb.mm(ch.psD.cols(hi * 128, (hi + 1) * 128), ch.kbg[:, hi, :], TT[:, hi, :])
                        yield
                        E(ACT, 'activation', out=flat(ch.nwT), in_=ch.psD.cols(), func=AF.Copy, scale=-1.0)
                        yield
                        for hi in range(NHC):
                            o2 = ch.psKV.cols(hi * 128, (hi + 1) * 128)
                            kb.mm(o2, TT[:, hi, :], ch.vb[:, hi, :], start=True, stop=False)
                            kb.mm(o2, ch.nwT[:, hi, :], ch.S[:, hi, :], start=False, stop=True)
                        yield
                        E(ACT, 'copy', out=flat(ch.vnew), in_=ch.psKV.cols())
                        yield
                        for hi in range(NHC):
                            o2 = ch.psE.cols(hi * 128, (hi + 1) * 128)
                            kb.mm(o2, ch.Sb[:, hi, :], ch.qd[:, hi, :], start=True, stop=False)
                            kb.mm(o2, ch.vnew[:, hi, :], ch.AT[:, hi, :], start=False, stop=True)
                        for hi in range(NHC):
                            kb.mm(ch.psS.cols(hi * 128, (hi + 1) * 128), ch.kd[:, hi, :], ch.vnew[:, hi, :])
                        yield
                        E(DVE, 'tensor_tensor', out=ch.Stmp[:], in0=asf(ch.S[:]), in1=ch.Ecol.v(bc_last(ch.Ecol.t[:, 2, :], 128)), op=ALU.mult)
                        E(DVE, 'tensor_tensor', out=flat(ch.S), in0=flat(ch.Stmp), in1=ch.psS.cols(), op=ALU.add)
                        E(ACT, 'copy', out=ch.Sb[:], in_=asf(ch.S[:]))
                        E(ACT, 'copy', out=ch.ost[:, :, cs], in_=View([ch.psE.bank], ch.psE.cols().ap.rearrange("p (h i) -> p h i", h=NHC)))
                        yield

                for d in range(2):
                    TRI = C('tri_le') if d == 0 else C('tri_ge')
                    NMR = CR('nm_f') if d == 0 else CR('nm_b')
                    TRIR = CR('tri_le') if d == 0 else CR('tri_ge')
                    TRICR = CR('tri_gt') if d == 0 else CR('tri_lt')
                    SM = CB('sm_f4') if d == 0 else CB('sm_b4')
                    o_ = CP['dtb'][0]
                    dtb = cpt[l][:, o_ + d * 4:o_ + d * 4 + 4]
                    nAd = negA[:, d * 4:d * 4 + 4]
                    for bi, b in enumerate(blk_order(d)):
                        t0 = b * NB
                        load_halo(raw, DQ, 12, b)
                        kb.dma(abT[:], View([kb.db('hT%d' % AB, b)], hT[AB, 0:16, t0:t0 + NB]))
                        if d == 1:
                            kb.dma(of_[:], View([kb.db('ofT%d' % c, b) for c in range(4)],
                                                ofT[0:4, :, t0:t0 + NB].rearrange("c p t -> p c t")))
                            kb.dma(gz[:], View([kb.db('hT%d' % (DZ + c), b) for c in range(4)],
                                               hT[DZ:DZ + 4, :, t0:t0 + NB].rearrange("c p t -> p c t")))
                        for ch in chs:
                            if bi == 0:
                                E(DVE, 'memset', ap=asf(ch.S[:]), constant=0.0)
                                E(POOL, 'memset', ap=ch.Sb[:], constant=0.0)
                            elif crosses(b, d):
                                E(DVE, 'tensor_scalar', out=ch.S[:], in0=asf(ch.S[:]), scalar1=flag[:, 0:1], scalar2=None, op0=ALU.mult)
                                E(DVE, 'tensor_copy', out=ch.Sb[:], in_=asf(ch.S[:]))
                        for t in range(12):
                            conv4(POOL if t % 2 else DVE, cv[:, t, :], raw, t, lambda k: cpv(l, 'dn_conv', t * 4 + k))
                        for t3 in range(3):
                            E(ACT, 'activation', out=cv[:, t3 * 4:t3 * 4 + 4, :], in_=cv[:, t3 * 4:t3 * 4 + 4, :], func=AF.Silu)
                        for t in range(8):
                            sq_, sd_ = sqb[t % 2], sd[t % 2]
                            E(ACT, 'activation', out=sq_[:, :], in_=cv[:, t, :], func=AF.Square)
                            kb.mm(psNv, CB('ones'), sq_[:, :])
                            E(DVE, 'tensor_scalar', out=sd_[:, :], in0=psNv, scalar1=1e-6, scalar2=None, op0=ALU.add)
                            E(ACT, 'activation', out=sd_[:, :], in_=sd_[:, :], func=AF.Ln)
                            E(ACT, 'activation', out=sd_[:, :], in_=sd_[:, :], func=AF.Exp, scale=-0.5)
                            dst = qn[:, t, :] if t < 4 else kn[:, t - 4, :]
                            E(DVE, 'scalar_tensor_tensor', out=dst, in0=cv[:, t, :], scalar=(128 ** -0.5 if t < 4 else 1.0),
                              in1=sd_[:, :], op0=ALU.mult, op1=ALU.mult)
                        E(ACT, 'copy', out=vTb[:], in_=cv[:, 8:12, :])
                        for j in range(4):
                            idt = C('ident')
                            kb.tr(psGT.cols(j * 16, (j + 1) * 16), abT[0:16, j * 128:(j + 1) * 128], View(idt.bufs, idt.ap[0:16, 0:16]))
                        E(ACT, 'copy', out=gt.v(gt.t[:, :, :].rearrange("p j c -> p (j c)")), in_=psGT.cols())
                        E(DVE, 'tensor_tensor', out=g_[:], in0=gt[:, :, d * 4:d * 4 + 4], in1=View(dtb.bufs, bc_mid(dtb.ap, 4)), op=ALU.add)
                        E(ACT, 'activation', out=g_[:], in_=g_[:], func=AF.Exp)
                        E(DVE, 'tensor_scalar', out=g_[:], in0=g_[:], scalar1=1.0, scalar2=None, op0=ALU.add)
                        E(ACT, 'activation', out=g_[:], in_=g_[:], func=AF.Ln)
                        E(DVE, 'tensor_tensor', out=gR[:], in0=g_[:], in1=View(nAd.bufs, bc_mid(nAd.ap, 4)), op=ALU.mult)
                        E(ACT, 'activation', out=beta[:], in_=gt[:, :, 8 + d * 4:12 + d * 4], func=AF.Sigmoid)
                        E(DVE, 'tensor_scalar', out=nbeta[:], in0=beta[:], scalar1=-1.0, scalar2=None, op0=ALU.mult)
                        order = list(range(4)) if d == 0 else list(range(3, -1, -1))
                        rr([chain(ch, d, order, TRI, TRIR, TRICR, NMR, SM) for ch in chs])
                        if d == 0:
                            for ch in chs:
                                h0 = ch.heads[0]
                                kb.dma(View([kb.db('ofT%d' % c, b) for c in ch.heads],
                                            ofT[h0:h0 + NHC, :, t0:t0 + NB].rearrange("c p t -> p c t")), ch.ost[:])
                        else:
                            E(ACT, 'activation', out=gz[:], in_=gz[:], func=AF.Silu)
                            for h in range(4):
                                ch = chs[h // NHC]
                                hi = h % NHC
                                oo, sq_, sd_ = oh[h % 2], sqb[h % 2], sd[h % 2]
                                E(DVE, 'tensor_tensor', out=oo[:, :], in0=ch.ost[:, hi, :], in1=of_[:, h, :], op=ALU.add)
                                E(ACT, 'activation', out=sq_[:, :], in_=oo[:, :], func=AF.Square)
                                kb.mm(psNv, CB('ones'), sq_[:, :])
                                E(DVE, 'tensor_scalar', out=sd_[:, :], in0=psNv, scalar1=1.0 / 128, scalar2=1e-6, op0=ALU.mult, op1=ALU.add)
                                E(ACT, 'activation', out=sd_[:, :], in_=sd_[:, :], func=AF.Ln)
                                E(ACT, 'activation', out=sd_[:, :], in_=sd_[:, :], func=AF.Exp, scale=-0.5)
                                E(POOL, 'tensor_tensor', out=oo[:, :], in0=oo[:, :], in1=sd_[:, :], op=ALU.mult)
                                E(DVE, 'scalar_tensor_tensor', out=obs[:, h, :], in0=oo[:, :], scalar=cpv(l, 'dn_nw'), in1=gz[:, h, :],
                                  op0=ALU.mult, op1=ALU.mult)
                            kb.dma(View([kb.db('obT', b)], obT[0:4, :, t0:t0 + NB].rearrange("c p t -> p c t")), obs[:])
            kb.barrier()

        MIX = {'lru': phase_lru, 'hg': phase_hg, 'dn': phase_dn}
        run = phases if phases is not None else ['wprep', 'embed', 'proj', 'lru', 'hg', 'dn', 'post']
        if 'wprep' in run:
            phase_wprep()
        if 'embed' in run:
            phase_embed()
        for l in range(nlayers):
            if 'proj' in run:
                phase_proj(l)
            for m in ['lru', 'hg', 'dn']:
                if m in run and m in MIX:
                    MIX[m](l)
            if 'post' in run:
                phase_post(l, l == nlayers - 1)
        kb.barrier()
        print("instructions:", kb.ninst)
    return nc


def _colparams(W):
    L = DEPTH
    cp = np.zeros((L, 128, NCP), np.float32)

    def put(l, name, arr):
        o, w = CP[name]
        cp[l, :, o:o + w] = arr

    def chan(v, nt):
        return np.asarray(v).reshape(nt, 128).T
    for l in range(L):
        put(l, 'dn_conv', np.asarray(W['dn_conv_w'][l]).reshape(4, 12, 128).transpose(2, 1, 0).reshape(128, 48))
        put(l, 'lru_conv', np.asarray(W['lru_conv_w'][l]).reshape(4, 4, 128).transpose(2, 1, 0).reshape(128, 16))
        put(l, 'lru_cb', chan(W['lru_conv_b'][l], 4))
        put(l, 'lru_ba', np.asarray(W['lru_ba'][l]).reshape(2, 4, 128).transpose(2, 0, 1).reshape(128, 8))
        put(l, 'lru_bx', np.asarray(W['lru_bx'][l]).reshape(2, 4, 128).transpose(2, 0, 1).reshape(128, 8))
        put(l, 'lru_lam', np.asarray(W['lru_lambda'][l]).reshape(2, 4, 128).transpose(2, 0, 1).reshape(128, 8))
        put(l, 'dn_nw', np.asarray(W['dn_norm_w'][l]).reshape(128, 1))
        put(l, 'hg_nw', np.asarray(W['hg_norm_w'][l]).reshape(128, 1))
        for n in ['ln1_g', 'ln1_b', 'ln2_g', 'ln2_b']:
            put(l, n, chan(W[n][l], 8))
        put(l, 'lb0', chan(W['hg_lb_logits'][0], 4))
        put(l, 'lb1', chan(W['hg_lb_logits'][1], 4))
        put(l, 'emb_g', chan(W['emb_ln_g'], 8))
        put(l, 'emb_b', chan(W['emb_ln_b'], 8))
        put(l, 'alog', np.tile(np.asarray(W['dn_A_log'][l]).reshape(1, 8), (128, 1)))
        put(l, 'dtb', np.tile(np.asarray(W['dn_dt_bias'][l]).reshape(1, 8), (128, 1)))
    return cp


def _blockdiag(W):
    bd = np.zeros((DEPTH, 128, 16, 128), np.float32)
    for l in range(DEPTH):
        for ai, nm in enumerate(['lru_wa', 'lru_wx']):
            w = np.asarray(W[nm][l])
            for d in range(2):
                for t in range(4):
                    idx = ai * 8 + d * 4 + t
                    for s in range(2):
                        bd[l, s * 64:(s + 1) * 64, idx, s * 64:(s + 1) * 64] = w[d, 2 * t + s]
    return bd


def make_in_maps(W, xs, ps_, flags):
    cp = _colparams(W)
    bd = _blockdiag(W)
    common = dict(cst=CONST_ARR, cp=cp, bd=bd,
                  w_in=np.ascontiguousarray(W['w_in'], dtype=np.float32), w_branch=np.asarray(W['w_branch'], np.float32),
                  w_out=np.asarray(W['w_out'], np.float32), w_mlp1=np.asarray(W['w_mlp1'], np.float32),
                  w_mlp2=np.asarray(W['w_mlp2'], np.float32), w_ple_gate=np.asarray(W['w_ple_gate'], np.float32),
                  w_ple_proj=np.asarray(W['w_ple_proj'], np.float32))
    maps = []
    for x, p, f in zip(xs, ps_, flags):
        m = dict(common)
        m['x'] = np.ascontiguousarray(x, dtype=np.float32)
        m['p'] = np.ascontiguousarray(p, dtype=np.float32)
        m['flag'] = np.full((128, 1), f, np.float32)
        maps.append(m)
    return maps


_NC_CACHE = {}


def kernel(x_prompt, x_sample, p_prompt, p_sample, **W):
    x_prompt = np.asarray(x_prompt)
    x_sample = np.asarray(x_sample)
    p_prompt = np.asarray(p_prompt)
    p_sample = np.asarray(p_sample)
    T, SEG = 8192, 4096
    assign = [(0, 1), (2, 3), (4, 4), (5, 5), (6, 6), (7, 7)]
    xs, ps_, flags = [], [], []
    for c in range(2):
        xs.append(x_sample[c])
        ps_.append(p_sample[:, c])
        flags.append(1.0)
    for a, b in assign:
        xs.append(np.concatenate([x_prompt[a], x_prompt[b]], axis=0))
        ps_.append(np.concatenate([p_prompt[:, a], p_prompt[:, b]], axis=1))
        flags.append(0.0)
    if 'nc' not in _NC_CACHE:
        _NC_CACHE['nc'] = build(T, SEG)
    nc = _NC_CACHE['nc']
    maps = make_in_maps(W, xs, ps_, flags)
    res = run_bass_kernel_spmd(nc, maps, core_ids=list(range(NCORES)))
    y_prompt = np.zeros((8, 4096, D), np.float32)
    y_sample = np.zeros((2, 8192, D), np.float32)
    for c in range(2):
        y_sample[c] = res.results[c]['y']
    for i, (a, b) in enumerate(assign):
        y = res.results[2 + i]['y']
        y_prompt[a] = y[:4096]
        if b != a:
            y_prompt[b] = y[4096:]
    return (y_prompt, y_sample)
```

```python
import numpy as np
from contextlib import ExitStack
import concourse.bass as bass
import concourse.mybir as mybir
from concourse.bass_utils import run_bass_kernel_spmd

F32 = mybir.dt.float32
BF16 = mybir.dt.bfloat16
F32R = mybir.dt.float32r
AF = mybir.ActivationFunctionType
ALU = mybir.AluOpType

D = 1024
NIN = 8720
DFF = 4096
DPLE = 256
DEPTH = 2
ALPHA = (2.0 * DEPTH) ** 0.25
NB = 512
NCORES = 8
DQ, DK, DV, DZ, HQ, HFF, HFB, HI, HGT, CX, CG, GA, GB, GC, AB = 0, 4, 8, 12, 16, 20, 24, 28, 32, 36, 40, 44, 52, 60, 68
NHT = 69
WIN_TILES = [(i * 128, 128) for i in range(16)] + [(2064 + 128 * i, 128) for i in range(52)] + [(2048, 16)]

def _consts():
    i = np.arange(128)
    c = {}
    c['ident'] = np.eye(128)
    c['ones'] = np.ones((128, 128))
    c['negones'] = -np.ones((128, 128))
    c['tri_le'] = (i[:, None] <= i[None, :]) * 1.0
    c['tri_ge'] = (i[:, None] >= i[None, :]) * 1.0
    c['tri_gt'] = (i[:, None] > i[None, :]) * 1.0
    c['tri_lt'] = (i[:, None] < i[None, :]) * 1.0
    c['nm_f'] = np.where(i[:, None] < i[None, :], -30000.0, 0.0)
    c['nm_b'] = np.where(i[:, None] > i[None, :], -30000.0, 0.0)
    c['sm_f4'] = np.tile((i[:, None] > i[None, :]) * 1.0, (1, 4))
    c['sm_b4'] = np.tile((i[:, None] < i[None, :]) * 1.0, (1, 4))
    c['ident4'] = np.tile(np.eye(128), (1, 4))
    j = np.arange(64)
    c['hgm_f4'] = np.tile((j[None, :] >= (i[:, None] % 64)) * 1.0, (1, 4))
    c['hgm_b4'] = np.tile((j[None, :] <= (i[:, None] % 64)) * 1.0, (1, 4))
    t = np.arange(NB)
    c['rst_f'] = np.tile(((t % 64) != 0) * 1.0, (128, 1))
    c['rst_b'] = np.tile(((t % 64) != 63) * 1.0, (128, 1))
    offs = {}
    o = 0
    arrs = []
    for k, v in c.items():
        offs[k] = (o, v.shape[1])
        o += v.shape[1]
        arrs.append(v.astype(np.float32))
    return np.ascontiguousarray(np.concatenate(arrs, axis=1)), offs


CONST_ARR, CONST_OFF = _consts()
CP = {}
_o = 0
for _n, _w in [('dn_conv', 48), ('lru_conv', 16), ('lru_cb', 4), ('lru_ba', 8), ('lru_bx', 8), ('lru_lam', 8),
               ('dn_nw', 1), ('hg_nw', 1), ('ln1_g', 8), ('ln1_b', 8), ('ln2_g', 8), ('ln2_b', 8),
               ('lb0', 4), ('lb1', 4), ('emb_g', 8), ('emb_b', 8), ('alog', 8), ('dtb', 8)]:
    CP[_n] = (_o, _w)
    _o += _w
NCP = _o


class View:
    def __init__(self, bufs, ap):
        self.bufs = bufs
        self.ap = ap

    def __getitem__(self, idx):
        return View(self.bufs, self.ap[idx])


class Buf:
    def __init__(self, t=None, name=""):
        self.t = t
        self.w = {}
        self.r = {}
        self.name = name

    def __getitem__(self, idx):
        return View([self], self.t[idx])

    def v(self, ap):
        return View([self], ap)


class Eng:
    def __init__(self, h, sid, name):
        self.h = h
        self.sid = sid
        self.cnt = 0
        self.waited = {}
        self.name = name
        self.old = []


class KB:
    NDS = 48
    EPOCH = 20000

    def __init__(self, nc, stack):
        self.nc = nc
        self.sems = {}
        self.nsid = 0

        def mk(n):
            s = stack.enter_context(nc.semaphore(n))
            sid = self.nsid
            self.nsid += 1
            self.sems[sid] = s
            return sid
        self.mk = mk
        self.PE = Eng(nc.tensor, mk("s_pe"), "pe")
        self.DVE = Eng(nc.vector, mk("s_dve"), "dve")
        self.ACT = Eng(nc.scalar, mk("s_act"), "act")
        self.POOL = Eng(nc.gpsimd, mk("s_pool"), "pool")
        self.SP = Eng(nc.sync, None, "sp")
        self.engs = [self.PE, self.DVE, self.ACT, self.POOL]
        self.dsid = [mk("s_dma%d" % i) for i in range(self.NDS)]
        self.dval = [0] * self.NDS
        self.dnext = 0
        self.dbufs = {}
        self.ninst = 0

    def db(self, name, blk):
        k = (name, blk)
        if k not in self.dbufs:
            self.dbufs[k] = Buf(None, "%s_%s" % (name, blk))
        return self.dbufs[k]

    def _wait(self, eng, sid, val):
        if val <= 0 or eng.waited.get(sid, 0) >= val:
            return
        eng.h.wait_ge(self.sems[sid], val)
        eng.waited[sid] = val

    def _sync(self, eng, reads, writes):
        own = eng.sid
        for b in reads:
            for sid, v in b.w.items():
                self._wait(eng, sid, v)
        for b in writes:
            for sid, v in b.w.items():
                if sid != own:
                    self._wait(eng, sid, v)
            for sid, v in b.r.items():
                if sid != own:
                    self._wait(eng, sid, v)

    def _mark(self, reads, writes, sid, val):
        for b in reads:
            b.r[sid] = max(b.r.get(sid, 0), val)
        for b in writes:
            b.w = {sid: val}
            b.r = {}

    def E(self, eng, meth, **kw):
        reads, writes, args = [], [], {}
        for k_, v in kw.items():
            if isinstance(v, View):
                if k_ in ('out', 'accum_out', 'ap') or any(getattr(b_, 'excl', False) for b_ in v.bufs):
                    writes.extend(v.bufs)
                else:
                    reads.extend(v.bufs)
                args[k_] = v.ap
            else:
                args[k_] = v
        if eng.cnt >= self.EPOCH:
            eng.old.append((eng.sid, eng.cnt))
            eng.sid = self.mk("s_%s_e%d" % (eng.name, len(eng.old)))
            eng.cnt = 0
        self._sync(eng, reads, writes)
        inst = getattr(eng.h, meth)(**args)
        inst.then_inc(self.sems[eng.sid], 1)
        eng.cnt += 1
        self.ninst += 1
        self._mark(reads, writes, eng.sid, eng.cnt)
        return inst

    def mm(self, out, lhsT, rhs, start=True, stop=True, extra_w=()):
        return self.E(self.PE, 'matmul', out=out, lhsT=lhsT, rhs=rhs, start=start, stop=stop)

    def tr(self, out, in_, ident):
        return self.E(self.PE, 'transpose', out=out, in_=in_, identity=ident)

    def dma(self, out, in_, q=None):
        eng = self.SP
        s = self.dnext
        self.dnext = (self.dnext + 1) % self.NDS
        sid = self.dsid[s]
        self._wait(eng, sid, self.dval[s])
        self._sync(eng, in_.bufs, out.bufs)
        self.dval[s] += 16
        eng.h.dma_start(out=out.ap, in_=in_.ap).then_inc(self.sems[sid], 16)
        self.ninst += 1
        self._mark(in_.bufs, out.bufs, sid, self.dval[s])

    def barrier(self):
        for e in self.engs + [self.SP]:
            for o in self.engs:
                if o is not e:
                    if o.cnt > 0:
                        self._wait(e, o.sid, o.cnt)
                    elif o.old:
                        self._wait(e, o.old[-1][0], o.old[-1][1])
            for s in range(self.NDS):
                self._wait(e, self.dsid[s], self.dval[s])


def bc_last(ap, n):
    l = [list(x) for x in ap.ap]
    return bass.AP(ap.tensor, ap.offset, l + [[0, n]])


def bc_mid(ap, n):
    l = [list(x) for x in ap.ap]
    return bass.AP(ap.tensor, ap.offset, [l[0], [0, n]] + l[1:])


def build(T, SEG, phases=None, debug_out=(), nlayers=DEPTH):
    NBLK = T // NB
    nc = bass.Bass("TRN2", target_bir_lowering=False)
    dt = nc.dram_tensor
    x_in = dt("x", [T, D], F32, kind="ExternalInput").ap()
    p_in = dt("p", [DEPTH, T, DPLE], F32, kind="ExternalInput").ap()
    cst_in = dt("cst", list(CONST_ARR.shape), F32, kind="ExternalInput").ap()
    cp_in = dt("cp", [DEPTH, 128, NCP], F32, kind="ExternalInput").ap()
    bd_in = dt("bd", [DEPTH, 128, 16, 128], F32, kind="ExternalInput").ap()
    flag_in = dt("flag", [128, 1], F32, kind="ExternalInput").ap()
    w_in = dt("w_in", [DEPTH, D, NIN], F32, kind="ExternalInput").ap()
    w_branch = dt("w_branch", [DEPTH, 3, 512, D], F32, kind="ExternalInput").ap()
    w_out = dt("w_out", [DEPTH, D, D], F32, kind="ExternalInput").ap()
    w_mlp1 = dt("w_mlp1", [DEPTH, D, DFF], F32, kind="ExternalInput").ap()
    w_mlp2 = dt("w_mlp2", [DEPTH, DFF, D], F32, kind="ExternalInput").ap()
    w_pg = dt("w_ple_gate", [DEPTH, D, D], F32, kind="ExternalInput").ap()
    w_pp = dt("w_ple_proj", [DEPTH, DPLE, D], F32, kind="ExternalInput").ap()
    y_out = dt("y", [T, D], F32, kind="ExternalOutput").ap()
    dbg = {}
    okind = lambda n: "ExternalOutput" if n in debug_out else "Internal"
    xT = dt("xT", [8, 128, T], F32, kind=okind("xT")).ap()
    class _HT:
        SPLIT = 36

        def __init__(self):
            self.a = dt("hT", [self.SPLIT, 128, T], F32, kind=okind("hT")).ap()
            self.b = dt("hTb", [NHT - self.SPLIT, 128, T], F32, kind=okind("hT")).ap()

        def __getitem__(self, idx):
            f = idx[0]
            rest = tuple(idx[1:])
            if isinstance(f, slice):
                if f.start < self.SPLIT:
                    assert f.stop <= self.SPLIT
                    return self.a[(f,) + rest]
                return self.b[(slice(f.start - self.SPLIT, f.stop - self.SPLIT),) + rest]
            if f < self.SPLIT:
                return self.a[(f,) + rest]
            return self.b[(f - self.SPLIT,) + rest]
    hT = _HT()
    ofT = dt("ofT", [12, 128, T], F32, kind=okind("ofT")).ap()
    obT = dt("obT", [12, 128, T], BF16, kind=okind("obT")).ap()
    wspec = {
        'win': (8, WIN_TILES, lambda l: w_in[l]),
        'wb0': (4, [(i * 128, 128) for i in range(8)], lambda l: w_branch[l, 0]),
        'wb1': (4, [(i * 128, 128) for i in range(8)], lambda l: w_branch[l, 1]),
        'wb2': (4, [(i * 128, 128) for i in range(8)], lambda l: w_branch[l, 2]),
        'wout': (8, [(i * 128, 128) for i in range(8)], lambda l: w_out[l]),
        'wm1': (8, [(i * 128, 128) for i in range(32)], lambda l: w_mlp1[l]),
        'wm2': (32, [(i * 128, 128) for i in range(8)], lambda l: w_mlp2[l]),
        'wpg': (8, [(i * 128, 128) for i in range(8)], lambda l: w_pg[l]),
        'wpp': (2, [(i * 128, 128) for i in range(8)], lambda l: w_pp[l]),
    }
    ws = {n: dt("ws_" + n, [DEPTH, len(s[1]), 128, s[0], 128], BF16, kind="Internal").ap() for n, s in wspec.items()}

    with ExitStack() as st0:
        kb = KB(nc, st0)
        PE, DVE, ACT, POOL = kb.PE, kb.DVE, kb.ACT, kb.POOL
        E = kb.E

        uniq = [0]

        def sb(st, name, shape, dtype=F32):
            uniq[0] += 1
            return Buf(st.enter_context(nc.sbuf_tensor("sb%d_%s" % (uniq[0], name), shape, dtype)), name)

        ps = [Buf(st0.enter_context(nc.psum_tensor("ps%d" % i, [128, 512], F32)), "ps%d" % i) for i in range(8)]
        for b_ in ps:
            b_.excl = True
        cst = sb(st0, "cst", list(CONST_ARR.shape))
        cpt = [sb(st0, "cp%d" % l, [128, NCP]) for l in range(DEPTH)]
        flag = sb(st0, "flag", [128, 1])
        cbf = sb(st0, "cbf", [128, 128 * 2 + 512 * 3 + 256 * 2], BF16)
        kb.dma(cst[:], View([], cst_in[:, :]))
        for l in range(DEPTH):
            kb.dma(cpt[l][:], View([], cp_in[l]))
        kb.dma(flag[:], View([], flag_in[:, :]))

        def C(name):
            o, w = CONST_OFF[name]
            return cst[:, o:o + w]

        cb_off = {}
        o = 0
        for n in ['ident', 'ones', 'sm_f4', 'sm_b4', 'ident4', 'hgm_f4', 'hgm_b4']:
            w = CONST_OFF[n][1]
            cb_off[n] = (o, w)
            E(DVE, 'tensor_copy', out=cbf[:, o:o + w], in_=C(n))
            o += w

        def CB(name):
            o, w = cb_off[name]
            return cbf[:, o:o + w]

        crn = ['ident', 'ones', 'negones', 'tri_le', 'tri_ge', 'tri_gt', 'tri_lt', 'nm_f', 'nm_b']
        crt = sb(st0, "crt", [128, 128 * len(crn)], F32R)
        for i_, n in enumerate(crn):
            E(DVE, 'tensor_copy', out=crt[:, i_ * 128:(i_ + 1) * 128], in_=C(n))

        def CR(name):
            i_ = crn.index(name)
            return crt[:, i_ * 128:(i_ + 1) * 128]

        def asf(v):
            return View(v.bufs, v.ap.bitcast(F32))

        def cpv(l, name, i=0, n=1):
            o, w = CP[name]
            return cpt[l][:, o + i:o + i + n]

        def phase_wprep():
            with ExitStack() as st:
                sf = [sb(st, "wpf%d" % i, [128, 4096]) for i in range(2)]
                sbf = [sb(st, "wpb%d" % i, [128, 4096], BF16) for i in range(2)]
                it = 0
                for l in range(DEPTH):
                    for name, (n_k, tiles, srcf) in wspec.items():
                        src = srcf(l).rearrange("(k p) n -> p k n", p=128)
                        gmax = 4 if n_k <= 8 else 1
                        ti = 0
                        while ti < len(tiles):
                            g = 1
                            while (g < gmax and ti + g < len(tiles) and tiles[ti + g][1] == 128 and tiles[ti][1] == 128
                                   and tiles[ti + g][0] == tiles[ti][0] + 128 * g):
                                g += 1
                            c0 = tiles[ti][0]
                            gw = sum(tiles[ti + j][1] for j in range(g))
                            s = it % 2
                            it += 1
                            fv = sf[s].t[:, 0:n_k * gw].rearrange("p (k c) -> p k c", k=n_k)
                            bv = sbf[s].t[:, 0:n_k * gw].rearrange("p (k c) -> p k c", k=n_k)
                            kb.dma(sf[s].v(fv), View([], src[:, :, c0:c0 + gw]))
                            E(POOL if it % 2 else ACT, 'tensor_copy' if it % 2 else 'copy', out=sbf[s].v(bv), in_=sf[s].v(fv))
                            for j in range(g):
                                wdt = tiles[ti + j][1]
                                kb.dma(View([kb.db('ws_' + name, l)], ws[name][l, ti + j, :, :, 0:wdt]),
                                       sbf[s].v(bv[:, :, j * 128:j * 128 + wdt]))
                            ti += g
            kb.barrier()

        wctr = [0]
        pctr = [0]

        def dense(wb, l, name, tile_ids, rhs_fn, N, evac, psbanks=(0, 1, 2)):
            n_k = wspec[name][0]
            tiles = wspec[name][1]
            tile_ids = list(tile_ids)
            depth = len(wb) - 1
            base = wctr[0]
            wctr[0] += len(tile_ids)

            def wload(i):
                kb.dma(wb[(base + i) % len(wb)][:, 0:n_k, :], View([kb.db('ws_' + name, l)], ws[name][l, tile_ids[i]]))
            for i in range(min(depth, len(tile_ids))):
                wload(i)
            for i, ti in enumerate(tile_ids):
                wdt = tiles[ti][1]
                s = (base + i) % len(wb)
                if i + depth < len(tile_ids):
                    wload(i + depth)
                pb = ps[psbanks[pctr[0] % len(psbanks)]]
                pctr[0] += 1
                for k in range(n_k):
                    kb.mm(pb[:, 0:N], wb[s][:, k, :], rhs_fn(k), start=(k == 0), stop=(k == n_k - 1))
                evac(ti, pb[0:wdt, 0:N], wdt)

        def layer_norm(st_bufs, src, g_fn, b_fn, dst_f=None, dst_b=None, N=NB):
            sq, xb_, mean, m2, rstd, tmp = st_bufs
            pm, pq = ps[3], ps[4]
            for c in range(8):
                E(ACT, 'activation', out=sq[c % 2][:, 0:N], in_=src[:, c, 0:N], func=AF.Square)
                E(DVE, 'tensor_copy', out=xb_[c % 2][:, 0:N], in_=src[:, c, 0:N])
                kb.mm(pm[:, 0:N], CB('ones'), xb_[c % 2][:, 0:N], start=(c == 0), stop=(c == 7))
                kb.mm(pq[:, 0:N], CB('ones'), sq[c % 2][:, 0:N], start=(c == 0), stop=(c == 7))
            E(ACT, 'activation', out=mean[:, 0:N], in_=pm[:, 0:N], func=AF.Copy, scale=1.0 / D)
            E(DVE, 'tensor_tensor', out=m2[:, 0:N], in0=mean[:, 0:N], in1=mean[:, 0:N], op=ALU.mult)
            E(DVE, 'scalar_tensor_tensor', out=m2[:, 0:N], in0=pq[:, 0:N], scalar=1.0 / D, in1=m2[:, 0:N],
              op0=ALU.mult, op1=ALU.subtract)
            E(DVE, 'tensor_scalar', out=m2[:, 0:N], in0=m2[:, 0:N], scalar1=0.0, scalar2=1e-5, op0=ALU.max, op1=ALU.add)
            E(ACT, 'activation', out=m2[:, 0:N], in_=m2[:, 0:N], func=AF.Ln)
            E(ACT, 'activation', out=rstd[:, 0:N], in_=m2[:, 0:N], func=AF.Exp, scale=-0.5)
            for c in range(8):
                t_ = tmp[c % 2]
                E(DVE, 'tensor_tensor', out=t_[:, 0:N], in0=src[:, c, 0:N], in1=mean[:, 0:N], op=ALU.subtract)
                E(DVE, 'tensor_tensor', out=t_[:, 0:N], in0=t_[:, 0:N], in1=rstd[:, 0:N], op=ALU.mult)
                if dst_f is not None:
                    E(ACT, 'activation', out=dst_f[:, c, 0:N], in_=t_[:, 0:N], func=AF.Identity, scale=g_fn(c), bias=b_fn(c))
                if dst_b is not None:
                    E(ACT, 'activation', out=dst_b[:, c, 0:N], in_=t_[:, 0:N], func=AF.Identity, scale=g_fn(c), bias=b_fn(c))

        def ln_bufs(st):
            return ([sb(st, "ln_sq%d" % i, [128, NB], BF16) for i in range(2)],
                    [sb(st, "ln_xb%d" % i, [128, NB], BF16) for i in range(2)],
                    sb(st, "ln_mean", [128, NB]), sb(st, "ln_m2", [128, NB]), sb(st, "ln_rstd", [128, NB]),
                    [sb(st, "ln_tmp%d" % i, [128, NB]) for i in range(2)])

        def phase_embed():
            with ExitStack() as st:
                xin = [sb(st, "e_xin%d" % i, [128, 4, D]) for i in range(2)]
                xf = sb(st, "e_xf", [128, 8, NB])
                xo = sb(st, "e_xo", [128, 8, NB])
                lb = ln_bufs(st)
                for b in range(NBLK):
                    t0 = b * NB
                    xi = xin[b % 2]
                    kb.dma(xi[:], View([], x_in[t0:t0 + NB, :].rearrange("(j p) d -> p j d", p=128)))
                    for c in range(8):
                        pb = ps[c % 3]
                        for j in range(4):
                            kb.tr(pb[:, j * 128:(j + 1) * 128], xi[:, j, c * 128:(c + 1) * 128], C('ident'))
                        E(ACT if c % 2 else DVE, 'copy' if c % 2 else 'tensor_copy', out=xf[:, c, :], in_=pb[:, :])
                    layer_norm(lb, xf, lambda c: cpv(0, 'emb_g', c), lambda c: cpv(0, 'emb_b', c), dst_f=xo)
                    kb.dma(View([kb.db('xT', b)], xT[:, :, t0:t0 + NB].rearrange("c p t -> p c t")), xo[:])
            kb.barrier()

        def phase_proj(l):
            with ExitStack() as st:
                xin = [sb(st, "p_xin%d" % i, [128, 8, NB]) for i in range(2)]
                xb_ = [sb(st, "p_xb%d" % i, [128, 8, NB], BF16) for i in range(2)]
                stg = [sb(st, "p_stg%d" % i, [128, NB]) for i in range(6)]
                wb = [sb(st, "p_wb%d" % i, [128, 8, 128], BF16) for i in range(4)]
                sctr = [0]
                kb.dma(xin[0][:], View([kb.db('xT', 0)], xT[:, :, 0:NB].rearrange("c p t -> p c t")))
                for b in range(NBLK):
                    t0 = b * NB
                    xi, xbb = xin[b % 2], xb_[b % 2]
                    if b + 1 < NBLK:
                        kb.dma(xin[(b + 1) % 2][:], View([kb.db('xT', b + 1)], xT[:, :, t0 + NB:t0 + 2 * NB].rearrange("c p t -> p c t")))
                    E(ACT, 'copy', out=xbb[:, 0:4, :], in_=xi[:, 0:4, :])
                    E(DVE, 'tensor_copy', out=xbb[:, 4:8, :], in_=xi[:, 4:8, :])

                    def evac(ti, pv, wdt):
                        sg_ = stg[sctr[0] % len(stg)]
                        sctr[0] += 1
                        if GA <= ti < AB:
                            E(ACT, 'activation', out=sg_[0:wdt, :], in_=pv, func=AF.Sigmoid)
                        elif sctr[0] % 2:
                            E(DVE, 'tensor_copy', out=sg_[0:wdt, :], in_=pv)
                        else:
                            E(ACT, 'copy', out=sg_[0:wdt, :], in_=pv)
                        kb.dma(View([kb.db('hT%d' % ti, b)], hT[ti, 0:wdt, t0:t0 + NB]), sg_[0:wdt, :])
                    dense(wb, l, 'win', range(NHT), lambda k: xbb[:, k, :], NB, evac)
            kb.barrier()

        def phase_post(l, last):
            with ExitStack() as st:
                xres = sb(st, "q_xres", [128, 8, NB])
                hid = sb(st, "q_hid", [128, 32, NB], BF16)
                sig = sb(st, "q_sig", [128, 8, NB])
                acc = sb(st, "q_acc", [128, 8 * NB])
                accb = sb(st, "q_accb", [128, 8, NB], BF16)
                x1 = sb(st, "q_x1", [128, 8, NB])
                x1b = sb(st, "q_x1b", [128, 8, NB], BF16)
                rl = [sb(st, "q_rl%d" % i, [128, NB]) for i in range(2)]
                pin = sb(st, "q_pin", [128, 4, DPLE])
                pT = sb(st, "q_pT", [128, 2, NB], BF16)
                wb = [sb(st, "q_wb%d" % i, [128, 32, 128], BF16) for i in range(3)]
                lb = ln_bufs(st)
                acc3 = acc.v(acc.t[:, :].rearrange("p (c t) -> p c t", c=8))
                accv = lambda c: acc.v(acc.t[:, c * NB:(c + 1) * NB])
                for b in range(NBLK):
                    t0 = b * NB
                    kb.dma(xres[:], View([kb.db('xT', b)], xT[:, :, t0:t0 + NB].rearrange("c p t -> p c t")))
                    kb.dma(hid[:, 0:12, :], View([kb.db('obT', b)], obT[:, :, t0:t0 + NB].rearrange("c p t -> p c t")))
                    kb.dma(pin[:], View([], p_in[l, t0:t0 + NB, :].rearrange("(j p) d -> p j d", p=128)))
                    for c in range(2):
                        pb = ps[5 + c]
                        for j in range(4):
                            kb.tr(pb[:, j * 128:(j + 1) * 128], pin[:, j, c * 128:(c + 1) * 128], C('ident'))
                        E(ACT, 'copy', out=pT[:, c, :], in_=pb[:, :])
                    for n in range(3):
                        gt0 = [GA, GB, GC][n]
                        kb.dma(sig[:], View([kb.db('hT%d' % (gt0 + c), b) for c in range(8)],
                                            hT[gt0:gt0 + 8, :, t0:t0 + NB].rearrange("c p t -> p c t")))

                        def evac(ti, pv, wdt, n=n):
                            if n == 0:
                                E(DVE, 'tensor_tensor', out=accv(ti), in0=pv, in1=sig[:, ti, :], op=ALU.mult)
                            else:
                                r_ = rl[ti % 2]
                                E(DVE, 'tensor_tensor', out=r_[:, :], in0=pv, in1=sig[:, ti, :], op=ALU.mult)
                                if n == 1:
                                    E(POOL, 'tensor_tensor', out=accv(ti), in0=accv(ti), in1=r_[:, :], op=ALU.add)
                                else:
                                    E(POOL, 'tensor_tensor', out=accb[:, ti, :], in0=accv(ti), in1=r_[:, :], op=ALU.add)
                        dense(wb, l, 'wb%d' % n, range(8), lambda k, n=n: hid[:, n * 4 + k, :], NB, evac)

                    def evac(ti, pv, wdt):
                        E(DVE, 'scalar_tensor_tensor', out=accv(ti), in0=xres[:, ti, :], scalar=ALPHA, in1=pv,
                          op0=ALU.mult, op1=ALU.add)
                    dense(wb, l, 'wout', range(8), lambda k: accb[:, k, :], NB, evac)
                    layer_norm(lb, acc3, lambda c: cpv(l, 'ln1_g', c), lambda c: cpv(l, 'ln1_b', c), dst_f=x1, dst_b=x1b)

                    def evac(ti, pv, wdt):
                        r_ = rl[ti % 2]
                        E(ACT, 'activation', out=r_[:, :], in_=pv, func=AF.Relu)
                        E(POOL if ti % 2 else DVE, 'tensor_tensor', out=hid[:, ti, :], in0=r_[:, :], in1=r_[:, :], op=ALU.mult)
                    dense(wb, l, 'wm1', range(32), lambda k: x1b[:, k, :], NB, evac)

                    def evac(ti, pv, wdt):
                        E(ACT, 'activation', out=sig[:, ti, :], in_=pv, func=AF.Sigmoid)
                    dense(wb, l, 'wpg', range(8), lambda k: x1b[:, k, :], NB, evac)

                    def evac(ti, pv, wdt):
                        E(DVE, 'tensor_tensor', out=sig[:, ti, :], in0=pv, in1=sig[:, ti, :], op=ALU.mult)
                    dense(wb, l, 'wpp', range(8), lambda k: pT[:, k, :], NB, evac)

                    def evac(ti, pv, wdt):
                        E(DVE, 'scalar_tensor_tensor', out=accv(ti), in0=x1[:, ti, :], scalar=ALPHA, in1=pv,
                          op0=ALU.mult, op1=ALU.add)
                        E(POOL, 'tensor_tensor', out=accv(ti), in0=accv(ti), in1=sig[:, ti, :], op=ALU.add)
                    dense(wb, l, 'wm2', range(8), lambda k: hid[:, k, :], NB, evac)
                    layer_norm(lb, acc3, lambda c: cpv(l, 'ln2_g', c), lambda c: cpv(l, 'ln2_b', c), dst_f=xres)
                    if not last:
                        kb.dma(View([kb.db('xT', b)], xT[:, :, t0:t0 + NB].rearrange("c p t -> p c t")), xres[:])
                    else:
                        ov = x1.v(x1.t[:, :, :].rearrange("p c t -> p (c t)").rearrange("p (j d) -> p j d", j=4))
                        for j in range(4):
                            for c2 in range(2):
                                pb = ps[5 + c2]
                                for cc in range(4):
                                    c = c2 * 4 + cc
                                    kb.tr(pb[:, cc * 128:(cc + 1) * 128], xres[:, c, j * 128:(j + 1) * 128], C('ident'))
                                E(ACT if c2 else DVE, 'copy' if c2 else 'tensor_copy',
                                  out=x1.v(ov.ap[:, j, c2 * 512:(c2 + 1) * 512]), in_=pb[:, :])
                        kb.dma(View([], y_out[t0:t0 + NB, :].rearrange("(j p) d -> p j d", p=128)), ov)
            kb.barrier()

        SEGB = SEG // NB

        def blk_order(d):
            return list(range(NBLK)) if d == 0 else list(range(NBLK - 1, -1, -1))

        def load_halo(raw, tile0, nt, b):
            t0 = b * NB
            lo, hi = max(t0 - 1, 0), min(t0 + NB + 2, T)
            if lo > t0 - 1:
                E(POOL, 'memset', ap=raw[:, :, 0:1], constant=0.0)
            if hi < t0 + NB + 2:
                E(POOL, 'memset', ap=raw[:, :, NB + 1:NB + 3], constant=0.0)
            bl = [kb.db('hT%d' % (tile0 + c), bb) for c in range(nt) for bb in (b - 1, b, b + 1) if 0 <= bb < NBLK]
            kb.dma(raw[:, :, lo - (t0 - 1):hi - (t0 - 1)], View(bl, hT[tile0:tile0 + nt, :, lo:hi].rearrange("c p t -> p c t")))
            if b % SEGB == 0 and b > 0:
                E(DVE, 'tensor_scalar', out=raw[:, :, 0:1], in0=raw[:, :, 0:1], scalar1=flag[:, 0:1], scalar2=None, op0=ALU.mult)
            if b % SEGB == SEGB - 1 and b < NBLK - 1:
                E(DVE, 'tensor_scalar', out=raw[:, :, NB + 1:NB + 3], in0=raw[:, :, NB + 1:NB + 3], scalar1=flag[:, 0:1],
                  scalar2=None, op0=ALU.mult)

        def conv4(eng, out, raw, t, wcol, bias=None, tmp=None):
            if bias is None:
                E(eng, 'tensor_scalar', out=out, in0=raw[:, t, 0:NB], scalar1=wcol(0), scalar2=0.0, op0=ALU.mult, op1=ALU.add)
            else:
                E(eng, 'tensor_scalar', out=out, in0=raw[:, t, 0:NB], scalar1=wcol(0), scalar2=bias, op0=ALU.mult, op1=ALU.add)
            for k in range(1, 4):
                if tmp is not None:
                    E(POOL, 'tensor_scalar', out=tmp, in0=raw[:, t, k:k + NB], scalar1=wcol(k), scalar2=0.0, op0=ALU.mult, op1=ALU.add)
                    E(POOL, 'tensor_tensor', out=out, in0=out, in1=tmp, op=ALU.add)
                else:
                    E(DVE, 'scalar_tensor_tensor', out=out, in0=raw[:, t, k:k + NB], scalar=wcol(k), in1=out, op0=ALU.mult, op1=ALU.add)

        def crosses(b, d):
            return (d == 0 and b > 0 and b % SEGB == 0) or (d == 1 and b < NBLK - 1 and b % SEGB == SEGB - 1)

        def phase_lru(l):
            with ExitStack() as st:
                bdf = sb(st, "l_bdf", [128, 16, 128])
                bdb = sb(st, "l_bdb", [128, 16, 128], BF16)
                ccol = sb(st, "l_ccol", [128, 8])
                raw = [sb(st, "l_raw%d" % i, [128, 4, NB + 3]) for i in range(2)]
                xc = sb(st, "l_xc", [128, 4, NB])
                xcb = sb(st, "l_xcb", [128, 4, NB], BF16)
                r_ = [sb(st, "l_r%d" % i, [128, NB]) for i in range(2)]
                i_ = [sb(st, "l_i%d" % i, [128, NB]) for i in range(2)]
                a_ = [sb(st, "l_a%d" % i, [128, NB]) for i in range(2)]
                u_ = [sb(st, "l_u%d" % i, [128, NB]) for i in range(2)]
                hh = [sb(st, "l_h%d" % i, [128, 4, NB]) for i in range(2)]
                hf = sb(st, "l_hf", [128, 4, NB])
                cg = sb(st, "l_cg", [128, 4, NB])
                ob = [sb(st, "l_ob%d" % i, [128, 4, NB], BF16) for i in range(2)]
                carry = sb(st, "l_carry", [128, 4])
                kb.dma(bdf[:], View([], bd_in[l]))
                E(DVE, 'tensor_copy', out=bdb[:], in_=bdf[:])
                o_, w_ = CP['lru_lam']
                E(ACT, 'activation', out=ccol[:], in_=cpt[l][:, o_:o_ + 8], func=AF.Sigmoid)
                E(ACT, 'activation', out=ccol[:], in_=ccol[:], func=AF.Ln)
                E(DVE, 'tensor_scalar', out=ccol[:], in0=ccol[:], scalar1=8.0, scalar2=None, op0=ALU.mult)
                for d in range(2):
                    for bi, b in enumerate(blk_order(d)):
                        t0 = b * NB
                        rw = raw[bi % 2]
                        h = hh[bi % 2]
                        load_halo(rw, CX, 4, b)
                        if d == 1:
                            kb.dma(hf[:], View([kb.db('ofT%d' % (8 + c), b) for c in range(4)],
                                               ofT[8:12, :, t0:t0 + NB].rearrange("c p t -> p c t")))
                            kb.dma(cg[:], View([kb.db('hT%d' % (CG + c), b) for c in range(4)],
                                               hT[CG:CG + 4, :, t0:t0 + NB].rearrange("c p t -> p c t")))
                        if bi == 0:
                            E(DVE, 'memset', ap=carry[:], constant=0.0)
                        elif crosses(b, d):
                            E(DVE, 'tensor_scalar', out=carry[:], in0=carry[:], scalar1=flag[:, 0:1], scalar2=None, op0=ALU.mult)
                        for t in range(4):
                            conv4(POOL, xc[:, t, :], rw, t, lambda k: cpv(l, 'lru_conv', t * 4 + k), bias=cpv(l, 'lru_cb', t))
                            E(ACT, 'copy', out=xcb[:, t, :], in_=xc[:, t, :])
                            pa, px = ps[(2 * t) % 4], ps[(2 * t + 1) % 4]
                            kb.mm(pa[:, :], bdb[:, d * 4 + t, :], xcb[:, t, :])
                            kb.mm(px[:, :], bdb[:, 8 + d * 4 + t, :], xcb[:, t, :])
                            rr, ii, aa, uu = r_[t % 2], i_[t % 2], a_[t % 2], u_[t % 2]
                            E(ACT, 'activation', out=rr[:, :], in_=pa[:, :], func=AF.Sigmoid, bias=cpv(l, 'lru_ba', d * 4 + t))
                            E(ACT, 'activation', out=ii[:, :], in_=px[:, :], func=AF.Sigmoid, bias=cpv(l, 'lru_bx', d * 4 + t))
                            E(ACT, 'activation', out=aa[:, :], in_=rr[:, :], func=AF.Exp, scale=ccol[:, d * 4 + t:d * 4 + t + 1])
                            E(POOL, 'tensor_tensor', out=rr[:, :], in0=aa[:, :], in1=aa[:, :], op=ALU.mult)
                            E(DVE, 'tensor_scalar', out=rr[:, :], in0=rr[:, :], scalar1=-1.0, scalar2=1.0, op0=ALU.mult, op1=ALU.add)
                            E(ACT, 'activation', out=rr[:, :], in_=rr[:, :], func=AF.Sqrt)
                            E(POOL, 'tensor_tensor', out=ii[:, :], in0=ii[:, :], in1=xc[:, t, :], op=ALU.mult)
                            E(DVE, 'tensor_tensor', out=uu[:, :], in0=rr[:, :], in1=ii[:, :], op=ALU.mult)
                            if d == 0:
                                E(DVE, 'tensor_tensor_scan', out=h[:, t, :], data0=aa[:, :], data1=uu[:, :],
                                  initial=carry[:, t:t + 1], op0=ALU.mult, op1=ALU.add)
                            else:
                                E(DVE, 'tensor_tensor_scan', out=h[:, t, ::-1], data0=aa[:, ::-1], data1=uu[:, ::-1],
                                  initial=carry[:, t:t + 1], op0=ALU.mult, op1=ALU.add)
                        E(DVE, 'tensor_copy', out=carry[:], in_=h[:, :, NB - 1] if d == 0 else h[:, :, 0])
                        if d == 0:
                            kb.dma(View([kb.db('ofT%d' % (8 + c), b) for c in range(4)],
                                        ofT[8:12, :, t0:t0 + NB].rearrange("c p t -> p c t")), h[:])
                        else:
                            o2 = ob[bi % 2]
                            E(ACT, 'activation', out=cg[:], in_=cg[:], func=AF.Gelu_apprx_tanh)
                            E(POOL, 'tensor_tensor', out=hf[:], in0=hf[:], in1=h[:], op=ALU.add)
                            E(DVE, 'tensor_tensor', out=o2[:], in0=hf[:], in1=cg[:], op=ALU.mult)
                            kb.dma(View([kb.db('obT', b)], obT[8:12, :, t0:t0 + NB].rearrange("c p t -> p c t")), o2[:])
            kb.barrier()

        def phase_hg(l):
            with ExitStack() as st:
                lbc = sb(st, "g_lbc", [128, 4])
                omlb = sb(st, "g_omlb", [128, 4])
                nomlb = sb(st, "g_nomlb", [128, 4])
                qT = [sb(st, "g_qT%d" % i, [128, 4, NB]) for i in range(2)]
                fT = [sb(st, "g_fT%d" % i, [128, 4, NB]) for i in range(2)]
                vT = [sb(st, "g_vT%d" % i, [128, 4, NB]) for i in range(2)]
                s_ = [sb(st, "g_s%d" % i, [128, NB]) for i in range(2)]
                f_ = [sb(st, "g_f%d" % i, [128, NB]) for i in range(2)]
                kk = [sb(st, "g_kk%d" % i, [128, NB]) for i in range(2)]
                bq = [sb(st, "g_bq%d" % i, [128, NB]) for i in range(2)]
                dq_ = [sb(st, "g_dq%d" % i, [128, NB]) for i in range(2)]
                e1 = [sb(st, "g_e1%d" % i, [128, NB]) for i in range(2)]
                e2 = [sb(st, "g_e2%d" % i, [128, NB]) for i in range(2)]
                qt = sb(st, "g_qt", [128, 4, NB], BF16)
                kdT = sb(st, "g_kdT", [128, 4, NB], BF16)
                ebl = sb(st, "g_ebl", [128, 4, 8])
                vtok = sb(st, "g_vtok", [128, 4, 512], BF16)
                ktok = sb(st, "g_ktok", [128, 4, 512], BF16)
                vTb = sb(st, "g_vTb", [128, 4, NB], BF16)
                AT = [sb(st, "g_AT%d" % i, [128, 4, 64], BF16) for i in range(2)]
                S = sb(st, "g_S", [128, 4, 128])
                Sp = sb(st, "g_Sp", [128, 4, 128])
                Spb = sb(st, "g_Spb", [128, 4, 128], BF16)
                ost = [sb(st, "g_ost%d" % i, [128, 4, NB]) for i in range(2)]
                of_ = sb(st, "g_of", [128, 4, NB])
                gz = sb(st, "g_gz", [128, 4, NB])
                oh = [sb(st, "g_oh%d" % i, [128, NB]) for i in range(2)]
                sqb = [sb(st, "g_sqb%d" % i, [128, NB], BF16) for i in range(2)]
                sd = [sb(st, "g_sd%d" % i, [128, NB]) for i in range(2)]
                obs = [sb(st, "g_obs%d" % i, [128, 4, NB], BF16) for i in range(2)]
                if l == 0:
                    E(DVE, 'memset', ap=lbc[:], constant=0.0)
                else:
                    o0, o1 = CP['lb0'][0], CP['lb1'][0]
                    E(DVE, 'tensor_tensor', out=lbc[:], in0=cpt[l][:, o1:o1 + 4], in1=cpt[l][:, o0:o0 + 4], op=ALU.subtract)
                    E(ACT, 'activation', out=lbc[:], in_=lbc[:], func=AF.Sigmoid)
                E(DVE, 'tensor_scalar', out=omlb[:], in0=lbc[:], scalar1=-1.0, scalar2=1.0, op0=ALU.mult, op1=ALU.add)
                E(DVE, 'tensor_scalar', out=nomlb[:], in0=omlb[:], scalar1=-1.0, scalar2=None, op0=ALU.mult)
                pso = [ps[0], ps[1], ps[2], ps[3]]
                psA, psS, psT, psN = ps[4], ps[5], ps[6], ps[7]
                psTb = psT.v(psT.t[:, :].bitcast(BF16))
                for d in range(2):
                    HF = HFF if d == 0 else HFB
                    mask4 = CB('hgm_f4') if d == 0 else CB('hgm_b4')
                    for bi, b in enumerate(blk_order(d)):
                        t0 = b * NB
                        q_, fz, v_ = qT[bi % 2], fT[bi % 2], vT[bi % 2]
                        for (dst, tl) in ((q_, HQ), (fz, HF), (v_, HI)):
                            kb.dma(dst[:], View([kb.db('hT%d' % (tl + c), b) for c in range(4)],
                                                hT[tl:tl + 4, :, t0:t0 + NB].rearrange("c p t -> p c t")))
                        if d == 1:
                            kb.dma(of_[:], View([kb.db('ofT%d' % (4 + c), b) for c in range(4)],
                                                ofT[4:8, :, t0:t0 + NB].rearrange("c p t -> p c t")))
                            kb.dma(gz[:], View([kb.db('hT%d' % (HGT + c), b) for c in range(4)],
                                               hT[HGT:HGT + 4, :, t0:t0 + NB].rearrange("c p t -> p c t")))
                        if bi == 0:
                            E(DVE, 'memset', ap=S[:], constant=0.0)
                        elif crosses(b, d):
                            E(DVE, 'tensor_scalar', out=S[:], in0=S[:], scalar1=flag[:, 0:1], scalar2=None, op0=ALU.mult)
                        E(ACT, 'copy', out=vTb[:], in_=v_[:])
                        for j in range(4):
                            for h in range(4):
                                kb.tr(psTb.bufs[0].v(psTb.ap[:, h * 128:(h + 1) * 128]), vTb[:, h, j * 128:(j + 1) * 128], CB('ident'))
                            E(ACT, 'copy', out=vtok[:, j, :], in_=psT.v(psTb.ap[:, 0:512]))
                        for h in range(4):
                            ss, ff, k2, bb, dd, x1_, x2_ = s_[h % 2], f_[h % 2], kk[h % 2], bq[h % 2], dq_[h % 2], e1[h % 2], e2[h % 2]
                            E(ACT, 'activation', out=ss[:, :], in_=fz[:, h, :], func=AF.Sigmoid)
                            E(DVE, 'tensor_scalar', out=ff[:, :], in0=ss[:, :], scalar1=omlb[:, h:h + 1], scalar2=lbc[:, h:h + 1],
                              op0=ALU.mult, op1=ALU.add)
                            E(ACT, 'activation', out=ff[:, :], in_=ff[:, :], func=AF.Ln)
                            E(POOL, 'tensor_scalar', out=k2[:, :], in0=ss[:, :], scalar1=nomlb[:, h:h + 1], scalar2=omlb[:, h:h + 1],
                              op0=ALU.mult, op1=ALU.add)
                            if d == 0:
                                E(DVE, 'tensor_tensor_scan', out=bb[:, :], data0=C('rst_f'), data1=ff[:, :], initial=0.0,
                                  op0=ALU.mult, op1=ALU.add)
                            else:
                                rb = C('rst_b')
                                E(DVE, 'tensor_tensor_scan', out=bb[:, ::-1], data0=View(rb.bufs, rb.ap[:, ::-1]), data1=ff[:, ::-1],
                                  initial=0.0, op0=ALU.mult, op1=ALU.add)
                            b3 = bb.t[:, :].rearrange("p (c j) -> p c j", j=64)
                            blv = b3[:, :, 63] if d == 0 else b3[:, :, 0]
                            E(ACT, 'activation', out=ebl[:, h, :], in_=bb.v(blv), func=AF.Exp)
                            E(DVE, 'tensor_tensor', out=dd.v(dd.t[:, :].rearrange("p (c j) -> p c j", j=64)), in0=bb.v(b3),
                              in1=bb.v(bc_last(blv, 64)), op=ALU.subtract)
                            E(ACT, 'activation', out=x1_[:, :], in_=dd[:, :], func=AF.Exp)
                            E(ACT, 'activation', out=x2_[:, :], in_=dd[:, :], func=AF.Exp, scale=-1.0)
                            E(POOL, 'tensor_tensor', out=qt[:, h, :], in0=q_[:, h, :], in1=x1_[:, :], op=ALU.mult)
                            E(DVE, 'tensor_tensor', out=kdT[:, h, :], in0=k2[:, :], in1=x2_[:, :], op=ALU.mult)
                        for j in range(4):
                            for h in range(4):
                                kb.tr(psTb.bufs[0].v(psTb.ap[:, h * 128:(h + 1) * 128]), kdT[:, h, j * 128:(j + 1) * 128], CB('ident'))
                            E(ACT, 'copy', out=ktok[:, j, :], in_=psT.v(psTb.ap[:, 0:512]))
                        jl = range(4) if d == 0 else range(3, -1, -1)
                        for j in jl:
                            at = AT[j % 2]
                            for hfh in range(2):
                                c = 2 * j + hfh
                                for h in range(4):
                                    kb.mm(psA[hfh * 64:(hfh + 1) * 64, h * 64:(h + 1) * 64], kdT[:, h, c * 64:(c + 1) * 64],
                                          qt[:, h, c * 64:(c + 1) * 64])
                            E(DVE, 'tensor_tensor', out=at.v(at.t[:, :, :].rearrange("p h i -> p (h i)")), in0=psA[:, 0:256], in1=mask4, op=ALU.mult)
                            for hfh in (range(2) if d == 0 else range(1, -1, -1)):
                                c = 2 * j + hfh
                                p0 = hfh * 64
                                eb = ebl.v(bc_last(ebl.t[:, :, c], 128))
                                E(DVE, 'tensor_tensor', out=Spb[:], in0=S[:], in1=eb, op=ALU.mult)
                                E(POOL, 'tensor_tensor', out=Sp[:], in0=S[:], in1=eb, op=ALU.mult)
                                for h in range(4):
                                    kb.mm(pso[h][:, c * 64:(c + 1) * 64], Spb[:, h, :], qt[:, h, c * 64:(c + 1) * 64], start=True, stop=False)
                                    kb.mm(pso[h][:, c * 64:(c + 1) * 64], vtok[p0:p0 + 64, j, h * 128:(h + 1) * 128], at[p0:p0 + 64, h, :],
                                          start=False, stop=True)
                                for h in range(4):
                                    kb.mm(psS[:, h * 128:(h + 1) * 128], ktok[p0:p0 + 64, j, h * 128:(h + 1) * 128],
                                          vtok[p0:p0 + 64, j, h * 128:(h + 1) * 128])
                                E(DVE, 'tensor_tensor', out=S.v(S.t[:, :, :].rearrange("p h d -> p (h d)")),
                                  in0=Sp.v(Sp.t[:, :, :].rearrange("p h d -> p (h d)")), in1=psS[:, :], op=ALU.add)
                        if d == 0:
                            o1 = ost[bi % 2]
                            for h in range(4):
                                E(ACT, 'copy', out=o1[:, h, :], in_=pso[h][:, :])
                            kb.dma(View([kb.db('ofT%d' % (4 + c), b) for c in range(4)],
                                        ofT[4:8, :, t0:t0 + NB].rearrange("c p t -> p c t")), o1[:])
                        else:
                            o2 = obs[bi % 2]
                            E(ACT, 'activation', out=gz[:], in_=gz[:], func=AF.Silu)
                            for h in range(4):
                                oo, sq_, sd_ = oh[h % 2], sqb[h % 2], sd[h % 2]
                                E(DVE, 'tensor_tensor', out=oo[:, :], in0=pso[h][:, :], in1=of_[:, h, :], op=ALU.add)
                                E(ACT, 'activation', out=sq_[:, :], in_=oo[:, :], func=AF.Square)
                                kb.mm(psN[:, :], CB('ones'), sq_[:, :])
                                E(DVE, 'tensor_scalar', out=sd_[:, :], in0=psN[:, :], scalar1=1.0 / 128, scalar2=1e-6, op0=ALU.mult, op1=ALU.add)
                                E(ACT, 'activation', out=sd_[:, :], in_=sd_[:, :], func=AF.Ln)
                                E(ACT, 'activation', out=sd_[:, :], in_=sd_[:, :], func=AF.Exp, scale=-0.5)
                                E(POOL, 'tensor_tensor', out=oo[:, :], in0=oo[:, :], in1=sd_[:, :], op=ALU.mult)
                                E(DVE, 'scalar_tensor_tensor', out=o2[:, h, :], in0=oo[:, :], scalar=cpv(l, 'hg_nw'), in1=gz[:, h, :],
                                  op0=ALU.mult, op1=ALU.mult)
                            kb.dma(View([kb.db('obT', b)], obT[4:8, :, t0:t0 + NB].rearrange("c p t -> p c t")), o2[:])
            kb.barrier()

        class ColBuf:
            def __init__(self, bank, c0, c1, name=""):
                self.bank = bank
                self.t = bank.t
                self.c0, self.c1 = c0, c1

            def cols(self, a=None, b=None, rows=slice(None)):
                a = 0 if a is None else a
                b = (self.c1 - self.c0) if b is None else b
                return View([self.bank], self.t[rows, self.c0 + a:self.c0 + b])

            def bf(self, a, b):
                return View([self.bank], self.t[:, :].bitcast(BF16)[:, 2 * self.c0 + a:2 * self.c0 + b])

        def rr(gens):
            gens = list(gens)
            while gens:
                for g in list(gens):
                    try:
                        next(g)
                    except StopIteration:
                        gens.remove(g)

        NCH = 2
        NHC = 4 // NCH

        def phase_dn(l):
            with ExitStack() as st:
                raw = sb(st, "d_raw", [128, 12, NB + 3])
                cv = sb(st, "d_cv", [128, 12, NB])
                abT = sb(st, "d_abT", [16, NB])
                qn2 = [sb(st, "d_qn%d" % i, [128, 4, NB], BF16) for i in range(2)]
                kn2 = [sb(st, "d_kn%d" % i, [128, 4, NB], BF16) for i in range(2)]
                vTb2 = [sb(st, "d_vTb%d" % i, [128, 4, NB], BF16) for i in range(2)]
                ctmp = sb(st, "d_ctmp", [128, NB])
                sqb = [sb(st, "d_sqb%d" % i, [128, NB], BF16) for i in range(2)]
                sd = [sb(st, "d_sd%d" % i, [128, NB]) for i in range(2)]
                negA = sb(st, "d_negA", [128, 8])
                gt = sb(st, "d_gt", [128, 4, 16])
                g_ = sb(st, "d_g", [128, 4, 4])
                gR2 = [sb(st, "d_gR%d" % i, [128, 4, 4], F32R) for i in range(2)]
                beta2 = [sb(st, "d_beta%d" % i, [128, 4, 4]) for i in range(2)]
                nbeta2 = [sb(st, "d_nbeta%d" % i, [128, 4, 4]) for i in range(2)]
                of_ = sb(st, "d_of", [128, 4, NB])
                gz = sb(st, "d_gz", [128, 4, NB])
                oh = [sb(st, "d_oh%d" % i, [128, NB]) for i in range(2)]
                sqf = [sb(st, "d_sqf%d" % i, [128, NB], BF16) for i in range(2)]
                sdf = [sb(st, "d_sdf%d" % i, [128, NB]) for i in range(2)]
                obs = sb(st, "d_obs", [128, 4, NB], BF16)

                class CH:
                    pass
                chs = []
                for c in range(NCH):
                    ch = CH()
                    ch.c = c
                    ch.heads = list(range(c * NHC, (c + 1) * NHC))
                    n3 = [128, NHC, 128]
                    mk_ = lambda nm, dt_=F32, c=c, n3=n3: sb(st, "d%d_%s" % (c, nm), n3, dt_)
                    ch.Mh = mk_("Mh", F32R)
                    ch.Ecol = sb(st, "d%d_Ecol" % c, [128, 3, NHC])
                    ch.sc1 = sb(st, "d%d_sc1" % c, [128, NHC])
                    ch.Dc, ch.Dm, ch.attn, ch.AT = mk_("Dc", BF16), mk_("Dm", BF16), mk_("attn", BF16), mk_("AT", BF16)
                    ch.Nm, ch.NT = mk_("Nm", F32R), mk_("NT", F32R)
                    ch.Pb = [mk_("P%d" % i, F32R) for i in range(2)]
                    ch.PTb = [mk_("PT%d" % i, F32R) for i in range(2)]
                    ch.Rb = [mk_("R%d" % i, F32R) for i in range(2)]
                    ch.kbg, ch.vb, ch.nwT = mk_("kbg", F32R), mk_("vb", F32R), mk_("nwT", F32R)
                    ch.kd, ch.qd, ch.vnew = mk_("kd", BF16), mk_("qd", BF16), mk_("vnew", BF16)
                    ch.EG, ch.Stmp = mk_("EG"), mk_("Stmp")
                    ch.S, ch.Sb = mk_("S", F32R), mk_("Sb", BF16)
                    ch.ost2 = [sb(st, "d%d_ost%d" % (c, i), [128, NHC, NB]) for i in range(2)]
                    W_ = NHC * 128
                    assert NCH == 2 and W_ == 256
                    B0, B1, B2, B3 = [ps[4 * c + i] for i in range(4)]
                    ch.psD = ColBuf(B0, 0, 256)
                    ch.psPT = ColBuf(B0, 0, 256)
                    ch.psGc = ColBuf(B0, 256, 272)
                    ch.psAT = ColBuf(B0, 272, 400)
                    ch.psE = ColBuf(B1, 0, 256)
                    ch.psKV = ColBuf(B1, 256, 512)
                    ch.psG = ColBuf(B2, 0, 256)
                    ch.psA = ColBuf(B2, 256, 512)
                    ch.psT = ColBuf(B3, 0, 256)
                    ch.psS = ColBuf(B2, 256, 512)
                    chs.append(ch)
                psGT = ColBuf(ps[0], 400, 464)
                psNp = ColBuf(ps[3], 256, 512)
                psNf = ColBuf(ps[7], 256, 512)
                flat = lambda bf: bf.v(bf.t[:, :, :].rearrange("p h i -> p (h i)"))
                o_ = CP['alog'][0]
                E(ACT, 'activation', out=negA[:], in_=cpt[l][:, o_:o_ + 8], func=AF.Exp)
                E(DVE, 'tensor_scalar', out=negA[:], in0=negA[:], scalar1=-1.0, scalar2=None, op0=ALU.mult)

                def chain(ch, d, order, TRI, TRIR, TRICR, NMR, SM, slot):
                    qn, kn, vTb, gR, beta, nbeta = qn2[slot], kn2[slot], vTb2[slot], gR2[slot], beta2[slot], nbeta2[slot]
                    ost = ch.ost2[slot]
                    hs_ = ch.heads
                    h0 = hs_[0]
                    W_ = NHC * 128
                    for j in order:
                        cs = slice(j * 128, (j + 1) * 128)
                        gj = gR[:, j, h0:h0 + NHC]
                        gjf = asf(gj)
                        E(DVE, 'tensor_tensor', out=ch.Mh[:], in0=View(TRI.bufs, bc_mid(TRI.ap, NHC)),
                          in1=View(gjf.bufs, bc_last(gjf.ap, 128)), op=ALU.mult)
                        yield
                        for hi in range(NHC):
                            o2 = ch.psD.cols(hi * 128, (hi + 1) * 128)
                            kb.mm(o2, ch.Mh[:, hi, :], CR('ones'), start=True, stop=False)
                            kb.mm(o2, CR('negones'), ch.Mh[:, hi, :], start=False, stop=False)
                            kb.mm(o2, CR('ident'), NMR, start=False, stop=True)
                        kb.mm(ch.psGc.cols(0, NHC), TRIR, gj)
                        kb.mm(ch.psGc.cols(NHC, 2 * NHC), TRICR, gj)
                        kb.mm(ch.psGc.cols(2 * NHC, 3 * NHC), CR('ones'), gj)
                        for hi, h in enumerate(hs_):
                            kb.mm(ch.psG.cols(hi * 128, (hi + 1) * 128), kn[:, h, cs], kn[:, h, cs])
                            kb.mm(ch.psA.cols(hi * 128, (hi + 1) * 128), qn[:, h, cs], kn[:, h, cs])
                        for hi in range(NHC):
                            kb.mm(ch.psE.cols(hi * 128, (hi + 1) * 128), CR('ones'), ch.Mh[:, hi, :])
                        yield
                        E(ACT, 'activation', out=ch.Ecol.v(ch.Ecol.t[:, :, :].rearrange("p a h -> p (a h)")), in_=ch.psGc.cols(0, 3 * NHC), func=AF.Exp)
                        E(ACT, 'activation', out=flat(ch.Dc), in_=ch.psD.cols(), func=AF.Exp)
                        E(ACT, 'activation', out=flat(ch.EG), in_=ch.psE.cols(), func=AF.Exp)
                        yield
                        E(POOL, 'tensor_tensor', out=flat(ch.Dm), in0=flat(ch.Dc), in1=View(SM.bufs, SM.ap[:, 0:W_]), op=ALU.mult)
                        E(POOL, 'tensor_tensor', out=ch.qd[:], in0=qn[:, h0:h0 + NHC, cs], in1=ch.EG[:], op=ALU.mult)
                        yield
                        for hi, h in enumerate(hs_):
                            E(DVE, 'scalar_tensor_tensor', out=ch.Nm[:, hi, :], in0=ch.psG.cols(hi * 128, (hi + 1) * 128),
                              scalar=nbeta[:, j, h:h + 1], in1=ch.Dm[:, hi, :], op0=ALU.mult, op1=ALU.mult)
                        E(DVE, 'tensor_tensor', out=flat(ch.attn), in0=ch.psA.cols(), in1=flat(ch.Dc), op=ALU.mult)
                        yield
                        for hi in range(NHC):
                            kb.tr(View([ch.psT.bank], ch.psT.t[:, ch.psT.c0 + hi * 128:ch.psT.c0 + (hi + 1) * 128].bitcast(F32R)), ch.Nm[:, hi, :], CR('ident'))
                            kb.tr(ch.psAT.bf(hi * 128, (hi + 1) * 128), ch.attn[:, hi, :], CB('ident'))
                        for hi, h in enumerate(hs_):
                            kb.tr(ch.psKV.bf(hi * 128, (hi + 1) * 128), kn[:, h, cs], CB('ident'))
                            kb.tr(ch.psKV.bf(W_ + hi * 128, W_ + (hi + 1) * 128), vTb[:, h, cs], CB('ident'))
                        yield
                        E(ACT, 'copy', out=flat(ch.NT), in_=ch.psT.cols())
                        E(DVE, 'tensor_copy', out=flat(ch.AT), in_=ch.psAT.bf(0, W_))
                        E(DVE, 'tensor_tensor', out=ch.sc1[:], in0=beta[:, j, h0:h0 + NHC], in1=ch.Ecol[:, 0, :], op=ALU.mult)
                        kv3 = lambda a: View([ch.psKV.bank], ch.psKV.bf(a, a + W_).ap.rearrange("p (h i) -> p h i", h=NHC))
                        E(DVE, 'tensor_tensor', out=ch.kbg[:], in0=kv3(0), in1=ch.sc1.v(bc_last(ch.sc1.t[:, 0:NHC], 128)), op=ALU.mult)
                        E(DVE, 'tensor_tensor', out=ch.kd[:], in0=kv3(0), in1=ch.Ecol.v(bc_last(ch.Ecol.t[:, 1, :], 128)), op=ALU.mult)
                        E(DVE, 'tensor_tensor', out=ch.vb[:], in0=kv3(W_), in1=beta.v(bc_last(beta.t[:, j, h0:h0 + NHC], 128)), op=ALU.mult)
                        yield
                        P, PT = ch.Nm, ch.NT
                        R = ch.Rb[0]
                        ident_n = C('ident4')
                        E(DVE, 'tensor_tensor', out=flat(R), in0=asf(flat(ch.NT)), in1=View(ident_n.bufs, ident_n.ap[:, 0:W_]), op=ALU.add)
                        yield
                        for k in range(6):
                            Pn, PTn, Rn = ch.Pb[k % 2], ch.PTb[k % 2], ch.Rb[(k + 1) % 2]
                            for hi in range(NHC):
                                kb.mm(ch.psG.cols(hi * 128, (hi + 1) * 128), PT[:, hi, :], P[:, hi, :])
                            if k < 5:
                                for hi in range(NHC):
                                    kb.mm(ch.psPT.cols(hi * 128, (hi + 1) * 128), P[:, hi, :], PT[:, hi, :])
                            yield
                            E(ACT, 'copy', out=flat(Pn), in_=ch.psG.cols())
                            if k < 5:
                                E(DVE, 'tensor_copy', out=flat(PTn), in_=ch.psPT.cols())
                            yield
                            for hi in range(NHC):
                                kb.mm(ch.psT.cols(hi * 128, (hi + 1) * 128), Pn[:, hi, :], R[:, hi, :])
                            yield
                            E(DVE, 'tensor_tensor', out=flat(Rn), in0=ch.psT.cols(), in1=asf(flat(R)), op=ALU.add)
                            yield
                            P, PT, R = Pn, PTn, Rn
                        TT = R
                        for hi in range(NHC):
                            kb.mm(ch.psD.cols(hi * 128, (hi + 1) * 128), ch.kbg[:, hi, :], TT[:, hi, :])
                        yield
                        E(ACT, 'activation', out=flat(ch.nwT), in_=ch.psD.cols(), func=AF.Copy, scale=-1.0)
                        yield
                        for hi in range(NHC):
                            o2 = ch.psKV.cols(hi * 128, (hi + 1) * 128)
                            kb.mm(o2, TT[:, hi, :], ch.vb[:, hi, :], start=True, stop=False)
                            kb.mm(o2, ch.nwT[:, hi, :], ch.S[:, hi, :], start=False, stop=True)
                        yield
                        E(ACT, 'copy', out=flat(ch.vnew), in_=ch.psKV.cols())
                        yield
                        for hi in range(NHC):
                            o2 = ch.psE.cols(hi * 128, (hi + 1) * 128)
                            kb.mm(o2, ch.Sb[:, hi, :], ch.qd[:, hi, :], start=True, stop=False)
                            kb.mm(o2, ch.vnew[:, hi, :], ch.AT[:, hi, :], start=False, stop=True)
                        for hi in range(NHC):
                            kb.mm(ch.psS.cols(hi * 128, (hi + 1) * 128), ch.kd[:, hi, :], ch.vnew[:, hi, :])
                        yield
                        E(DVE, 'tensor_tensor', out=ch.Stmp[:], in0=asf(ch.S[:]), in1=ch.Ecol.v(bc_last(ch.Ecol.t[:, 2, :], 128)), op=ALU.mult)
                        E(DVE, 'tensor_tensor', out=flat(ch.S), in0=flat(ch.Stmp), in1=ch.psS.cols(), op=ALU.add)
                        E(ACT, 'copy', out=ch.Sb[:], in_=asf(ch.S[:]))
                        E(ACT, 'copy', out=ost[:, :, cs], in_=View([ch.psE.bank], ch.psE.cols().ap.rearrange("p (h i) -> p h i", h=NHC)))
                        yield

                for d in range(2):
                    TRI = C('tri_le') if d == 0 else C('tri_ge')
                    NMR = CR('nm_f') if d == 0 else CR('nm_b')
                    TRIR = CR('tri_le') if d == 0 else CR('tri_ge')
                    TRICR = CR('tri_gt') if d == 0 else CR('tri_lt')
                    SM = CB('sm_f4') if d == 0 else CB('sm_b4')
                    o_ = CP['dtb'][0]
                    dtb = cpt[l][:, o_ + d * 4:o_ + d * 4 + 4]
                    nAd = negA[:, d * 4:d * 4 + 4]
                    def prep_gen(b, slot):
                        t0 = b * NB
                        qn, kn, vTb, gR, beta, nbeta = qn2[slot], kn2[slot], vTb2[slot], gR2[slot], beta2[slot], nbeta2[slot]
                        load_halo(raw, DQ, 12, b)
                        kb.dma(abT[:], View([kb.db('hT%d' % AB, b)], hT[AB, 0:16, t0:t0 + NB]))
                        yield
                        for t in range(12):
                            if t % 2:
                                conv4(POOL, cv[:, t, :], raw, t, lambda k: cpv(l, 'dn_conv', t * 4 + k), tmp=ctmp[:, :])
                            else:
                                conv4(DVE, cv[:, t, :], raw, t, lambda k: cpv(l, 'dn_conv', t * 4 + k))
                            yield
                        for t3 in range(3):
                            E(ACT, 'activation', out=cv[:, t3 * 4:t3 * 4 + 4, :], in_=cv[:, t3 * 4:t3 * 4 + 4, :], func=AF.Silu)
                            yield
                        for t in range(8):
                            sq_, sd_ = sqb[t % 2], sd[t % 2]
                            E(ACT, 'activation', out=sq_[:, :], in_=cv[:, t, :], func=AF.Square)
                            for hf_ in range(2):
                                kb.mm(psNp.cols(), CB('ones'), sq_[:, hf_ * 256:(hf_ + 1) * 256])
                                E(DVE, 'tensor_scalar', out=sd_[:, hf_ * 256:(hf_ + 1) * 256], in0=psNp.cols(), scalar1=1e-6, scalar2=None, op0=ALU.add)
                            yield
                            E(ACT, 'activation', out=sd_[:, :], in_=sd_[:, :], func=AF.Ln)
                            E(ACT, 'activation', out=sd_[:, :], in_=sd_[:, :], func=AF.Exp, scale=-0.5)
                            dst = qn[:, t, :] if t < 4 else kn[:, t - 4, :]
                            E(DVE, 'scalar_tensor_tensor', out=dst, in0=cv[:, t, :], scalar=(128 ** -0.5 if t < 4 else 1.0),
                              in1=sd_[:, :], op0=ALU.mult, op1=ALU.mult)
                            yield
                        E(ACT, 'copy', out=vTb[:], in_=cv[:, 8:12, :])
                        yield
                        for j in range(4):
                            idt = C('ident')
                            kb.tr(psGT.cols(j * 16, (j + 1) * 16), abT[0:16, j * 128:(j + 1) * 128], View(idt.bufs, idt.ap[0:16, 0:16]))
                        E(ACT, 'copy', out=gt.v(gt.t[:, :, :].rearrange("p j c -> p (j c)")), in_=psGT.cols())
                        yield
                        E(DVE, 'tensor_tensor', out=g_[:], in0=gt[:, :, d * 4:d * 4 + 4], in1=View(dtb.bufs, bc_mid(dtb.ap, 4)), op=ALU.add)
                        E(ACT, 'activation', out=g_[:], in_=g_[:], func=AF.Exp)
                        E(DVE, 'tensor_scalar', out=g_[:], in0=g_[:], scalar1=1.0, scalar2=None, op0=ALU.add)
                        E(ACT, 'activation', out=g_[:], in_=g_[:], func=AF.Ln)
                        E(DVE, 'tensor_tensor', out=gR[:], in0=g_[:], in1=View(nAd.bufs, bc_mid(nAd.ap, 4)), op=ALU.mult)
                        E(ACT, 'activation', out=beta[:], in_=gt[:, :, 8 + d * 4:12 + d * 4], func=AF.Sigmoid)
                        E(DVE, 'tensor_scalar', out=nbeta[:], in0=beta[:], scalar1=-1.0, scalar2=None, op0=ALU.mult)
                        yield

                    def final_gen(b, slot):
                        t0 = b * NB
                        kb.dma(of_[:], View([kb.db('ofT%d' % c, b) for c in range(4)],
                                            ofT[0:4, :, t0:t0 + NB].rearrange("c p t -> p c t")))
                        kb.dma(gz[:], View([kb.db('hT%d' % (DZ + c), b) for c in range(4)],
                                           hT[DZ:DZ + 4, :, t0:t0 + NB].rearrange("c p t -> p c t")))
                        yield
                        E(ACT, 'activation', out=gz[:], in_=gz[:], func=AF.Silu)
                        yield
                        for h in range(4):
                            ch = chs[h // NHC]
                            hi = h % NHC
                            oo, sq_, sd_ = oh[h % 2], sqf[h % 2], sdf[h % 2]
                            E(DVE, 'tensor_tensor', out=oo[:, :], in0=ch.ost2[slot][:, hi, :], in1=of_[:, h, :], op=ALU.add)
                            E(ACT, 'activation', out=sq_[:, :], in_=oo[:, :], func=AF.Square)
                            for hf_ in range(2):
                                kb.mm(psNf.cols(), CB('ones'), sq_[:, hf_ * 256:(hf_ + 1) * 256])
                                E(DVE, 'tensor_scalar', out=sd_[:, hf_ * 256:(hf_ + 1) * 256], in0=psNf.cols(), scalar1=1.0 / 128, scalar2=1e-6,
                                  op0=ALU.mult, op1=ALU.add)
                            yield
                            E(ACT, 'activation', out=sd_[:, :], in_=sd_[:, :], func=AF.Ln)
                            E(ACT, 'activation', out=sd_[:, :], in_=sd_[:, :], func=AF.Exp, scale=-0.5)
                            E(POOL, 'tensor_tensor', out=oo[:, :], in0=oo[:, :], in1=sd_[:, :], op=ALU.mult)
                            E(DVE, 'scalar_tensor_tensor', out=obs[:, h, :], in0=oo[:, :], scalar=cpv(l, 'dn_nw'), in1=gz[:, h, :],
                              op0=ALU.mult, op1=ALU.mult)
                            yield
                        kb.dma(View([kb.db('obT', b)], obT[0:4, :, t0:t0 + NB].rearrange("c p t -> p c t")), obs[:])

                    blks = blk_order(d)
                    rr([prep_gen(blks[0], 0)])
                    pending = None
                    for bi, b in enumerate(blks):
                        t0 = b * NB
                        slot = bi % 2
                        for ch in chs:
                            if bi == 0:
                                E(DVE, 'memset', ap=asf(ch.S[:]), constant=0.0)
                                E(POOL, 'memset', ap=ch.Sb[:], constant=0.0)
                            elif crosses(b, d):
                                E(DVE, 'tensor_scalar', out=ch.S[:], in0=asf(ch.S[:]), scalar1=flag[:, 0:1], scalar2=None, op0=ALU.mult)
                                E(DVE, 'tensor_copy', out=ch.Sb[:], in_=asf(ch.S[:]))
                        order = list(range(4)) if d == 0 else list(range(3, -1, -1))
                        gens = [chain(ch, d, order, TRI, TRIR, TRICR, NMR, SM, slot) for ch in chs]
                        if bi + 1 < len(blks):
                            gens.append(prep_gen(blks[bi + 1], 1 - slot))
                        if pending is not None:
                            gens.append(final_gen(*pending))
                            pending = None
                        rr(gens)
                        if d == 0:
                            for ch in chs:
                                h0 = ch.heads[0]
                                kb.dma(View([kb.db('ofT%d' % c, b) for c in ch.heads],
                                            ofT[h0:h0 + NHC, :, t0:t0 + NB].rearrange("c p t -> p c t")), ch.ost2[slot][:])
                        else:
                            pending = (b, slot)
                    if pending is not None:
                        rr([final_gen(*pending)])
            kb.barrier()

        MIX = {'lru': phase_lru, 'hg': phase_hg, 'dn': phase_dn}
        run = phases if phases is not None else ['wprep', 'embed', 'proj', 'lru', 'hg', 'dn', 'post']
        if 'wprep' in run:
            phase_wprep()
        if 'embed' in run:
            phase_embed()
        for l in range(nlayers):
            if 'proj' in run:
                phase_proj(l)
            for m in ['lru', 'hg', 'dn']:
                if m in run and m in MIX:
                    MIX[m](l)
            if 'post' in run:
                phase_post(l, l == nlayers - 1)
        kb.barrier()
        print("instructions:", kb.ninst)
    return nc


def _colparams(W):
    L = DEPTH
    cp = np.zeros((L, 128, NCP), np.float32)

    def put(l, name, arr):
        o, w = CP[name]
        cp[l, :, o:o + w] = arr

    def chan(v, nt):
        return np.asarray(v).reshape(nt, 128).T
    for l in range(L):
        put(l, 'dn_conv', np.asarray(W['dn_conv_w'][l]).reshape(4, 12, 128).transpose(2, 1, 0).reshape(128, 48))
        put(l, 'lru_conv', np.asarray(W['lru_conv_w'][l]).reshape(4, 4, 128).transpose(2, 1, 0).reshape(128, 16))
        put(l, 'lru_cb', chan(W['lru_conv_b'][l], 4))
        put(l, 'lru_ba', np.asarray(W['lru_ba'][l]).reshape(2, 4, 128).transpose(2, 0, 1).reshape(128, 8))
        put(l, 'lru_bx', np.asarray(W['lru_bx'][l]).reshape(2, 4, 128).transpose(2, 0, 1).reshape(128, 8))
        put(l, 'lru_lam', np.asarray(W['lru_lambda'][l]).reshape(2, 4, 128).transpose(2, 0, 1).reshape(128, 8))
        put(l, 'dn_nw', np.asarray(W['dn_norm_w'][l]).reshape(128, 1))
        put(l, 'hg_nw', np.asarray(W['hg_norm_w'][l]).reshape(128, 1))
        for n in ['ln1_g', 'ln1_b', 'ln2_g', 'ln2_b']:
            put(l, n, chan(W[n][l], 8))
        put(l, 'lb0', chan(W['hg_lb_logits'][0], 4))
        put(l, 'lb1', chan(W['hg_lb_logits'][1], 4))
        put(l, 'emb_g', chan(W['emb_ln_g'], 8))
        put(l, 'emb_b', chan(W['emb_ln_b'], 8))
        put(l, 'alog', np.tile(np.asarray(W['dn_A_log'][l]).reshape(1, 8), (128, 1)))
        put(l, 'dtb', np.tile(np.asarray(W['dn_dt_bias'][l]).reshape(1, 8), (128, 1)))
    return cp


def _blockdiag(W):
    bd = np.zeros((DEPTH, 128, 16, 128), np.float32)
    for l in range(DEPTH):
        for ai, nm in enumerate(['lru_wa', 'lru_wx']):
            w = np.asarray(W[nm][l])
            for d in range(2):
                for t in range(4):
                    idx = ai * 8 + d * 4 + t
                    for s in range(2):
                        bd[l, s * 64:(s + 1) * 64, idx, s * 64:(s + 1) * 64] = w[d, 2 * t + s]
    return bd


def make_in_maps(W, xs, ps_, flags):
    cp = _colparams(W)
    bd = _blockdiag(W)
    common = dict(cst=CONST_ARR, cp=cp, bd=bd,
                  w_in=np.ascontiguousarray(W['w_in'], dtype=np.float32), w_branch=np.asarray(W['w_branch'], np.float32),
                  w_out=np.asarray(W['w_out'], np.float32), w_mlp1=np.asarray(W['w_mlp1'], np.float32),
                  w_mlp2=np.asarray(W['w_mlp2'], np.float32), w_ple_gate=np.asarray(W['w_ple_gate'], np.float32),
                  w_ple_proj=np.asarray(W['w_ple_proj'], np.float32))
    maps = []
    for x, p, f in zip(xs, ps_, flags):
        m = dict(common)
        m['x'] = np.ascontiguousarray(x, dtype=np.float32)
        m['p'] = np.ascontiguousarray(p, dtype=np.float32)
        m['flag'] = np.full((128, 1), f, np.float32)
        maps.append(m)
    return maps


_NC_CACHE = {}


def kernel(x_prompt, x_sample, p_prompt, p_sample, **W):
    x_prompt = np.asarray(x_prompt)
    x_sample = np.asarray(x_sample)
    p_prompt = np.asarray(p_prompt)
    p_sample = np.asarray(p_sample)
    T, SEG = 8192, 4096
    assign = [(0, 1), (2, 3), (4, 4), (5, 5), (6, 6), (7, 7)]
    xs, ps_, flags = [], [], []
    for c in range(2):
        xs.append(x_sample[c])
        ps_.append(p_sample[:, c])
        flags.append(1.0)
    for a, b in assign:
        xs.append(np.concatenate([x_prompt[a], x_prompt[b]], axis=0))
        ps_.append(np.concatenate([p_prompt[:, a], p_prompt[:, b]], axis=1))
        flags.append(0.0)
    if 'nc' not in _NC_CACHE:
        _NC_CACHE['nc'] = build(T, SEG)
    nc = _NC_CACHE['nc']
    maps = make_in_maps(W, xs, ps_, flags)
    res = run_bass_kernel_spmd(nc, maps, core_ids=list(range(NCORES)))
    y_prompt = np.zeros((8, 4096, D), np.float32)
    y_sample = np.zeros((2, 8192, D), np.float32)
    for c in range(2):
        y_sample[c] = res.results[c]['y']
    for i, (a, b) in enumerate(assign):
        y = res.results[2 + i]['y']
        y_prompt[a] = y[:4096]
        if b != a:
            y_prompt[b] = y[4096:]
    return (y_prompt, y_sample)
```

```python
import numpy as np
from contextlib import ExitStack
import concourse.bass as bass
import concourse.mybir as mybir
from concourse.bass_utils import run_bass_kernel_spmd

F32 = mybir.dt.float32
BF16 = mybir.dt.bfloat16
F32R = mybir.dt.float32r
AF = mybir.ActivationFunctionType
ALU = mybir.AluOpType

D = 1024
NIN = 8720
DFF = 4096
DPLE = 256
DEPTH = 2
ALPHA = (2.0 * DEPTH) ** 0.25
NB = 512
NCORES = 8
DQ, DK, DV, DZ, HQ, HFF, HFB, HI, HGT, CX, CG, GA, GB, GC, AB = 0, 4, 8, 12, 16, 20, 24, 28, 32, 36, 40, 44, 52, 60, 68
NHT = 69
WIN_TILES = [(i * 128, 128) for i in range(16)] + [(2064 + 128 * i, 128) for i in range(52)] + [(2048, 16)]

def _consts():
    i = np.arange(128)
    c = {}
    c['ident'] = np.eye(128)
    c['ones'] = np.ones((128, 128))
    c['negones'] = -np.ones((128, 128))
    c['tri_le'] = (i[:, None] <= i[None, :]) * 1.0
    c['tri_ge'] = (i[:, None] >= i[None, :]) * 1.0
    c['tri_gt'] = (i[:, None] > i[None, :]) * 1.0
    c['tri_lt'] = (i[:, None] < i[None, :]) * 1.0
    c['nm_f'] = np.where(i[:, None] < i[None, :], -30000.0, 0.0)
    c['nm_b'] = np.where(i[:, None] > i[None, :], -30000.0, 0.0)
    c['sm_f4'] = np.tile((i[:, None] > i[None, :]) * 1.0, (1, 4))
    c['sm_b4'] = np.tile((i[:, None] < i[None, :]) * 1.0, (1, 4))
    c['ident4'] = np.tile(np.eye(128), (1, 4))
    j = np.arange(64)
    c['hgm_f4'] = np.tile((j[None, :] >= (i[:, None] % 64)) * 1.0, (1, 4))
    c['hgm_b4'] = np.tile((j[None, :] <= (i[:, None] % 64)) * 1.0, (1, 4))
    t = np.arange(NB)
    c['rst_f'] = np.tile(((t % 64) != 0) * 1.0, (128, 1))
    c['rst_b'] = np.tile(((t % 64) != 63) * 1.0, (128, 1))
    offs = {}
    o = 0
    arrs = []
    for k, v in c.items():
        offs[k] = (o, v.shape[1])
        o += v.shape[1]
        arrs.append(v.astype(np.float32))
    return np.ascontiguousarray(np.concatenate(arrs, axis=1)), offs


CONST_ARR, CONST_OFF = _consts()
CP = {}
_o = 0
for _n, _w in [('dn_conv', 48), ('lru_conv', 16), ('lru_cb', 4), ('lru_ba', 8), ('lru_bx', 8), ('lru_lam', 8),
               ('dn_nw', 1), ('hg_nw', 1), ('ln1_g', 8), ('ln1_b', 8), ('ln2_g', 8), ('ln2_b', 8),
               ('lb0', 4), ('lb1', 4), ('emb_g', 8), ('emb_b', 8), ('alog', 8), ('dtb', 8)]:
    CP[_n] = (_o, _w)
    _o += _w
NCP = _o


class View:
    def __init__(self, bufs, ap):
        self.bufs = bufs
        self.ap = ap

    def __getitem__(self, idx):
        return View(self.bufs, self.ap[idx])


class Buf:
    def __init__(self, t=None, name=""):
        self.t = t
        self.w = {}
        self.r = {}
        self.name = name

    def __getitem__(self, idx):
        return View([self], self.t[idx])

    def v(self, ap):
        return View([self], ap)


class Eng:
    def __init__(self, h, sid, name):
        self.h = h
        self.sid = sid
        self.cnt = 0
        self.waited = {}
        self.name = name
        self.old = []


class KB:
    NDS = 48
    EPOCH = 20000

    def __init__(self, nc, stack):
        self.nc = nc
        self.sems = {}
        self.nsid = 0

        def mk(n):
            s = stack.enter_context(nc.semaphore(n))
            sid = self.nsid
            self.nsid += 1
            self.sems[sid] = s
            return sid
        self.mk = mk
        self.PE = Eng(nc.tensor, mk("s_pe"), "pe")
        self.DVE = Eng(nc.vector, mk("s_dve"), "dve")
        self.ACT = Eng(nc.scalar, mk("s_act"), "act")
        self.POOL = Eng(nc.gpsimd, mk("s_pool"), "pool")
        self.SP = Eng(nc.sync, None, "sp")
        self.engs = [self.PE, self.DVE, self.ACT, self.POOL]
        self.dsid = [mk("s_dma%d" % i) for i in range(self.NDS)]
        self.dval = [0] * self.NDS
        self.dnext = 0
        self.dbufs = {}
        self.ninst = 0

    def db(self, name, blk):
        k = (name, blk)
        if k not in self.dbufs:
            self.dbufs[k] = Buf(None, "%s_%s" % (name, blk))
        return self.dbufs[k]

    def _wait(self, eng, sid, val):
        if val <= 0 or eng.waited.get(sid, 0) >= val:
            return
        eng.h.wait_ge(self.sems[sid], val)
        eng.waited[sid] = val

    def _sync(self, eng, reads, writes):
        own = eng.sid
        for b in reads:
            for sid, v in b.w.items():
                self._wait(eng, sid, v)
        for b in writes:
            for sid, v in b.w.items():
                if sid != own:
                    self._wait(eng, sid, v)
            for sid, v in b.r.items():
                if sid != own:
                    self._wait(eng, sid, v)

    def _mark(self, reads, writes, sid, val):
        for b in reads:
            b.r[sid] = max(b.r.get(sid, 0), val)
        for b in writes:
            b.w = {sid: val}
            b.r = {}

    def E(self, eng, meth, **kw):
        reads, writes, args = [], [], {}
        for k_, v in kw.items():
            if isinstance(v, View):
                if k_ in ('out', 'accum_out', 'ap') or any(getattr(b_, 'excl', False) for b_ in v.bufs):
                    writes.extend(v.bufs)
                else:
                    reads.extend(v.bufs)
                args[k_] = v.ap
            else:
                args[k_] = v
        if eng.cnt >= self.EPOCH:
            eng.old.append((eng.sid, eng.cnt))
            eng.sid = self.mk("s_%s_e%d" % (eng.name, len(eng.old)))
            eng.cnt = 0
        self._sync(eng, reads, writes)
        inst = getattr(eng.h, meth)(**args)
        inst.then_inc(self.sems[eng.sid], 1)
        eng.cnt += 1
        self.ninst += 1
        self._mark(reads, writes, eng.sid, eng.cnt)
        return inst

    def mm(self, out, lhsT, rhs, start=True, stop=True, extra_w=()):
        return self.E(self.PE, 'matmul', out=out, lhsT=lhsT, rhs=rhs, start=start, stop=stop)

    def tr(self, out, in_, ident):
        return self.E(self.PE, 'transpose', out=out, in_=in_, identity=ident)

    def dma(self, out, in_, q=None):
        eng = self.SP
        s = self.dnext
        self.dnext = (self.dnext + 1) % self.NDS
        sid = self.dsid[s]
        self._wait(eng, sid, self.dval[s])
        self._sync(eng, in_.bufs, out.bufs)
        self.dval[s] += 16
        eng.h.dma_start(out=out.ap, in_=in_.ap).then_inc(self.sems[sid], 16)
        self.ninst += 1
        self._mark(in_.bufs, out.bufs, sid, self.dval[s])

    def barrier(self):
        for e in self.engs + [self.SP]:
            for o in self.engs:
                if o is not e:
                    if o.cnt > 0:
                        self._wait(e, o.sid, o.cnt)
                    elif o.old:
                        self._wait(e, o.old[-1][0], o.old[-1][1])
            for s in range(self.NDS):
                self._wait(e, self.dsid[s], self.dval[s])


def bc_last(ap, n):
    l = [list(x) for x in ap.ap]
    return bass.AP(ap.tensor, ap.offset, l + [[0, n]])


def bc_mid(ap, n):
    l = [list(x) for x in ap.ap]
    return bass.AP(ap.tensor, ap.offset, [l[0], [0, n]] + l[1:])


def build(T, SEG, phases=None, debug_out=(), nlayers=DEPTH):
    NBLK = T // NB
    nc = bass.Bass("TRN2", target_bir_lowering=False)
    dt = nc.dram_tensor
    x_in = dt("x", [T, D], F32, kind="ExternalInput").ap()
    p_in = dt("p", [DEPTH, T, DPLE], F32, kind="ExternalInput").ap()
    cst_in = dt("cst", list(CONST_ARR.shape), F32, kind="ExternalInput").ap()
    cp_in = dt("cp", [DEPTH, 128, NCP], F32, kind="ExternalInput").ap()
    bd_in = dt("bd", [DEPTH, 128, 16, 128], F32, kind="ExternalInput").ap()
    flag_in = dt("flag", [128, 1], F32, kind="ExternalInput").ap()
    w_in = dt("w_in", [DEPTH, D, NIN], F32, kind="ExternalInput").ap()
    w_branch = dt("w_branch", [DEPTH, 3, 512, D], F32, kind="ExternalInput").ap()
    w_out = dt("w_out", [DEPTH, D, D], F32, kind="ExternalInput").ap()
    w_mlp1 = dt("w_mlp1", [DEPTH, D, DFF], F32, kind="ExternalInput").ap()
    w_mlp2 = dt("w_mlp2", [DEPTH, DFF, D], F32, kind="ExternalInput").ap()
    w_pg = dt("w_ple_gate", [DEPTH, D, D], F32, kind="ExternalInput").ap()
    w_pp = dt("w_ple_proj", [DEPTH, DPLE, D], F32, kind="ExternalInput").ap()
    y_out = dt("y", [T, D], F32, kind="ExternalOutput").ap()
    dbg = {}
    okind = lambda n: "ExternalOutput" if n in debug_out else "Internal"
    xT = dt("xT", [8, 128, T], F32, kind=okind("xT")).ap()
    class _HT:
        SPLIT = 36

        def __init__(self):
            self.a = dt("hT", [self.SPLIT, 128, T], F32, kind=okind("hT")).ap()
            self.b = dt("hTb", [NHT - self.SPLIT, 128, T], F32, kind=okind("hT")).ap()

        def __getitem__(self, idx):
            f = idx[0]
            rest = tuple(idx[1:])
            if isinstance(f, slice):
                if f.start < self.SPLIT:
                    assert f.stop <= self.SPLIT
                    return self.a[(f,) + rest]
                return self.b[(slice(f.start - self.SPLIT, f.stop - self.SPLIT),) + rest]
            if f < self.SPLIT:
                return self.a[(f,) + rest]
            return self.b[(f - self.SPLIT,) + rest]
    hT = _HT()
    ofT = dt("ofT", [12, 128, T], F32, kind=okind("ofT")).ap()
    obT = dt("obT", [12, 128, T], BF16, kind=okind("obT")).ap()
    wspec = {
        'win': (8, WIN_TILES, lambda l: w_in[l]),
        'wb0': (4, [(i * 128, 128) for i in range(8)], lambda l: w_branch[l, 0]),
        'wb1': (4, [(i * 128, 128) for i in range(8)], lambda l: w_branch[l, 1]),
        'wb2': (4, [(i * 128, 128) for i in range(8)], lambda l: w_branch[l, 2]),
        'wout': (8, [(i * 128, 128) for i in range(8)], lambda l: w_out[l]),
        'wm1': (8, [(i * 128, 128) for i in range(32)], lambda l: w_mlp1[l]),
        'wm2': (32, [(i * 128, 128) for i in range(8)], lambda l: w_mlp2[l]),
        'wpg': (8, [(i * 128, 128) for i in range(8)], lambda l: w_pg[l]),
        'wpp': (2, [(i * 128, 128) for i in range(8)], lambda l: w_pp[l]),
    }
    ws = {n: dt("ws_" + n, [DEPTH, len(s[1]), 128, s[0], 128], BF16, kind="Internal").ap() for n, s in wspec.items()}

    with ExitStack() as st0:
        kb = KB(nc, st0)
        PE, DVE, ACT, POOL = kb.PE, kb.DVE, kb.ACT, kb.POOL
        E = kb.E

        uniq = [0]

        def sb(st, name, shape, dtype=F32):
            uniq[0] += 1
            return Buf(st.enter_context(nc.sbuf_tensor("sb%d_%s" % (uniq[0], name), shape, dtype)), name)

        ps = [Buf(st0.enter_context(nc.psum_tensor("ps%d" % i, [128, 512], F32)), "ps%d" % i) for i in range(8)]
        for b_ in ps:
            b_.excl = True
        cst = sb(st0, "cst", list(CONST_ARR.shape))
        cpt = [sb(st0, "cp%d" % l, [128, NCP]) for l in range(DEPTH)]
        flag = sb(st0, "flag", [128, 1])
        cbf = sb(st0, "cbf", [128, 128 * 2 + 512 * 3 + 256 * 2], BF16)
        kb.dma(cst[:], View([], cst_in[:, :]))
        for l in range(DEPTH):
            kb.dma(cpt[l][:], View([], cp_in[l]))
        kb.dma(flag[:], View([], flag_in[:, :]))

        def C(name):
            o, w = CONST_OFF[name]
            return cst[:, o:o + w]

        cb_off = {}
        o = 0
        for n in ['ident', 'ones', 'sm_f4', 'sm_b4', 'ident4', 'hgm_f4', 'hgm_b4']:
            w = CONST_OFF[n][1]
            cb_off[n] = (o, w)
            E(DVE, 'tensor_copy', out=cbf[:, o:o + w], in_=C(n))
            o += w

        def CB(name):
            o, w = cb_off[name]
            return cbf[:, o:o + w]

        crn = ['ident', 'ones', 'negones', 'tri_le', 'tri_ge', 'tri_gt', 'tri_lt', 'nm_f', 'nm_b']
        crt = sb(st0, "crt", [128, 128 * len(crn)], F32R)
        for i_, n in enumerate(crn):
            E(DVE, 'tensor_copy', out=crt[:, i_ * 128:(i_ + 1) * 128], in_=C(n))

        def CR(name):
            i_ = crn.index(name)
            return crt[:, i_ * 128:(i_ + 1) * 128]

        def asf(v):
            return View(v.bufs, v.ap.bitcast(F32))

        def cpv(l, name, i=0, n=1):
            o, w = CP[name]
            return cpt[l][:, o + i:o + i + n]

        def phase_wprep():
            with ExitStack() as st:
                sf = [sb(st, "wpf%d" % i, [128, 4096]) for i in range(2)]
                sbf = [sb(st, "wpb%d" % i, [128, 4096], BF16) for i in range(2)]
                it = 0
                for l in range(DEPTH):
                    for name, (n_k, tiles, srcf) in wspec.items():
                        src = srcf(l).rearrange("(k p) n -> p k n", p=128)
                        gmax = 4 if n_k <= 8 else 1
                        ti = 0
                        while ti < len(tiles):
                            g = 1
                            while (g < gmax and ti + g < len(tiles) and tiles[ti + g][1] == 128 and tiles[ti][1] == 128
                                   and tiles[ti + g][0] == tiles[ti][0] + 128 * g):
                                g += 1
                            c0 = tiles[ti][0]
                            gw = sum(tiles[ti + j][1] for j in range(g))
                            s = it % 2
                            it += 1
                            fv = sf[s].t[:, 0:n_k * gw].rearrange("p (k c) -> p k c", k=n_k)
                            bv = sbf[s].t[:, 0:n_k * gw].rearrange("p (k c) -> p k c", k=n_k)
                            kb.dma(sf[s].v(fv), View([], src[:, :, c0:c0 + gw]))
                            E(POOL if it % 2 else ACT, 'tensor_copy' if it % 2 else 'copy', out=sbf[s].v(bv), in_=sf[s].v(fv))
                            for j in range(g):
                                wdt = tiles[ti + j][1]
                                kb.dma(View([kb.db('ws_' + name, l)], ws[name][l, ti + j, :, :, 0:wdt]),
                                       sbf[s].v(bv[:, :, j * 128:j * 128 + wdt]))
                            ti += g
            kb.barrier()

        wctr = [0]
        pctr = [0]

        def dense(wb, l, name, tile_ids, rhs_fn, N, evac, psbanks=(0, 1, 2)):
            n_k = wspec[name][0]
            tiles = wspec[name][1]
            tile_ids = list(tile_ids)
            depth = len(wb) - 1
            base = wctr[0]
            wctr[0] += len(tile_ids)

            def wload(i):
                kb.dma(wb[(base + i) % len(wb)][:, 0:n_k, :], View([kb.db('ws_' + name, l)], ws[name][l, tile_ids[i]]))
            for i in range(min(depth, len(tile_ids))):
                wload(i)
            for i, ti in enumerate(tile_ids):
                wdt = tiles[ti][1]
                s = (base + i) % len(wb)
                if i + depth < len(tile_ids):
                    wload(i + depth)
                pb = ps[psbanks[pctr[0] % len(psbanks)]]
                pctr[0] += 1
                for k in range(n_k):
                    kb.mm(pb[:, 0:N], wb[s][:, k, :], rhs_fn(k), start=(k == 0), stop=(k == n_k - 1))
                evac(ti, pb[0:wdt, 0:N], wdt)

        def layer_norm(st_bufs, src, g_fn, b_fn, dst_f=None, dst_b=None, N=NB):
            sq, xb_, mean, m2, rstd, tmp = st_bufs
            pm, pq = ps[3], ps[4]
            for c in range(8):
                E(ACT, 'activation', out=sq[c % 2][:, 0:N], in_=src[:, c, 0:N], func=AF.Square)
                E(DVE, 'tensor_copy', out=xb_[c % 2][:, 0:N], in_=src[:, c, 0:N])
                kb.mm(pm[:, 0:N], CB('ones'), xb_[c % 2][:, 0:N], start=(c == 0), stop=(c == 7))
                kb.mm(pq[:, 0:N], CB('ones'), sq[c % 2][:, 0:N], start=(c == 0), stop=(c == 7))
            E(ACT, 'activation', out=mean[:, 0:N], in_=pm[:, 0:N], func=AF.Copy, scale=1.0 / D)
            E(DVE, 'tensor_tensor', out=m2[:, 0:N], in0=mean[:, 0:N], in1=mean[:, 0:N], op=ALU.mult)
            E(DVE, 'scalar_tensor_tensor', out=m2[:, 0:N], in0=pq[:, 0:N], scalar=1.0 / D, in1=m2[:, 0:N],
              op0=ALU.mult, op1=ALU.subtract)
            E(DVE, 'tensor_scalar', out=m2[:, 0:N], in0=m2[:, 0:N], scalar1=0.0, scalar2=1e-5, op0=ALU.max, op1=ALU.add)
            E(ACT, 'activation', out=m2[:, 0:N], in_=m2[:, 0:N], func=AF.Ln)
            E(ACT, 'activation', out=rstd[:, 0:N], in_=m2[:, 0:N], func=AF.Exp, scale=-0.5)
            for c in range(8):
                t_ = tmp[c % 2]
                E(DVE, 'tensor_tensor', out=t_[:, 0:N], in0=src[:, c, 0:N], in1=mean[:, 0:N], op=ALU.subtract)
                E(DVE, 'tensor_tensor', out=t_[:, 0:N], in0=t_[:, 0:N], in1=rstd[:, 0:N], op=ALU.mult)
                if dst_f is not None:
                    E(ACT, 'activation', out=dst_f[:, c, 0:N], in_=t_[:, 0:N], func=AF.Identity, scale=g_fn(c), bias=b_fn(c))
                if dst_b is not None:
                    E(ACT, 'activation', out=dst_b[:, c, 0:N], in_=t_[:, 0:N], func=AF.Identity, scale=g_fn(c), bias=b_fn(c))

        def ln_bufs(st):
            return ([sb(st, "ln_sq%d" % i, [128, NB], BF16) for i in range(2)],
                    [sb(st, "ln_xb%d" % i, [128, NB], BF16) for i in range(2)],
                    sb(st, "ln_mean", [128, NB]), sb(st, "ln_m2", [128, NB]), sb(st, "ln_rstd", [128, NB]),
                    [sb(st, "ln_tmp%d" % i, [128, NB]) for i in range(2)])

        def phase_embed():
            with ExitStack() as st:
                xin = [sb(st, "e_xin%d" % i, [128, 4, D]) for i in range(2)]
                xf = sb(st, "e_xf", [128, 8, NB])
                xo = sb(st, "e_xo", [128, 8, NB])
                lb = ln_bufs(st)
                for b in range(NBLK):
                    t0 = b * NB
                    xi = xin[b % 2]
                    kb.dma(xi[:], View([], x_in[t0:t0 + NB, :].rearrange("(j p) d -> p j d", p=128)))
                    for c in range(8):
                        pb = ps[c % 3]
                        for j in range(4):
                            kb.tr(pb[:, j * 128:(j + 1) * 128], xi[:, j, c * 128:(c + 1) * 128], C('ident'))
                        E(ACT if c % 2 else DVE, 'copy' if c % 2 else 'tensor_copy', out=xf[:, c, :], in_=pb[:, :])
                    layer_norm(lb, xf, lambda c: cpv(0, 'emb_g', c), lambda c: cpv(0, 'emb_b', c), dst_f=xo)
                    kb.dma(View([kb.db('xT', b)], xT[:, :, t0:t0 + NB].rearrange("c p t -> p c t")), xo[:])
            kb.barrier()

        def phase_proj(l):
            with ExitStack() as st:
                xin = [sb(st, "p_xin%d" % i, [128, 8, NB]) for i in range(2)]
                xb_ = [sb(st, "p_xb%d" % i, [128, 8, NB], BF16) for i in range(2)]
                stg = [sb(st, "p_stg%d" % i, [128, NB]) for i in range(6)]
                wb = [sb(st, "p_wb%d" % i, [128, 8, 128], BF16) for i in range(4)]
                sctr = [0]
                kb.dma(xin[0][:], View([kb.db('xT', 0)], xT[:, :, 0:NB].rearrange("c p t -> p c t")))
                for b in range(NBLK):
                    t0 = b * NB
                    xi, xbb = xin[b % 2], xb_[b % 2]
                    if b + 1 < NBLK:
                        kb.dma(xin[(b + 1) % 2][:], View([kb.db('xT', b + 1)], xT[:, :, t0 + NB:t0 + 2 * NB].rearrange("c p t -> p c t")))
                    E(ACT, 'copy', out=xbb[:, 0:4, :], in_=xi[:, 0:4, :])
                    E(DVE, 'tensor_copy', out=xbb[:, 4:8, :], in_=xi[:, 4:8, :])

                    def evac(ti, pv, wdt):
                        sg_ = stg[sctr[0] % len(stg)]
                        sctr[0] += 1
                        if GA <= ti < AB:
                            E(ACT, 'activation', out=sg_[0:wdt, :], in_=pv, func=AF.Sigmoid)
                        elif sctr[0] % 2:
                            E(DVE, 'tensor_copy', out=sg_[0:wdt, :], in_=pv)
                        else:
                            E(ACT, 'copy', out=sg_[0:wdt, :], in_=pv)
                        kb.dma(View([kb.db('hT%d' % ti, b)], hT[ti, 0:wdt, t0:t0 + NB]), sg_[0:wdt, :])
                    dense(wb, l, 'win', range(NHT), lambda k: xbb[:, k, :], NB, evac)
            kb.barrier()

        def phase_post(l, last):
            with ExitStack() as st:
                xres = sb(st, "q_xres", [128, 8, NB])
                hid = sb(st, "q_hid", [128, 32, NB], BF16)
                sig = sb(st, "q_sig", [128, 8, NB])
                acc = sb(st, "q_acc", [128, 8 * NB])
                accb = sb(st, "q_accb", [128, 8, NB], BF16)
                x1 = sb(st, "q_x1", [128, 8, NB])
                x1b = sb(st, "q_x1b", [128, 8, NB], BF16)
                rl = [sb(st, "q_rl%d" % i, [128, NB]) for i in range(2)]
                pin = sb(st, "q_pin", [128, 4, DPLE])
                pT = sb(st, "q_pT", [128, 2, NB], BF16)
                wb = [sb(st, "q_wb%d" % i, [128, 32, 128], BF16) for i in range(3)]
                lb = ln_bufs(st)
                acc3 = acc.v(acc.t[:, :].rearrange("p (c t) -> p c t", c=8))
                accv = lambda c: acc.v(acc.t[:, c * NB:(c + 1) * NB])
                for b in range(NBLK):
                    t0 = b * NB
                    kb.dma(xres[:], View([kb.db('xT', b)], xT[:, :, t0:t0 + NB].rearrange("c p t -> p c t")))
                    kb.dma(hid[:, 0:12, :], View([kb.db('obT', b)], obT[:, :, t0:t0 + NB].rearrange("c p t -> p c t")))
                    kb.dma(pin[:], View([], p_in[l, t0:t0 + NB, :].rearrange("(j p) d -> p j d", p=128)))
                    for c in range(2):
                        pb = ps[5 + c]
                        for j in range(4):
                            kb.tr(pb[:, j * 128:(j + 1) * 128], pin[:, j, c * 128:(c + 1) * 128], C('ident'))
                        E(ACT, 'copy', out=pT[:, c, :], in_=pb[:, :])
                    for n in range(3):
                        gt0 = [GA, GB, GC][n]
                        kb.dma(sig[:], View([kb.db('hT%d' % (gt0 + c), b) for c in range(8)],
                                            hT[gt0:gt0 + 8, :, t0:t0 + NB].rearrange("c p t -> p c t")))

                        def evac(ti, pv, wdt, n=n):
                            if n == 0:
                                E(DVE, 'tensor_tensor', out=accv(ti), in0=pv, in1=sig[:, ti, :], op=ALU.mult)
                            else:
                                r_ = rl[ti % 2]
                                E(DVE, 'tensor_tensor', out=r_[:, :], in0=pv, in1=sig[:, ti, :], op=ALU.mult)
                                if n == 1:
                                    E(POOL, 'tensor_tensor', out=accv(ti), in0=accv(ti), in1=r_[:, :], op=ALU.add)
                                else:
                                    E(POOL, 'tensor_tensor', out=accb[:, ti, :], in0=accv(ti), in1=r_[:, :], op=ALU.add)
                        dense(wb, l, 'wb%d' % n, range(8), lambda k, n=n: hid[:, n * 4 + k, :], NB, evac)

                    def evac(ti, pv, wdt):
                        E(DVE, 'scalar_tensor_tensor', out=accv(ti), in0=xres[:, ti, :], scalar=ALPHA, in1=pv,
                          op0=ALU.mult, op1=ALU.add)
                    dense(wb, l, 'wout', range(8), lambda k: accb[:, k, :], NB, evac)
                    layer_norm(lb, acc3, lambda c: cpv(l, 'ln1_g', c), lambda c: cpv(l, 'ln1_b', c), dst_f=x1, dst_b=x1b)

                    def evac(ti, pv, wdt):
                        r_ = rl[ti % 2]
                        E(ACT, 'activation', out=r_[:, :], in_=pv, func=AF.Relu)
                        E(POOL if ti % 2 else DVE, 'tensor_tensor', out=hid[:, ti, :], in0=r_[:, :], in1=r_[:, :], op=ALU.mult)
                    dense(wb, l, 'wm1', range(32), lambda k: x1b[:, k, :], NB, evac)

                    def evac(ti, pv, wdt):
                        E(ACT, 'activation', out=sig[:, ti, :], in_=pv, func=AF.Sigmoid)
                    dense(wb, l, 'wpg', range(8), lambda k: x1b[:, k, :], NB, evac)

                    def evac(ti, pv, wdt):
                        E(DVE, 'tensor_tensor', out=sig[:, ti, :], in0=pv, in1=sig[:, ti, :], op=ALU.mult)
                    dense(wb, l, 'wpp', range(8), lambda k: pT[:, k, :], NB, evac)

                    def evac(ti, pv, wdt):
                        E(DVE, 'scalar_tensor_tensor', out=accv(ti), in0=x1[:, ti, :], scalar=ALPHA, in1=pv,
                          op0=ALU.mult, op1=ALU.add)
                        E(POOL, 'tensor_tensor', out=accv(ti), in0=accv(ti), in1=sig[:, ti, :], op=ALU.add)
                    dense(wb, l, 'wm2', range(8), lambda k: hid[:, k, :], NB, evac)
                    layer_norm(lb, acc3, lambda c: cpv(l, 'ln2_g', c), lambda c: cpv(l, 'ln2_b', c), dst_f=xres)
                    if not last:
                        kb.dma(View([kb.db('xT', b)], xT[:, :, t0:t0 + NB].rearrange("c p t -> p c t")), xres[:])
                    else:
                        ov = x1.v(x1.t[:, :, :].rearrange("p c t -> p (c t)").rearrange("p (j d) -> p j d", j=4))
                        for j in range(4):
                            for c2 in range(2):
                                pb = ps[5 + c2]
                                for cc in range(4):
                                    c = c2 * 4 + cc
                                    kb.tr(pb[:, cc * 128:(cc + 1) * 128], xres[:, c, j * 128:(j + 1) * 128], C('ident'))
                                E(ACT if c2 else DVE, 'copy' if c2 else 'tensor_copy',
                                  out=x1.v(ov.ap[:, j, c2 * 512:(c2 + 1) * 512]), in_=pb[:, :])
                        kb.dma(View([], y_out[t0:t0 + NB, :].rearrange("(j p) d -> p j d", p=128)), ov)
            kb.barrier()

        SEGB = SEG // NB

        def blk_order(d):
            return list(range(NBLK)) if d == 0 else list(range(NBLK - 1, -1, -1))

        def load_halo(raw, tile0, nt, b):
            t0 = b * NB
            lo, hi = max(t0 - 1, 0), min(t0 + NB + 2, T)
            if lo > t0 - 1:
                E(POOL, 'memset', ap=raw[:, :, 0:1], constant=0.0)
            if hi < t0 + NB + 2:
                E(POOL, 'memset', ap=raw[:, :, NB + 1:NB + 3], constant=0.0)
            bl = [kb.db('hT%d' % (tile0 + c), bb) for c in range(nt) for bb in (b - 1, b, b + 1) if 0 <= bb < NBLK]
            kb.dma(raw[:, :, lo - (t0 - 1):hi - (t0 - 1)], View(bl, hT[tile0:tile0 + nt, :, lo:hi].rearrange("c p t -> p c t")))
            if b % SEGB == 0 and b > 0:
                E(DVE, 'tensor_scalar', out=raw[:, :, 0:1], in0=raw[:, :, 0:1], scalar1=flag[:, 0:1], scalar2=None, op0=ALU.mult)
            if b % SEGB == SEGB - 1 and b < NBLK - 1:
                E(DVE, 'tensor_scalar', out=raw[:, :, NB + 1:NB + 3], in0=raw[:, :, NB + 1:NB + 3], scalar1=flag[:, 0:1],
                  scalar2=None, op0=ALU.mult)

        def conv4(eng, out, raw, t, wcol, bias=None, tmp=None):
            if bias is None:
                E(eng, 'tensor_scalar', out=out, in0=raw[:, t, 0:NB], scalar1=wcol(0), scalar2=0.0, op0=ALU.mult, op1=ALU.add)
            else:
                E(eng, 'tensor_scalar', out=out, in0=raw[:, t, 0:NB], scalar1=wcol(0), scalar2=bias, op0=ALU.mult, op1=ALU.add)
            for k in range(1, 4):
                if tmp is not None:
                    E(POOL, 'tensor_scalar', out=tmp, in0=raw[:, t, k:k + NB], scalar1=wcol(k), scalar2=0.0, op0=ALU.mult, op1=ALU.add)
                    E(POOL, 'tensor_tensor', out=out, in0=out, in1=tmp, op=ALU.add)
                else:
                    E(DVE, 'scalar_tensor_tensor', out=out, in0=raw[:, t, k:k + NB], scalar=wcol(k), in1=out, op0=ALU.mult, op1=ALU.add)

        def crosses(b, d):
            return (d == 0 and b > 0 and b % SEGB == 0) or (d == 1 and b < NBLK - 1 and b % SEGB == SEGB - 1)

        def phase_lru(l):
            with ExitStack() as st:
                bdf = sb(st, "l_bdf", [128, 16, 128])
                bdb = sb(st, "l_bdb", [128, 16, 128], BF16)
                ccol = sb(st, "l_ccol", [128, 8])
                raw = [sb(st, "l_raw%d" % i, [128, 4, NB + 3]) for i in range(2)]
                xc = sb(st, "l_xc", [128, 4, NB])
                xcb = sb(st, "l_xcb", [128, 4, NB], BF16)
                r_ = [sb(st, "l_r%d" % i, [128, NB]) for i in range(4)]
                i_ = [sb(st, "l_i%d" % i, [128, NB]) for i in range(4)]
                a_ = [sb(st, "l_a%d" % i, [128, NB]) for i in range(4)]
                u_ = [sb(st, "l_u%d" % i, [128, NB]) for i in range(4)]
                hh = [sb(st, "l_h%d" % i, [128, 4, NB]) for i in range(2)]
                hf = sb(st, "l_hf", [128, 4, NB])
                cg = sb(st, "l_cg", [128, 4, NB])
                ob = [sb(st, "l_ob%d" % i, [128, 4, NB], BF16) for i in range(2)]
                carry = sb(st, "l_carry", [128, 4])
                kb.dma(bdf[:], View([], bd_in[l]))
                E(DVE, 'tensor_copy', out=bdb[:], in_=bdf[:])
                o_, w_ = CP['lru_lam']
                E(ACT, 'activation', out=ccol[:], in_=cpt[l][:, o_:o_ + 8], func=AF.Sigmoid)
                E(ACT, 'activation', out=ccol[:], in_=ccol[:], func=AF.Ln)
                E(DVE, 'tensor_scalar', out=ccol[:], in0=ccol[:], scalar1=8.0, scalar2=None, op0=ALU.mult)
                for d in range(2):
                    for bi, b in enumerate(blk_order(d)):
                        t0 = b * NB
                        rw = raw[bi % 2]
                        h = hh[bi % 2]
                        load_halo(rw, CX, 4, b)
                        if d == 1:
                            kb.dma(hf[:], View([kb.db('ofT%d' % (8 + c), b) for c in range(4)],
                                               ofT[8:12, :, t0:t0 + NB].rearrange("c p t -> p c t")))
                            kb.dma(cg[:], View([kb.db('hT%d' % (CG + c), b) for c in range(4)],
                                               hT[CG:CG + 4, :, t0:t0 + NB].rearrange("c p t -> p c t")))
                        if bi == 0:
                            E(DVE, 'memset', ap=carry[:], constant=0.0)
                        elif crosses(b, d):
                            E(DVE, 'tensor_scalar', out=carry[:], in0=carry[:], scalar1=flag[:, 0:1], scalar2=None, op0=ALU.mult)
                        for t in range(4):
                            conv4(POOL, xc[:, t, :], rw, t, lambda k: cpv(l, 'lru_conv', t * 4 + k), bias=cpv(l, 'lru_cb', t))
                            E(ACT, 'copy', out=xcb[:, t, :], in_=xc[:, t, :])
                            kb.mm(ps[2 * t][:, :], bdb[:, d * 4 + t, :], xcb[:, t, :])
                            kb.mm(ps[2 * t + 1][:, :], bdb[:, 8 + d * 4 + t, :], xcb[:, t, :])
                        for t in range(4):
                            E(ACT, 'activation', out=r_[t][:, :], in_=ps[2 * t][:, :], func=AF.Sigmoid, bias=cpv(l, 'lru_ba', d * 4 + t))
                            E(ACT, 'activation', out=i_[t][:, :], in_=ps[2 * t + 1][:, :], func=AF.Sigmoid, bias=cpv(l, 'lru_bx', d * 4 + t))
                        for t in range(4):
                            E(ACT, 'activation', out=a_[t][:, :], in_=r_[t][:, :], func=AF.Exp, scale=ccol[:, d * 4 + t:d * 4 + t + 1])
                        for t in range(4):
                            E(POOL, 'tensor_tensor', out=r_[t][:, :], in0=a_[t][:, :], in1=a_[t][:, :], op=ALU.mult)
                            E(DVE, 'tensor_scalar', out=r_[t][:, :], in0=r_[t][:, :], scalar1=-1.0, scalar2=1.0, op0=ALU.mult, op1=ALU.add)
                            E(POOL, 'tensor_tensor', out=i_[t][:, :], in0=i_[t][:, :], in1=xc[:, t, :], op=ALU.mult)
                        for t in range(4):
                            E(ACT, 'activation', out=r_[t][:, :], in_=r_[t][:, :], func=AF.Sqrt)
                        for t in range(4):
                            E(DVE, 'tensor_tensor', out=u_[t][:, :], in0=r_[t][:, :], in1=i_[t][:, :], op=ALU.mult)
                            if d == 0:
                                E(DVE, 'tensor_tensor_scan', out=h[:, t, :], data0=a_[t][:, :], data1=u_[t][:, :],
                                  initial=carry[:, t:t + 1], op0=ALU.mult, op1=ALU.add)
                            else:
                                E(DVE, 'tensor_tensor_scan', out=h[:, t, ::-1], data0=a_[t][:, ::-1], data1=u_[t][:, ::-1],
                                  initial=carry[:, t:t + 1], op0=ALU.mult, op1=ALU.add)
                        E(DVE, 'tensor_copy', out=carry[:], in_=h[:, :, NB - 1] if d == 0 else h[:, :, 0])
                        if d == 0:
                            kb.dma(View([kb.db('ofT%d' % (8 + c), b) for c in range(4)],
                                        ofT[8:12, :, t0:t0 + NB].rearrange("c p t -> p c t")), h[:])
                        else:
                            o2 = ob[bi % 2]
                            E(ACT, 'activation', out=cg[:], in_=cg[:], func=AF.Gelu_apprx_tanh)
                            E(POOL, 'tensor_tensor', out=hf[:], in0=hf[:], in1=h[:], op=ALU.add)
                            E(DVE, 'tensor_tensor', out=o2[:], in0=hf[:], in1=cg[:], op=ALU.mult)
                            kb.dma(View([kb.db('obT', b)], obT[8:12, :, t0:t0 + NB].rearrange("c p t -> p c t")), o2[:])
            kb.barrier()

        def phase_hg(l):
            with ExitStack() as st:
                lbc = sb(st, "g_lbc", [128, 4])
                omlb = sb(st, "g_omlb", [128, 4])
                nomlb = sb(st, "g_nomlb", [128, 4])
                qT = [sb(st, "g_qT%d" % i, [128, 4, NB]) for i in range(2)]
                fT = [sb(st, "g_fT%d" % i, [128, 4, NB]) for i in range(2)]
                vT = [sb(st, "g_vT%d" % i, [128, 4, NB]) for i in range(2)]
                s_ = [sb(st, "g_s%d" % i, [128, NB]) for i in range(4)]
                f_ = [sb(st, "g_f%d" % i, [128, NB]) for i in range(4)]
                kk = [sb(st, "g_kk%d" % i, [128, NB]) for i in range(4)]
                bq = [sb(st, "g_bq%d" % i, [128, NB]) for i in range(4)]
                dq_ = [sb(st, "g_dq%d" % i, [128, NB]) for i in range(4)]
                e1 = [sb(st, "g_e1%d" % i, [128, NB]) for i in range(4)]
                e2 = [sb(st, "g_e2%d" % i, [128, NB]) for i in range(4)]
                qt = sb(st, "g_qt", [128, 4, NB], BF16)
                kdT = sb(st, "g_kdT", [128, 4, NB], BF16)
                ebl = sb(st, "g_ebl", [128, 4, 8])
                vtok = sb(st, "g_vtok", [128, 4, 512], BF16)
                ktok = sb(st, "g_ktok", [128, 4, 512], BF16)
                vTb = sb(st, "g_vTb", [128, 4, NB], BF16)
                AT = [sb(st, "g_AT%d" % i, [128, 4, 64], BF16) for i in range(2)]
                S = sb(st, "g_S", [128, 4, 128])
                Sp = sb(st, "g_Sp", [128, 4, 128])
                Spb = sb(st, "g_Spb", [128, 4, 128], BF16)
                ost = [sb(st, "g_ost%d" % i, [128, 4, NB]) for i in range(2)]
                of_ = sb(st, "g_of", [128, 4, NB])
                gz = sb(st, "g_gz", [128, 4, NB])
                oh = [sb(st, "g_oh%d" % i, [128, NB]) for i in range(2)]
                sqb = [sb(st, "g_sqb%d" % i, [128, NB], BF16) for i in range(2)]
                sd = [sb(st, "g_sd%d" % i, [128, NB]) for i in range(2)]
                obs = [sb(st, "g_obs%d" % i, [128, 4, NB], BF16) for i in range(2)]
                if l == 0:
                    E(DVE, 'memset', ap=lbc[:], constant=0.0)
                else:
                    o0, o1 = CP['lb0'][0], CP['lb1'][0]
                    E(DVE, 'tensor_tensor', out=lbc[:], in0=cpt[l][:, o1:o1 + 4], in1=cpt[l][:, o0:o0 + 4], op=ALU.subtract)
                    E(ACT, 'activation', out=lbc[:], in_=lbc[:], func=AF.Sigmoid)
                E(DVE, 'tensor_scalar', out=omlb[:], in0=lbc[:], scalar1=-1.0, scalar2=1.0, op0=ALU.mult, op1=ALU.add)
                E(DVE, 'tensor_scalar', out=nomlb[:], in0=omlb[:], scalar1=-1.0, scalar2=None, op0=ALU.mult)
                pso = [ps[0], ps[1], ps[2], ps[3]]
                psA, psS, psT, psN = ps[4], ps[5], ps[6], ps[7]
                psTb = psT.v(psT.t[:, :].bitcast(BF16))
                for d in range(2):
                    HF = HFF if d == 0 else HFB
                    mask4 = CB('hgm_f4') if d == 0 else CB('hgm_b4')
                    for bi, b in enumerate(blk_order(d)):
                        t0 = b * NB
                        q_, fz, v_ = qT[bi % 2], fT[bi % 2], vT[bi % 2]
                        for (dst, tl) in ((q_, HQ), (fz, HF), (v_, HI)):
                            kb.dma(dst[:], View([kb.db('hT%d' % (tl + c), b) for c in range(4)],
                                                hT[tl:tl + 4, :, t0:t0 + NB].rearrange("c p t -> p c t")))
                        if d == 1:
                            kb.dma(of_[:], View([kb.db('ofT%d' % (4 + c), b) for c in range(4)],
                                                ofT[4:8, :, t0:t0 + NB].rearrange("c p t -> p c t")))
                            kb.dma(gz[:], View([kb.db('hT%d' % (HGT + c), b) for c in range(4)],
                                               hT[HGT:HGT + 4, :, t0:t0 + NB].rearrange("c p t -> p c t")))
                        if bi == 0:
                            E(DVE, 'memset', ap=S[:], constant=0.0)
                        elif crosses(b, d):
                            E(DVE, 'tensor_scalar', out=S[:], in0=S[:], scalar1=flag[:, 0:1], scalar2=None, op0=ALU.mult)
                        E(ACT, 'copy', out=vTb[:], in_=v_[:])
                        for j in range(4):
                            for h in range(4):
                                kb.tr(psTb.bufs[0].v(psTb.ap[:, h * 128:(h + 1) * 128]), vTb[:, h, j * 128:(j + 1) * 128], CB('ident'))
                            E(ACT, 'copy', out=vtok[:, j, :], in_=psT.v(psTb.ap[:, 0:512]))
                        for h in range(4):
                            E(ACT, 'activation', out=s_[h][:, :], in_=fz[:, h, :], func=AF.Sigmoid)
                        for h in range(4):
                            E(DVE, 'tensor_scalar', out=f_[h][:, :], in0=s_[h][:, :], scalar1=omlb[:, h:h + 1], scalar2=lbc[:, h:h + 1],
                              op0=ALU.mult, op1=ALU.add)
                            E(POOL, 'tensor_scalar', out=kk[h][:, :], in0=s_[h][:, :], scalar1=nomlb[:, h:h + 1], scalar2=omlb[:, h:h + 1],
                              op0=ALU.mult, op1=ALU.add)
                        for h in range(4):
                            E(ACT, 'activation', out=f_[h][:, :], in_=f_[h][:, :], func=AF.Ln)
                        blvs = []
                        for h in range(4):
                            bb, ff = bq[h], f_[h]
                            if d == 0:
                                E(DVE, 'tensor_tensor_scan', out=bb[:, :], data0=C('rst_f'), data1=ff[:, :], initial=0.0,
                                  op0=ALU.mult, op1=ALU.add)
                            else:
                                rb = C('rst_b')
                                E(DVE, 'tensor_tensor_scan', out=bb[:, ::-1], data0=View(rb.bufs, rb.ap[:, ::-1]), data1=ff[:, ::-1],
                                  initial=0.0, op0=ALU.mult, op1=ALU.add)
                            b3 = bb.t[:, :].rearrange("p (c j) -> p c j", j=64)
                            blv = b3[:, :, 63] if d == 0 else b3[:, :, 0]
                            E(DVE, 'tensor_tensor', out=dq_[h].v(dq_[h].t[:, :].rearrange("p (c j) -> p c j", j=64)), in0=bb.v(b3),
                              in1=bb.v(bc_last(blv, 64)), op=ALU.subtract)
                            blvs.append(blv)
                        for h in range(4):
                            E(ACT, 'activation', out=ebl[:, h, :], in_=bq[h].v(blvs[h]), func=AF.Exp)
                            E(ACT, 'activation', out=e1[h][:, :], in_=dq_[h][:, :], func=AF.Exp)
                            E(ACT, 'activation', out=e2[h][:, :], in_=dq_[h][:, :], func=AF.Exp, scale=-1.0)
                        for h in range(4):
                            E(POOL, 'tensor_tensor', out=qt[:, h, :], in0=q_[:, h, :], in1=e1[h][:, :], op=ALU.mult)
                            E(DVE, 'tensor_tensor', out=kdT[:, h, :], in0=kk[h][:, :], in1=e2[h][:, :], op=ALU.mult)
                        for j in range(4):
                            for h in range(4):
                                kb.tr(psTb.bufs[0].v(psTb.ap[:, h * 128:(h + 1) * 128]), kdT[:, h, j * 128:(j + 1) * 128], CB('ident'))
                            E(ACT, 'copy', out=ktok[:, j, :], in_=psT.v(psTb.ap[:, 0:512]))
                        jl = range(4) if d == 0 else range(3, -1, -1)
                        for j in jl:
                            at = AT[j % 2]
                            for hfh in range(2):
                                c = 2 * j + hfh
                                for h in range(4):
                                    kb.mm(psA[hfh * 64:(hfh + 1) * 64, h * 64:(h + 1) * 64], kdT[:, h, c * 64:(c + 1) * 64],
                                          qt[:, h, c * 64:(c + 1) * 64])
                            E(DVE, 'tensor_tensor', out=at.v(at.t[:, :, :].rearrange("p h i -> p (h i)")), in0=psA[:, 0:256], in1=mask4, op=ALU.mult)
                            for hfh in (range(2) if d == 0 else range(1, -1, -1)):
                                c = 2 * j + hfh
                                p0 = hfh * 64
                                eb = ebl.v(bc_last(ebl.t[:, :, c], 128))
                                E(DVE, 'tensor_tensor', out=Spb[:], in0=S[:], in1=eb, op=ALU.mult)
                                E(POOL, 'tensor_tensor', out=Sp[:], in0=S[:], in1=eb, op=ALU.mult)
                                for h in range(4):
                                    kb.mm(pso[h][:, c * 64:(c + 1) * 64], Spb[:, h, :], qt[:, h, c * 64:(c + 1) * 64], start=True, stop=False)
                                    kb.mm(pso[h][:, c * 64:(c + 1) * 64], vtok[p0:p0 + 64, j, h * 128:(h + 1) * 128], at[p0:p0 + 64, h, :],
                                          start=False, stop=True)
                                for h in range(4):
                                    kb.mm(psS[:, h * 128:(h + 1) * 128], ktok[p0:p0 + 64, j, h * 128:(h + 1) * 128],
                                          vtok[p0:p0 + 64, j, h * 128:(h + 1) * 128])
                                E(DVE, 'tensor_tensor', out=S.v(S.t[:, :, :].rearrange("p h d -> p (h d)")),
                                  in0=Sp.v(Sp.t[:, :, :].rearrange("p h d -> p (h d)")), in1=psS[:, :], op=ALU.add)
                        if d == 0:
                            o1 = ost[bi % 2]
                            for h in range(4):
                                E(ACT, 'copy', out=o1[:, h, :], in_=pso[h][:, :])
                            kb.dma(View([kb.db('ofT%d' % (4 + c), b) for c in range(4)],
                                        ofT[4:8, :, t0:t0 + NB].rearrange("c p t -> p c t")), o1[:])
                        else:
                            o2 = obs[bi % 2]
                            E(ACT, 'activation', out=gz[:], in_=gz[:], func=AF.Silu)
                            for h in range(4):
                                oo, sq_, sd_ = oh[h % 2], sqb[h % 2], sd[h % 2]
                                E(DVE, 'tensor_tensor', out=oo[:, :], in0=pso[h][:, :], in1=of_[:, h, :], op=ALU.add)
                                E(ACT, 'activation', out=sq_[:, :], in_=oo[:, :], func=AF.Square)
                                kb.mm(psN[:, :], CB('ones'), sq_[:, :])
                                E(DVE, 'tensor_scalar', out=sd_[:, :], in0=psN[:, :], scalar1=1.0 / 128, scalar2=1e-6, op0=ALU.mult, op1=ALU.add)
                                E(ACT, 'activation', out=sd_[:, :], in_=sd_[:, :], func=AF.Ln)
                                E(ACT, 'activation', out=sd_[:, :], in_=sd_[:, :], func=AF.Exp, scale=-0.5)
                                E(POOL, 'tensor_tensor', out=oo[:, :], in0=oo[:, :], in1=sd_[:, :], op=ALU.mult)
                                E(DVE, 'scalar_tensor_tensor', out=o2[:, h, :], in0=oo[:, :], scalar=cpv(l, 'hg_nw'), in1=gz[:, h, :],
                                  op0=ALU.mult, op1=ALU.mult)
                            kb.dma(View([kb.db('obT', b)], obT[4:8, :, t0:t0 + NB].rearrange("c p t -> p c t")), o2[:])
            kb.barrier()

        class ColBuf:
            def __init__(self, bank, c0, c1, name=""):
                self.bank = bank
                self.t = bank.t
                self.c0, self.c1 = c0, c1

            def cols(self, a=None, b=None, rows=slice(None)):
                a = 0 if a is None else a
                b = (self.c1 - self.c0) if b is None else b
                return View([self.bank], self.t[rows, self.c0 + a:self.c0 + b])

            def bf(self, a, b):
                return View([self.bank], self.t[:, :].bitcast(BF16)[:, 2 * self.c0 + a:2 * self.c0 + b])

        def rr(gens):
            gens = list(gens)
            while gens:
                for g in list(gens):
                    try:
                        next(g)
                    except StopIteration:
                        gens.remove(g)

        NCH = 2
        NHC = 4 // NCH

        def phase_dn(l):
            with ExitStack() as st:
                raw = sb(st, "d_raw", [128, 12, NB + 3])
                cv = sb(st, "d_cv", [128, 12, NB])
                abT = sb(st, "d_abT", [16, NB])
                qn2 = [sb(st, "d_qn%d" % i, [128, 4, NB], BF16) for i in range(2)]
                kn2 = [sb(st, "d_kn%d" % i, [128, 4, NB], BF16) for i in range(2)]
                vTb2 = [sb(st, "d_vTb%d" % i, [128, 4, NB], BF16) for i in range(2)]
                ctmp = sb(st, "d_ctmp", [128, NB])
                sqb = [sb(st, "d_sqb%d" % i, [128, NB], BF16) for i in range(2)]
                sd = [sb(st, "d_sd%d" % i, [128, NB]) for i in range(2)]
                negA = sb(st, "d_negA", [128, 8])
                gt = sb(st, "d_gt", [128, 4, 16])
                g_ = sb(st, "d_g", [128, 4, 4])
                gR2 = [sb(st, "d_gR%d" % i, [128, 4, 4], F32R) for i in range(2)]
                beta2 = [sb(st, "d_beta%d" % i, [128, 4, 4]) for i in range(2)]
                nbeta2 = [sb(st, "d_nbeta%d" % i, [128, 4, 4]) for i in range(2)]
                of_ = sb(st, "d_of", [128, 4, NB])
                gz = sb(st, "d_gz", [128, 4, NB])
                oh = [sb(st, "d_oh%d" % i, [128, NB]) for i in range(2)]
                sqf = [sb(st, "d_sqf%d" % i, [128, NB], BF16) for i in range(2)]
                sdf = [sb(st, "d_sdf%d" % i, [128, NB]) for i in range(2)]
                obs = sb(st, "d_obs", [128, 4, NB], BF16)

                class CH:
                    pass
                chs = []
                for c in range(NCH):
                    ch = CH()
                    ch.c = c
                    ch.heads = list(range(c * NHC, (c + 1) * NHC))
                    n3 = [128, NHC, 128]
                    mk_ = lambda nm, dt_=F32, c=c, n3=n3: sb(st, "d%d_%s" % (c, nm), n3, dt_)
                    ch.Mh = mk_("Mh", F32R)
                    ch.Ecol = sb(st, "d%d_Ecol" % c, [128, 3, NHC])
                    ch.sc1 = sb(st, "d%d_sc1" % c, [128, NHC])
                    ch.Dc, ch.Dm, ch.attn, ch.AT = mk_("Dc", BF16), mk_("Dm", BF16), mk_("attn", BF16), mk_("AT", BF16)
                    ch.Nm, ch.NT = mk_("Nm", F32R), mk_("NT", F32R)
                    ch.Pb = [mk_("P%d" % i, F32R) for i in range(2)]
                    ch.PTb = [mk_("PT%d" % i, F32R) for i in range(2)]
                    ch.Rb = [mk_("R%d" % i, F32R) for i in range(2)]
                    ch.kbg, ch.vb, ch.nwT = mk_("kbg", F32R), mk_("vb", F32R), mk_("nwT", F32R)
                    ch.kd, ch.qd, ch.vnew = mk_("kd", BF16), mk_("qd", BF16), mk_("vnew", BF16)
                    ch.EG, ch.Stmp = mk_("EG"), mk_("Stmp")
                    ch.S, ch.Sb = mk_("S", F32R), mk_("Sb", BF16)
                    ch.ost2 = [sb(st, "d%d_ost%d" % (c, i), [128, NHC, NB]) for i in range(2)]
                    W_ = NHC * 128
                    assert NCH == 2 and W_ == 256
                    B0, B1, B2, B3 = [ps[4 * c + i] for i in range(4)]
                    ch.psD = ColBuf(B0, 0, 256)
                    ch.psPT = ColBuf(B0, 0, 256)
                    ch.psGc = ColBuf(B0, 256, 272)
                    ch.psAT = ColBuf(B0, 272, 400)
                    ch.psE = ColBuf(B1, 0, 256)
                    ch.psKV = ColBuf(B1, 256, 512)
                    ch.psG = ColBuf(B2, 0, 256)
                    ch.psA = ColBuf(B2, 256, 512)
                    ch.psT = ColBuf(B3, 0, 256)
                    ch.psS = ColBuf(B2, 256, 512)
                    chs.append(ch)
                psGT = ColBuf(ps[0], 400, 464)
                psNp = ColBuf(ps[3], 256, 512)
                psNf = ColBuf(ps[7], 256, 512)
                flat = lambda bf: bf.v(bf.t[:, :, :].rearrange("p h i -> p (h i)"))
                o_ = CP['alog'][0]
                E(ACT, 'activation', out=negA[:], in_=cpt[l][:, o_:o_ + 8], func=AF.Exp)
                E(DVE, 'tensor_scalar', out=negA[:], in0=negA[:], scalar1=-1.0, scalar2=None, op0=ALU.mult)

                def chain(ch, d, order, TRI, TRIR, TRICR, NMR, SM, slot):
                    qn, kn, vTb, gR, beta, nbeta = qn2[slot], kn2[slot], vTb2[slot], gR2[slot], beta2[slot], nbeta2[slot]
                    ost = ch.ost2[slot]
                    hs_ = ch.heads
                    h0 = hs_[0]
                    W_ = NHC * 128
                    for j in order:
                        cs = slice(j * 128, (j + 1) * 128)
                        gj = gR[:, j, h0:h0 + NHC]
                        gjf = asf(gj)
                        E(DVE, 'tensor_tensor', out=ch.Mh[:], in0=View(TRI.bufs, bc_mid(TRI.ap, NHC)),
                          in1=View(gjf.bufs, bc_last(gjf.ap, 128)), op=ALU.mult)
                        yield
                        for hi in range(NHC):
                            o2 = ch.psD.cols(hi * 128, (hi + 1) * 128)
                            kb.mm(o2, ch.Mh[:, hi, :], CR('ones'), start=True, stop=False)
                            kb.mm(o2, CR('negones'), ch.Mh[:, hi, :], start=False, stop=False)
                            kb.mm(o2, CR('ident'), NMR, start=False, stop=True)
                        kb.mm(ch.psGc.cols(0, NHC), TRIR, gj)
                        kb.mm(ch.psGc.cols(NHC, 2 * NHC), TRICR, gj)
                        kb.mm(ch.psGc.cols(2 * NHC, 3 * NHC), CR('ones'), gj)
                        for hi, h in enumerate(hs_):
                            kb.mm(ch.psG.cols(hi * 128, (hi + 1) * 128), kn[:, h, cs], kn[:, h, cs])
                            kb.mm(ch.psA.cols(hi * 128, (hi + 1) * 128), qn[:, h, cs], kn[:, h, cs])
                        for hi in range(NHC):
                            kb.mm(ch.psE.cols(hi * 128, (hi + 1) * 128), CR('ones'), ch.Mh[:, hi, :])
                        yield
                        E(ACT, 'activation', out=ch.Ecol.v(ch.Ecol.t[:, :, :].rearrange("p a h -> p (a h)")), in_=ch.psGc.cols(0, 3 * NHC), func=AF.Exp)
                        E(ACT, 'activation', out=flat(ch.Dc), in_=ch.psD.cols(), func=AF.Exp)
                        E(ACT, 'activation', out=flat(ch.EG), in_=ch.psE.cols(), func=AF.Exp)
                        yield
                        E(POOL, 'tensor_tensor', out=flat(ch.Dm), in0=flat(ch.Dc), in1=View(SM.bufs, SM.ap[:, 0:W_]), op=ALU.mult)
                        E(POOL, 'tensor_tensor', out=ch.qd[:], in0=qn[:, h0:h0 + NHC, cs], in1=ch.EG[:], op=ALU.mult)
                        yield
                        for hi, h in enumerate(hs_):
                            E(DVE, 'scalar_tensor_tensor', out=ch.Nm[:, hi, :], in0=ch.psG.cols(hi * 128, (hi + 1) * 128),
                              scalar=nbeta[:, j, h:h + 1], in1=ch.Dm[:, hi, :], op0=ALU.mult, op1=ALU.mult)
                        E(DVE, 'tensor_tensor', out=flat(ch.attn), in0=ch.psA.cols(), in1=flat(ch.Dc), op=ALU.mult)
                        yield
                        for hi in range(NHC):
                            kb.tr(View([ch.psT.bank], ch.psT.t[:, ch.psT.c0 + hi * 128:ch.psT.c0 + (hi + 1) * 128].bitcast(F32R)), ch.Nm[:, hi, :], CR('ident'))
                            kb.tr(ch.psAT.bf(hi * 128, (hi + 1) * 128), ch.attn[:, hi, :], CB('ident'))
                        for hi, h in enumerate(hs_):
                            kb.tr(ch.psKV.bf(hi * 128, (hi + 1) * 128), kn[:, h, cs], CB('ident'))
                            kb.tr(ch.psKV.bf(W_ + hi * 128, W_ + (hi + 1) * 128), vTb[:, h, cs], CB('ident'))
                        yield
                        E(ACT, 'copy', out=flat(ch.NT), in_=ch.psT.cols())
                        E(DVE, 'tensor_copy', out=flat(ch.AT), in_=ch.psAT.bf(0, W_))
                        E(DVE, 'tensor_tensor', out=ch.sc1[:], in0=beta[:, j, h0:h0 + NHC], in1=ch.Ecol[:, 0, :], op=ALU.mult)
                        kv3 = lambda a: View([ch.psKV.bank], ch.psKV.bf(a, a + W_).ap.rearrange("p (h i) -> p h i", h=NHC))
                        E(DVE, 'tensor_tensor', out=ch.kbg[:], in0=kv3(0), in1=ch.sc1.v(bc_last(ch.sc1.t[:, 0:NHC], 128)), op=ALU.mult)
                        E(DVE, 'tensor_tensor', out=ch.kd[:], in0=kv3(0), in1=ch.Ecol.v(bc_last(ch.Ecol.t[:, 1, :], 128)), op=ALU.mult)
                        E(DVE, 'tensor_tensor', out=ch.vb[:], in0=kv3(W_), in1=beta.v(bc_last(beta.t[:, j, h0:h0 + NHC], 128)), op=ALU.mult)
                        yield
                        P, PT = ch.Nm, ch.NT
                        R = ch.Rb[0]
                        ident_n = C('ident4')
                        E(DVE, 'tensor_tensor', out=flat(R), in0=asf(flat(ch.NT)), in1=View(ident_n.bufs, ident_n.ap[:, 0:W_]), op=ALU.add)
                        yield
                        for k in range(6):
                            Pn, PTn, Rn = ch.Pb[k % 2], ch.PTb[k % 2], ch.Rb[(k + 1) % 2]
                            for hi in range(NHC):
                                kb.mm(ch.psG.cols(hi * 128, (hi + 1) * 128), PT[:, hi, :], P[:, hi, :])
                            if k < 5:
                                for hi in range(NHC):
                                    kb.mm(ch.psPT.cols(hi * 128, (hi + 1) * 128), P[:, hi, :], PT[:, hi, :])
                            yield
                            E(ACT, 'copy', out=flat(Pn), in_=ch.psG.cols())
                            if k < 5:
                                E(DVE, 'tensor_copy', out=flat(PTn), in_=ch.psPT.cols())
                            yield
                            for hi in range(NHC):
                                kb.mm(ch.psT.cols(hi * 128, (hi + 1) * 128), Pn[:, hi, :], R[:, hi, :])
                            yield
                            E(DVE, 'tensor_tensor', out=flat(Rn), in0=ch.psT.cols(), in1=asf(flat(R)), op=ALU.add)
                            yield
                            P, PT, R = Pn, PTn, Rn
                        TT = R
                        for hi in range(NHC):
                            kb.mm(ch.psD.cols(hi * 128, (hi + 1) * 128), ch.kbg[:, hi, :], TT[:, hi, :])
                        yield
                        E(ACT, 'activation', out=flat(ch.nwT), in_=ch.psD.cols(), func=AF.Copy, scale=-1.0)
                        yield
                        for hi in range(NHC):
                            o2 = ch.psKV.cols(hi * 128, (hi + 1) * 128)
                            kb.mm(o2, TT[:, hi, :], ch.vb[:, hi, :], start=True, stop=False)
                            kb.mm(o2, ch.nwT[:, hi, :], ch.S[:, hi, :], start=False, stop=True)
                        yield
                        E(ACT, 'copy', out=flat(ch.vnew), in_=ch.psKV.cols())
                        yield
                        for hi in range(NHC):
                            o2 = ch.psE.cols(hi * 128, (hi + 1) * 128)
                            kb.mm(o2, ch.Sb[:, hi, :], ch.qd[:, hi, :], start=True, stop=False)
                            kb.mm(o2, ch.vnew[:, hi, :], ch.AT[:, hi, :], start=False, stop=True)
                        for hi in range(NHC):
                            kb.mm(ch.psS.cols(hi * 128, (hi + 1) * 128), ch.kd[:, hi, :], ch.vnew[:, hi, :])
                        yield
                        E(DVE, 'tensor_tensor', out=ch.Stmp[:], in0=asf(ch.S[:]), in1=ch.Ecol.v(bc_last(ch.Ecol.t[:, 2, :], 128)), op=ALU.mult)
                        E(DVE, 'tensor_tensor', out=flat(ch.S), in0=flat(ch.Stmp), in1=ch.psS.cols(), op=ALU.add)
                        E(ACT, 'copy', out=ch.Sb[:], in_=asf(ch.S[:]))
                        E(ACT, 'copy', out=ost[:, :, cs], in_=View([ch.psE.bank], ch.psE.cols().ap.rearrange("p (h i) -> p h i", h=NHC)))
                        yield

                for d in range(2):
                    TRI = C('tri_le') if d == 0 else C('tri_ge')
                    NMR = CR('nm_f') if d == 0 else CR('nm_b')
                    TRIR = CR('tri_le') if d == 0 else CR('tri_ge')
                    TRICR = CR('tri_gt') if d == 0 else CR('tri_lt')
                    SM = CB('sm_f4') if d == 0 else CB('sm_b4')
                    o_ = CP['dtb'][0]
                    dtb = cpt[l][:, o_ + d * 4:o_ + d * 4 + 4]
                    nAd = negA[:, d * 4:d * 4 + 4]
                    def prep_gen(b, slot):
                        t0 = b * NB
                        qn, kn, vTb, gR, beta, nbeta = qn2[slot], kn2[slot], vTb2[slot], gR2[slot], beta2[slot], nbeta2[slot]
                        load_halo(raw, DQ, 12, b)
                        kb.dma(abT[:], View([kb.db('hT%d' % AB, b)], hT[AB, 0:16, t0:t0 + NB]))
                        yield
                        for t in range(12):
                            if t % 2:
                                conv4(POOL, cv[:, t, :], raw, t, lambda k: cpv(l, 'dn_conv', t * 4 + k), tmp=ctmp[:, :])
                            else:
                                conv4(DVE, cv[:, t, :], raw, t, lambda k: cpv(l, 'dn_conv', t * 4 + k))
                            yield
                        for t3 in range(3):
                            E(ACT, 'activation', out=cv[:, t3 * 4:t3 * 4 + 4, :], in_=cv[:, t3 * 4:t3 * 4 + 4, :], func=AF.Silu)
                            yield
                        for t in range(8):
                            sq_, sd_ = sqb[t % 2], sd[t % 2]
                            E(ACT, 'activation', out=sq_[:, :], in_=cv[:, t, :], func=AF.Square)
                            for hf_ in range(2):
                                kb.mm(psNp.cols(), CB('ones'), sq_[:, hf_ * 256:(hf_ + 1) * 256])
                                E(DVE, 'tensor_scalar', out=sd_[:, hf_ * 256:(hf_ + 1) * 256], in0=psNp.cols(), scalar1=1e-6, scalar2=None, op0=ALU.add)
                            yield
                            E(ACT, 'activation', out=sd_[:, :], in_=sd_[:, :], func=AF.Ln)
                            E(ACT, 'activation', out=sd_[:, :], in_=sd_[:, :], func=AF.Exp, scale=-0.5)
                            dst = qn[:, t, :] if t < 4 else kn[:, t - 4, :]
                            E(DVE, 'scalar_tensor_tensor', out=dst, in0=cv[:, t, :], scalar=(128 ** -0.5 if t < 4 else 1.0),
                              in1=sd_[:, :], op0=ALU.mult, op1=ALU.mult)
                            yield
                        E(ACT, 'copy', out=vTb[:], in_=cv[:, 8:12, :])
                        yield
                        for j in range(4):
                            idt = C('ident')
                            kb.tr(psGT.cols(j * 16, (j + 1) * 16), abT[0:16, j * 128:(j + 1) * 128], View(idt.bufs, idt.ap[0:16, 0:16]))
                        E(ACT, 'copy', out=gt.v(gt.t[:, :, :].rearrange("p j c -> p (j c)")), in_=psGT.cols())
                        yield
                        E(DVE, 'tensor_tensor', out=g_[:], in0=gt[:, :, d * 4:d * 4 + 4], in1=View(dtb.bufs, bc_mid(dtb.ap, 4)), op=ALU.add)
                        E(ACT, 'activation', out=g_[:], in_=g_[:], func=AF.Exp)
                        E(DVE, 'tensor_scalar', out=g_[:], in0=g_[:], scalar1=1.0, scalar2=None, op0=ALU.add)
                        E(ACT, 'activation', out=g_[:], in_=g_[:], func=AF.Ln)
                        E(DVE, 'tensor_tensor', out=gR[:], in0=g_[:], in1=View(nAd.bufs, bc_mid(nAd.ap, 4)), op=ALU.mult)
                        E(ACT, 'activation', out=beta[:], in_=gt[:, :, 8 + d * 4:12 + d * 4], func=AF.Sigmoid)
                        E(DVE, 'tensor_scalar', out=nbeta[:], in0=beta[:], scalar1=-1.0, scalar2=None, op0=ALU.mult)
                        yield

                    def final_gen(b, slot):
                        t0 = b * NB
                        kb.dma(of_[:], View([kb.db('ofT%d' % c, b) for c in range(4)],
                                            ofT[0:4, :, t0:t0 + NB].rearrange("c p t -> p c t")))
                        kb.dma(gz[:], View([kb.db('hT%d' % (DZ + c), b) for c in range(4)],
                                           hT[DZ:DZ + 4, :, t0:t0 + NB].rearrange("c p t -> p c t")))
                        yield
                        E(ACT, 'activation', out=gz[:], in_=gz[:], func=AF.Silu)
                        yield
                        for h in range(4):
                            ch = chs[h // NHC]
                            hi = h % NHC
                            oo, sq_, sd_ = oh[h % 2], sqf[h % 2], sdf[h % 2]
                            E(DVE, 'tensor_tensor', out=oo[:, :], in0=ch.ost2[slot][:, hi, :], in1=of_[:, h, :], op=ALU.add)
                            E(ACT, 'activation', out=sq_[:, :], in_=oo[:, :], func=AF.Square)
                            for hf_ in range(2):
                                kb.mm(psNf.cols(), CB('ones'), sq_[:, hf_ * 256:(hf_ + 1) * 256])
                                E(DVE, 'tensor_scalar', out=sd_[:, hf_ * 256:(hf_ + 1) * 256], in0=psNf.cols(), scalar1=1.0 / 128, scalar2=1e-6,
                                  op0=ALU.mult, op1=ALU.add)
                            yield
                            E(ACT, 'activation', out=sd_[:, :], in_=sd_[:, :], func=AF.Ln)
                            E(ACT, 'activation', out=sd_[:, :], in_=sd_[:, :], func=AF.Exp, scale=-0.5)
                            E(POOL, 'tensor_tensor', out=oo[:, :], in0=oo[:, :], in1=sd_[:, :], op=ALU.mult)
                            E(DVE, 'scalar_tensor_tensor', out=obs[:, h, :], in0=oo[:, :], scalar=cpv(l, 'dn_nw'), in1=gz[:, h, :],
                              op0=ALU.mult, op1=ALU.mult)
                            yield
                        kb.dma(View([kb.db('obT', b)], obT[0:4, :, t0:t0 + NB].rearrange("c p t -> p c t")), obs[:])

                    blks = blk_order(d)
                    rr([prep_gen(blks[0], 0)])
                    pending = None
                    for bi, b in enumerate(blks):
                        t0 = b * NB
                        slot = bi % 2
                        for ch in chs:
                            if bi == 0:
                                E(DVE, 'memset', ap=asf(ch.S[:]), constant=0.0)
                                E(POOL, 'memset', ap=ch.Sb[:], constant=0.0)
                            elif crosses(b, d):
                                E(DVE, 'tensor_scalar', out=ch.S[:], in0=asf(ch.S[:]), scalar1=flag[:, 0:1], scalar2=None, op0=ALU.mult)
                                E(DVE, 'tensor_copy', out=ch.Sb[:], in_=asf(ch.S[:]))
                        order = list(range(4)) if d == 0 else list(range(3, -1, -1))
                        gens = [chain(ch, d, order, TRI, TRIR, TRICR, NMR, SM, slot) for ch in chs]
                        if bi + 1 < len(blks):
                            gens.append(prep_gen(blks[bi + 1], 1 - slot))
                        if pending is not None:
                            gens.append(final_gen(*pending))
                            pending = None
                        rr(gens)
                        if d == 0:
                            for ch in chs:
                                h0 = ch.heads[0]
                                kb.dma(View([kb.db('ofT%d' % c, b) for c in ch.heads],
                                            ofT[h0:h0 + NHC, :, t0:t0 + NB].rearrange("c p t -> p c t")), ch.ost2[slot][:])
                        else:
                            pending = (b, slot)
                    if pending is not None:
                        rr([final_gen(*pending)])
            kb.barrier()

        MIX = {'lru': phase_lru, 'hg': phase_hg, 'dn': phase_dn}
        run = phases if phases is not None else ['wprep', 'embed', 'proj', 'lru', 'hg', 'dn', 'post']
        if 'wprep' in run:
            phase_wprep()
        if 'embed' in run:
            phase_embed()
        for l in range(nlayers):
            if 'proj' in run:
                phase_proj(l)
            for m in ['lru', 'hg', 'dn']:
                if m in run and m in MIX:
                    MIX[m](l)
            if 'post' in run:
                phase_post(l, l == nlayers - 1)
        kb.barrier()
        print("instructions:", kb.ninst)
    return nc


def _colparams(W):
    L = DEPTH
    cp = np.zeros((L, 128, NCP), np.float32)

    def put(l, name, arr):
        o, w = CP[name]
        cp[l, :, o:o + w] = arr

    def chan(v, nt):
        return np.asarray(v).reshape(nt, 128).T
    for l in range(L):
        put(l, 'dn_conv', np.asarray(W['dn_conv_w'][l]).reshape(4, 12, 128).transpose(2, 1, 0).reshape(128, 48))
        put(l, 'lru_conv', np.asarray(W['lru_conv_w'][l]).reshape(4, 4, 128).transpose(2, 1, 0).reshape(128, 16))
        put(l, 'lru_cb', chan(W['lru_conv_b'][l], 4))
        put(l, 'lru_ba', np.asarray(W['lru_ba'][l]).reshape(2, 4, 128).transpose(2, 0, 1).reshape(128, 8))
        put(l, 'lru_bx', np.asarray(W['lru_bx'][l]).reshape(2, 4, 128).transpose(2, 0, 1).reshape(128, 8))
        put(l, 'lru_lam', np.asarray(W['lru_lambda'][l]).reshape(2, 4, 128).transpose(2, 0, 1).reshape(128, 8))
        put(l, 'dn_nw', np.asarray(W['dn_norm_w'][l]).reshape(128, 1))
        put(l, 'hg_nw', np.asarray(W['hg_norm_w'][l]).reshape(128, 1))
        for n in ['ln1_g', 'ln1_b', 'ln2_g', 'ln2_b']:
            put(l, n, chan(W[n][l], 8))
        put(l, 'lb0', chan(W['hg_lb_logits'][0], 4))
        put(l, 'lb1', chan(W['hg_lb_logits'][1], 4))
        put(l, 'emb_g', chan(W['emb_ln_g'], 8))
        put(l, 'emb_b', chan(W['emb_ln_b'], 8))
        put(l, 'alog', np.tile(np.asarray(W['dn_A_log'][l]).reshape(1, 8), (128, 1)))
        put(l, 'dtb', np.tile(np.asarray(W['dn_dt_bias'][l]).reshape(1, 8), (128, 1)))
    return cp


def _blockdiag(W):
    bd = np.zeros((DEPTH, 128, 16, 128), np.float32)
    for l in range(DEPTH):
        for ai, nm in enumerate(['lru_wa', 'lru_wx']):
            w = np.asarray(W[nm][l])
            for d in range(2):
                for t in range(4):
                    idx = ai * 8 + d * 4 + t
                    for s in range(2):
                        bd[l, s * 64:(s + 1) * 64, idx, s * 64:(s + 1) * 64] = w[d, 2 * t + s]
    return bd


def make_in_maps(W, xs, ps_, flags):
    cp = _colparams(W)
    bd = _blockdiag(W)
    common = dict(cst=CONST_ARR, cp=cp, bd=bd,
                  w_in=np.ascontiguousarray(W['w_in'], dtype=np.float32), w_branch=np.asarray(W['w_branch'], np.float32),
                  w_out=np.asarray(W['w_out'], np.float32), w_mlp1=np.asarray(W['w_mlp1'], np.float32),
                  w_mlp2=np.asarray(W['w_mlp2'], np.float32), w_ple_gate=np.asarray(W['w_ple_gate'], np.float32),
                  w_ple_proj=np.asarray(W['w_ple_proj'], np.float32))
    maps = []
    for x, p, f in zip(xs, ps_, flags):
        m = dict(common)
        m['x'] = np.ascontiguousarray(x, dtype=np.float32)
        m['p'] = np.ascontiguousarray(p, dtype=np.float32)
        m['flag'] = np.full((128, 1), f, np.float32)
        maps.append(m)
    return maps


_NC_CACHE = {}


def kernel(x_prompt, x_sample, p_prompt, p_sample, **W):
    x_prompt = np.asarray(x_prompt)
    x_sample = np.asarray(x_sample)
    p_prompt = np.asarray(p_prompt)
    p_sample = np.asarray(p_sample)
    T, SEG = 8192, 4096
    assign = [(0, 1), (2, 3), (4, 4), (5, 5), (6, 6), (7, 7)]
    xs, ps_, flags = [], [], []
    for c in range(2):
        xs.append(x_sample[c])
        ps_.append(p_sample[:, c])
        flags.append(1.0)
    for a, b in assign:
        xs.append(np.concatenate([x_prompt[a], x_prompt[b]], axis=0))
        ps_.append(np.concatenate([p_prompt[:, a], p_prompt[:, b]], axis=1))
        flags.append(0.0)
    if 'nc' not in _NC_CACHE:
        _NC_CACHE['nc'] = build(T, SEG)
    nc = _NC_CACHE['nc']
    maps = make_in_maps(W, xs, ps_, flags)
    res = run_bass_kernel_spmd(nc, maps, core_ids=list(range(NCORES)))
    y_prompt = np.zeros((8, 4096, D), np.float32)
    y_sample = np.zeros((2, 8192, D), np.float32)
    for c in range(2):
        y_sample[c] = res.results[c]['y']
    for i, (a, b) in enumerate(assign):
        y = res.results[2 + i]['y']
        y_prompt[a] = y[:4096]
        if b != a:
            y_prompt[b] = y[4096:]
    return (y_prompt, y_sample)
```

```python
import numpy as np
from contextlib import ExitStack
import concourse.bass as bass
import concourse.mybir as mybir
from concourse.bass_utils import run_bass_kernel_spmd

F32 = mybir.dt.float32
BF16 = mybir.dt.bfloat16
F32R = mybir.dt.float32r
AF = mybir.ActivationFunctionType
ALU = mybir.AluOpType

D = 1024
NIN = 8720
DFF = 4096
DPLE = 256
DEPTH = 2
ALPHA = (2.0 * DEPTH) ** 0.25
NB = 512
NCORES = 8
DQ, DK, DV, DZ, HQ, HFF, HFB, HI, HGT, CX, CG, GA, GB, GC, AB = 0, 4, 8, 12, 16, 20, 24, 28, 32, 36, 40, 44, 52, 60, 68
NHT = 69
WIN_TILES = [(i * 128, 128) for i in range(16)] + [(2064 + 128 * i, 128) for i in range(52)] + [(2048, 16)]

def _consts():
    i = np.arange(128)
    c = {}
    c['ident'] = np.eye(128)
    c['ones'] = np.ones((128, 128))
    c['negones'] = -np.ones((128, 128))
    c['tri_le'] = (i[:, None] <= i[None, :]) * 1.0
    c['tri_ge'] = (i[:, None] >= i[None, :]) * 1.0
    c['tri_gt'] = (i[:, None] > i[None, :]) * 1.0
    c['tri_lt'] = (i[:, None] < i[None, :]) * 1.0
    c['nm_f'] = np.where(i[:, None] < i[None, :], -30000.0, 0.0)
    c['nm_b'] = np.where(i[:, None] > i[None, :], -30000.0, 0.0)
    c['sm_f4'] = np.tile((i[:, None] > i[None, :]) * 1.0, (1, 4))
    c['sm_b4'] = np.tile((i[:, None] < i[None, :]) * 1.0, (1, 4))
    c['ident4'] = np.tile(np.eye(128), (1, 4))
    j = np.arange(64)
    c['hgm_f4'] = np.tile((j[None, :] >= (i[:, None] % 64)) * 1.0, (1, 4))
    c['hgm_b4'] = np.tile((j[None, :] <= (i[:, None] % 64)) * 1.0, (1, 4))
    t = np.arange(NB)
    c['rst_f'] = np.tile(((t % 64) != 0) * 1.0, (128, 1))
    c['rst_b'] = np.tile(((t % 64) != 63) * 1.0, (128, 1))
    offs = {}
    o = 0
    arrs = []
    for k, v in c.items():
        offs[k] = (o, v.shape[1])
        o += v.shape[1]
        arrs.append(v.astype(np.float32))
    return np.ascontiguousarray(np.concatenate(arrs, axis=1)), offs


CONST_ARR, CONST_OFF = _consts()
CP = {}
_o = 0
for _n, _w in [('dn_conv', 48), ('lru_conv', 16), ('lru_cb', 4), ('lru_ba', 8), ('lru_bx', 8), ('lru_lam', 8),
               ('dn_nw', 1), ('hg_nw', 1), ('ln1_g', 8), ('ln1_b', 8), ('ln2_g', 8), ('ln2_b', 8),
               ('lb0', 4), ('lb1', 4), ('emb_g', 8), ('emb_b', 8), ('alog', 8), ('dtb', 8)]:
    CP[_n] = (_o, _w)
    _o += _w
NCP = _o


class View:
    def __init__(self, bufs, ap):
        self.bufs = bufs
        self.ap = ap

    def __getitem__(self, idx):
        return View(self.bufs, self.ap[idx])


class Buf:
    def __init__(self, t=None, name=""):
        self.t = t
        self.w = {}
        self.r = {}
        self.name = name

    def __getitem__(self, idx):
        return View([self], self.t[idx])

    def v(self, ap):
        return View([self], ap)


class Eng:
    def __init__(self, h, sid, name):
        self.h = h
        self.sid = sid
        self.cnt = 0
        self.waited = {}
        self.name = name
        self.old = []


class KB:
    NDS = 48
    EPOCH = 20000

    def __init__(self, nc, stack):
        self.nc = nc
        self.sems = {}
        self.nsid = 0

        def mk(n):
            s = stack.enter_context(nc.semaphore(n))
            sid = self.nsid
            self.nsid += 1
            self.sems[sid] = s
            return sid
        self.mk = mk
        self.PE = Eng(nc.tensor, mk("s_pe"), "pe")
        self.DVE = Eng(nc.vector, mk("s_dve"), "dve")
        self.ACT = Eng(nc.scalar, mk("s_act"), "act")
        self.POOL = Eng(nc.gpsimd, mk("s_pool"), "pool")
        self.SP = Eng(nc.sync, None, "sp")
        self.engs = [self.PE, self.DVE, self.ACT, self.POOL]
        self.dsid = [mk("s_dma%d" % i) for i in range(self.NDS)]
        self.dval = [0] * self.NDS
        self.dnext = 0
        self.dbufs = {}
        self.ninst = 0

    def db(self, name, blk):
        k = (name, blk)
        if k not in self.dbufs:
            self.dbufs[k] = Buf(None, "%s_%s" % (name, blk))
        return self.dbufs[k]

    def _wait(self, eng, sid, val):
        if val <= 0 or eng.waited.get(sid, 0) >= val:
            return
        eng.h.wait_ge(self.sems[sid], val)
        eng.waited[sid] = val

    def _sync(self, eng, reads, writes):
        own = eng.sid
        for b in reads:
            for sid, v in b.w.items():
                self._wait(eng, sid, v)
        for b in writes:
            for sid, v in b.w.items():
                if sid != own:
                    self._wait(eng, sid, v)
            for sid, v in b.r.items():
                if sid != own:
                    self._wait(eng, sid, v)

    def _mark(self, reads, writes, sid, val):
        for b in reads:
            b.r[sid] = max(b.r.get(sid, 0), val)
        for b in writes:
            b.w = {sid: val}
            b.r = {}

    def E(self, eng, meth, **kw):
        reads, writes, args = [], [], {}
        for k_, v in kw.items():
            if isinstance(v, View):
                if k_ in ('out', 'accum_out', 'ap') or any(getattr(b_, 'excl', False) for b_ in v.bufs):
                    writes.extend(v.bufs)
                else:
                    reads.extend(v.bufs)
                args[k_] = v.ap
            else:
                args[k_] = v
        if eng.cnt >= self.EPOCH:
            eng.old.append((eng.sid, eng.cnt))
            eng.sid = self.mk("s_%s_e%d" % (eng.name, len(eng.old)))
            eng.cnt = 0
        self._sync(eng, reads, writes)
        inst = getattr(eng.h, meth)(**args)
        inst.then_inc(self.sems[eng.sid], 1)
        eng.cnt += 1
        self.ninst += 1
        self._mark(reads, writes, eng.sid, eng.cnt)
        return inst

    def mm(self, out, lhsT, rhs, start=True, stop=True, extra_w=()):
        return self.E(self.PE, 'matmul', out=out, lhsT=lhsT, rhs=rhs, start=start, stop=stop)

    def tr(self, out, in_, ident):
        return self.E(self.PE, 'transpose', out=out, in_=in_, identity=ident)

    def dma(self, out, in_, q=None):
        eng = self.SP
        s = self.dnext
        self.dnext = (self.dnext + 1) % self.NDS
        sid = self.dsid[s]
        self._wait(eng, sid, self.dval[s])
        self._sync(eng, in_.bufs, out.bufs)
        self.dval[s] += 16
        eng.h.dma_start(out=out.ap, in_=in_.ap).then_inc(self.sems[sid], 16)
        self.ninst += 1
        self._mark(in_.bufs, out.bufs, sid, self.dval[s])

    def barrier(self):
        for e in self.engs + [self.SP]:
            for o in self.engs:
                if o is not e:
                    if o.cnt > 0:
                        self._wait(e, o.sid, o.cnt)
                    elif o.old:
                        self._wait(e, o.old[-1][0], o.old[-1][1])
            for s in range(self.NDS):
                self._wait(e, self.dsid[s], self.dval[s])


def bc_last(ap, n):
    l = [list(x) for x in ap.ap]
    return bass.AP(ap.tensor, ap.offset, l + [[0, n]])


def bc_mid(ap, n):
    l = [list(x) for x in ap.ap]
    return bass.AP(ap.tensor, ap.offset, [l[0], [0, n]] + l[1:])


def build(T, SEG, phases=None, debug_out=(), nlayers=DEPTH):
    NBLK = T // NB
    nc = bass.Bass("TRN2", target_bir_lowering=False)
    dt = nc.dram_tensor
    x_in = dt("x", [T, D], F32, kind="ExternalInput").ap()
    p_in = dt("p", [DEPTH, T, DPLE], F32, kind="ExternalInput").ap()
    cst_in = dt("cst", list(CONST_ARR.shape), F32, kind="ExternalInput").ap()
    cp_in = dt("cp", [DEPTH, 128, NCP], F32, kind="ExternalInput").ap()
    bd_in = dt("bd", [DEPTH, 128, 16, 128], F32, kind="ExternalInput").ap()
    flag_in = dt("flag", [128, 1], F32, kind="ExternalInput").ap()
    w_in = dt("w_in", [DEPTH, D, NIN], F32, kind="ExternalInput").ap()
    w_branch = dt("w_branch", [DEPTH, 3, 512, D], F32, kind="ExternalInput").ap()
    w_out = dt("w_out", [DEPTH, D, D], F32, kind="ExternalInput").ap()
    w_mlp1 = dt("w_mlp1", [DEPTH, D, DFF], F32, kind="ExternalInput").ap()
    w_mlp2 = dt("w_mlp2", [DEPTH, DFF, D], F32, kind="ExternalInput").ap()
    w_pg = dt("w_ple_gate", [DEPTH, D, D], F32, kind="ExternalInput").ap()
    w_pp = dt("w_ple_proj", [DEPTH, DPLE, D], F32, kind="ExternalInput").ap()
    y_out = dt("y", [T, D], F32, kind="ExternalOutput").ap()
    dbg = {}
    okind = lambda n: "ExternalOutput" if n in debug_out else "Internal"
    xT = dt("xT", [8, 128, T], F32, kind=okind("xT")).ap()
    class _HT:
        SPLIT = 36

        def __init__(self):
            self.a = dt("hT", [self.SPLIT, 128, T], F32, kind=okind("hT")).ap()
            self.b = dt("hTb", [NHT - self.SPLIT, 128, T], F32, kind=okind("hT")).ap()

        def __getitem__(self, idx):
            f = idx[0]
            rest = tuple(idx[1:])
            if isinstance(f, slice):
                if f.start < self.SPLIT:
                    assert f.stop <= self.SPLIT
                    return self.a[(f,) + rest]
                return self.b[(slice(f.start - self.SPLIT, f.stop - self.SPLIT),) + rest]
            if f < self.SPLIT:
                return self.a[(f,) + rest]
            return self.b[(f - self.SPLIT,) + rest]
    hT = _HT()
    ofT = dt("ofT", [12, 128, T], F32, kind=okind("ofT")).ap()
    obT = dt("obT", [12, 128, T], BF16, kind=okind("obT")).ap()
    wspec = {
        'win': (8, WIN_TILES, lambda l: w_in[l]),
        'wb0': (4, [(i * 128, 128) for i in range(8)], lambda l: w_branch[l, 0]),
        'wb1': (4, [(i * 128, 128) for i in range(8)], lambda l: w_branch[l, 1]),
        'wb2': (4, [(i * 128, 128) for i in range(8)], lambda l: w_branch[l, 2]),
        'wout': (8, [(i * 128, 128) for i in range(8)], lambda l: w_out[l]),
        'wm1': (8, [(i * 128, 128) for i in range(32)], lambda l: w_mlp1[l]),
        'wm2': (32, [(i * 128, 128) for i in range(8)], lambda l: w_mlp2[l]),
        'wpg': (8, [(i * 128, 128) for i in range(8)], lambda l: w_pg[l]),
        'wpp': (2, [(i * 128, 128) for i in range(8)], lambda l: w_pp[l]),
    }
    ws = {n: dt("ws_" + n, [DEPTH, len(s[1]), 128, s[0], 128], BF16, kind="Internal").ap() for n, s in wspec.items()}

    with ExitStack() as st0:
        kb = KB(nc, st0)
        PE, DVE, ACT, POOL = kb.PE, kb.DVE, kb.ACT, kb.POOL
        E = kb.E

        uniq = [0]

        def sb(st, name, shape, dtype=F32):
            uniq[0] += 1
            return Buf(st.enter_context(nc.sbuf_tensor("sb%d_%s" % (uniq[0], name), shape, dtype)), name)

        ps = [Buf(st0.enter_context(nc.psum_tensor("ps%d" % i, [128, 512], F32)), "ps%d" % i) for i in range(8)]
        for b_ in ps:
            b_.excl = True
        cst = sb(st0, "cst", list(CONST_ARR.shape))
        cpt = [sb(st0, "cp%d" % l, [128, NCP]) for l in range(DEPTH)]
        flag = sb(st0, "flag", [128, 1])
        cbf = sb(st0, "cbf", [128, 128 * 2 + 512 * 3 + 256 * 2], BF16)
        kb.dma(cst[:], View([], cst_in[:, :]))
        for l in range(DEPTH):
            kb.dma(cpt[l][:], View([], cp_in[l]))
        kb.dma(flag[:], View([], flag_in[:, :]))

        def C(name):
            o, w = CONST_OFF[name]
            return cst[:, o:o + w]

        cb_off = {}
        o = 0
        for n in ['ident', 'ones', 'sm_f4', 'sm_b4', 'ident4', 'hgm_f4', 'hgm_b4']:
            w = CONST_OFF[n][1]
            cb_off[n] = (o, w)
            E(DVE, 'tensor_copy', out=cbf[:, o:o + w], in_=C(n))
            o += w

        def CB(name):
            o, w = cb_off[name]
            return cbf[:, o:o + w]

        crn = ['ident', 'ones', 'negones', 'tri_le', 'tri_ge', 'tri_gt', 'tri_lt', 'nm_f', 'nm_b']
        crt = sb(st0, "crt", [128, 128 * len(crn)], F32R)
        for i_, n in enumerate(crn):
            E(DVE, 'tensor_copy', out=crt[:, i_ * 128:(i_ + 1) * 128], in_=C(n))

        def CR(name):
            i_ = crn.index(name)
            return crt[:, i_ * 128:(i_ + 1) * 128]

        def asf(v):
            return View(v.bufs, v.ap.bitcast(F32))

        def cpv(l, name, i=0, n=1):
            o, w = CP[name]
            return cpt[l][:, o + i:o + i + n]

        def phase_wprep():
            with ExitStack() as st:
                sf = [sb(st, "wpf%d" % i, [128, 4096]) for i in range(2)]
                sbf = [sb(st, "wpb%d" % i, [128, 4096], BF16) for i in range(2)]
                it = 0
                for l in range(DEPTH):
                    for name, (n_k, tiles, srcf) in wspec.items():
                        src = srcf(l).rearrange("(k p) n -> p k n", p=128)
                        gmax = 4 if n_k <= 8 else 1
                        ti = 0
                        while ti < len(tiles):
                            g = 1
                            while (g < gmax and ti + g < len(tiles) and tiles[ti + g][1] == 128 and tiles[ti][1] == 128
                                   and tiles[ti + g][0] == tiles[ti][0] + 128 * g):
                                g += 1
                            c0 = tiles[ti][0]
                            gw = sum(tiles[ti + j][1] for j in range(g))
                            s = it % 2
                            it += 1
                            fv = sf[s].t[:, 0:n_k * gw].rearrange("p (k c) -> p k c", k=n_k)
                            bv = sbf[s].t[:, 0:n_k * gw].rearrange("p (k c) -> p k c", k=n_k)
                            kb.dma(sf[s].v(fv), View([], src[:, :, c0:c0 + gw]))
                            E(DVE if it % 2 else ACT, 'tensor_copy' if it % 2 else 'copy', out=sbf[s].v(bv), in_=sf[s].v(fv))
                            for j in range(g):
                                wdt = tiles[ti + j][1]
                                kb.dma(View([kb.db('ws_' + name, l)], ws[name][l, ti + j, :, :, 0:wdt]),
                                       sbf[s].v(bv[:, :, j * 128:j * 128 + wdt]))
                            ti += g
            kb.barrier()

        wctr = [0]
        pctr = [0]

        def dense(wb, l, name, tile_ids, rhs_fn, N, evac, psbanks=(0, 1, 2)):
            n_k = wspec[name][0]
            tiles = wspec[name][1]
            tile_ids = list(tile_ids)
            depth = len(wb) - 1
            base = wctr[0]
            wctr[0] += len(tile_ids)

            def wload(i):
                kb.dma(wb[(base + i) % len(wb)][:, 0:n_k, :], View([kb.db('ws_' + name, l)], ws[name][l, tile_ids[i]]))
            for i in range(min(depth, len(tile_ids))):
                wload(i)
            for i, ti in enumerate(tile_ids):
                wdt = tiles[ti][1]
                s = (base + i) % len(wb)
                if i + depth < len(tile_ids):
                    wload(i + depth)
                pb = ps[psbanks[pctr[0] % len(psbanks)]]
                pctr[0] += 1
                for k in range(n_k):
                    kb.mm(pb[:, 0:N], wb[s][:, k, :], rhs_fn(k), start=(k == 0), stop=(k == n_k - 1))
                evac(ti, pb[0:wdt, 0:N], wdt)

        def layer_norm(st_bufs, src, g_fn, b_fn, dst_f=None, dst_b=None, N=NB):
            sq, xb_, mean, m2, rstd, tmp = st_bufs
            pm, pq = ps[3], ps[4]
            for c in range(8):
                E(ACT, 'activation', out=sq[c % 2][:, 0:N], in_=src[:, c, 0:N], func=AF.Square)
                E(DVE, 'tensor_copy', out=xb_[c % 2][:, 0:N], in_=src[:, c, 0:N])
                kb.mm(pm[:, 0:N], CB('ones'), xb_[c % 2][:, 0:N], start=(c == 0), stop=(c == 7))
                kb.mm(pq[:, 0:N], CB('ones'), sq[c % 2][:, 0:N], start=(c == 0), stop=(c == 7))
            E(ACT, 'activation', out=mean[:, 0:N], in_=pm[:, 0:N], func=AF.Copy, scale=1.0 / D)
            E(DVE, 'tensor_tensor', out=m2[:, 0:N], in0=mean[:, 0:N], in1=mean[:, 0:N], op=ALU.mult)
            E(DVE, 'scalar_tensor_tensor', out=m2[:, 0:N], in0=pq[:, 0:N], scalar=1.0 / D, in1=m2[:, 0:N],
              op0=ALU.mult, op1=ALU.subtract)
            E(DVE, 'tensor_scalar', out=m2[:, 0:N], in0=m2[:, 0:N], scalar1=0.0, scalar2=1e-5, op0=ALU.max, op1=ALU.add)
            E(ACT, 'activation', out=m2[:, 0:N], in_=m2[:, 0:N], func=AF.Ln)
            E(ACT, 'activation', out=rstd[:, 0:N], in_=m2[:, 0:N], func=AF.Exp, scale=-0.5)
            for c in range(8):
                t_ = tmp[c % 2]
                E(DVE, 'tensor_tensor', out=t_[:, 0:N], in0=src[:, c, 0:N], in1=mean[:, 0:N], op=ALU.subtract)
                E(DVE, 'tensor_tensor', out=t_[:, 0:N], in0=t_[:, 0:N], in1=rstd[:, 0:N], op=ALU.mult)
                if dst_f is not None:
                    E(ACT, 'activation', out=dst_f[:, c, 0:N], in_=t_[:, 0:N], func=AF.Identity, scale=g_fn(c), bias=b_fn(c))
                if dst_b is not None:
                    E(ACT, 'activation', out=dst_b[:, c, 0:N], in_=t_[:, 0:N], func=AF.Identity, scale=g_fn(c), bias=b_fn(c))

        def ln_bufs(st):
            return ([sb(st, "ln_sq%d" % i, [128, NB], BF16) for i in range(2)],
                    [sb(st, "ln_xb%d" % i, [128, NB], BF16) for i in range(2)],
                    sb(st, "ln_mean", [128, NB]), sb(st, "ln_m2", [128, NB]), sb(st, "ln_rstd", [128, NB]),
                    [sb(st, "ln_tmp%d" % i, [128, NB]) for i in range(2)])

        def phase_embed():
            with ExitStack() as st:
                xin = [sb(st, "e_xin%d" % i, [128, 4, D]) for i in range(2)]
                xf = sb(st, "e_xf", [128, 8, NB])
                xo = sb(st, "e_xo", [128, 8, NB])
                lb = ln_bufs(st)
                for b in range(NBLK):
                    t0 = b * NB
                    xi = xin[b % 2]
                    kb.dma(xi[:], View([], x_in[t0:t0 + NB, :].rearrange("(j p) d -> p j d", p=128)))
                    for c in range(8):
                        pb = ps[c % 3]
                        for j in range(4):
                            kb.tr(pb[:, j * 128:(j + 1) * 128], xi[:, j, c * 128:(c + 1) * 128], C('ident'))
                        E(ACT if c % 2 else DVE, 'copy' if c % 2 else 'tensor_copy', out=xf[:, c, :], in_=pb[:, :])
                    layer_norm(lb, xf, lambda c: cpv(0, 'emb_g', c), lambda c: cpv(0, 'emb_b', c), dst_f=xo)
                    kb.dma(View([kb.db('xT', b)], xT[:, :, t0:t0 + NB].rearrange("c p t -> p c t")), xo[:])
            kb.barrier()

        def phase_proj(l):
            with ExitStack() as st:
                xin = [sb(st, "p_xin%d" % i, [128, 8, NB]) for i in range(2)]
                xb_ = [sb(st, "p_xb%d" % i, [128, 8, NB], BF16) for i in range(2)]
                stg = [sb(st, "p_stg%d" % i, [128, NB]) for i in range(6)]
                wb = [sb(st, "p_wb%d" % i, [128, 8, 128], BF16) for i in range(4)]
                sctr = [0]
                kb.dma(xin[0][:], View([kb.db('xT', 0)], xT[:, :, 0:NB].rearrange("c p t -> p c t")))
                for b in range(NBLK):
                    t0 = b * NB
                    xi, xbb = xin[b % 2], xb_[b % 2]
                    if b + 1 < NBLK:
                        kb.dma(xin[(b + 1) % 2][:], View([kb.db('xT', b + 1)], xT[:, :, t0 + NB:t0 + 2 * NB].rearrange("c p t -> p c t")))
                    E(ACT, 'copy', out=xbb[:, 0:4, :], in_=xi[:, 0:4, :])
                    E(DVE, 'tensor_copy', out=xbb[:, 4:8, :], in_=xi[:, 4:8, :])

                    def evac(ti, pv, wdt):
                        sg_ = stg[sctr[0] % len(stg)]
                        sctr[0] += 1
                        if GA <= ti < AB:
                            E(ACT, 'activation', out=sg_[0:wdt, :], in_=pv, func=AF.Sigmoid)
                        elif sctr[0] % 2:
                            E(DVE, 'tensor_copy', out=sg_[0:wdt, :], in_=pv)
                        else:
                            E(ACT, 'copy', out=sg_[0:wdt, :], in_=pv)
                        kb.dma(View([kb.db('hT%d' % ti, b)], hT[ti, 0:wdt, t0:t0 + NB]), sg_[0:wdt, :])
                    dense(wb, l, 'win', range(NHT), lambda k: xbb[:, k, :], NB, evac)
            kb.barrier()

        def phase_post(l, last):
            with ExitStack() as st:
                xres = sb(st, "q_xres", [128, 8, NB])
                hid = sb(st, "q_hid", [128, 32, NB], BF16)
                sig = sb(st, "q_sig", [128, 8, NB])
                acc = sb(st, "q_acc", [128, 8 * NB])
                accb = sb(st, "q_accb", [128, 8, NB], BF16)
                x1 = sb(st, "q_x1", [128, 8, NB])
                x1b = sb(st, "q_x1b", [128, 8, NB], BF16)
                rl = [sb(st, "q_rl%d" % i, [128, NB]) for i in range(2)]
                pin = sb(st, "q_pin", [128, 4, DPLE])
                pT = sb(st, "q_pT", [128, 2, NB], BF16)
                wb = [sb(st, "q_wb%d" % i, [128, 32, 128], BF16) for i in range(3)]
                lb = ln_bufs(st)
                acc3 = acc.v(acc.t[:, :].rearrange("p (c t) -> p c t", c=8))
                accv = lambda c: acc.v(acc.t[:, c * NB:(c + 1) * NB])
                for b in range(NBLK):
                    t0 = b * NB
                    kb.dma(xres[:], View([kb.db('xT', b)], xT[:, :, t0:t0 + NB].rearrange("c p t -> p c t")))
                    kb.dma(hid[:, 0:12, :], View([kb.db('obT', b)], obT[:, :, t0:t0 + NB].rearrange("c p t -> p c t")))
                    kb.dma(pin[:], View([], p_in[l, t0:t0 + NB, :].rearrange("(j p) d -> p j d", p=128)))
                    for c in range(2):
                        pb = ps[5 + c]
                        for j in range(4):
                            kb.tr(pb[:, j * 128:(j + 1) * 128], pin[:, j, c * 128:(c + 1) * 128], C('ident'))
                        E(ACT, 'copy', out=pT[:, c, :], in_=pb[:, :])
                    for n in range(3):
                        gt0 = [GA, GB, GC][n]
                        kb.dma(sig[:], View([kb.db('hT%d' % (gt0 + c), b) for c in range(8)],
                                            hT[gt0:gt0 + 8, :, t0:t0 + NB].rearrange("c p t -> p c t")))

                        def evac(ti, pv, wdt, n=n):
                            if n == 0:
                                E(DVE, 'tensor_tensor', out=accv(ti), in0=pv, in1=sig[:, ti, :], op=ALU.mult)
                            else:
                                r_ = rl[ti % 2]
                                E(DVE, 'tensor_tensor', out=r_[:, :], in0=pv, in1=sig[:, ti, :], op=ALU.mult)
                                if n == 1:
                                    E(POOL, 'tensor_tensor', out=accv(ti), in0=accv(ti), in1=r_[:, :], op=ALU.add)
                                else:
                                    E(POOL, 'tensor_tensor', out=accb[:, ti, :], in0=accv(ti), in1=r_[:, :], op=ALU.add)
                        dense(wb, l, 'wb%d' % n, range(8), lambda k, n=n: hid[:, n * 4 + k, :], NB, evac)

                    def evac(ti, pv, wdt):
                        E(DVE, 'scalar_tensor_tensor', out=accv(ti), in0=xres[:, ti, :], scalar=ALPHA, in1=pv,
                          op0=ALU.mult, op1=ALU.add)
                    dense(wb, l, 'wout', range(8), lambda k: accb[:, k, :], NB, evac)
                    layer_norm(lb, acc3, lambda c: cpv(l, 'ln1_g', c), lambda c: cpv(l, 'ln1_b', c), dst_f=x1, dst_b=x1b)

                    def evac(ti, pv, wdt):
                        r_ = rl[ti % 2]
                        E(ACT, 'activation', out=r_[:, :], in_=pv, func=AF.Relu)
                        E(POOL if ti % 2 else DVE, 'tensor_tensor', out=hid[:, ti, :], in0=r_[:, :], in1=r_[:, :], op=ALU.mult)
                    dense(wb, l, 'wm1', range(32), lambda k: x1b[:, k, :], NB, evac)

                    def evac(ti, pv, wdt):
                        E(ACT, 'activation', out=sig[:, ti, :], in_=pv, func=AF.Sigmoid)
                    dense(wb, l, 'wpg', range(8), lambda k: x1b[:, k, :], NB, evac)

                    def evac(ti, pv, wdt):
                        E(DVE, 'tensor_tensor', out=sig[:, ti, :], in0=pv, in1=sig[:, ti, :], op=ALU.mult)
                    dense(wb, l, 'wpp', range(8), lambda k: pT[:, k, :], NB, evac)

                    def evac(ti, pv, wdt):
                        E(DVE, 'scalar_tensor_tensor', out=accv(ti), in0=x1[:, ti, :], scalar=ALPHA, in1=pv,
                          op0=ALU.mult, op1=ALU.add)
                        E(POOL, 'tensor_tensor', out=accv(ti), in0=accv(ti), in1=sig[:, ti, :], op=ALU.add)
                    dense(wb, l, 'wm2', range(8), lambda k: hid[:, k, :], NB, evac)
                    layer_norm(lb, acc3, lambda c: cpv(l, 'ln2_g', c), lambda c: cpv(l, 'ln2_b', c), dst_f=xres)
                    if not last:
                        kb.dma(View([kb.db('xT', b)], xT[:, :, t0:t0 + NB].rearrange("c p t -> p c t")), xres[:])
                    else:
                        ov = x1.v(x1.t[:, :, :].rearrange("p c t -> p (c t)").rearrange("p (j d) -> p j d", j=4))
                        for j in range(4):
                            for c2 in range(2):
                                pb = ps[5 + c2]
                                for cc in range(4):
                                    c = c2 * 4 + cc
                                    kb.tr(pb[:, cc * 128:(cc + 1) * 128], xres[:, c, j * 128:(j + 1) * 128], C('ident'))
                                E(ACT if c2 else DVE, 'copy' if c2 else 'tensor_copy',
                                  out=x1.v(ov.ap[:, j, c2 * 512:(c2 + 1) * 512]), in_=pb[:, :])
                        kb.dma(View([], y_out[t0:t0 + NB, :].rearrange("(j p) d -> p j d", p=128)), ov)
            kb.barrier()

        SEGB = SEG // NB

        def blk_order(d):
            return list(range(NBLK)) if d == 0 else list(range(NBLK - 1, -1, -1))

        def load_halo(raw, tile0, nt, b):
            t0 = b * NB
            lo, hi = max(t0 - 1, 0), min(t0 + NB + 2, T)
            if lo > t0 - 1:
                E(POOL, 'memset', ap=raw[:, :, 0:1], constant=0.0)
            if hi < t0 + NB + 2:
                E(POOL, 'memset', ap=raw[:, :, NB + 1:NB + 3], constant=0.0)
            bl = [kb.db('hT%d' % (tile0 + c), bb) for c in range(nt) for bb in (b - 1, b, b + 1) if 0 <= bb < NBLK]
            kb.dma(raw[:, :, lo - (t0 - 1):hi - (t0 - 1)], View(bl, hT[tile0:tile0 + nt, :, lo:hi].rearrange("c p t -> p c t")))
            if b % SEGB == 0 and b > 0:
                E(DVE, 'tensor_scalar', out=raw[:, :, 0:1], in0=raw[:, :, 0:1], scalar1=flag[:, 0:1], scalar2=None, op0=ALU.mult)
            if b % SEGB == SEGB - 1 and b < NBLK - 1:
                E(DVE, 'tensor_scalar', out=raw[:, :, NB + 1:NB + 3], in0=raw[:, :, NB + 1:NB + 3], scalar1=flag[:, 0:1],
                  scalar2=None, op0=ALU.mult)

        def conv4(eng, out, raw, t, wcol, bias=None, tmp=None):
            if bias is None:
                E(eng, 'tensor_scalar', out=out, in0=raw[:, t, 0:NB], scalar1=wcol(0), scalar2=0.0, op0=ALU.mult, op1=ALU.add)
            else:
                E(eng, 'tensor_scalar', out=out, in0=raw[:, t, 0:NB], scalar1=wcol(0), scalar2=bias, op0=ALU.mult, op1=ALU.add)
            for k in range(1, 4):
                if tmp is not None:
                    E(POOL, 'tensor_scalar', out=tmp, in0=raw[:, t, k:k + NB], scalar1=wcol(k), scalar2=0.0, op0=ALU.mult, op1=ALU.add)
                    E(POOL, 'tensor_tensor', out=out, in0=out, in1=tmp, op=ALU.add)
                else:
                    E(DVE, 'scalar_tensor_tensor', out=out, in0=raw[:, t, k:k + NB], scalar=wcol(k), in1=out, op0=ALU.mult, op1=ALU.add)

        def crosses(b, d):
            return (d == 0 and b > 0 and b % SEGB == 0) or (d == 1 and b < NBLK - 1 and b % SEGB == SEGB - 1)

        def phase_lru(l):
            with ExitStack() as st:
                bdf = sb(st, "l_bdf", [128, 16, 128])
                bdb = sb(st, "l_bdb", [128, 16, 128], BF16)
                ccol = sb(st, "l_ccol", [128, 8])
                raw = [sb(st, "l_raw%d" % i, [128, 4, NB + 3]) for i in range(2)]
                xc = sb(st, "l_xc", [128, 4, NB])
                xcb = sb(st, "l_xcb", [128, 4, NB], BF16)
                r_ = [sb(st, "l_r%d" % i, [128, NB]) for i in range(4)]
                i_ = [sb(st, "l_i%d" % i, [128, NB]) for i in range(4)]
                a_ = [sb(st, "l_a%d" % i, [128, NB]) for i in range(4)]
                u_ = [sb(st, "l_u%d" % i, [128, NB]) for i in range(4)]
                hh = [sb(st, "l_h%d" % i, [128, 4, NB]) for i in range(2)]
                hf = sb(st, "l_hf", [128, 4, NB])
                cg = sb(st, "l_cg", [128, 4, NB])
                ob = [sb(st, "l_ob%d" % i, [128, 4, NB], BF16) for i in range(2)]
                carry = sb(st, "l_carry", [128, 4])
                kb.dma(bdf[:], View([], bd_in[l]))
                E(DVE, 'tensor_copy', out=bdb[:], in_=bdf[:])
                o_, w_ = CP['lru_lam']
                E(ACT, 'activation', out=ccol[:], in_=cpt[l][:, o_:o_ + 8], func=AF.Sigmoid)
                E(ACT, 'activation', out=ccol[:], in_=ccol[:], func=AF.Ln)
                E(DVE, 'tensor_scalar', out=ccol[:], in0=ccol[:], scalar1=8.0, scalar2=None, op0=ALU.mult)
                for d in range(2):
                    for bi, b in enumerate(blk_order(d)):
                        t0 = b * NB
                        rw = raw[bi % 2]
                        h = hh[bi % 2]
                        load_halo(rw, CX, 4, b)
                        if d == 1:
                            kb.dma(hf[:], View([kb.db('ofT%d' % (8 + c), b) for c in range(4)],
                                               ofT[8:12, :, t0:t0 + NB].rearrange("c p t -> p c t")))
                            kb.dma(cg[:], View([kb.db('hT%d' % (CG + c), b) for c in range(4)],
                                               hT[CG:CG + 4, :, t0:t0 + NB].rearrange("c p t -> p c t")))
                        if bi == 0:
                            E(DVE, 'memset', ap=carry[:], constant=0.0)
                        elif crosses(b, d):
                            E(DVE, 'tensor_scalar', out=carry[:], in0=carry[:], scalar1=flag[:, 0:1], scalar2=None, op0=ALU.mult)
                        for t in range(4):
                            conv4(POOL, xc[:, t, :], rw, t, lambda k: cpv(l, 'lru_conv', t * 4 + k), bias=cpv(l, 'lru_cb', t))
                            E(ACT, 'copy', out=xcb[:, t, :], in_=xc[:, t, :])
                            kb.mm(ps[2 * t][:, :], bdb[:, d * 4 + t, :], xcb[:, t, :])
                            kb.mm(ps[2 * t + 1][:, :], bdb[:, 8 + d * 4 + t, :], xcb[:, t, :])
                        for t in range(4):
                            E(ACT, 'activation', out=r_[t][:, :], in_=ps[2 * t][:, :], func=AF.Sigmoid, bias=cpv(l, 'lru_ba', d * 4 + t))
                            E(ACT, 'activation', out=i_[t][:, :], in_=ps[2 * t + 1][:, :], func=AF.Sigmoid, bias=cpv(l, 'lru_bx', d * 4 + t))
                        for t in range(4):
                            E(ACT, 'activation', out=a_[t][:, :], in_=r_[t][:, :], func=AF.Exp, scale=ccol[:, d * 4 + t:d * 4 + t + 1])
                        for t in range(4):
                            E(POOL, 'tensor_tensor', out=r_[t][:, :], in0=a_[t][:, :], in1=a_[t][:, :], op=ALU.mult)
                            E(DVE, 'tensor_scalar', out=r_[t][:, :], in0=r_[t][:, :], scalar1=-1.0, scalar2=1.0, op0=ALU.mult, op1=ALU.add)
                            E(POOL, 'tensor_tensor', out=i_[t][:, :], in0=i_[t][:, :], in1=xc[:, t, :], op=ALU.mult)
                        for t in range(4):
                            E(ACT, 'activation', out=r_[t][:, :], in_=r_[t][:, :], func=AF.Sqrt)
                        for t in range(4):
                            E(DVE, 'tensor_tensor', out=u_[t][:, :], in0=r_[t][:, :], in1=i_[t][:, :], op=ALU.mult)
                            if d == 0:
                                E(DVE, 'tensor_tensor_scan', out=h[:, t, :], data0=a_[t][:, :], data1=u_[t][:, :],
                                  initial=carry[:, t:t + 1], op0=ALU.mult, op1=ALU.add)
                            else:
                                E(DVE, 'tensor_tensor_scan', out=h[:, t, ::-1], data0=a_[t][:, ::-1], data1=u_[t][:, ::-1],
                                  initial=carry[:, t:t + 1], op0=ALU.mult, op1=ALU.add)
                        E(DVE, 'tensor_copy', out=carry[:], in_=h[:, :, NB - 1] if d == 0 else h[:, :, 0])
                        if d == 0:
                            kb.dma(View([kb.db('ofT%d' % (8 + c), b) for c in range(4)],
                                        ofT[8:12, :, t0:t0 + NB].rearrange("c p t -> p c t")), h[:])
                        else:
                            o2 = ob[bi % 2]
                            E(ACT, 'activation', out=cg[:], in_=cg[:], func=AF.Gelu_apprx_tanh)
                            E(POOL, 'tensor_tensor', out=hf[:], in0=hf[:], in1=h[:], op=ALU.add)
                            E(DVE, 'tensor_tensor', out=o2[:], in0=hf[:], in1=cg[:], op=ALU.mult)
                            kb.dma(View([kb.db('obT', b)], obT[8:12, :, t0:t0 + NB].rearrange("c p t -> p c t")), o2[:])
            kb.barrier()

        def phase_hg(l):
            with ExitStack() as st:
                lbc = sb(st, "g_lbc", [128, 4])
                omlb = sb(st, "g_omlb", [128, 4])
                nomlb = sb(st, "g_nomlb", [128, 4])
                qT = [sb(st, "g_qT%d" % i, [128, 4, NB]) for i in range(2)]
                fT = [sb(st, "g_fT%d" % i, [128, 4, NB]) for i in range(2)]
                vT = [sb(st, "g_vT%d" % i, [128, 4, NB]) for i in range(2)]
                s_ = [sb(st, "g_s%d" % i, [128, NB]) for i in range(4)]
                f_ = [sb(st, "g_f%d" % i, [128, NB]) for i in range(4)]
                kk = [sb(st, "g_kk%d" % i, [128, NB]) for i in range(4)]
                bq = [sb(st, "g_bq%d" % i, [128, NB]) for i in range(4)]
                dq_ = [sb(st, "g_dq%d" % i, [128, NB]) for i in range(4)]
                e1 = [sb(st, "g_e1%d" % i, [128, NB]) for i in range(4)]
                e2 = [sb(st, "g_e2%d" % i, [128, NB]) for i in range(4)]
                qt = sb(st, "g_qt", [128, 4, NB], BF16)
                kdT = sb(st, "g_kdT", [128, 4, NB], BF16)
                ebl = sb(st, "g_ebl", [128, 4, 8])
                vtok = sb(st, "g_vtok", [128, 4, 512], BF16)
                ktok = sb(st, "g_ktok", [128, 4, 512], BF16)
                vTb = sb(st, "g_vTb", [128, 4, NB], BF16)
                AT = [sb(st, "g_AT%d" % i, [128, 4, 64], BF16) for i in range(2)]
                S = sb(st, "g_S", [128, 4, 128])
                Spb2 = [sb(st, "g_Spb%d" % i, [128, 4, 128], BF16) for i in range(2)]
                ost = [sb(st, "g_ost%d" % i, [128, 4, NB]) for i in range(2)]
                of_ = sb(st, "g_of", [128, 4, NB])
                gz = sb(st, "g_gz", [128, 4, NB])
                oh = [sb(st, "g_oh%d" % i, [128, NB]) for i in range(2)]
                sqb = [sb(st, "g_sqb%d" % i, [128, NB], BF16) for i in range(2)]
                sd = [sb(st, "g_sd%d" % i, [128, NB]) for i in range(2)]
                obs = [sb(st, "g_obs%d" % i, [128, 4, NB], BF16) for i in range(2)]
                if l == 0:
                    E(DVE, 'memset', ap=lbc[:], constant=0.0)
                else:
                    o0, o1 = CP['lb0'][0], CP['lb1'][0]
                    E(DVE, 'tensor_tensor', out=lbc[:], in0=cpt[l][:, o1:o1 + 4], in1=cpt[l][:, o0:o0 + 4], op=ALU.subtract)
                    E(ACT, 'activation', out=lbc[:], in_=lbc[:], func=AF.Sigmoid)
                E(DVE, 'tensor_scalar', out=omlb[:], in0=lbc[:], scalar1=-1.0, scalar2=1.0, op0=ALU.mult, op1=ALU.add)
                E(DVE, 'tensor_scalar', out=nomlb[:], in0=omlb[:], scalar1=-1.0, scalar2=None, op0=ALU.mult)
                pso = [ps[0], ps[1], ps[2], ps[3]]
                psA, psT, psN = ps[4], ps[6], ps[7]
                psS2 = [ps[5], ps[7]]
                psTb = psT.v(psT.t[:, :].bitcast(BF16))
                for d in range(2):
                    HF = HFF if d == 0 else HFB
                    mask4 = CB('hgm_f4') if d == 0 else CB('hgm_b4')
                    for bi, b in enumerate(blk_order(d)):
                        t0 = b * NB
                        q_, fz, v_ = qT[bi % 2], fT[bi % 2], vT[bi % 2]
                        for (dst, tl) in ((q_, HQ), (fz, HF), (v_, HI)):
                            kb.dma(dst[:], View([kb.db('hT%d' % (tl + c), b) for c in range(4)],
                                                hT[tl:tl + 4, :, t0:t0 + NB].rearrange("c p t -> p c t")))
                        if d == 1:
                            kb.dma(of_[:], View([kb.db('ofT%d' % (4 + c), b) for c in range(4)],
                                                ofT[4:8, :, t0:t0 + NB].rearrange("c p t -> p c t")))
                            kb.dma(gz[:], View([kb.db('hT%d' % (HGT + c), b) for c in range(4)],
                                               hT[HGT:HGT + 4, :, t0:t0 + NB].rearrange("c p t -> p c t")))
                        if bi == 0:
                            E(DVE, 'memset', ap=S[:], constant=0.0)
                        elif crosses(b, d):
                            E(DVE, 'tensor_scalar', out=S[:], in0=S[:], scalar1=flag[:, 0:1], scalar2=None, op0=ALU.mult)
                        E(ACT, 'copy', out=vTb[:], in_=v_[:])
                        for j in range(4):
                            for h in range(4):
                                kb.tr(psTb.bufs[0].v(psTb.ap[:, h * 128:(h + 1) * 128]), vTb[:, h, j * 128:(j + 1) * 128], CB('ident'))
                            E(ACT, 'copy', out=vtok[:, j, :], in_=psT.v(psTb.ap[:, 0:512]))
                        for h in range(4):
                            E(ACT, 'activation', out=s_[h][:, :], in_=fz[:, h, :], func=AF.Sigmoid)
                        for h in range(4):
                            E(DVE, 'tensor_scalar', out=f_[h][:, :], in0=s_[h][:, :], scalar1=omlb[:, h:h + 1], scalar2=lbc[:, h:h + 1],
                              op0=ALU.mult, op1=ALU.add)
                            E(POOL, 'tensor_scalar', out=kk[h][:, :], in0=s_[h][:, :], scalar1=nomlb[:, h:h + 1], scalar2=omlb[:, h:h + 1],
                              op0=ALU.mult, op1=ALU.add)
                        for h in range(4):
                            E(ACT, 'activation', out=f_[h][:, :], in_=f_[h][:, :], func=AF.Ln)
                        blvs = []
                        for h in range(4):
                            bb, ff = bq[h], f_[h]
                            if d == 0:
                                E(DVE, 'tensor_tensor_scan', out=bb[:, :], data0=C('rst_f'), data1=ff[:, :], initial=0.0,
                                  op0=ALU.mult, op1=ALU.add)
                            else:
                                rb = C('rst_b')
                                E(DVE, 'tensor_tensor_scan', out=bb[:, ::-1], data0=View(rb.bufs, rb.ap[:, ::-1]), data1=ff[:, ::-1],
                                  initial=0.0, op0=ALU.mult, op1=ALU.add)
                            b3 = bb.t[:, :].rearrange("p (c j) -> p c j", j=64)
                            blv = b3[:, :, 63] if d == 0 else b3[:, :, 0]
                            E(DVE, 'tensor_tensor', out=dq_[h].v(dq_[h].t[:, :].rearrange("p (c j) -> p c j", j=64)), in0=bb.v(b3),
                              in1=bb.v(bc_last(blv, 64)), op=ALU.subtract)
                            blvs.append(blv)
                        for h in range(4):
                            E(ACT, 'activation', out=ebl[:, h, :], in_=bq[h].v(blvs[h]), func=AF.Exp)
                            E(ACT, 'activation', out=e1[h][:, :], in_=dq_[h][:, :], func=AF.Exp)
                            E(ACT, 'activation', out=e2[h][:, :], in_=dq_[h][:, :], func=AF.Exp, scale=-1.0)
                        for h in range(4):
                            E(POOL, 'tensor_tensor', out=qt[:, h, :], in0=q_[:, h, :], in1=e1[h][:, :], op=ALU.mult)
                            E(DVE, 'tensor_tensor', out=kdT[:, h, :], in0=kk[h][:, :], in1=e2[h][:, :], op=ALU.mult)
                        for j in range(4):
                            for h in range(4):
                                kb.tr(psTb.bufs[0].v(psTb.ap[:, h * 128:(h + 1) * 128]), kdT[:, h, j * 128:(j + 1) * 128], CB('ident'))
                            E(ACT, 'copy', out=ktok[:, j, :], in_=psT.v(psTb.ap[:, 0:512]))
                        jl = range(4) if d == 0 else range(3, -1, -1)
                        for j in jl:
                            at = AT[j % 2]
                            for hfh in range(2):
                                c = 2 * j + hfh
                                for h in range(4):
                                    kb.mm(psA[hfh * 64:(hfh + 1) * 64, h * 64:(h + 1) * 64], kdT[:, h, c * 64:(c + 1) * 64],
                                          qt[:, h, c * 64:(c + 1) * 64])
                            E(DVE, 'tensor_tensor', out=at.v(at.t[:, :, :].rearrange("p h i -> p (h i)")), in0=psA[:, 0:256], in1=mask4, op=ALU.mult)
                            for hfh in (range(2) if d == 0 else range(1, -1, -1)):
                                c = 2 * j + hfh
                                p0 = hfh * 64
                                eb = ebl.v(bc_last(ebl.t[:, :, c], 128))
                                Spb = Spb2[c % 2]
                                psS = psS2[c % 2]
                                for h in range(4):
                                    kb.mm(psS[:, h * 128:(h + 1) * 128], ktok[p0:p0 + 64, j, h * 128:(h + 1) * 128],
                                          vtok[p0:p0 + 64, j, h * 128:(h + 1) * 128])
                                E(DVE, 'tensor_tensor', out=Spb[:], in0=S[:], in1=eb, op=ALU.mult)
                                for h in range(4):
                                    kb.mm(pso[h][:, c * 64:(c + 1) * 64], Spb[:, h, :], qt[:, h, c * 64:(c + 1) * 64], start=True, stop=False)
                                    kb.mm(pso[h][:, c * 64:(c + 1) * 64], vtok[p0:p0 + 64, j, h * 128:(h + 1) * 128], at[p0:p0 + 64, h, :],
                                          start=False, stop=True)
                                for h in range(4):
                                    E(DVE, 'scalar_tensor_tensor', out=S[:, h, :], in0=S[:, h, :], scalar=ebl[:, h, c:c + 1],
                                      in1=psS[:, h * 128:(h + 1) * 128], op0=ALU.mult, op1=ALU.add)
                        if d == 0:
                            o1 = ost[bi % 2]
                            for h in range(4):
                                E(ACT, 'copy', out=o1[:, h, :], in_=pso[h][:, :])
                            kb.dma(View([kb.db('ofT%d' % (4 + c), b) for c in range(4)],
                                        ofT[4:8, :, t0:t0 + NB].rearrange("c p t -> p c t")), o1[:])
                        else:
                            o2 = obs[bi % 2]
                            E(ACT, 'activation', out=gz[:], in_=gz[:], func=AF.Silu)
                            for h in range(4):
                                oo, sq_, sd_ = oh[h % 2], sqb[h % 2], sd[h % 2]
                                E(DVE, 'tensor_tensor', out=oo[:, :], in0=pso[h][:, :], in1=of_[:, h, :], op=ALU.add)
                                E(ACT, 'activation', out=sq_[:, :], in_=oo[:, :], func=AF.Square)
                                kb.mm(psN[:, :], CB('ones'), sq_[:, :])
                                E(DVE, 'tensor_scalar', out=sd_[:, :], in0=psN[:, :], scalar1=1.0 / 128, scalar2=1e-6, op0=ALU.mult, op1=ALU.add)
                                E(ACT, 'activation', out=sd_[:, :], in_=sd_[:, :], func=AF.Ln)
                                E(ACT, 'activation', out=sd_[:, :], in_=sd_[:, :], func=AF.Exp, scale=-0.5)
                                E(POOL, 'tensor_tensor', out=oo[:, :], in0=oo[:, :], in1=sd_[:, :], op=ALU.mult)
                                E(DVE, 'scalar_tensor_tensor', out=o2[:, h, :], in0=oo[:, :], scalar=cpv(l, 'hg_nw'), in1=gz[:, h, :],
                                  op0=ALU.mult, op1=ALU.mult)
                            kb.dma(View([kb.db('obT', b)], obT[4:8, :, t0:t0 + NB].rearrange("c p t -> p c t")), o2[:])
            kb.barrier()

        class ColBuf:
            def __init__(self, bank, c0, c1, name=""):
                self.bank = bank
                self.t = bank.t
                self.c0, self.c1 = c0, c1

            def cols(self, a=None, b=None, rows=slice(None)):
                a = 0 if a is None else a
                b = (self.c1 - self.c0) if b is None else b
                return View([self.bank], self.t[rows, self.c0 + a:self.c0 + b])

            def bf(self, a, b):
                return View([self.bank], self.t[:, :].bitcast(BF16)[:, 2 * self.c0 + a:2 * self.c0 + b])

        def rr(gens):
            gens = list(gens)
            while gens:
                for g in list(gens):
                    try:
                        next(g)
                    except StopIteration:
                        gens.remove(g)

        NCH = 2
        NHC = 4 // NCH

        def phase_dn(l):
            with ExitStack() as st:
                raw = sb(st, "d_raw", [128, 12, NB + 3])
                cv = sb(st, "d_cv", [128, 12, NB])
                abT = sb(st, "d_abT", [16, NB])
                qn2 = [sb(st, "d_qn%d" % i, [128, 4, NB], BF16) for i in range(2)]
                kn2 = [sb(st, "d_kn%d" % i, [128, 4, NB], BF16) for i in range(2)]
                vTb2 = [sb(st, "d_vTb%d" % i, [128, 4, NB], BF16) for i in range(2)]
                ctmp = sb(st, "d_ctmp", [128, NB])
                sqb = [sb(st, "d_sqb%d" % i, [128, NB], BF16) for i in range(2)]
                sd = [sb(st, "d_sd%d" % i, [128, NB]) for i in range(2)]
                negA = sb(st, "d_negA", [128, 8])
                gt = sb(st, "d_gt", [128, 4, 16])
                g_ = sb(st, "d_g", [128, 4, 4])
                gR2 = [sb(st, "d_gR%d" % i, [128, 4, 4], F32R) for i in range(2)]
                beta2 = [sb(st, "d_beta%d" % i, [128, 4, 4]) for i in range(2)]
                nbeta2 = [sb(st, "d_nbeta%d" % i, [128, 4, 4]) for i in range(2)]
                of_ = sb(st, "d_of", [128, 4, NB])
                gz = sb(st, "d_gz", [128, 4, NB])
                oh = [sb(st, "d_oh%d" % i, [128, NB]) for i in range(2)]
                sqf = [sb(st, "d_sqf%d" % i, [128, NB], BF16) for i in range(2)]
                sdf = [sb(st, "d_sdf%d" % i, [128, NB]) for i in range(2)]
                obs = sb(st, "d_obs", [128, 4, NB], BF16)

                class CH:
                    pass
                chs = []
                for c in range(NCH):
                    ch = CH()
                    ch.c = c
                    ch.heads = list(range(c * NHC, (c + 1) * NHC))
                    n3 = [128, NHC, 128]
                    mk_ = lambda nm, dt_=F32, c=c, n3=n3: sb(st, "d%d_%s" % (c, nm), n3, dt_)
                    ch.Mh = mk_("Mh", F32R)
                    ch.Ecol = sb(st, "d%d_Ecol" % c, [128, 3, NHC])
                    ch.sc1 = sb(st, "d%d_sc1" % c, [128, NHC])
                    ch.Dc, ch.Dm, ch.attn, ch.AT = mk_("Dc", BF16), mk_("Dm", BF16), mk_("attn", BF16), mk_("AT", BF16)
                    ch.Nm, ch.NT = mk_("Nm", F32R), mk_("NT", F32R)
                    ch.Pb = [mk_("P%d" % i, F32R) for i in range(2)]
                    ch.PTb = [mk_("PT%d" % i, F32R) for i in range(2)]
                    ch.Rb = [mk_("R%d" % i, F32R) for i in range(2)]
                    ch.kbg, ch.vb, ch.nwT = mk_("kbg", F32R), mk_("vb", F32R), mk_("nwT", F32R)
                    ch.kd, ch.qd, ch.vnew = mk_("kd", BF16), mk_("qd", BF16), mk_("vnew", BF16)
                    ch.EG, ch.Stmp = mk_("EG"), mk_("Stmp")
                    ch.S, ch.Sb = mk_("S", F32R), mk_("Sb", BF16)
                    ch.ost2 = [sb(st, "d%d_ost%d" % (c, i), [128, NHC, NB]) for i in range(2)]
                    W_ = NHC * 128
                    assert NCH == 2 and W_ == 256
                    B0, B1, B2, B3 = [ps[4 * c + i] for i in range(4)]
                    ch.psD = ColBuf(B0, 0, 256)
                    ch.psPT = ColBuf(B0, 0, 256)
                    ch.psGc = ColBuf(B0, 256, 272)
                    ch.psAT = ColBuf(B0, 272, 400)
                    ch.psE = ColBuf(B1, 0, 256)
                    ch.psKV = ColBuf(B1, 256, 512)
                    ch.psG = ColBuf(B2, 0, 256)
                    ch.psA = ColBuf(B2, 256, 512)
                    ch.psT = ColBuf(B3, 0, 256)
                    ch.psS = ColBuf(B2, 256, 512)
                    chs.append(ch)
                psGT = ColBuf(ps[0], 400, 464)
                psNp = ColBuf(ps[3], 256, 512)
                psNf = ColBuf(ps[7], 256, 512)
                flat = lambda bf: bf.v(bf.t[:, :, :].rearrange("p h i -> p (h i)"))
                o_ = CP['alog'][0]
                E(ACT, 'activation', out=negA[:], in_=cpt[l][:, o_:o_ + 8], func=AF.Exp)
                E(DVE, 'tensor_scalar', out=negA[:], in0=negA[:], scalar1=-1.0, scalar2=None, op0=ALU.mult)

                def chain(ch, d, order, TRI, TRIR, TRICR, NMR, SM, slot):
                    qn, kn, vTb, gR, beta, nbeta = qn2[slot], kn2[slot], vTb2[slot], gR2[slot], beta2[slot], nbeta2[slot]
                    ost = ch.ost2[slot]
                    hs_ = ch.heads
                    h0 = hs_[0]
                    W_ = NHC * 128
                    for j in order:
                        cs = slice(j * 128, (j + 1) * 128)
                        gj = gR[:, j, h0:h0 + NHC]
                        gjf = asf(gj)
                        E(DVE, 'tensor_tensor', out=ch.Mh[:], in0=View(TRI.bufs, bc_mid(TRI.ap, NHC)),
                          in1=View(gjf.bufs, bc_last(gjf.ap, 128)), op=ALU.mult)
                        yield
                        for hi in range(NHC):
                            o2 = ch.psD.cols(hi * 128, (hi + 1) * 128)
                            kb.mm(o2, ch.Mh[:, hi, :], CR('ones'), start=True, stop=False)
                            kb.mm(o2, CR('negones'), ch.Mh[:, hi, :], start=False, stop=False)
                            kb.mm(o2, CR('ident'), NMR, start=False, stop=True)
                        kb.mm(ch.psGc.cols(0, NHC), TRIR, gj)
                        kb.mm(ch.psGc.cols(NHC, 2 * NHC), TRICR, gj)
                        kb.mm(ch.psGc.cols(2 * NHC, 3 * NHC), CR('ones'), gj)
                        for hi, h in enumerate(hs_):
                            kb.mm(ch.psG.cols(hi * 128, (hi + 1) * 128), kn[:, h, cs], kn[:, h, cs])
                            kb.mm(ch.psA.cols(hi * 128, (hi + 1) * 128), qn[:, h, cs], kn[:, h, cs])
                        for hi in range(NHC):
                            kb.mm(ch.psE.cols(hi * 128, (hi + 1) * 128), CR('ones'), ch.Mh[:, hi, :])
                        yield
                        E(ACT, 'activation', out=ch.Ecol.v(ch.Ecol.t[:, :, :].rearrange("p a h -> p (a h)")), in_=ch.psGc.cols(0, 3 * NHC), func=AF.Exp)
                        E(ACT, 'activation', out=flat(ch.Dc), in_=ch.psD.cols(), func=AF.Exp)
                        E(ACT, 'activation', out=flat(ch.EG), in_=ch.psE.cols(), func=AF.Exp)
                        yield
                        E(POOL, 'tensor_tensor', out=flat(ch.Dm), in0=flat(ch.Dc), in1=View(SM.bufs, SM.ap[:, 0:W_]), op=ALU.mult)
                        E(POOL, 'tensor_tensor', out=ch.qd[:], in0=qn[:, h0:h0 + NHC, cs], in1=ch.EG[:], op=ALU.mult)
                        yield
                        for hi, h in enumerate(hs_):
                            E(DVE, 'scalar_tensor_tensor', out=ch.Nm[:, hi, :], in0=ch.psG.cols(hi * 128, (hi + 1) * 128),
                              scalar=nbeta[:, j, h:h + 1], in1=ch.Dm[:, hi, :], op0=ALU.mult, op1=ALU.mult)
                        E(DVE, 'tensor_tensor', out=flat(ch.attn), in0=ch.psA.cols(), in1=flat(ch.Dc), op=ALU.mult)
                        yield
                        for hi in range(NHC):
                            kb.tr(View([ch.psT.bank], ch.psT.t[:, ch.psT.c0 + hi * 128:ch.psT.c0 + (hi + 1) * 128].bitcast(F32R)), ch.Nm[:, hi, :], CR('ident'))
                            kb.tr(ch.psAT.bf(hi * 128, (hi + 1) * 128), ch.attn[:, hi, :], CB('ident'))
                        for hi, h in enumerate(hs_):
                            kb.tr(ch.psKV.bf(hi * 128, (hi + 1) * 128), kn[:, h, cs], CB('ident'))
                            kb.tr(ch.psKV.bf(W_ + hi * 128, W_ + (hi + 1) * 128), vTb[:, h, cs], CB('ident'))
                        yield
                        E(ACT, 'copy', out=flat(ch.NT), in_=ch.psT.cols())
                        E(DVE, 'tensor_copy', out=flat(ch.AT), in_=ch.psAT.bf(0, W_))
                        E(DVE, 'tensor_tensor', out=ch.sc1[:], in0=beta[:, j, h0:h0 + NHC], in1=ch.Ecol[:, 0, :], op=ALU.mult)
                        kv3 = lambda a: View([ch.psKV.bank], ch.psKV.bf(a, a + W_).ap.rearrange("p (h i) -> p h i", h=NHC))
                        E(DVE, 'tensor_tensor', out=ch.kbg[:], in0=kv3(0), in1=ch.sc1.v(bc_last(ch.sc1.t[:, 0:NHC], 128)), op=ALU.mult)
                        E(DVE, 'tensor_tensor', out=ch.kd[:], in0=kv3(0), in1=ch.Ecol.v(bc_last(ch.Ecol.t[:, 1, :], 128)), op=ALU.mult)
                        E(DVE, 'tensor_tensor', out=ch.vb[:], in0=kv3(W_), in1=beta.v(bc_last(beta.t[:, j, h0:h0 + NHC], 128)), op=ALU.mult)
                        yield
                        P, PT = ch.Nm, ch.NT
                        R = ch.Rb[0]
                        ident_n = C('ident4')
                        E(DVE, 'tensor_tensor', out=flat(R), in0=asf(flat(ch.NT)), in1=View(ident_n.bufs, ident_n.ap[:, 0:W_]), op=ALU.add)
                        yield
                        for k in range(6):
                            Pn, PTn, Rn = ch.Pb[k % 2], ch.PTb[k % 2], ch.Rb[(k + 1) % 2]
                            for hi in range(NHC):
                                kb.mm(ch.psG.cols(hi * 128, (hi + 1) * 128), PT[:, hi, :], P[:, hi, :])
                            if k < 5:
                                for hi in range(NHC):
                                    kb.mm(ch.psPT.cols(hi * 128, (hi + 1) * 128), P[:, hi, :], PT[:, hi, :])
                            yield
                            E(ACT, 'copy', out=flat(Pn), in_=ch.psG.cols())
                            if k < 5:
                                E(DVE, 'tensor_copy', out=flat(PTn), in_=ch.psPT.cols())
                            yield
                            for hi in range(NHC):
                                kb.mm(ch.psT.cols(hi * 128, (hi + 1) * 128), Pn[:, hi, :], R[:, hi, :])
                            yield
                            E(DVE, 'tensor_tensor', out=flat(Rn), in0=ch.psT.cols(), in1=asf(flat(R)), op=ALU.add)
                            yield
                            P, PT, R = Pn, PTn, Rn
                        TT = R
                        for hi in range(NHC):
                            kb.mm(ch.psD.cols(hi * 128, (hi + 1) * 128), ch.kbg[:, hi, :], TT[:, hi, :])
                        yield
                        E(ACT, 'activation', out=flat(ch.nwT), in_=ch.psD.cols(), func=AF.Copy, scale=-1.0)
                        yield
                        for hi in range(NHC):
                            o2 = ch.psKV.cols(hi * 128, (hi + 1) * 128)
                            kb.mm(o2, TT[:, hi, :], ch.vb[:, hi, :], start=True, stop=False)
                            kb.mm(o2, ch.nwT[:, hi, :], ch.S[:, hi, :], start=False, stop=True)
                        yield
                        E(ACT, 'copy', out=flat(ch.vnew), in_=ch.psKV.cols())
                        yield
                        for hi in range(NHC):
                            o2 = ch.psE.cols(hi * 128, (hi + 1) * 128)
                            kb.mm(o2, ch.Sb[:, hi, :], ch.qd[:, hi, :], start=True, stop=False)
                            kb.mm(o2, ch.vnew[:, hi, :], ch.AT[:, hi, :], start=False, stop=True)
                        for hi in range(NHC):
                            kb.mm(ch.psS.cols(hi * 128, (hi + 1) * 128), ch.kd[:, hi, :], ch.vnew[:, hi, :])
                        yield
                        E(DVE, 'tensor_tensor', out=ch.Stmp[:], in0=asf(ch.S[:]), in1=ch.Ecol.v(bc_last(ch.Ecol.t[:, 2, :], 128)), op=ALU.mult)
                        E(DVE, 'tensor_tensor', out=flat(ch.S), in0=flat(ch.Stmp), in1=ch.psS.cols(), op=ALU.add)
                        E(ACT, 'copy', out=ch.Sb[:], in_=asf(ch.S[:]))
                        E(ACT, 'copy', out=ost[:, :, cs], in_=View([ch.psE.bank], ch.psE.cols().ap.rearrange("p (h i) -> p h i", h=NHC)))
                        yield

                for d in range(2):
                    TRI = C('tri_le') if d == 0 else C('tri_ge')
                    NMR = CR('nm_f') if d == 0 else CR('nm_b')
                    TRIR = CR('tri_le') if d == 0 else CR('tri_ge')
                    TRICR = CR('tri_gt') if d == 0 else CR('tri_lt')
                    SM = CB('sm_f4') if d == 0 else CB('sm_b4')
                    o_ = CP['dtb'][0]
                    dtb = cpt[l][:, o_ + d * 4:o_ + d * 4 + 4]
                    nAd = negA[:, d * 4:d * 4 + 4]
                    def prep_gen(b, slot):
                        t0 = b * NB
                        qn, kn, vTb, gR, beta, nbeta = qn2[slot], kn2[slot], vTb2[slot], gR2[slot], beta2[slot], nbeta2[slot]
                        load_halo(raw, DQ, 12, b)
                        kb.dma(abT[:], View([kb.db('hT%d' % AB, b)], hT[AB, 0:16, t0:t0 + NB]))
                        yield
                        for t in range(12):
                            if t % 2:
                                conv4(POOL, cv[:, t, :], raw, t, lambda k: cpv(l, 'dn_conv', t * 4 + k), tmp=ctmp[:, :])
                            else:
                                conv4(DVE, cv[:, t, :], raw, t, lambda k: cpv(l, 'dn_conv', t * 4 + k))
                            yield
                        for t3 in range(3):
                            E(ACT, 'activation', out=cv[:, t3 * 4:t3 * 4 + 4, :], in_=cv[:, t3 * 4:t3 * 4 + 4, :], func=AF.Silu)
                            yield
                        for t in range(8):
                            sq_, sd_ = sqb[t % 2], sd[t % 2]
                            E(ACT, 'activation', out=sq_[:, :], in_=cv[:, t, :], func=AF.Square)
                            for hf_ in range(2):
                                kb.mm(psNp.cols(), CB('ones'), sq_[:, hf_ * 256:(hf_ + 1) * 256])
                                E(DVE, 'tensor_scalar', out=sd_[:, hf_ * 256:(hf_ + 1) * 256], in0=psNp.cols(), scalar1=1e-6, scalar2=None, op0=ALU.add)
                            yield
                            E(ACT, 'activation', out=sd_[:, :], in_=sd_[:, :], func=AF.Ln)
                            E(ACT, 'activation', out=sd_[:, :], in_=sd_[:, :], func=AF.Exp, scale=-0.5)
                            dst = qn[:, t, :] if t < 4 else kn[:, t - 4, :]
                            E(DVE, 'scalar_tensor_tensor', out=dst, in0=cv[:, t, :], scalar=(128 ** -0.5 if t < 4 else 1.0),
                              in1=sd_[:, :], op0=ALU.mult, op1=ALU.mult)
                            yield
                        E(ACT, 'copy', out=vTb[:], in_=cv[:, 8:12, :])
                        yield
                        for j in range(4):
                            idt = C('ident')
                            kb.tr(psGT.cols(j * 16, (j + 1) * 16), abT[0:16, j * 128:(j + 1) * 128], View(idt.bufs, idt.ap[0:16, 0:16]))
                        E(ACT, 'copy', out=gt.v(gt.t[:, :, :].rearrange("p j c -> p (j c)")), in_=psGT.cols())
                        yield
                        E(DVE, 'tensor_tensor', out=g_[:], in0=gt[:, :, d * 4:d * 4 + 4], in1=View(dtb.bufs, bc_mid(dtb.ap, 4)), op=ALU.add)
                        E(ACT, 'activation', out=g_[:], in_=g_[:], func=AF.Exp)
                        E(DVE, 'tensor_scalar', out=g_[:], in0=g_[:], scalar1=1.0, scalar2=None, op0=ALU.add)
                        E(ACT, 'activation', out=g_[:], in_=g_[:], func=AF.Ln)
                        E(DVE, 'tensor_tensor', out=gR[:], in0=g_[:], in1=View(nAd.bufs, bc_mid(nAd.ap, 4)), op=ALU.mult)
                        E(ACT, 'activation', out=beta[:], in_=gt[:, :, 8 + d * 4:12 + d * 4], func=AF.Sigmoid)
                        E(DVE, 'tensor_scalar', out=nbeta[:], in0=beta[:], scalar1=-1.0, scalar2=None, op0=ALU.mult)
                        yield

                    def final_gen(b, slot):
                        t0 = b * NB
                        kb.dma(of_[:], View([kb.db('ofT%d' % c, b) for c in range(4)],
                                            ofT[0:4, :, t0:t0 + NB].rearrange("c p t -> p c t")))
                        kb.dma(gz[:], View([kb.db('hT%d' % (DZ + c), b) for c in range(4)],
                                           hT[DZ:DZ + 4, :, t0:t0 + NB].rearrange("c p t -> p c t")))
                        yield
                        E(ACT, 'activation', out=gz[:], in_=gz[:], func=AF.Silu)
                        yield
                        for h in range(4):
                            ch = chs[h // NHC]
                            hi = h % NHC
                            oo, sq_, sd_ = oh[h % 2], sqf[h % 2], sdf[h % 2]
                            E(DVE, 'tensor_tensor', out=oo[:, :], in0=ch.ost2[slot][:, hi, :], in1=of_[:, h, :], op=ALU.add)
                            E(ACT, 'activation', out=sq_[:, :], in_=oo[:, :], func=AF.Square)
                            for hf_ in range(2):
                                kb.mm(psNf.cols(), CB('ones'), sq_[:, hf_ * 256:(hf_ + 1) * 256])
                                E(DVE, 'tensor_scalar', out=sd_[:, hf_ * 256:(hf_ + 1) * 256], in0=psNf.cols(), scalar1=1.0 / 128, scalar2=1e-6,
                                  op0=ALU.mult, op1=ALU.add)
                            yield
                            E(ACT, 'activation', out=sd_[:, :], in_=sd_[:, :], func=AF.Ln)
                            E(ACT, 'activation', out=sd_[:, :], in_=sd_[:, :], func=AF.Exp, scale=-0.5)
                            E(POOL, 'tensor_tensor', out=oo[:, :], in0=oo[:, :], in1=sd_[:, :], op=ALU.mult)
                            E(DVE, 'scalar_tensor_tensor', out=obs[:, h, :], in0=oo[:, :], scalar=cpv(l, 'dn_nw'), in1=gz[:, h, :],
                              op0=ALU.mult, op1=ALU.mult)
                            yield
                        kb.dma(View([kb.db('obT', b)], obT[0:4, :, t0:t0 + NB].rearrange("c p t -> p c t")), obs[:])

                    blks = blk_order(d)
                    rr([prep_gen(blks[0], 0)])
                    pending = None
                    for bi, b in enumerate(blks):
                        t0 = b * NB
                        slot = bi % 2
                        for ch in chs:
                            if bi == 0:
                                E(DVE, 'memset', ap=asf(ch.S[:]), constant=0.0)
                                E(POOL, 'memset', ap=ch.Sb[:], constant=0.0)
                            elif crosses(b, d):
                                E(DVE, 'tensor_scalar', out=ch.S[:], in0=asf(ch.S[:]), scalar1=flag[:, 0:1], scalar2=None, op0=ALU.mult)
                                E(DVE, 'tensor_copy', out=ch.Sb[:], in_=asf(ch.S[:]))
                        order = list(range(4)) if d == 0 else list(range(3, -1, -1))
                        gens = [chain(ch, d, order, TRI, TRIR, TRICR, NMR, SM, slot) for ch in chs]
                        if bi + 1 < len(blks):
                            gens.append(prep_gen(blks[bi + 1], 1 - slot))
                        if pending is not None:
                            gens.append(final_gen(*pending))
                            pending = None
                        rr(gens)
                        if d == 0:
                            for ch in chs:
                                h0 = ch.heads[0]
                                kb.dma(View([kb.db('ofT%d' % c, b) for c in ch.heads],
                                            ofT[h0:h0 + NHC, :, t0:t0 + NB].rearrange("c p t -> p c t")), ch.ost2[slot][:])
                        else:
                            pending = (b, slot)
                    if pending is not None:
                        rr([final_gen(*pending)])
            kb.barrier()

        MIX = {'lru': phase_lru, 'hg': phase_hg, 'dn': phase_dn}
        run = phases if phases is not None else ['wprep', 'embed', 'proj', 'lru', 'hg', 'dn', 'post']
        if 'wprep' in run:
            phase_wprep()
        if 'embed' in run:
            phase_embed()
        for l in range(nlayers):
            if 'proj' in run:
                phase_proj(l)
            for m in ['lru', 'hg', 'dn']:
                if m in run and m in MIX:
                    MIX[m](l)
            if 'post' in run:
                phase_post(l, l == nlayers - 1)
        kb.barrier()
        print("instructions:", kb.ninst)
    return nc


def _colparams(W):
    L = DEPTH
    cp = np.zeros((L, 128, NCP), np.float32)

    def put(l, name, arr):
        o, w = CP[name]
        cp[l, :, o:o + w] = arr

    def chan(v, nt):
        return np.asarray(v).reshape(nt, 128).T
    for l in range(L):
        put(l, 'dn_conv', np.asarray(W['dn_conv_w'][l]).reshape(4, 12, 128).transpose(2, 1, 0).reshape(128, 48))
        put(l, 'lru_conv', np.asarray(W['lru_conv_w'][l]).reshape(4, 4, 128).transpose(2, 1, 0).reshape(128, 16))
        put(l, 'lru_cb', chan(W['lru_conv_b'][l], 4))
        put(l, 'lru_ba', np.asarray(W['lru_ba'][l]).reshape(2, 4, 128).transpose(2, 0, 1).reshape(128, 8))
        put(l, 'lru_bx', np.asarray(W['lru_bx'][l]).reshape(2, 4, 128).transpose(2, 0, 1).reshape(128, 8))
        put(l, 'lru_lam', np.asarray(W['lru_lambda'][l]).reshape(2, 4, 128).transpose(2, 0, 1).reshape(128, 8))
        put(l, 'dn_nw', np.asarray(W['dn_norm_w'][l]).reshape(128, 1))
        put(l, 'hg_nw', np.asarray(W['hg_norm_w'][l]).reshape(128, 1))
        for n in ['ln1_g', 'ln1_b', 'ln2_g', 'ln2_b']:
            put(l, n, chan(W[n][l], 8))
        put(l, 'lb0', chan(W['hg_lb_logits'][0], 4))
        put(l, 'lb1', chan(W['hg_lb_logits'][1], 4))
        put(l, 'emb_g', chan(W['emb_ln_g'], 8))
        put(l, 'emb_b', chan(W['emb_ln_b'], 8))
        put(l, 'alog', np.tile(np.asarray(W['dn_A_log'][l]).reshape(1, 8), (128, 1)))
        put(l, 'dtb', np.tile(np.asarray(W['dn_dt_bias'][l]).reshape(1, 8), (128, 1)))
    return cp


def _blockdiag(W):
    bd = np.zeros((DEPTH, 128, 16, 128), np.float32)
    for l in range(DEPTH):
        for ai, nm in enumerate(['lru_wa', 'lru_wx']):
            w = np.asarray(W[nm][l])
            for d in range(2):
                for t in range(4):
                    idx = ai * 8 + d * 4 + t
                    for s in range(2):
                        bd[l, s * 64:(s + 1) * 64, idx, s * 64:(s + 1) * 64] = w[d, 2 * t + s]
    return bd


def make_in_maps(W, xs, ps_, flags):
    cp = _colparams(W)
    bd = _blockdiag(W)
    common = dict(cst=CONST_ARR, cp=cp, bd=bd,
                  w_in=np.ascontiguousarray(W['w_in'], dtype=np.float32), w_branch=np.asarray(W['w_branch'], np.float32),
                  w_out=np.asarray(W['w_out'], np.float32), w_mlp1=np.asarray(W['w_mlp1'], np.float32),
                  w_mlp2=np.asarray(W['w_mlp2'], np.float32), w_ple_gate=np.asarray(W['w_ple_gate'], np.float32),
                  w_ple_proj=np.asarray(W['w_ple_proj'], np.float32))
    maps = []
    for x, p, f in zip(xs, ps_, flags):
        m = dict(common)
        m['x'] = np.ascontiguousarray(x, dtype=np.float32)
        m['p'] = np.ascontiguousarray(p, dtype=np.float32)
        m['flag'] = np.full((128, 1), f, np.float32)
        maps.append(m)
    return maps


_NC_CACHE = {}


def kernel(x_prompt, x_sample, p_prompt, p_sample, **W):
    x_prompt = np.asarray(x_prompt)
    x_sample = np.asarray(x_sample)
    p_prompt = np.asarray(p_prompt)
    p_sample = np.asarray(p_sample)
    T, SEG = 8192, 4096
    assign = [(0, 1), (2, 3), (4, 4), (5, 5), (6, 6), (7, 7)]
    xs, ps_, flags = [], [], []
    for c in range(2):
        xs.append(x_sample[c])
        ps_.append(p_sample[:, c])
        flags.append(1.0)
    for a, b in assign:
        xs.append(np.concatenate([x_prompt[a], x_prompt[b]], axis=0))
        ps_.append(np.concatenate([p_prompt[:, a], p_prompt[:, b]], axis=1))
        flags.append(0.0)
    if 'nc' not in _NC_CACHE:
        _NC_CACHE['nc'] = build(T, SEG)
    nc = _NC_CACHE['nc']
    maps = make_in_maps(W, xs, ps_, flags)
    res = run_bass_kernel_spmd(nc, maps, core_ids=list(range(NCORES)))
    y_prompt = np.zeros((8, 4096, D), np.float32)
    y_sample = np.zeros((2, 8192, D), np.float32)
    for c in range(2):
        y_sample[c] = res.results[c]['y']
    for i, (a, b) in enumerate(assign):
        y = res.results[2 + i]['y']
        y_prompt[a] = y[:4096]
        if b != a:
            y_prompt[b] = y[4096:]
    return (y_prompt, y_sample)
```
